# Optimizing a Trainium2 kernel written in Bass

```python
import math
import jax, jax.numpy as jnp
from jax import lax
import numpy as np

D_MODEL = 1024
BATCH = 8
SEQ = 2048
DEPTH = 4
DEC_BATCH = 128
DEC_SEQ = 1
PAST_LEN = 8192
PAGE_SIZE = 128

D_MIX = 2 * D_MODEL
ATT_HEADS = 8
ATT_KV_HEADS = 2
ATT_GROUP = ATT_HEADS // ATT_KV_HEADS
ATT_HEAD_DIM = 64
ATT_WIDTH = ATT_HEADS * ATT_HEAD_DIM
ATT_KV_WIDTH = ATT_KV_HEADS * ATT_HEAD_DIM
ATT_SCALE = ATT_HEAD_DIM ** -0.5
WINDOW = 128
ROPE_THETA = 500000.0
ROPE_DIM = ATT_HEAD_DIM // 4
LRU_WIDTH = 3 * D_MIX // 8
LRU_BLOCKS = 8
LRU_BLOCK = LRU_WIDTH // LRU_BLOCKS
LRU_C = 8.0
CONV_W = 4
SSD_WIDTH = D_MIX - ATT_WIDTH - LRU_WIDTH
SSD_HEAD_DIM = 64
SSD_HEADS = SSD_WIDTH // SSD_HEAD_DIM
SSD_GROUPS = 2
SSD_HPG = SSD_HEADS // SSD_GROUPS
SSD_STATE = 128
SSD_CHUNK = 128
SSD_CONV_CH = SSD_WIDTH + 2 * SSD_GROUPS * SSD_STATE
SPLIT_SIZES = (ATT_WIDTH, ATT_KV_WIDTH, ATT_KV_WIDTH, ATT_WIDTH,
               LRU_WIDTH, LRU_WIDTH,
               SSD_WIDTH, SSD_CONV_CH, SSD_HEADS)
SPLIT_POINTS = tuple(int(s) for s in np.cumsum(SPLIT_SIZES)[:-1])
D_IN_PROJ = int(sum(SPLIT_SIZES))
DEEPNORM_ALPHA = (2.0 * DEPTH) ** 0.25
DEEPNORM_BETA = (8.0 * DEPTH) ** -0.25
NORM_EPS = 1e-5
F32 = jnp.float32

kernel_name = 'hybrid_swa_rglru_ssd_step'


def layer_norm(x, g, b):
    xf = x.astype(F32)
    mu = jnp.mean(xf, -1, keepdims=True)
    var = jnp.mean(jnp.square(xf - mu), -1, keepdims=True)
    return ((xf - mu) * lax.rsqrt(var + NORM_EPS) * g + b).astype(x.dtype)


def partial_rope(x, pos):
    half = ROPE_DIM // 2
    inv = ROPE_THETA ** (-jnp.arange(half, dtype=F32) / half)
    ang = pos.astype(F32)[:, None] * inv[None, :]
    cos = jnp.cos(ang)[None, :, None, :]
    sin = jnp.sin(ang)[None, :, None, :]
    xr = x[..., :ROPE_DIM].astype(F32)
    x1, x2 = xr[..., :half], xr[..., half:]
    rot = jnp.concatenate([x1 * cos - x2 * sin, x2 * cos + x1 * sin], -1).astype(x.dtype)
    return jnp.concatenate([rot, x[..., ROPE_DIM:]], -1)


def sink_softmax(s, mask, sink):
    s = jnp.where(mask, s, -jnp.inf)
    sink = sink.astype(F32)
    m = jnp.maximum(jnp.max(s, -1, keepdims=True), sink)
    e = jnp.exp(s - m)
    return e / (jnp.sum(e, -1, keepdims=True) + jnp.exp(sink - m))


def swa_prompt(q, k, v, sinks):
    B, S = q.shape[:2]
    nb = S // WINDOW
    qb = q.reshape(B, nb, WINDOW, ATT_KV_HEADS, ATT_GROUP, ATT_HEAD_DIM)

    def band_keys(t):
        tb = t.reshape(B, nb, WINDOW, ATT_KV_HEADS, ATT_HEAD_DIM)
        prev = jnp.pad(tb[:, :-1], ((0, 0), (1, 0), (0, 0), (0, 0), (0, 0)))
        return jnp.concatenate([prev, tb], axis=2)

    kk, vv = band_keys(k), band_keys(v)
    i = jnp.arange(WINDOW)[:, None]
    j = jnp.arange(2 * WINDOW)[None, :]
    band = (j >= i) & (j <= i + WINDOW)
    blk = jnp.arange(nb)[:, None, None]
    mask = band[None] & ((blk > 0) | (j[None] >= WINDOW))
    s = jnp.einsum('bnqhgd,bnkhd->bnhgqk', qb, kk, preferred_element_type=F32) * ATT_SCALE
    p = sink_softmax(s, mask[None, :, None, None],
                     sinks.reshape(ATT_KV_HEADS, ATT_GROUP)[None, None, :, :, None, None])
    o = jnp.einsum('bnhgqk,bnkhd->bnqhgd', p.astype(v.dtype), vv)
    return o.reshape(B, S, ATT_WIDTH)


def swa_sample(q, k, v, ck, cv, sinks):
    B, L = q.shape[:2]
    kk = jnp.concatenate([ck.astype(k.dtype), k], 1)
    vv = jnp.concatenate([cv.astype(v.dtype), v], 1)
    qpos = WINDOW + jnp.arange(L)[:, None]
    kpos = jnp.arange(WINDOW + L)[None, :]
    mask = (kpos <= qpos) & (kpos >= qpos - WINDOW)
    qg = q.reshape(B, L, ATT_KV_HEADS, ATT_GROUP, ATT_HEAD_DIM)
    s = jnp.einsum('bqhgd,bkhd->bhgqk', qg, kk, preferred_element_type=F32) * ATT_SCALE
    p = sink_softmax(s, mask[None, None, None],
                     sinks.reshape(ATT_KV_HEADS, ATT_GROUP)[None, :, :, None, None])
    o = jnp.einsum('bhgqk,bkhd->bqhgd', p.astype(vv.dtype), vv).reshape(B, L, ATT_WIDTH)
    return o, kk[:, -WINDOW:].astype(ck.dtype), vv[:, -WINDOW:].astype(cv.dtype)


def causal_conv(x, buf, w, b):
    L = x.shape[1]
    xp = jnp.concatenate([buf.astype(x.dtype), x], 1)
    y = xp[:, 0:L] * w[0]
    for t in range(1, CONV_W):
        y = y + xp[:, t:t + L] * w[t]
    return y + b, xp[:, -(CONV_W - 1):].astype(buf.dtype)


def rglru(x, h0, w_a, b_a, w_x, b_x, lam):
    B, L, W = x.shape
    xb = x.reshape(B, L, LRU_BLOCKS, LRU_BLOCK)
    r = jax.nn.sigmoid(jnp.einsum('blnc,ncd->blnd', xb, w_a).reshape(B, L, W).astype(F32) + b_a)
    ig = jax.nn.sigmoid(jnp.einsum('blnc,ncd->blnd', xb, w_x).reshape(B, L, W).astype(F32) + b_x)
    log_a = (-LRU_C * jax.nn.softplus(-lam.astype(F32))) * r
    a = jnp.exp(log_a)
    bterm = jnp.sqrt(-jnp.expm1(2.0 * log_a)) * (ig * x.astype(F32))
    bterm = bterm.at[:, 0].add(a[:, 0] * h0.astype(F32))

    def combine(left, right):
        a1, b1 = left
        a2, b2 = right
        return a1 * a2, a2 * b1 + b2

    _, h = lax.associative_scan(combine, (a, bterm), axis=1)
    return h.astype(x.dtype), h[:, -1].astype(h0.dtype)


def ssd_scan(xh, dt, a_head, bm, cm, h0):
    Bsz, L = xh.shape[:2]
    q = min(SSD_CHUNK, L)
    lp = -(-L // q) * q
    pad = lp - L
    if pad:
        xh = jnp.pad(xh, ((0, 0), (0, pad), (0, 0), (0, 0)))
        dt = jnp.pad(dt, ((0, 0), (0, pad), (0, 0)))
        bm = jnp.pad(bm, ((0, 0), (0, pad), (0, 0), (0, 0)))
        cm = jnp.pad(cm, ((0, 0), (0, pad), (0, 0), (0, 0)))
    nc = lp // q
    x = (xh.astype(F32) * dt[..., None]).reshape(Bsz, nc, q, SSD_GROUPS, SSD_HPG, SSD_HEAD_DIM)
    a = (dt * a_head).reshape(Bsz, nc, q, SSD_GROUPS, SSD_HPG)
    bc = bm.astype(F32).reshape(Bsz, nc, q, SSD_GROUPS, SSD_STATE)
    cc = cm.astype(F32).reshape(Bsz, nc, q, SSD_GROUPS, SSD_STATE)
    a_cs = jnp.cumsum(a, axis=2)
    diff = a_cs[:, :, :, None] - a_cs[:, :, None, :]
    causal = jnp.tril(jnp.ones((q, q), dtype=bool))[None, None, :, :, None, None]
    lmat = jnp.exp(jnp.where(causal, diff, -jnp.inf))
    cb = jnp.einsum('bcqgn,bcsgn->bcqsg', cc, bc)
    y_diag = jnp.einsum('bcqsg,bcqsge,bcsgep->bcqgep', cb, lmat, x)
    decay_to_end = jnp.exp(a_cs[:, :, -1:] - a_cs)
    states = jnp.einsum('bcqgn,bcqge,bcqgep->bcgepn', bc, decay_to_end, x)
    chunk_decay = jnp.exp(a_cs[:, :, -1])
    hg0 = h0.astype(F32).reshape(Bsz, SSD_GROUPS, SSD_HPG, SSD_HEAD_DIM, SSD_STATE)

    def step(h, inp):
        dec, st = inp
        return dec[..., None, None] * h + st, h

    h_last, h_in = lax.scan(step, hg0, (jnp.swapaxes(chunk_decay, 0, 1), jnp.swapaxes(states, 0, 1)))
    h_in = jnp.swapaxes(h_in, 0, 1)
    y_off = jnp.einsum('bcqgn,bcgepn,bcqge->bcqgep', cc, h_in, jnp.exp(a_cs))
    y = (y_diag + y_off).reshape(Bsz, lp, SSD_HEADS, SSD_HEAD_DIM)[:, :L]
    return y, h_last.reshape(Bsz, SSD_HEADS, SSD_HEAD_DIM, SSD_STATE).astype(h0.dtype)


def mixer_layer(x, pos, att_cache, lru_conv0, lru_h0, ssd_conv0, ssd_h0, p):
    B, L, _ = x.shape
    proj = jnp.einsum('bld,de->ble', x, p['w_in'])
    q, k, v, g_att, x_lru, g_lru, z_ssd, xbc, dt_raw = jnp.split(proj, SPLIT_POINTS, axis=-1)
    q = partial_rope(q.reshape(B, L, ATT_HEADS, ATT_HEAD_DIM), pos)
    k = partial_rope(k.reshape(B, L, ATT_KV_HEADS, ATT_HEAD_DIM), pos)
    v = v.reshape(B, L, ATT_KV_HEADS, ATT_HEAD_DIM)
    if att_cache is None:
        att = swa_prompt(q, k, v, p['att_sinks'])
        k_buf, v_buf = k[:, -WINDOW:], v[:, -WINDOW:]
    else:
        att, k_buf, v_buf = swa_sample(q, k, v, att_cache[0], att_cache[1], p['att_sinks'])
    br_a = att * jax.nn.silu(g_att)
    xl, lru_conv1 = causal_conv(x_lru, lru_conv0, p['lru_conv_w'], p['lru_conv_b'])
    hl, lru_h1 = rglru(xl, lru_h0, p['lru_wa'], p['lru_ba'], p['lru_wx'], p['lru_bx'], p['lru_lambda'])
    br_b = hl * jax.nn.silu(g_lru)
    xbc_c, ssd_conv1 = causal_conv(xbc, ssd_conv0, p['ssd_conv_w'], p['ssd_conv_b'])
    xbc_c = jax.nn.silu(xbc_c)
    xs, b_ssm, c_ssm = jnp.split(xbc_c, (SSD_WIDTH, SSD_WIDTH + SSD_GROUPS * SSD_STATE), axis=-1)
    dt = jax.nn.softplus(dt_raw.astype(F32) + p['ssd_dt_bias'].astype(F32))
    a_head = -jnp.exp(p['ssd_a_log'].astype(F32))
    xh = xs.reshape(B, L, SSD_HEADS, SSD_HEAD_DIM)
    y, ssd_h1 = ssd_scan(xh, dt, a_head,
                         b_ssm.reshape(B, L, SSD_GROUPS, SSD_STATE),
                         c_ssm.reshape(B, L, SSD_GROUPS, SSD_STATE), ssd_h0)
    y = y + p['ssd_d'].astype(F32)[:, None] * xh.astype(F32)
    y = y.reshape(B, L, SSD_WIDTH) * jax.nn.silu(z_ssd.astype(F32))
    y = y * lax.rsqrt(jnp.mean(jnp.square(y), -1, keepdims=True) + NORM_EPS) * p['ssd_norm_g']
    br_c = y.astype(x.dtype)
    mix = jnp.concatenate([br_a.astype(x.dtype), br_b.astype(x.dtype), br_c], -1)
    out = jnp.einsum('ble,ed->bld', mix, p['w_out'])
    y_out = layer_norm(DEEPNORM_ALPHA * x + out, p['ln_g'], p['ln_b'])
    return y_out, (k_buf, v_buf, lru_conv1, lru_h1, ssd_conv1, ssd_h1)


def setup_inputs(seed: int = 0) -> dict:
    key = jax.random.key(seed)
    ks = jax.random.split(key, 32)
    nrm = lambda k, shape, s: jax.random.normal(k, shape, F32) * s
    u = jax.random.uniform(ks[20], (DEPTH, LRU_WIDTH), F32, 0.9, 0.999)
    a_lru = u ** (1.0 / LRU_C)
    lru_lambda = jnp.log(a_lru) - jnp.log1p(-a_lru)
    dt0 = jnp.exp(jax.random.uniform(ks[21], (DEPTH, SSD_HEADS), F32, math.log(1e-3), math.log(1e-1)))
    ssd_dt_bias = dt0 + jnp.log(-jnp.expm1(-dt0))
    ssd_a_log = jnp.log(jax.random.uniform(ks[22], (DEPTH, SSD_HEADS), F32, 1.0, 16.0))
    return {
        'x_prompt': nrm(ks[0], (BATCH, SEQ, D_MODEL), 1.0),
        'x_sample': nrm(ks[1], (DEC_BATCH, DEC_SEQ, D_MODEL), 1.0),
        'cache_swa_k': nrm(ks[2], (DEPTH, DEC_BATCH, WINDOW, ATT_KV_HEADS, ATT_HEAD_DIM), 1.0),
        'cache_swa_v': nrm(ks[3], (DEPTH, DEC_BATCH, WINDOW, ATT_KV_HEADS, ATT_HEAD_DIM), 1.0),
        'state_lru_conv': nrm(ks[4], (DEPTH, DEC_BATCH, CONV_W - 1, LRU_WIDTH), 1.0),
        'state_lru_h': nrm(ks[5], (DEPTH, DEC_BATCH, LRU_WIDTH), 0.5),
        'state_ssd_conv': nrm(ks[6], (DEPTH, DEC_BATCH, CONV_W - 1, SSD_CONV_CH), 1.0),
        'state_ssd_h': nrm(ks[7], (DEPTH, DEC_BATCH, SSD_HEADS, SSD_HEAD_DIM, SSD_STATE), 0.1),
        'w_in': nrm(ks[8], (DEPTH, D_MODEL, D_IN_PROJ), D_MODEL ** -0.5),
        'w_out': nrm(ks[9], (DEPTH, D_MIX, D_MODEL), DEEPNORM_BETA * D_MIX ** -0.5),
        'att_sinks': nrm(ks[10], (DEPTH, ATT_HEADS), 0.5),
        'lru_conv_w': nrm(ks[11], (DEPTH, CONV_W, LRU_WIDTH), CONV_W ** -0.5),
        'lru_conv_b': nrm(ks[12], (DEPTH, LRU_WIDTH), 0.02),
        'lru_wa': nrm(ks[13], (DEPTH, LRU_BLOCKS, LRU_BLOCK, LRU_BLOCK), LRU_BLOCK ** -0.5),
        'lru_ba': nrm(ks[14], (DEPTH, LRU_WIDTH), 0.02),
        'lru_wx': nrm(ks[15], (DEPTH, LRU_BLOCKS, LRU_BLOCK, LRU_BLOCK), LRU_BLOCK ** -0.5),
        'lru_bx': nrm(ks[16], (DEPTH, LRU_WIDTH), 0.02),
        'lru_lambda': lru_lambda,
        'ssd_conv_w': nrm(ks[17], (DEPTH, CONV_W, SSD_CONV_CH), CONV_W ** -0.5),
        'ssd_conv_b': nrm(ks[18], (DEPTH, SSD_CONV_CH), 0.02),
        'ssd_dt_bias': ssd_dt_bias,
        'ssd_a_log': ssd_a_log,
        'ssd_d': 1.0 + nrm(ks[23], (DEPTH, SSD_HEADS), 0.05),
        'ssd_norm_g': 1.0 + nrm(ks[24], (DEPTH, SSD_WIDTH), 0.05),
        'ln_g': 1.0 + nrm(ks[25], (DEPTH, D_MODEL), 0.05),
        'ln_b': nrm(ks[26], (DEPTH, D_MODEL), 0.02),
    }


def reference(x_prompt, x_sample, cache_swa_k, cache_swa_v, state_lru_conv, state_lru_h,
              state_ssd_conv, state_ssd_h, w_in, w_out, att_sinks, lru_conv_w, lru_conv_b,
              lru_wa, lru_ba, lru_wx, lru_bx, lru_lambda, ssd_conv_w, ssd_conv_b,
              ssd_dt_bias, ssd_a_log, ssd_d, ssd_norm_g, ln_g, ln_b):
    bp = x_prompt.shape[0]
    pos_p = jnp.arange(x_prompt.shape[1], dtype=jnp.int32)
    pos_s = PAST_LEN + jnp.arange(x_sample.shape[1], dtype=jnp.int32)
    zero_lru_conv = jnp.zeros((bp, CONV_W - 1, LRU_WIDTH), x_prompt.dtype)
    zero_lru_h = jnp.zeros((bp, LRU_WIDTH), F32)
    zero_ssd_conv = jnp.zeros((bp, CONV_W - 1, SSD_CONV_CH), x_prompt.dtype)
    zero_ssd_h = jnp.zeros((bp, SSD_HEADS, SSD_HEAD_DIM, SSD_STATE), F32)
    xp, xs = x_prompt, x_sample
    new_p = [[] for _ in range(6)]
    new_s = [[] for _ in range(6)]
    for l in range(DEPTH):
        p = {'w_in': w_in[l], 'w_out': w_out[l], 'att_sinks': att_sinks[l],
             'lru_conv_w': lru_conv_w[l], 'lru_conv_b': lru_conv_b[l],
             'lru_wa': lru_wa[l], 'lru_ba': lru_ba[l], 'lru_wx': lru_wx[l], 'lru_bx': lru_bx[l],
             'lru_lambda': lru_lambda[l], 'ssd_conv_w': ssd_conv_w[l], 'ssd_conv_b': ssd_conv_b[l],
             'ssd_dt_bias': ssd_dt_bias[l], 'ssd_a_log': ssd_a_log[l], 'ssd_d': ssd_d[l],
             'ssd_norm_g': ssd_norm_g[l], 'ln_g': ln_g[l], 'ln_b': ln_b[l]}
        xp, st_p = mixer_layer(xp, pos_p, None, zero_lru_conv, zero_lru_h, zero_ssd_conv, zero_ssd_h, p)
        xs, st_s = mixer_layer(xs, pos_s, (cache_swa_k[l], cache_swa_v[l]), state_lru_conv[l],
                               state_lru_h[l], state_ssd_conv[l], state_ssd_h[l], p)
        for lst, t in zip(new_p, st_p):
            lst.append(t)
        for lst, t in zip(new_s, st_s):
            lst.append(t)
    p_swa_k, p_swa_v, p_lru_conv, p_lru_h, p_ssd_conv, p_ssd_h = [jnp.stack(t) for t in new_p]
    s_swa_k, s_swa_v, s_lru_conv, s_lru_h, s_ssd_conv, s_ssd_h = [jnp.stack(t) for t in new_s]
    return (xp, xs, p_swa_k, p_swa_v, p_lru_conv, p_lru_h, p_ssd_conv, p_ssd_h,
            s_swa_k, s_swa_v, s_lru_conv, s_lru_h, s_ssd_conv, s_ssd_h)
```

```python
import contextlib
import math
import numpy as np
import concourse.bass as bass
import concourse.mybir as mybir
from concourse.bass_utils import run_bass_kernel_spmd

F32 = mybir.dt.float32
BF16 = mybir.dt.bfloat16
I32 = mybir.dt.int32
AF = mybir.ActivationFunctionType
ALU = mybir.AluOpType
AX = mybir.AxisListType

D = 1024
SEQ = 2048
DEPTH = 4
NS = 16
DIN = 4876
C_Q, C_K, C_V, C_GA, C_XL, C_GL, C_Z, C_XBC, C_DT = 0, 512, 640, 768, 1280, 2048, 2816, 3584, 4864
SCALE = 64 ** -0.5
ALPHA = (2.0 * DEPTH) ** 0.25
EPS = 1e-5
H = 512
NT = H // 128
NPASS = SEQ // H
PAST = 8192.0
NEG = -30000.0
import os
SSD_STOP = int(os.environ.get('SSD_STOP', '0'))

K_ID, K_U, K_ONE, K_MG, K_MF, K_POS, K_INV, K_HM, K_EX, K_END = 0, 128, 256, 384, 640, 896, 912, 920, 922, 922 + 768


class T:
    __slots__ = ("name", "lw", "rd")

    def __init__(self, name):
        self.name = name
        self.lw = None
        self.rd = []


class Op:
    __slots__ = ("eng", "fn", "deps", "signal", "val", "is_dma", "dsem", "dval")

    def __init__(self, eng, fn, is_dma):
        self.eng = eng
        self.fn = fn
        self.deps = []
        self.signal = False
        self.val = None
        self.is_dma = is_dma
        self.dsem = None
        self.dval = None


class Prog:
    ENGS = ("pe", "act", "dve", "pool", "sp")

    def __init__(self, nc, n_dma_sems=24):
        self.nc = nc
        self.ops = []
        self.tiles = {}
        self.n_dma_sems = n_dma_sems

    def t(self, *key):
        if key not in self.tiles:
            self.tiles[key] = T(key)
        return self.tiles[key]

    def op(self, eng, fn, reads=(), writes=(), dma=False):
        o = Op(eng, fn, dma)
        deps = set()
        for t in reads:
            if t.lw is not None:
                deps.add(t.lw)
        for t in writes:
            if t.lw is not None:
                deps.add(t.lw)
            for r in t.rd:
                deps.add(r)
        for t in reads:
            t.rd.append(o)
        for t in writes:
            t.lw = o
            t.rd = []
        for d in deps:
            if d is o:
                continue
            if (not d.is_dma) and (not dma) and d.eng == "pe" and eng == "pe":
                continue
            o.deps.append(d)
            d.signal = True
        self.ops.append(o)
        return o

    def dma(self, eng, out, in_, reads=(), writes=(), **kw):
        return self.op(eng, lambda e: e.dma_start(out=out, in_=in_, **kw), reads, writes, dma=True)

    def fence(self):
        allt = list(self.tiles.values())
        ft = self.t("__fence__")
        first = True
        for e in ("pe", "act", "dve", "pool", "sp"):
            self.op(e, lambda eng: eng.nop(), [], (allt + [ft]) if first else [ft])
            first = False

    def emit(self):
        nc = self.nc
        cnt = {e: 0 for e in self.ENGS}
        dma_rr = {e: 0 for e in self.ENGS}
        dma_cnt = {}
        for o in self.ops:
            if o.is_dma:
                k = dma_rr[o.eng] % self.n_dma_sems
                dma_rr[o.eng] += 1
                key = (o.eng, k)
                dma_cnt[key] = dma_cnt.get(key, 0) + 16
                o.dsem = key
                o.dval = dma_cnt[key]
            elif o.signal:
                cnt[o.eng] += 1
                o.val = cnt[o.eng]
        with contextlib.ExitStack() as st:
            sems = {e: st.enter_context(nc.semaphore("s_" + e)) for e in ("pe", "act", "dve", "pool")}
            dsems = {}
            for e in self.ENGS:
                for k in range(min(self.n_dma_sems, dma_rr[e])):
                    dsems[(e, k)] = st.enter_context(nc.semaphore("d_%s_%d" % (e, k)))
            block = st.enter_context(nc.Block())
            per = {e: [o for o in self.ops if o.eng == e] for e in self.ENGS}

            def run(ename):
                def body(eng):
                    waited = {}
                    for o in per[ename]:
                        need = {}
                        for d in o.deps:
                            if d.is_dma:
                                s, v, sk = dsems[d.dsem], d.dval, ("d",) + d.dsem
                            else:
                                s, v, sk = sems[d.eng], d.val, ("c", d.eng)
                            if need.get(sk, (None, 0))[1] < v:
                                need[sk] = (s, v)
                        if o.is_dma and o.dval > 16:
                            sk = ("d",) + o.dsem
                            if need.get(sk, (None, 0))[1] < o.dval - 16:
                                need[sk] = (dsems[o.dsem], o.dval - 16)
                        for sk, (s, v) in need.items():
                            if waited.get(sk, 0) >= v:
                                continue
                            eng.wait_ge(s, v)
                            waited[sk] = v
                        ins = o.fn(eng)
                        if o.is_dma:
                            ins.then_inc(dsems[o.dsem], 16)
                        elif o.signal:
                            ins.then_inc(sems[ename], 1)
                    if ename == "sp":
                        for key, v in dma_cnt.items():
                            eng.wait_ge(dsems[key], v)
                        for e in ("pe", "act", "dve", "pool"):
                            if cnt[e]:
                                eng.wait_ge(sems[e], cnt[e])
                return body

            block.tensor(run("pe"))
            block.scalar(run("act"))
            block.vector(run("dve"))
            block.gpsimd(run("pool"))
            block.sync(run("sp"))


def bc(ap, shape, axis):
    return ap.unsqueeze(axis).to_broadcast(shape)


class Builder:
    def __init__(self, dbg=None):
        self.dbg = dbg or {}
        self.nc = nc = bass.Bass("TRN2", target_bir_lowering=False)
        self.P = Prog(nc)
        self.st = contextlib.ExitStack()
        self.wrr = 0

    def sb(self, name, shape, dt=F32):
        return self.st.enter_context(self.nc.sbuf_tensor(name, shape, dt))

    def din(self, name, shape, dt=F32):
        return self.nc.dram_tensor(name, shape, dt, kind="ExternalInput").ap()

    def dout(self, name, shape, dt=F32):
        return self.nc.dram_tensor(name, shape, dt, kind="ExternalOutput").ap()

    def declare_io(self):
        L = DEPTH
        self.x_prompt = self.din("x_prompt", [SEQ, D])
        self.x_sample = self.din("x_sample", [NS, D])
        self.cache_k = self.din("cache_swa_k", [L, NS, 128, 128])
        self.cache_v = self.din("cache_swa_v", [L, NS, 128, 128])
        self.st_lru_conv = self.din("state_lru_conv", [L, NS, 3, 768])
        self.st_lru_h = self.din("state_lru_h", [L, NS, 768])
        self.st_ssd_conv = self.din("state_ssd_conv", [L, NS, 3, 1280])
        self.st_ssd_h = self.din("state_ssd_h", [L, NS, 12, 64, 128])
        self.w_in = self.din("w_in", [L, D, DIN])
        self.w_out = self.din("w_out", [L, 2048, D])
        self.att_sinks = self.din("att_sinks", [L, 8])
        self.lru_conv_w = self.din("lru_conv_w", [L, 4, 768])
        self.lru_conv_b = self.din("lru_conv_b", [L, 768])
        self.lru_wa = self.din("lru_wa", [L, 8, 96, 96])
        self.lru_ba = self.din("lru_ba", [L, 768])
        self.lru_wx = self.din("lru_wx", [L, 8, 96, 96])
        self.lru_bx = self.din("lru_bx", [L, 768])
        self.lru_lambda = self.din("lru_lambda", [L, 768])
        self.ssd_conv_w = self.din("ssd_conv_w", [L, 4, 1280])
        self.ssd_conv_b = self.din("ssd_conv_b", [L, 1280])
        self.ssd_dt_bias = self.din("ssd_dt_bias", [L, 12])
        self.ssd_a_log = self.din("ssd_a_log", [L, 12])
        self.ssd_d = self.din("ssd_d", [L, 12])
        self.ssd_norm_g = self.din("ssd_norm_g", [L, 768])
        self.ln_g = self.din("ln_g", [L, D])
        self.ln_b = self.din("ln_b", [L, D])
        self.consts = self.din("consts", [128, K_END])
        self.y_prompt = self.dout("y_prompt", [SEQ, D])
        self.y_sample = self.dout("y_sample", [NS, D])
        self.p_swa_k = self.dout("p_swa_k", [L, 128, 128])
        self.p_swa_v = self.dout("p_swa_v", [L, 128, 128])
        self.p_lru_conv = self.dout("p_lru_conv", [L, 3, 768])
        self.p_lru_h = self.dout("p_lru_h", [L, 768])
        self.p_ssd_conv = self.dout("p_ssd_conv", [L, 3, 1280])
        self.p_ssd_h = self.dout("p_ssd_h", [L, 768, 128])
        self.s_swa_k = self.dout("s_swa_k", [L, NS, 128, 128])
        self.s_swa_v = self.dout("s_swa_v", [L, NS, 128, 128])
        self.s_lru_conv = self.dout("s_lru_conv", [L, NS, 3, 768])
        self.s_lru_h = self.dout("s_lru_h", [L, NS, 768])
        self.s_ssd_conv = self.dout("s_ssd_conv", [L, NS, 3, 1280])
        self.s_ssd_h = self.dout("s_ssd_h", [L, NS, 768, 128])
        self.w_in_bf = self.nc.dram_tensor("w_in_bf", [L, D, DIN], BF16).ap()
        self.w_out_bf = self.nc.dram_tensor("w_out_bf", [L, 2048, D], BF16).ap()
        self.dbg_out = {k: self.dout("dbg_" + k, list(v)) for k, v in self.dbg.items()}

    def tap(self, name, ap, reads):
        if name in self.dbg_out:
            self.P.dma("pool", self.dbg_out[name], ap, reads=reads)

    def alloc(self):
        sb = self.sb
        self.cst = sb("cst", [128, K_END])
        self.ident_bf = sb("ident_bf", [128, 128], BF16)
        self.maskbf = sb("maskbf", [128, 2, 256], BF16)
        self.cosT = sb("cosT", [128, 16, 8]); self.sinT = sb("sinT", [128, 16, 8]); self.nsinT = sb("nsinT", [128, 16, 8])
        self.ropeS = sb("ropeS", [128, 3, 8])
        self.xT32 = sb("xT32", [128, 8, H])
        self.xTb = sb("xTb", [128, 8, H], BF16)
        self.mixT = sb("mixT", [128, 18, H], BF16)
        self.wbuf = [sb("wbuf%d" % i, [128, 8, 128], BF16) for i in range(3)]
        self.wqkv = sb("wqkv", [128, 8, 768], BF16)
        self.kT = sb("kT", [128, 5 * 128], BF16)
        self.vtm = sb("vtm", [128, 5, 128], BF16)
        self.ck = sb("ck", [128, DEPTH, 128], BF16); self.cv = sb("cv", [128, DEPTH, 128], BF16)
        self.hist_l = sb("hist_l", [128, DEPTH, 8, 3]); self.hist_s = sb("hist_s", [128, DEPTH, 10, 3])
        self.hcar = sb("hcar", [128, DEPTH, 8])
        self.hT32 = sb("hT32", [128, DEPTH, 768]); self.hTb = sb("hTb", [128, 768], BF16)
        self.lp = sb("lp", [128, DEPTH, 8, 8])
        self.la = sb("la", [128, DEPTH, 8, 2])
        self.spp = sb("spp", [128, DEPTH, 10, 5])
        self.gS = sb("gS", [128, DEPTH, 6])
        self.lnp = sb("lnp", [128, DEPTH, 8, 2])
        self.sinkb = sb("sinkb", [128, DEPTH, 8]); self.nsinkb = sb("nsinkb", [128, DEPTH, 8])
        self.Ab = sb("Ab", [128, DEPTH, 12]); self.Db = sb("Db", [128, DEPTH, 12])
        self.dtb = sb("dtb", [128, DEPTH])
        self.wab = sb("wab", [128, 8, 96], BF16); self.wxb = sb("wxb", [128, 8, 96], BF16)
        self.pstage = sb("pstage", [8, 1280])
        self.adc = sb("adc", [128, DEPTH, 2]); self.sinkc = sb("sinkc", [128, DEPTH, 2])
        self.qkv32 = [sb("qkv32_%d" % i, [128, 768]) for i in range(2)]
        self.qkbf = sb("qkbf", [128, 640], BF16)
        self.ropet = sb("ropet", [128, 2, 10, 16])
        self.qT = sb("qT", [128, 4, H], BF16)
        self.G = [sb("G%d" % i, [128, H]) for i in range(2)]
        self.Pm = [sb("Pm%d" % i, [128, 512], BF16) for i in range(2)]
        self.PT = [sb("PT%d" % i, [128, 512], BF16) for i in range(2)]
        self.ast = [sb("ast%d" % i, [128, 8, 2]) for i in range(2)]
        self.xpad = sb("xpad", [128, H + 3]); self.xl = sb("xl", [128, H]); self.xlb = sb("xlb", [128, H], BF16)
        self.rr = sb("rr", [128, H]); self.ig = sb("ig", [128, H]); self.EE = sb("EE", [128, H])
        self.zs = sb("zs", [128, 6, H]); self.xsT = sb("xsT", [128, 6, H], BF16)
        self.BT = sb("BT", [128, 2, H], BF16); self.CT = sb("CT", [128, 2, H], BF16)
        self.dtT = sb("dtT", [128, H]); self.dtt = sb("dtt", [128, H])
        self.xstm = sb("xstm", [128, 768], BF16); self.Btm = sb("Btm", [128, 256], BF16)
        self.tsm = sb("tsm", [128, 8, 12])
        self.aU = sb("aU", [128, 12, 128]); self.Dm = sb("Dm", [128, 12, 128])
        self.Ebc = sb("Ebc", [128, 12, 128], BF16); self.GT = sb("GT", [128, 12, 128], BF16)
        self.CE = sb("CE", [128, 12, 128], BF16); self.CBm = sb("CBm", [128, 2, 128])
        self.xdt = sb("xdt", [128, 768], BF16); self.xdd = sb("xdd", [128, 768], BF16); self.xsD = sb("xsD", [128, 768], BF16)
        self.yg = sb("yg", [128, 6, 128]); self.sq = sb("sq", [128, 6, 128]); self.rstd = sb("rstd", [128, 128])
        self.lnq = [sb("lnq%d" % i, [128, H]) for i in range(2)]
        self.mean = sb("mean", [128, H]); self.lrs = sb("lrs", [128, H])
        self.iost = sb("iost", [128, D])
        self.ps = self.st.enter_context(self.nc.psum_tensor("ps", [128, 4096], F32))

    def bank(self, b, n=1):
        return self.ps[:, b * 512:(b + n) * 512]

    def bankbf(self, b):
        return self.ps[:, b * 512:(b + 1) * 512].bitcast(BF16)

    def pT(self, b):
        return self.P.t("ps", b)

    def act(self, out, in_, func, r, w, **kw):
        return self.P.op("act", lambda e: e.activation(out=out, in_=in_, func=func, **kw), r, w)

    def acopy(self, out, in_, r, w):
        return self.P.op("act", lambda e: e.copy(out=out, in_=in_), r, w)

    def vcopy(self, out, in_, r, w, eng="dve"):
        return self.P.op(eng, lambda e: e.tensor_copy(out=out, in_=in_), r, w)

    def tt(self, out, a, b, op, r, w, eng="dve"):
        return self.P.op(eng, lambda e: e.tensor_tensor(out=out, in0=a, in1=b, op=op), r, w)

    def ts(self, out, a, s1, s2, op0, op1, r, w, eng="dve"):
        if op1 is None:
            return self.P.op(eng, lambda e: e.tensor_scalar(out=out, in0=a, scalar1=s1, scalar2=None, op0=op0), r, w)
        return self.P.op(eng, lambda e: e.tensor_scalar(out=out, in0=a, scalar1=s1, scalar2=s2, op0=op0, op1=op1), r, w)

    def stt(self, out, a, s, b, op0, op1, r, w, eng="dve"):
        return self.P.op(eng, lambda e: e.scalar_tensor_tensor(out=out, in0=a, scalar=s, in1=b, op0=op0, op1=op1), r, w)

    def mm(self, out, lhsT, rhs, start, stop, r, w):
        return self.P.op("pe", lambda e: e.matmul(out, lhsT=lhsT, rhs=rhs, start=start, stop=stop), r, w)

    def tr(self, out, in_, ident, r, w):
        return self.P.op("pe", lambda e: e.transpose(out, in_, ident), r, w)

    def next_wbuf(self):
        k = self.wrr % 3
        self.wrr += 1
        return k

    def win_T(self, l):
        return [self.P.t("winbf", l, i) for i in range(8)]

    def wout_T(self, l):
        return [self.P.t("woutbf", l, i) for i in range(16)]

    def load_win(self, l, pieces):
        k = self.next_wbuf()
        src = self.w_in_bf[l].rearrange("(kc p) c -> p kc c", p=128)
        for (c0, n, d0) in pieces:
            self.P.dma("sp", self.wbuf[k][:, :, d0:d0 + n], src[:, :, c0:c0 + n],
                       reads=self.win_T(l), writes=[self.P.t("wbuf", k)])
        return k

    def inproj(self, l, k, M, psout, psT, ncols=H, x=None, xT=None):
        x = self.xTb if x is None else x
        xT = [self.P.t("xTb", kc) for kc in range(8)] if xT is None else xT
        for kc in range(8):
            self.mm(psout[0:M, 0:ncols], self.wbuf[k][:, kc, 0:M], x[:, kc, 0:ncols], kc == 0, kc == 7,
                    [self.P.t("wbuf", k)] + xT, [psT])

    def prologue(self):
        P, t = self.P, self.P.t
        cst = self.cst
        P.dma("sp", cst[:], self.consts, writes=[t("cst")])
        for l in range(DEPTH):
            for i in range(8):
                P.dma("pool", self.w_in_bf[l, i * 128:(i + 1) * 128, :], self.w_in[l, i * 128:(i + 1) * 128, :],
                      writes=[t("winbf", l, i)])
            for i in range(16):
                P.dma("pool", self.w_out_bf[l, i * 128:(i + 1) * 128, :], self.w_out[l, i * 128:(i + 1) * 128, :],
                      writes=[t("woutbf", l, i)])
        self.vcopy(self.ident_bf[:], cst[:, K_ID:K_ID + 128], [t("cst")], [t("ident_bf")])
        self.vcopy(self.maskbf[:, 0, :], cst[:, K_MG:K_MG + 256], [t("cst")], [t("maskbf")])
        self.vcopy(self.maskbf[:, 1, :], cst[:, K_MF:K_MF + 256], [t("cst")], [t("maskbf")])
        for buf, nm in ((self.ck, "ck"), (self.cv, "cv")):
            P.op("dve", lambda e, buf=buf: e.memset(buf[:], 0.0), [], [t(nm, l) for l in range(DEPTH)])
        P.op("dve", lambda e: e.memset(self.hist_l[:], 0.0), [], [t("hist_l", l) for l in range(DEPTH)])
        P.op("dve", lambda e: e.memset(self.hist_s[:], 0.0), [], [t("hist_s", l) for l in range(DEPTH)])
        P.op("dve", lambda e: e.memset(self.hcar[:], 0.0), [], [t("hcar", l) for l in range(DEPTH)])
        P.op("dve", lambda e: e.memset(self.hT32[:], 0.0), [], [t("hT32", l) for l in range(DEPTH)])
        self.rope_tables()
        for l in range(DEPTH):
            self.layer_params(l)

    def sincos(self, ang, n, outs, rd):
        P, t = self.P, self.P.t
        tmp = self.iost
        c1 = float(np.float32(2 * np.pi)); c2 = float(2 * np.pi - c1)
        for j, (shift, out) in enumerate(((0.5 * np.pi, outs[0]), (0.0, outs[1]))):
            a = tmp[:, 0:n]; kf = tmp[:, n:2 * n]; ki = tmp[:, 2 * n:3 * n].bitcast(I32); m = tmp[:, 3 * n:4 * n]
            W = [t("iost")]
            self.ts(a, ang, float(shift), None, ALU.add, None, rd + W, W)
            self.ts(kf, a, float(1.0 / (2 * np.pi)), None, ALU.mult, None, W, W)
            self.vcopy(ki, kf, W, W)
            self.vcopy(kf, ki, W, W)
            self.stt(a, kf, -c1, a, ALU.mult, ALU.add, W, W)
            self.stt(a, kf, -c2, a, ALU.mult, ALU.add, W, W)
            self.ts(m, a, float(np.pi), float(-2 * np.pi), ALU.is_gt, ALU.mult, W, W)
            self.tt(a, a, m, ALU.add, W, W)
            self.ts(m, a, float(-np.pi), float(2 * np.pi), ALU.is_lt, ALU.mult, W, W)
            self.tt(a, a, m, ALU.add, W, W)
            self.act(out, a, AF.Sin, W, [t("rope")])
        self.ts(outs[2], outs[1], -1.0, None, ALU.mult, None, [t("rope")], [t("rope")])

    def rope_tables(self):
        t = self.P.t
        cst = self.cst
        ang = self.iost[:, 512:640]
        self.tt(ang.rearrange("p (a b) -> p a b", b=8), bc(cst[:, K_POS:K_POS + 16], [128, 16, 8], 2),
                bc(cst[:, K_INV:K_INV + 8], [128, 16, 8], 1), ALU.mult, [t("cst")], [t("iost")])
        f = lambda x: x[:].rearrange("p a b -> p (a b)")
        self.sincos(ang, 128, (f(self.cosT), f(self.sinT), f(self.nsinT)), [t("iost")])
        angs = self.iost[:, 640:648]
        self.ts(angs, cst[:, K_INV:K_INV + 8], PAST, None, ALU.mult, None, [t("cst")], [t("iost")])
        self.sincos(angs, 8, (self.ropeS[:, 0, :], self.ropeS[:, 1, :], self.ropeS[:, 2, :]), [t("iost")])

    def layer_params(self, l):
        P, t = self.P, self.P.t
        cst = self.cst
        idf = cst[:, K_ID:K_ID + 128]
        stg = self.pstage
        S = [t("pstage")]
        W = [t("par", l)]
        P.dma("sp", stg[0:4, 0:768], self.lru_conv_w[l], writes=S)
        for i, src in enumerate((self.lru_conv_b, self.lru_ba, self.lru_bx, self.lru_lambda)):
            P.dma("sp", stg[4 + i:5 + i, 0:768], src[l:l + 1, :], writes=S)
        pb = self.bank(0)
        for n in range(8):
            self.tr(pb[0:96, n * 8:(n + 1) * 8], stg[0:8, n * 96:(n + 1) * 96], idf[0:8, 0:8], S + [t("cst")], [self.pT(0)])
        self.vcopy(self.lp[0:96, l, :, :], pb[0:96, 0:64].rearrange("p (a b) -> p a b", b=8), [self.pT(0)], W)
        la = self.la
        self.act(la[0:96, l, :, 0], self.lp[0:96, l, :, 7], AF.Exp, W, W, scale=-1.0)
        self.act(la[0:96, l, :, 0], la[0:96, l, :, 0], AF.Ln, W, W, bias=1.0)
        self.ts(la[0:96, l, :, 1], la[0:96, l, :, 0], -16.0, None, ALU.mult, None, W, W)
        self.ts(la[0:96, l, :, 0], la[0:96, l, :, 0], -8.0, None, ALU.mult, None, W, W)
        P.dma("sp", stg[0:4, 0:1280], self.ssd_conv_w[l], writes=S)
        P.dma("sp", stg[4:5, 0:1280], self.ssd_conv_b[l:l + 1, :], writes=S)
        pb = self.bank(1)
        for c in range(10):
            self.tr(pb[:, c * 5:(c + 1) * 5], stg[0:5, c * 128:(c + 1) * 128], idf[0:5, 0:5], S + [t("cst")], [self.pT(1)])
        self.vcopy(self.spp[:, l, :, :], pb[:, 0:50].rearrange("p (a b) -> p a b", b=5), [self.pT(1)], W)
        P.dma("sp", stg[0:1, 0:768], self.ssd_norm_g[l:l + 1, :], writes=S)
        P.dma("sp", stg[1:2, 0:1024], self.ln_g[l:l + 1, :], writes=S)
        P.dma("sp", stg[2:3, 0:1024], self.ln_b[l:l + 1, :], writes=S)
        pb = self.bank(2)
        for c in range(6):
            self.tr(pb[:, c:c + 1], stg[0:1, c * 128:(c + 1) * 128], idf[0:1, 0:1], S + [t("cst")], [self.pT(2)])
        self.ts(self.gS[:, l, :], pb[:, 0:6], float(math.sqrt(768.0)), None, ALU.mult, None, [self.pT(2)], W)
        P.dma("sp", stg[0:1, 0:1024], self.ln_g[l:l + 1, :], writes=S)
        P.dma("sp", stg[1:2, 0:1024], self.ln_b[l:l + 1, :], writes=S)
        pb = self.bank(3)
        for c in range(8):
            self.tr(pb[:, 2 * c:2 * c + 2], stg[0:2, c * 128:(c + 1) * 128], idf[0:2, 0:2], S + [t("cst")], [self.pT(3)])
        self.vcopy(self.lnp[:, l, :, :], pb[:, 0:16].rearrange("p (a b) -> p a b", b=2), [self.pT(3)], W)
        P.dma("sp", self.sinkb[:, l, :], self.att_sinks[l:l + 1, :].partition_broadcast(128), writes=W)
        self.ts(self.nsinkb[:, l, :], self.sinkb[:, l, :], -1.0, None, ALU.mult, None, W, W)
        P.dma("sp", self.Ab[:, l, :], self.ssd_a_log[l:l + 1, :].partition_broadcast(128), writes=W)
        self.act(self.Ab[:, l, :], self.Ab[:, l, :], AF.Exp, W, W)
        self.ts(self.Ab[:, l, :], self.Ab[:, l, :], -1.0, None, ALU.mult, None, W, W)
        P.dma("sp", self.Db[:, l, :], self.ssd_d[l:l + 1, :].partition_broadcast(128), writes=W)
        P.dma("sp", self.dtb[0:12, l:l + 1], self.ssd_dt_bias[l].rearrange("(a b) -> a b", b=1), writes=W)
        P.dma("sp", self.adc[0:12, l, 0:1], self.ssd_a_log[l].rearrange("(a b) -> a b", b=1), writes=W)
        self.act(self.adc[0:12, l, 0:1], self.adc[0:12, l, 0:1], AF.Exp, W, W)
        self.ts(self.adc[0:12, l, 0:1], self.adc[0:12, l, 0:1], -1.0, None, ALU.mult, None, W, W)
        P.dma("sp", self.adc[0:12, l, 1:2], self.ssd_d[l].rearrange("(a b) -> a b", b=1), writes=W)
        P.dma("sp", self.sinkc[0:8, l, 0:1], self.att_sinks[l].rearrange("(a b) -> a b", b=1), writes=W)
        self.ts(self.sinkc[0:8, l, 1:2], self.sinkc[0:8, l, 0:1], -1.0, None, ALU.mult, None, W, W)

    def job(self, l, p):
        ph = getattr(self, "phases", "xjqalso")
        if "x" in ph: self.load_x(l, p)
        if "j" in ph: self.job_prologue(l, p)
        if "q" in ph: self.phase_qkv(l, p)
        if "a" in ph: self.phase_att(l, p)
        if "l" in ph: self.phase_lru(l, p)
        if "s" in ph: self.phase_ssd(l, p)
        if l == 0 and p == 0:
            self.tap("mixT", self.mixT[:], [self.P.t("mixT", ec, tl) for ec in range(18) for tl in range(NT)])
        if "o" in ph: self.phase_out(l, p)

    def load_x(self, l, p):
        P, t = self.P, self.P.t
        if l != 0:
            return
        idf = self.cst[:, K_ID:K_ID + 128]
        for tl in range(NT):
            gt = p * NT + tl
            P.dma("sp", self.iost[:], self.x_prompt[gt * 128:(gt + 1) * 128, :], writes=[t("iost")])
            for hb in range(2):
                pb = self.bank(hb)
                for j in range(4):
                    dc = hb * 4 + j
                    self.tr(pb[:, j * 128:(j + 1) * 128], self.iost[:, dc * 128:(dc + 1) * 128], idf,
                            [t("iost"), t("cst")], [self.pT(hb)])
                dst = self.xT32[:, hb * 4:hb * 4 + 4, tl * 128:(tl + 1) * 128]
                self.vcopy(dst, pb.rearrange("p (a b) -> p a b", b=128), [self.pT(hb)],
                           [t("xT32", dc) for dc in range(hb * 4, hb * 4 + 4)])
        for dc in range(8):
            self.acopy(self.xTb[:, dc, :], self.xT32[:, dc, :], [t("xT32", dc)], [t("xTb", dc)])
        if p == 0:
            self.tap("xT_in", self.xT32[:], [t("xT32", dc) for dc in range(8)])

    def job_prologue(self, l, p):
        P, t = self.P, self.P.t
        self.vcopy(self.kT[:, 0:128], self.ck[:, l, :], [t("ck", l)], [t("kT", 0)], eng="pool")
        self.vcopy(self.vtm[:, 0, :], self.cv[:, l, :], [t("cv", l)], [t("v", 0)], eng="pool")
        self.vcopy(self.hTb[:], self.hT32[:, l, :], [t("hT32", l)], [t("hTb")], eng="pool")
        P.dma("pool", self.wab[0:96, :, :], self.lru_wa[l].rearrange("n c d -> c n d"), writes=[t("wab")])
        P.dma("pool", self.wxb[0:96, :, :], self.lru_wx[l].rearrange("n c d -> c n d"), writes=[t("wxb")])
        src = self.w_in_bf[l].rearrange("(kc p) c -> p kc c", p=128)
        P.dma("sp", self.wqkv[:], src[:, :, 0:768], reads=self.win_T(l), writes=[t("wqkv")])

    def phase_qkv(self, l, p):
        P, t = self.P, self.P.t
        xT = [t("xTb", kc) for kc in range(8)]
        last = (p == NPASS - 1)
        for tl in range(NT):
            gt = p * NT + tl
            pa, pb_, pc = (0, 1, 4) if tl % 2 == 0 else (2, 3, 5)
            A, B = self.bank(pa), self.bank(pb_)
            for kc in range(8):
                lhs = self.xTb[:, kc, tl * 128:(tl + 1) * 128]
                self.mm(A, lhs, self.wqkv[:, kc, 0:512], kc == 0, kc == 7, xT + [t("wqkv")], [self.pT(pa)])
                self.mm(B[:, 0:256], lhs, self.wqkv[:, kc, 512:768], kc == 0, kc == 7, xT + [t("wqkv")], [self.pT(pb_)])
            q32 = self.qkv32[tl % 2]
            Q = [t("qkv32", tl % 2)]
            self.acopy(q32[:, 0:512], A, [self.pT(pa)], Q)
            self.acopy(q32[:, 512:768], B[:, 0:256], [self.pT(pb_)], Q)
            hv = q32[:, 0:640].rearrange("p (h d) -> p h d", d=64)
            x1, x2 = hv[:, :, 0:8], hv[:, :, 8:16]
            cs = bc(self.cosT[:, gt, :], [128, 10, 8], 1)
            sn = bc(self.sinT[:, gt, :], [128, 10, 8], 1)
            ns = bc(self.nsinT[:, gt, :], [128, 10, 8], 1)
            R = [t("ropet")]
            ra, rb = self.ropet[:, 0, :, :], self.ropet[:, 1, :, :]
            self.tt(rb[:, :, 0:8], x2, ns, ALU.mult, Q + [t("rope")], R)
            self.tt(rb[:, :, 8:16], x1, sn, ALU.mult, Q + [t("rope")], R)
            self.tt(ra[:, :, 0:8], x1, cs, ALU.mult, Q + [t("rope")], R)
            self.tt(ra[:, :, 8:16], x2, cs, ALU.mult, Q + [t("rope")], R)
            self.tt(hv[:, :, 0:16], ra, rb, ALU.add, R, Q)
            if l == 0 and p == 0 and tl == 1:
                self.tap("qkv_t1", q32[:], Q)
            if last and tl == NT - 1:
                P.dma("pool", self.p_swa_k[l], q32[:, 512:640], reads=Q)
                P.dma("pool", self.p_swa_v[l], q32[:, 640:768], reads=Q)
            self.acopy(self.qkbf[:, 0:512].rearrange("p (c w d) -> p c w d", c=4, w=2),
                       q32[:, 0:512].rearrange("p (w c d) -> p c w d", w=2, c=4), Q, [t("qkbf")])
            self.acopy(self.qkbf[:, 512:640], q32[:, 512:640], Q, [t("qkbf")])
            self.vcopy(self.vtm[:, 1 + tl, :], q32[:, 640:768], Q, [t("v", 1 + tl)], eng="pool")
            C = self.bankbf(pc)
            for j in range(5):
                self.tr(C[:, j * 128:(j + 1) * 128], self.qkbf[:, j * 128:(j + 1) * 128], self.ident_bf[:],
                        [t("qkbf"), t("ident_bf")], [self.pT(pc)])
            self.vcopy(self.qT[:, :, tl * 128:(tl + 1) * 128], C[:, 0:512].rearrange("p (c q) -> p c q", q=128),
                       [self.pT(pc)], [t("qT", tl)])
            self.vcopy(self.kT[:, (1 + tl) * 128:(2 + tl) * 128], C[:, 512:640], [self.pT(pc)], [t("kT", 1 + tl)])
        self.vcopy(self.ck[:, l, :], self.kT[:, 512:640], [t("kT", 4)], [t("ck", l)], eng="pool")
        self.vcopy(self.cv[:, l, :], self.vtm[:, 4, :], [t("v", 4)], [t("cv", l)], eng="pool")

    def phase_att(self, l, p):
        P, t = self.P, self.P.t
        cnt = 0
        for c in range(4):
            k = self.load_win(l, [(C_GA + c * 64, 64, 0), (C_GA + (c + 4) * 64, 64, 64)])
            gb = c % 2
            pg = self.bank(gb)
            self.inproj(l, k, 128, pg, self.pT(gb))
            G = self.G[gb]
            self.act(G[:], pg, AF.Silu, [self.pT(gb)], [t("G", gb)])
            for tl in range(NT):
                gt = p * NT + tl
                i = cnt % 2
                cnt += 1
                bS, bP, bO = 2 + i, 4 + i, 6 + i
                S = self.bank(bS)
                mk = self.maskbf[:, 1 if gt == 0 else 0, :]
                for h in range(2):
                    rows = slice(h * 64, (h + 1) * 64)
                    self.mm(S[:, h * 256:(h + 1) * 256], self.qT[rows, c, tl * 128:(tl + 1) * 128],
                            self.kT[rows, tl * 128:tl * 128 + 256], True, False,
                            [t("qT", tl), t("kT", tl), t("kT", tl + 1)], [self.pT(bS)])
                    self.mm(S[:, h * 256:(h + 1) * 256], self.ident_bf[:], mk, False, True,
                            [t("ident_bf"), t("maskbf")], [self.pT(bS)])
                st = self.ast[i]
                A = [t("ast", i)]
                mx, negm, ssum, es, den, rden = (st[:, j, :] for j in range(6))
                P.op("dve", lambda e, mx=mx, S=S: e.reduce_max(out=mx, in_=S.rearrange("p (h k) -> p h k", k=256), axis=AX.X),
                     [self.pT(bS)], A)
                hsel = self.nsinkb[:, l, c:c + 5:4]
                self.stt(negm, mx, -SCALE, hsel, ALU.mult, ALU.min, A + [t("par", l)], A)
                Pm = self.Pm[i]
                for h in range(2):
                    self.act(Pm[:, h * 256:(h + 1) * 256], S[:, h * 256:(h + 1) * 256], AF.Exp,
                             [self.pT(bS)] + A, [t("Pm", i)] + A, scale=SCALE, bias=negm[:, h:h + 1], accum_out=ssum[:, h:h + 1])
                self.tt(es, negm, self.sinkb[:, l, c:c + 5:4], ALU.add, A + [t("par", l)], A)
                self.act(es, es, AF.Exp, A, A)
                self.tt(den, ssum, es, ALU.add, A, A)
                P.op("dve", lambda e, rden=rden, den=den: e.reciprocal(out=rden, in_=den), A, A)
                pv = Pm[:].rearrange("p (h k) -> p h k", k=256)
                self.tt(pv, pv, bc(rden, [128, 2, 256], 2), ALU.mult, [t("Pm", i)] + A, [t("Pm", i)])
                PTp = self.bankbf(bP)
                for h in range(2):
                    for kb in range(2):
                        j = h * 2 + kb
                        self.tr(PTp[:, j * 128:(j + 1) * 128], Pm[:, h * 256 + kb * 128:h * 256 + (kb + 1) * 128],
                                self.ident_bf[:], [t("Pm", i), t("ident_bf")], [self.pT(bP)])
                PT = self.PT[i]
                self.acopy(PT[:], PTp[:, 0:512], [self.pT(bP)], [t("PT", i)])
                O = self.bank(bO)
                for h in range(2):
                    for kb in range(2):
                        j = h * 2 + kb
                        self.mm(O[h * 64:(h + 1) * 64, 0:128], self.vtm[:, tl + kb, h * 64:(h + 1) * 64],
                                PT[:, j * 128:(j + 1) * 128], kb == 0, kb == 1,
                                [t("v", tl + kb), t("PT", i)], [self.pT(bO)])
                self.tt(self.mixT[:, c, tl * 128:(tl + 1) * 128], O[:, 0:128], G[:, tl * 128:(tl + 1) * 128], ALU.mult,
                        [self.pT(bO), t("G", gb)], [t("mixT", c, tl)])

    def conv4(self, out, xpad, par, n, M, rd, wr):
        self.ts(out[0:M, :], xpad[0:M, 0:H], par[0:M, 0:1], par[0:M, 4:5], ALU.mult, ALU.add, rd, wr)
        for k in range(1, 4):
            self.stt(out[0:M, :], xpad[0:M, k:k + H], par[0:M, k:k + 1], out[0:M, :], ALU.mult, ALU.add, rd + wr, wr)

    def phase_lru(self, l, p):
        P, t = self.P, self.P.t
        last = (p == NPASS - 1)
        for n in range(8):
            kx = self.load_win(l, [(C_XL + n * 96, 96, 0)])
            kg = self.load_win(l, [(C_GL + n * 96, 96, 0)])
            b0 = 0 if n % 2 == 0 else 4
            px, pg, pr, pi = (self.bank(b0 + j) for j in range(4))
            pxT, pgT, prT, piT = (self.pT(b0 + j) for j in range(4))
            self.inproj(l, kx, 96, px, pxT)
            self.inproj(l, kg, 96, pg, pgT)
            par = self.lp[:, l, n, :]
            PR = [t("par", l)]
            xp, xl, xlb, rr, ig, EE = self.xpad, self.xl, self.xlb, self.rr, self.ig, self.EE
            self.vcopy(xp[0:96, 0:3], self.hist_l[0:96, l, n, :], [t("hist_l", l)], [t("xpad")])
            self.acopy(xp[0:96, 3:3 + H], px[0:96, :], [pxT], [t("xpad")])
            self.vcopy(self.hist_l[0:96, l, n, :], xp[0:96, H:H + 3], [t("xpad")], [t("hist_l", l)])
            self.conv4(xl, xp, par, n, 96, [t("xpad")] + PR, [t("xl")])
            self.acopy(xlb[0:96, :], xl[0:96, :], [t("xl")], [t("xlb")])
            self.mm(pr[0:96, :], self.wab[0:96, n, :], xlb[0:96, :], True, True, [t("wab"), t("xlb")], [prT])
            self.mm(pi[0:96, :], self.wxb[0:96, n, :], xlb[0:96, :], True, True, [t("wxb"), t("xlb")], [piT])
            self.act(rr[0:96, :], pr[0:96, :], AF.Sigmoid, [prT] + PR, [t("rr")], bias=par[0:96, 5:6])
            self.act(ig[0:96, :], pi[0:96, :], AF.Sigmoid, [piT] + PR, [t("ig")], bias=par[0:96, 6:7])
            self.act(EE[0:96, :], rr[0:96, :], AF.Exp, [t("rr")] + PR, [t("EE")], scale=self.la[0:96, l, n, 1:2])
            self.act(EE[0:96, :], EE[0:96, :], AF.Sqrt, [t("EE")], [t("EE")], scale=-1.0, bias=1.0)
            self.act(rr[0:96, :], rr[0:96, :], AF.Exp, [t("rr")] + PR, [t("rr")], scale=self.la[0:96, l, n, 0:1])
            self.tt(ig[0:96, :], ig[0:96, :], xl[0:96, :], ALU.mult, [t("ig"), t("xl")], [t("ig")])
            self.tt(ig[0:96, :], ig[0:96, :], EE[0:96, :], ALU.mult, [t("ig"), t("EE")], [t("ig")])
            P.op("dve", lambda e, n=n: e.tensor_tensor_scan(out=EE[0:96, :], data0=rr[0:96, :], data1=ig[0:96, :],
                                                            initial=self.hcar[0:96, l, n:n + 1], op0=ALU.mult, op1=ALU.add),
                 [t("rr"), t("ig"), t("hcar", l)], [t("EE")])
            self.vcopy(self.hcar[0:96, l, n:n + 1], EE[0:96, H - 1:H], [t("EE")], [t("hcar", l)])
            self.act(xl[0:96, :], pg[0:96, :], AF.Silu, [pgT], [t("xl")])
            self.tt(self.mixT[0:96, 4 + n, :], EE[0:96, :], xl[0:96, :], ALU.mult, [t("EE"), t("xl")],
                    [t("mixT", 4 + n, tl) for tl in range(NT)])
        if last:
            for k3 in range(3):
                P.dma("pool", self.p_lru_conv[l, k3].rearrange("(n p) -> p n", p=96), self.hist_l[0:96, l, :, k3],
                      reads=[t("hist_l", l)], allow_slow_non_contiguous=True)
            P.dma("pool", self.p_lru_h[l].rearrange("(n p) -> p n", p=96), self.hcar[0:96, l, :],
                  reads=[t("hcar", l)], allow_slow_non_contiguous=True)

    def phase_ssd(self, l, p):
        P, t = self.P, self.P.t
        last = (p == NPASS - 1)
        cst = self.cst
        idf = cst[:, K_ID:K_ID + 128]
        U = cst[:, K_U:K_U + 128]
        ones = cst[:, K_ONE:K_ONE + 128]
        PR = [t("par", l)]
        nb = 0
        for c in range(6):
            k = self.load_win(l, [(C_Z + c * 128, 128, 0)])
            b = nb % 2; nb += 1
            pb = self.bank(b); pbT = self.pT(b)
            self.inproj(l, k, 128, pb, pbT)
            self.act(self.zs[:, c, :], pb, AF.Silu, [pbT], [t("zs", c)])
        xp, acc = self.xpad, self.xl
        for c in range(10):
            k = self.load_win(l, [(C_XBC + c * 128, 128, 0)])
            b = nb % 2; nb += 1
            pb = self.bank(b); pbT = self.pT(b)
            self.inproj(l, k, 128, pb, pbT)
            self.vcopy(xp[:, 0:3], self.hist_s[:, l, c, :], [t("hist_s", l)], [t("xpad")])
            self.acopy(xp[:, 3:3 + H], pb, [pbT], [t("xpad")])
            self.vcopy(self.hist_s[:, l, c, :], xp[:, H:H + 3], [t("xpad")], [t("hist_s", l)])
            self.conv4(acc, xp, self.spp[:, l, c, :], c, 128, [t("xpad")] + PR, [t("xl")])
            if c < 6:
                dst, dT = self.xsT[:, c, :], t("xsT", c)
            elif c < 8:
                dst, dT = self.BT[:, c - 6, :], t("BT", c - 6)
            else:
                dst, dT = self.CT[:, c - 8, :], t("CT", c - 8)
            self.act(dst, acc[:], AF.Silu, [t("xl")], [dT])
        k = self.load_win(l, [(C_DT, 12, 0)])
        b = nb % 2; nb += 1
        pb = self.bank(b); pbT = self.pT(b)
        self.inproj(l, k, 12, pb, pbT)
        u, v = self.dtT[0:12, :], self.dtt[0:12, :]
        self.act(u, pb[0:12, :], AF.Identity, [pbT] + PR, [t("dtT")], bias=self.dtb[0:12, l:l + 1])
        self.act(v, u, AF.Abs, [t("dtT")], [t("dtt")])
        self.act(v, v, AF.Exp, [t("dtt")], [t("dtt")], scale=-1.0)
        self.act(v, v, AF.Ln, [t("dtt")], [t("dtt")], bias=1.0)
        self.stt(u, u, 0.0, v, ALU.max, ALU.add, [t("dtT"), t("dtt")], [t("dtT")])
        if last:
            for k3 in range(3):
                P.dma("pool", self.p_ssd_conv[l, k3].rearrange("(c p) -> p c", p=128), self.hist_s[:, l, :, k3],
                      reads=[t("hist_s", l)], allow_slow_non_contiguous=True)
        if SSD_STOP == 1: return
        tsm = self.tsm
        dt_tm, a_tm, acs_tm, cd, tmp, dec, dtdec = (tsm[:, j, :] for j in range(7))
        TS = [t("tsm")]
        pacs = self.bank(3, 3)
        pacsT = [self.pT(3), self.pT(4), self.pT(5)]
        py = self.bank(6, 2)
        pyT = [self.pT(6), self.pT(7)]
        for tl in range(NT):
            sl = slice(tl * 128, (tl + 1) * 128)
            p0 = self.bankbf(0)
            for c in range(6):
                self.tr(p0[:, c * 128:(c + 1) * 128], self.xsT[:, c, sl], self.ident_bf[:], [t("xsT", c), t("ident_bf")], [self.pT(0)])
            self.acopy(self.xstm[:], p0[:, 0:768], [self.pT(0)], [t("xstm")])
            p1b = self.bankbf(1)
            p1 = self.bank(1)
            for g in range(2):
                self.tr(p1b[:, g * 128:(g + 1) * 128], self.BT[:, g, sl], self.ident_bf[:], [t("BT", g), t("ident_bf")], [self.pT(1)])
            p2 = self.bank(2)
            self.tr(p2[:, 256:268], self.dtT[0:12, sl], idf[0:12, 0:12], [t("dtT"), t("cst")], [self.pT(2)])
            self.acopy(self.Btm[:], p1b[:, 0:256], [self.pT(1)], [t("Btm")])
            self.vcopy(dt_tm, p2[:, 256:268], [self.pT(2)], TS)
            self.tt(a_tm, dt_tm, self.Ab[:, l, :], ALU.mult, TS + PR, TS)
            for g in range(2):
                self.mm(p2[:, g * 128:(g + 1) * 128], self.BT[:, g, sl], self.CT[:, g, sl], True, True,
                        [t("BT", g), t("CT", g)], [self.pT(2)])
            self.tt(self.CBm[:], p2[:, 0:256].rearrange("p (g q) -> p g q", q=128), bc(U, [128, 2, 128], 1), ALU.mult,
                    [self.pT(2), t("cst")], [t("CBm")])
            if SSD_STOP == 2: return
            self.tt(self.aU[:], bc(U, [128, 12, 128], 1), bc(a_tm, [128, 12, 128], 2), ALU.mult, TS + [t("cst")], [t("aU")])
            aUf = self.aU[:].rearrange("p e q -> p (e q)")
            for j in range(3):
                self.mm(pacs[:, j * 512:(j + 1) * 512], ones, aUf[:, j * 512:(j + 1) * 512], True, True,
                        [t("cst"), t("aU")], [pacsT[j]])
            self.mm(p2[:, 300:312], U, a_tm, True, True, [t("cst")] + TS, [self.pT(2)])
            self.vcopy(acs_tm, p2[:, 300:312], [self.pT(2)], TS)
            pav = pacs.rearrange("p (e q) -> p e q", q=128)
            for e in range(12):
                self.ts(self.Dm[:, e, :], pav[:, e, :], acs_tm[:, e:e + 1], 0.0, ALU.subtract, ALU.min,
                        [pacsT[e // 4]] + TS, [t("Dm")])
            if SSD_STOP == 3: return
            self.act(self.Dm[:], self.Dm[:], AF.Exp, [t("Dm")], [t("Dm")])
            self.tt(self.GT[:].rearrange("p (g e) q -> p g e q", g=2), self.Dm[:].rearrange("p (g e) q -> p g e q", g=2),
                    bc(self.CBm[:], [128, 2, 6, 128], 2), ALU.mult, [t("Dm"), t("CBm")], [t("GT")])
            self.act(self.Ebc[:], pav, AF.Exp, pacsT, [t("Ebc")])
            self.act(cd, pav[:, :, 127], AF.Exp, pacsT, TS)
            self.tt(tmp, pav[:, :, 127], acs_tm, ALU.subtract, pacsT + TS, TS)
            self.act(dec, tmp, AF.Exp, TS, TS)
            self.tt(dtdec, dt_tm, dec, ALU.mult, TS, TS)
            self.tt(self.CE[:].rearrange("p (g e) q -> p g e q", g=2), self.Ebc[:].rearrange("p (g e) q -> p g e q", g=2),
                    bc(self.CT[:, :, sl], [128, 2, 6, 128], 2), ALU.mult, [t("Ebc"), t("CT", 0), t("CT", 1)], [t("CE")])
            if SSD_STOP == 4: return
            xs3 = self.xstm[:].rearrange("p (e d) -> p e d", d=64)
            f3 = lambda x: x[:].rearrange("p (e d) -> p e d", d=64)
            self.tt(f3(self.xdt), xs3, bc(dt_tm, [128, 12, 64], 2), ALU.mult, [t("xstm")] + TS, [t("xdt")])
            self.tt(f3(self.xdd), xs3, bc(dtdec, [128, 12, 64], 2), ALU.mult, [t("xstm")] + TS, [t("xdd")])
            self.tt(f3(self.xsD), xs3, bc(self.Db[:, l, :], [128, 12, 64], 2), ALU.mult, [t("xstm")] + PR, [t("xsD")])
            for e in range(12):
                o = py[(e % 2) * 64:(e % 2) * 64 + 64, (e // 2) * 128:(e // 2) * 128 + 128]
                es_ = slice(e * 64, (e + 1) * 64)
                W = [pyT[(e // 2) // 4]]
                self.mm(o, self.xdt[:, es_], self.GT[:, e, :], True, False, [t("xdt"), t("GT")], W)
                self.mm(o, self.hTb[:, es_], self.CE[:, e, :], False, False, [t("hTb"), t("CE")], W)
                self.mm(o, self.xsD[:, es_], self.ident_bf[:], False, True, [t("xsD"), t("ident_bf")], W)
            if SSD_STOP == 5: return
            for (c0, c1, g) in ((0, 384, 0), (384, 512, 1), (512, 768, 1)):
                self.mm(pacs[:, c0:c1], self.Btm[:, g * 128:(g + 1) * 128], self.xdd[:, c0:c1], True, True,
                        [t("Btm"), t("xdd")], pacsT[0:2])
            h3 = self.hT32[:, l, :].rearrange("p (e d) -> p e d", d=64)
            self.tt(h3, h3, bc(cd, [128, 12, 64], 2), ALU.mult, [t("hT32", l)] + TS, [t("hT32", l)])
            self.tt(self.hT32[:, l, :], self.hT32[:, l, :], pacs[:, 0:768], ALU.add, [t("hT32", l)] + pacsT[0:2], [t("hT32", l)])
            self.acopy(self.hTb[:], self.hT32[:, l, :], [t("hT32", l)], [t("hTb")])
            if SSD_STOP == 6: return
            self.tt(self.yg[:], py[:, 0:768].rearrange("p (c q) -> p c q", q=128), self.zs[:, :, sl], ALU.mult,
                    pyT + [t("zs", c) for c in range(6)], [t("yg")])
            self.act(self.sq[:], self.yg[:], AF.Square, [t("yg")], [t("sq")])
            p0f = self.bank(0)
            for c in range(6):
                self.mm(p0f[:, 0:128], ones, self.sq[:, c, :], c == 0, c == 5, [t("cst"), t("sq")], [self.pT(0)])
            self.act(self.rstd[:], p0f[:, 0:128], AF.Sqrt, [self.pT(0)], [t("rstd")], bias=768.0 * EPS)
            P.op("dve", lambda e: e.reciprocal(out=self.rstd[:], in_=self.rstd[:]), [t("rstd")], [t("rstd")])
            for c in range(6):
                self.stt(self.mixT[:, 12 + c, sl], self.yg[:, c, :], self.gS[:, l, c:c + 1], self.rstd[:], ALU.mult, ALU.mult,
                         [t("yg"), t("rstd")] + PR, [t("mixT", 12 + c, tl)])
        if last:
            for hb in range(2):
                pb = self.bank(hb)
                for j in range(3):
                    c = hb * 3 + j
                    self.tr(pb[:, j * 128:(j + 1) * 128], self.hT32[:, l, c * 128:(c + 1) * 128], idf, [t("hT32", l), t("cst")], [self.pT(hb)])
                self.vcopy(self.iost[:, hb * 384:(hb + 1) * 384], pb[:, 0:384], [self.pT(hb)], [t("iost")])
            P.dma("pool", self.p_ssd_h[l].rearrange("(c p) n -> p c n", p=128), self.iost[:, 0:768].rearrange("p (c n) -> p c n", n=128),
                  reads=[t("iost")])

    def phase_out(self, l, p, nco=H, x32=None, xb=None, tag=""):
        P, t = self.P, self.P.t
        cst = self.cst
        idf = cst[:, K_ID:K_ID + 128]
        ones = cst[:, K_ONE:K_ONE + 128]
        PR = [t("par", l)]
        src = self.w_out_bf[l]
        x32 = self.xT32 if x32 is None else x32
        xb = self.xTb if xb is None else xb
        smp = (nco != H)
        XT = (lambda dc: t("SxT32", dc)) if smp else (lambda dc: t("xT32", dc))
        XB = (lambda dc: t("SxTb", dc)) if smp else (lambda dc: t("xTb", dc))
        MT = (lambda ec: [t("SmixT", ec)]) if smp else (lambda ec: [t("mixT", ec, tl) for tl in range(NT)])
        bk = lambda i: self.bank(i)[:, 0:nco]
        for ec in range(18):
            k = self.next_wbuf()
            wv = self.wbuf[k][:].rearrange("p a b -> p (a b)")
            if ec < 4:
                R = 128
                pieces = [(ec * 64, 64, 0), ((ec + 4) * 64, 64, 64)]
            elif ec < 12:
                R = 96
                pieces = [(512 + (ec - 4) * 96, 96, 0)]
            else:
                R = 128
                pieces = [(1280 + (ec - 12) * 128, 128, 0)]
            for (r0, n, d0) in pieces:
                P.dma("sp", wv[d0:d0 + n, :], src[r0:r0 + n, :], reads=self.wout_T(l), writes=[t("wbuf", k)])
            for dc in range(8):
                self.mm(bk(dc), wv[0:R, dc * 128:(dc + 1) * 128], self.mixT[0:R, ec, 0:nco], ec == 0, ec == 17,
                        [t("wbuf", k)] + MT(ec), [self.pT(dc)])
        for dc in range(8):
            X = [XT(dc)]
            self.stt(x32[:, dc, :], x32[:, dc, :], ALPHA, bk(dc), ALU.mult, ALU.add, X + [self.pT(dc)], X)
        for dc in range(8):
            X = [XT(dc)]
            q = self.lnq[dc % 2]
            self.act(q[:, 0:nco], x32[:, dc, :], AF.Square, X, [t("lnq", dc % 2)])
            self.mm(bk(0), ones, x32[:, dc, :], dc == 0, dc == 7, [t("cst")] + X, [self.pT(0)])
            self.mm(bk(1), ones, q[:, 0:nco], dc == 0, dc == 7, [t("cst"), t("lnq", dc % 2)], [self.pT(1)])
        M, Rs = [t("mean")], [t("lrs")]
        mean, lrs = self.mean[:, 0:nco], self.lrs[:, 0:nco]
        self.ts(mean, bk(0), 1.0 / D, None, ALU.mult, None, [self.pT(0)], M)
        self.tt(lrs, mean, mean, ALU.mult, M, Rs)
        self.stt(lrs, bk(1), 1.0 / D, lrs, ALU.mult, ALU.subtract, [self.pT(1)] + Rs, Rs)
        self.act(lrs, lrs, AF.Sqrt, Rs, Rs, bias=EPS)
        P.op("dve", lambda e: e.reciprocal(out=lrs, in_=lrs), Rs, Rs)
        for dc in range(8):
            X = [XT(dc)]
            xv = x32[:, dc, :]
            self.tt(xv, xv, mean, ALU.subtract, X + M, X)
            self.tt(xv, xv, lrs, ALU.mult, X + Rs, X)
            self.act(xv, xv, AF.Identity, X + PR, X, scale=self.lnp[:, l, dc, 0:1], bias=self.lnp[:, l, dc, 1:2])
            if l < DEPTH - 1:
                self.acopy(xb[:, dc, 0:nco], xv, X, [XB(dc)])
        if l == 0 and p == 0:
            self.tap("x_l0p0", self.xT32[:], [t("xT32", dc) for dc in range(8)])
        if smp:
            if l == DEPTH - 1:
                pb = self.bank(2, 2)
                for dc in range(8):
                    self.tr(pb[0:NS, dc * 128:(dc + 1) * 128], x32[:, dc, :], idf, [XT(dc), t("cst")], [self.pT(2 + dc // 4)])
                self.vcopy(self.iost[0:NS, :], pb[0:NS, :], [self.pT(2), self.pT(3)], [t("iost")])
                P.dma("pool", self.y_sample, self.iost[0:NS, :], reads=[t("iost")])
        elif l == DEPTH - 1:
            for tl in range(NT):
                gt = p * NT + tl
                for hb in range(2):
                    pb = self.bank(2 + hb)
                    for j in range(4):
                        dc = hb * 4 + j
                        self.tr(pb[:, j * 128:(j + 1) * 128], self.xT32[:, dc, tl * 128:(tl + 1) * 128], idf,
                                [t("xT32", dc), t("cst")], [self.pT(2 + hb)])
                    self.vcopy(self.iost[:, hb * 512:(hb + 1) * 512], pb, [self.pT(2 + hb)], [t("iost")])
                P.dma("pool", self.y_prompt[gt * 128:(gt + 1) * 128, :], self.iost[:], reads=[t("iost")])

    def sample_job(self, l):
        P, t = self.P, self.P.t
        cst = self.cst
        idf = cst[:, K_ID:K_ID + 128]
        ones = cst[:, K_ONE:K_ONE + 128]
        i16 = idf[0:NS, 0:NS]
        PR = [t("par", l)]
        x32 = self.rstd[:].rearrange("p (a b) -> p a b", b=NS)
        xb = self.xTb
        XB = [t("SxTb", kc) for kc in range(8)]
        XT = [t("SxT32", dc) for dc in range(8)]
        Bt = lambda n: [t("B", n)]
        f2 = lambda ap: ap.rearrange("p a b -> p (a b)")

        def fm(k, M, out, bT):
            for kc in range(8):
                self.mm(out, self.wbuf[k][:, kc, 0:M], xb[:, kc, 0:NS], kc == 0, kc == 7, [t("wbuf", k)] + XB, [bT])

        def tm(k, M, out, bT):
            for kc in range(8):
                self.mm(out, xb[:, kc, 0:NS], self.wbuf[k][:, kc, 0:M], kc == 0, kc == 7, [t("wbuf", k)] + XB, [bT])

        if l == 0:
            P.dma("sp", self.iost[0:NS, :], self.x_sample, writes=[t("iost")])
            pb = self.bank(0)
            for dc in range(8):
                self.tr(pb[:, dc * NS:(dc + 1) * NS], self.iost[0:NS, dc * 128:(dc + 1) * 128], i16, [t("iost"), t("cst")], [self.pT(0)])
            self.vcopy(x32, pb[:, 0:128].rearrange("p (a b) -> p a b", b=NS), [self.pT(0)], XT)
            for dc in range(8):
                self.acopy(xb[:, dc, 0:NS], x32[:, dc, :], [XT[dc]], [XB[dc]])
        P.dma("pool", self.wab[0:96, :, :], self.lru_wa[l].rearrange("n c d -> c n d"), writes=[t("wab")])
        P.dma("pool", self.wxb[0:96, :, :], self.lru_wx[l].rearrange("n c d -> c n d"), writes=[t("wxb")])

        b0, b1 = self.bank(0), self.bank(1)
        for j in range(6):
            k = self.load_win(l, [(j * 128, 128, 0)])
            if j < 4:
                tm(k, 128, b0[0:NS, j * 128:(j + 1) * 128], self.pT(0))
            else:
                tm(k, 128, b1[0:NS, (j - 4) * 128:(j - 3) * 128], self.pT(1))
        q32 = self.qkv32[0]
        Q = Bt("q0")
        self.acopy(q32[0:NS, 0:512], b0[0:NS, :], [self.pT(0)], Q)
        self.acopy(q32[0:NS, 512:768], b1[0:NS, 0:256], [self.pT(1)], Q)
        hv = q32[0:NS, 0:640].rearrange("p (h d) -> p h d", d=64)
        x1, x2 = hv[:, :, 0:8], hv[:, :, 8:16]
        cs = bc(self.ropeS[0:NS, 0, :], [NS, 10, 8], 1)
        sn = bc(self.ropeS[0:NS, 1, :], [NS, 10, 8], 1)
        ns = bc(self.ropeS[0:NS, 2, :], [NS, 10, 8], 1)
        R = Bt("ropet")
        ra, rb = self.ropet[0:NS, 0, :, :], self.ropet[0:NS, 1, :, :]
        self.tt(rb[:, :, 0:8], x2, ns, ALU.mult, Q + [t("rope")], R)
        self.tt(rb[:, :, 8:16], x1, sn, ALU.mult, Q + [t("rope")], R)
        self.tt(ra[:, :, 0:8], x1, cs, ALU.mult, Q + [t("rope")], R)
        self.tt(ra[:, :, 8:16], x2, cs, ALU.mult, Q + [t("rope")], R)
        self.tt(hv[:, :, 0:16], ra, rb, ALU.add, R, Q)
        P.dma("pool", self.s_swa_k[l][:, 0:127, :], self.cache_k[l][:, 1:128, :])
        P.dma("pool", self.s_swa_v[l][:, 0:127, :], self.cache_v[l][:, 1:128, :])
        P.dma("pool", self.s_swa_k[l][:, 127, :], q32[0:NS, 512:640], reads=Q)
        P.dma("pool", self.s_swa_v[l][:, 127, :], q32[0:NS, 640:768], reads=Q)
        b2 = self.bank(2)
        for c in range(4):
            k = self.load_win(l, [(C_GA + c * 64, 64, 0), (C_GA + (c + 4) * 64, 64, 64)])
            fm(k, 128, b2[:, c * NS:(c + 1) * NS], self.pT(2))
        GsT = self.G[0][:, 0:64].rearrange("p (c b) -> p c b", b=NS)
        self.act(f2(GsT), b2[:, 0:64], AF.Silu, [self.pT(2)], Bt("G0"))
        kbf = self.qkbf[0:NS, 512:640]
        vnb = self.qkbf[0:NS, 0:128]
        self.acopy(kbf, q32[0:NS, 512:640], Q, Bt("qkbf"))
        self.acopy(vnb, q32[0:NS, 640:768], Q, Bt("qkbf"))
        qz = f2(self.CE[:])[0:NS, 0:1024].rearrange("p (h f) -> p h f", f=128)
        P.op("dve", lambda e: e.memset(f2(self.CE[:])[0:NS, 0:1024], 0.0), [], Bt("CE"))
        qh = q32[0:NS, 0:512].rearrange("p (h d) -> p h d", d=64)
        self.vcopy(qz[:, 0:4, 0:64], qh[:, 0:4, :], Q + Bt("CE"), Bt("CE"))
        self.vcopy(qz[:, 4:8, 64:128], qh[:, 4:8, :], Q + Bt("CE"), Bt("CE"))
        p3 = self.bankbf(3)
        ib16 = self.ident_bf[0:NS, 0:NS]
        for h in range(8):
            self.tr(p3[:, h * NS:(h + 1) * NS], qz[:, h, :], ib16, Bt("CE") + [t("ident_bf")], [self.pT(3)])
        self.tr(p3[:, 128:128 + NS], kbf, ib16, Bt("qkbf") + [t("ident_bf")], [self.pT(3)])
        qblk = self.PT[0][:, 0:128].rearrange("p (b h) -> p b h", h=8)
        knT = self.PT[0][:, 128:128 + NS]
        self.vcopy(qblk.rearrange("p b h -> p h b"), p3[:, 0:128].rearrange("p (h b) -> p h b", b=NS), [self.pT(3)], Bt("PT0"))
        self.vcopy(knT, p3[:, 128:128 + NS], [self.pT(3)], Bt("PT0"))
        st8 = self.wqkv[:].rearrange("p a b -> p (a b)").bitcast(F32)[:, 0:2048].rearrange("p (b f) -> p b f", f=128)
        P.dma("sp", st8, self.cache_k[l].rearrange("b k f -> k b f"), writes=Bt("st8"))
        for b in range(NS):
            self.tr(self.bank(4 + b // 4)[:, (b % 4) * 128:(b % 4 + 1) * 128], st8[:, b, :], idf, Bt("st8") + [t("cst")], [self.pT(4 + b // 4)])
        KcT = f2(self.xsT[:])[:, 0:2048].rearrange("p (b k) -> p b k", k=128)
        for i in range(4):
            self.acopy(f2(KcT[:, 4 * i:4 * i + 4, :]), self.bank(4 + i), [self.pT(4 + i)], Bt("xsT"))
        for b in range(NS):
            self.mm(self.bank(b // 4)[0:8, (b % 4) * 128:(b % 4 + 1) * 128], qblk[:, b, :], KcT[:, b, :], True, True,
                    Bt("PT0") + Bt("xsT"), [self.pT(b // 4)])
        b4 = self.bank(4)
        for b in range(NS):
            self.mm(b4[0:8, b:b + 1], qblk[:, b, :], knT[:, b:b + 1], True, True, Bt("PT0"), [self.pT(4)])
        stt_ = f2(self.tsm[:])[0:8, 0:96].rearrange("p (j b) -> p j b", b=NS)
        mx, negm, ssum, pnew, es, rden = (stt_[:, j, :] for j in range(6))
        A = Bt("tsm")
        S4 = self.ps[0:8, 0:2048].rearrange("p (b k) -> p b k", k=128)
        ST = [self.pT(i) for i in range(4)]
        P.op("dve", lambda e: e.reduce_max(out=mx, in_=S4, axis=AX.X), ST, A)
        self.tt(mx, mx, b4[0:8, 0:NS], ALU.max, A + [self.pT(4)], A)
        self.stt(negm, mx, -SCALE, self.sinkc[0:8, l, 1:2].to_broadcast([8, NS]), ALU.mult, ALU.min, A + PR, A)
        Pms = f2(self.Dm[:]).bitcast(BF16)[0:8, 0:2048]
        for b in range(NS):
            self.act(Pms[:, b * 128:(b + 1) * 128], S4[:, b, :], AF.Exp, [self.pT(b // 4)] + A, Bt("Dm") + A,
                     scale=SCALE, bias=negm[:, b:b + 1], accum_out=ssum[:, b:b + 1])
        self.stt(pnew, b4[0:8, 0:NS], SCALE, negm, ALU.mult, ALU.add, A + [self.pT(4)], A)
        self.act(pnew, pnew, AF.Exp, A, A)
        self.act(es, negm, AF.Exp, A + PR, A, bias=self.sinkc[0:8, l, 0:1])
        self.tt(ssum, ssum, pnew, ALU.add, A, A)
        self.tt(ssum, ssum, es, ALU.add, A, A)
        P.op("dve", lambda e: e.reciprocal(out=rden, in_=ssum), A, A)
        pv = Pms.rearrange("p (b k) -> p b k", k=128)
        self.tt(pv, pv, bc(rden, [8, NS, 128], 2), ALU.mult, Bt("Dm") + A, Bt("Dm"))
        self.tt(pnew, pnew, rden, ALU.mult, A, A)
        p5 = self.bankbf(5)
        for b in range(NS):
            self.tr(p5[:, b * 8:(b + 1) * 8], Pms[:, b * 128:(b + 1) * 128], self.ident_bf[0:8, 0:8], Bt("Dm") + [t("ident_bf")], [self.pT(5)])
        PTs = self.Pm[0][:, 0:128].rearrange("p (b h) -> p b h", h=8)
        self.acopy(f2(PTs), p5[:, 0:128], [self.pT(5)], Bt("Pm0"))
        b6 = self.bank(6)
        self.tr(b6[0:NS, 0:8], pnew, idf[0:8, 0:8], A + [t("cst")], [self.pT(6)])
        pnt = f2(self.ast[1][:])[0:NS, 0:8]
        self.vcopy(pnt, b6[0:NS, 0:8], [self.pT(6)], Bt("ast1"))
        psel = self.PT[1][0:NS, 0:128].rearrange("p (b h) -> p b h", h=8)
        self.tt(psel, bc(pnt, [NS, NS, 8], 1), bc(i16, [NS, NS, 8], 2), ALU.mult, Bt("ast1") + [t("cst")], Bt("PT1"))
        P.dma("sp", st8, self.cache_v[l].rearrange("b k f -> k b f"), writes=Bt("st8"))
        Vc = f2(self.qT[:]).rearrange("p (b f) -> p b f", f=128)
        self.vcopy(f2(Vc), f2(st8), Bt("st8"), Bt("qT"))
        b7 = self.bank(7)
        oT = b7[:, 0:128].rearrange("p (b h) -> p b h", h=8)
        for b in range(NS):
            self.mm(b7[:, b * 8:(b + 1) * 8], Vc[:, b, :], PTs[:, b, :], True, False, Bt("qT") + Bt("Pm0"), [self.pT(7)])
            self.mm(b7[:, b * 8:(b + 1) * 8], vnb, psel[:, b, :], False, True, Bt("qkbf") + Bt("PT1"), [self.pT(7)])
        for c in range(4):
            for w in range(2):
                rows = slice(w * 64, (w + 1) * 64)
                self.tt(self.mixT[rows, c, 0:NS], oT[rows, :, c + 4 * w], GsT[rows, c, :], ALU.mult,
                        [self.pT(7)] + Bt("G0"), [t("SmixT", c)])

        zf = f2(self.zs[:])
        stc = zf[0:NS, 0:2304]
        sth = zf[0:NS, 2304:3072]
        P.dma("sp", stc.rearrange("p (k f) -> p k f", f=768), self.st_lru_conv[l], writes=Bt("zs"))
        P.dma("sp", sth, self.st_lru_h[l], writes=Bt("zs"))
        P.dma("pool", self.s_lru_conv[l][:, 0:2, :], self.st_lru_conv[l][:, 1:3, :])
        b0, b1 = self.bank(0), self.bank(1)
        for n in range(8):
            for k3 in range(3):
                j = n * 3 + k3
                self.tr(b0[0:96, j * NS:(j + 1) * NS], stc[:, k3 * 768 + n * 96:k3 * 768 + (n + 1) * 96], i16, Bt("zs") + [t("cst")], [self.pT(0)])
            self.tr(b1[0:96, n * NS:(n + 1) * NS], sth[:, n * 96:(n + 1) * 96], i16, Bt("zs") + [t("cst")], [self.pT(1)])
        aUf = f2(self.aU[:])
        hs = aUf[0:96, 0:384].rearrange("p (n k b) -> p n k b", k=3, b=NS)
        h0T = aUf[0:96, 384:512].rearrange("p (n b) -> p n b", b=NS)
        self.vcopy(aUf[0:96, 0:384], b0[0:96, 0:384], [self.pT(0)], Bt("aU"))
        self.vcopy(aUf[0:96, 384:512], b1[0:96, 0:128], [self.pT(1)], Bt("aU"))
        b2, b5 = self.bank(2), self.bank(5)
        for n in range(8):
            kx = self.load_win(l, [(C_XL + n * 96, 96, 0)])
            fm(kx, 96, b2[0:96, n * NS:(n + 1) * NS], self.pT(2))
            tm(kx, 96, self.bank(3 + n // 4)[0:NS, (n % 4) * 128:(n % 4) * 128 + 96], self.pT(3 + n // 4))
            kg = self.load_win(l, [(C_GL + n * 96, 96, 0)])
            fm(kg, 96, b5[0:96, n * NS:(n + 1) * NS], self.pT(5))
        xltm = self.qkv32[1][0:NS, 0:768]
        self.vcopy(xltm.rearrange("p (n c) -> p n c", c=96),
                   self.ps[0:NS, 3 * 512:5 * 512].rearrange("p (n c) -> p n c", c=128)[:, :, 0:96], [self.pT(3), self.pT(4)], Bt("q1"))
        P.dma("pool", self.s_lru_conv[l][:, 2, :], xltm, reads=Bt("q1"))
        v3 = lambda ap: ap.rearrange("p (n b) -> p n b", b=NS)
        lpb = lambda j: bc(self.lp[0:96, l, :, j], [96, 8, NS], 2)
        acc = v3(self.xl[0:96, 0:128]); tmp = v3(self.xlb[0:96, 0:256].bitcast(F32))
        Xl, Tm = Bt("xl"), Bt("xlb")
        self.tt(acc, v3(b2[0:96, 0:128]), lpb(3), ALU.mult, [self.pT(2)] + PR, Xl)
        self.tt(acc, acc, lpb(4), ALU.add, Xl + PR, Xl)
        for k3 in range(3):
            self.tt(tmp, hs[:, :, k3, :], lpb(k3), ALU.mult, Bt("aU") + PR, Tm)
            self.tt(acc, acc, tmp, ALU.add, Xl + Tm, Xl)
        xlbf = self.dtt[0:96, 0:64].bitcast(BF16)
        self.acopy(xlbf, self.xl[0:96, 0:128], Xl, Bt("dtt"))
        b6, b7 = self.bank(6), self.bank(7)
        for n in range(8):
            self.mm(b6[0:96, n * NS:(n + 1) * NS], self.wab[0:96, n, :], xlbf[:, n * NS:(n + 1) * NS], True, True, [t("wab")] + Bt("dtt"), [self.pT(6)])
            self.mm(b7[0:96, n * NS:(n + 1) * NS], self.wxb[0:96, n, :], xlbf[:, n * NS:(n + 1) * NS], True, True, [t("wxb")] + Bt("dtt"), [self.pT(7)])
        rr, ig, EE = v3(self.rr[0:96, 0:128]), v3(self.ig[0:96, 0:128]), v3(self.EE[0:96, 0:128])
        Rr, Ig, Ee = Bt("rr"), Bt("ig"), Bt("EE")
        self.tt(rr, v3(b6[0:96, 0:128]), lpb(5), ALU.add, [self.pT(6)] + PR, Rr)
        self.act(rr, rr, AF.Sigmoid, Rr, Rr)
        self.tt(ig, v3(b7[0:96, 0:128]), lpb(6), ALU.add, [self.pT(7)] + PR, Ig)
        self.act(ig, ig, AF.Sigmoid, Ig, Ig)
        self.tt(EE, rr, bc(self.la[0:96, l, :, 1], [96, 8, NS], 2), ALU.mult, Rr + PR, Ee)
        self.act(EE, EE, AF.Exp, Ee, Ee)
        self.act(EE, EE, AF.Sqrt, Ee, Ee, scale=-1.0, bias=1.0)
        self.tt(rr, rr, bc(self.la[0:96, l, :, 0], [96, 8, NS], 2), ALU.mult, Rr + PR, Rr)
        self.act(rr, rr, AF.Exp, Rr, Rr)
        self.tt(ig, ig, acc, ALU.mult, Ig + Xl, Ig)
        self.tt(ig, ig, EE, ALU.mult, Ig + Ee, Ig)
        self.tt(EE, rr, h0T, ALU.mult, Rr + Bt("aU"), Ee)
        self.tt(EE, EE, ig, ALU.add, Ee + Ig, Ee)
        for n in range(8):
            self.tr(self.bank(3 + n // 4)[0:NS, (n % 4) * 128:(n % 4) * 128 + 96], self.EE[0:96, n * NS:(n + 1) * NS], idf[0:96, 0:96],
                    Ee + [t("cst")], [self.pT(3 + n // 4)])
        h1tm = f2(self.yg[:])[0:NS, 0:768]
        self.vcopy(h1tm.rearrange("p (n c) -> p n c", c=96),
                   self.ps[0:NS, 3 * 512:5 * 512].rearrange("p (n c) -> p n c", c=128)[:, :, 0:96], [self.pT(3), self.pT(4)], Bt("yg"))
        P.dma("pool", self.s_lru_h[l], h1tm, reads=Bt("yg"))
        sg = v3(self.dtT[0:96, 0:128])
        self.act(sg, v3(b5[0:96, 0:128]), AF.Silu, [self.pT(5)], Bt("dtT"))
        self.tt(self.mixT[0:96, 4:12, 0:NS], EE, sg, ALU.mult, Ee + Bt("dtT"), [t("SmixT", 4 + n) for n in range(8)])

        Dmf = f2(self.Dm[:])
        srcs = (zf[0:NS, 0:1280], zf[0:NS, 1280:2560], Dmf[0:NS, 0:1280])
        srcT = (Bt("zs"), Bt("zs"), Bt("Dm"))
        for k3 in range(3):
            P.dma("sp", srcs[k3], self.st_ssd_conv[l][:, k3, :], writes=srcT[k3])
        P.dma("pool", self.s_ssd_conv[l][:, 0:2, :], self.st_ssd_conv[l][:, 1:3, :])
        b0 = self.bank(0)
        for c in range(10):
            for k3 in range(3):
                j = c * 3 + k3
                self.tr(b0[:, j * NS:(j + 1) * NS], srcs[k3][:, c * 128:(c + 1) * 128], i16, srcT[k3] + [t("cst")], [self.pT(0)])
        sqf = f2(self.sq[:])
        hss = sqf[:, 0:480].rearrange("p (c k b) -> p c k b", k=3, b=NS)
        self.vcopy(sqf[:, 0:480], b0[:, 0:480], [self.pT(0)], Bt("sq"))
        b1 = self.bank(1)
        for c in range(10):
            k = self.load_win(l, [(C_XBC + c * 128, 128, 0)])
            fm(k, 128, b1[:, c * NS:(c + 1) * NS], self.pT(1))
            tm(k, 128, self.bank(2 + c // 4)[0:NS, (c % 4) * 128:(c % 4 + 1) * 128], self.pT(2 + c // 4))
        xbtm = aUf[0:NS, 0:1280]
        self.vcopy(xbtm, self.ps[0:NS, 2 * 512:2 * 512 + 1280], [self.pT(2), self.pT(3), self.pT(4)], Bt("aU"))
        P.dma("pool", self.s_ssd_conv[l][:, 2, :], xbtm, reads=Bt("aU"))
        b5, b6 = self.bank(5), self.bank(6)
        for c in range(6):
            k = self.load_win(l, [(C_Z + c * 128, 128, 0)])
            fm(k, 128, b5[:, c * NS:(c + 1) * NS], self.pT(5))
        k = self.load_win(l, [(C_DT, 12, 0)])
        fm(k, 12, b6[0:12, 0:NS], self.pT(6))
        spb = lambda j: bc(self.spp[:, l, :, j], [128, 10, NS], 2)
        acc = v3(self.xl[:, 0:160]); tmp = v3(self.xlb[:, 0:320].bitcast(F32))
        self.tt(acc, v3(b1[:, 0:160]), spb(3), ALU.mult, [self.pT(1)] + PR, Xl)
        self.tt(acc, acc, spb(4), ALU.add, Xl + PR, Xl)
        for k3 in range(3):
            self.tt(tmp, hss[:, :, k3, :], spb(k3), ALU.mult, Bt("sq") + PR, Tm)
            self.tt(acc, acc, tmp, ALU.add, Xl + Tm, Xl)
        xc = v3(self.rr[:, 0:160])
        self.act(xc, acc, AF.Silu, Xl, Rr)
        xsTs, BsT, CsT = xc[:, 0:6, :], xc[:, 6:8, :], xc[:, 8:10, :]
        zsT = v3(self.ig[:, 0:96])
        self.act(zsT, v3(b5[:, 0:96]), AF.Silu, [self.pT(5)], Ig)
        u, v = self.dtT[0:12, 0:NS], self.dtt[0:12, 0:NS]
        self.act(u, b6[0:12, 0:NS], AF.Identity, [self.pT(6)] + PR, Bt("dtT"), bias=self.dtb[0:12, l:l + 1])
        self.act(v, u, AF.Abs, Bt("dtT"), Bt("dtt"))
        self.act(v, v, AF.Exp, Bt("dtt"), Bt("dtt"), scale=-1.0)
        self.act(v, v, AF.Ln, Bt("dtt"), Bt("dtt"), bias=1.0)
        self.stt(u, u, 0.0, v, ALU.max, ALU.add, Bt("dtT") + Bt("dtt"), Bt("dtT"))
        Ex = cst[0:12, K_EX:K_EX + 768]
        b7 = self.bank(7)
        for c in range(6):
            Exc = Ex[:, c * 128:(c + 1) * 128]
            self.mm(b7[:, c * NS:(c + 1) * NS], Exc, u, True, True, [t("cst")] + Bt("dtT"), [self.pT(7)])
            self.mm(b7[:, 96 + c:97 + c], Exc, self.adc[0:12, l, 0:1], True, True, [t("cst")] + PR, [self.pT(7)])
            self.mm(b7[:, 104 + c:105 + c], Exc, self.adc[0:12, l, 1:2], True, True, [t("cst")] + PR, [self.pT(7)])
        ex = self.EE[:, 0:112]
        self.vcopy(ex, b7[:, 0:112], [self.pT(7)], Ee)
        dtx = v3(ex[:, 0:96]); Ax = ex[:, 96:102]; Dx = ex[:, 104:110]
        decT = v3(self.mean[:, 0:96]); x0T = v3(self.lrs[:, 0:96])
        self.tt(decT, dtx, bc(Ax, [128, 6, NS], 2), ALU.mult, Ee, [t("mean")])
        self.act(decT, decT, AF.Exp, [t("mean")], [t("mean")])
        self.tt(x0T, xsTs, dtx, ALU.mult, Rr + Ee, [t("lrs")])
        yT = v3(self.lnq[0][:, 0:96])
        hbuf = [f2(self.xT32[:])[:, i * 2048:(i + 1) * 2048].rearrange("p (b n) -> p b n", n=128) for i in range(2)]
        st8v = st8
        Bbc = self.ps[:, 0:2048].rearrange("p (b n) -> p b n", n=128)
        Cbc = self.ps[:, 2048:4096].rearrange("p (b n) -> p b n", n=128)
        idb = bc(idf, [128, NS, 128], 1)
        for g in range(2):
            self.tt(st8v, bc(BsT[:, g, :], [128, NS, 128], 2), idb, ALU.mult, Rr + [t("cst")], Bt("st8"))
            for j in range(4):
                self.mm(self.bank(j), ones, f2(st8v)[:, j * 512:(j + 1) * 512], True, True, [t("cst")] + Bt("st8"), [self.pT(j)])
            self.tt(st8v, bc(CsT[:, g, :], [128, NS, 128], 2), idb, ALU.mult, Rr + [t("cst")], Bt("st8"))
            for j in range(4):
                self.mm(self.bank(4 + j), ones, f2(st8v)[:, j * 512:(j + 1) * 512], True, True, [t("cst")] + Bt("st8"), [self.pT(4 + j)])
            for c in range(3 * g, 3 * g + 3):
                hb = hbuf[c % 2]
                Hb = [t("B", "hb", c % 2)]
                P.dma("sp", hb, self.st_ssd_h[l][:, 2 * c:2 * c + 2].rearrange("b e p n -> (e p) b n"), writes=Hb)
                self.tt(hb, hb, bc(decT[:, c, :], [128, NS, 128], 2), ALU.mult, Hb + [t("mean")], Hb)
                self.tt(st8v, Bbc, bc(x0T[:, c, :], [128, NS, 128], 2), ALU.mult, [self.pT(j) for j in range(4)] + [t("lrs")], Bt("st8"))
                self.tt(hb, hb, st8v, ALU.add, Hb + Bt("st8"), Hb)
                P.dma("pool", self.s_ssd_h[l][:, c * 128:(c + 1) * 128, :].rearrange("b f n -> f b n"), hb, reads=Hb)
                self.tt(st8v, hb, Cbc, ALU.mult, Hb + [self.pT(4 + j) for j in range(4)], Bt("st8"))
                P.op("dve", lambda e, c=c: e.reduce_sum(out=yT[:, c, :], in_=st8v, axis=AX.X), Bt("st8"), [t("lnq", 0)])
        self.tt(tmp[:, 0:6, :], xsTs, bc(Dx, [128, 6, NS], 2), ALU.mult, Rr + Ee, Tm)
        self.tt(yT, yT, tmp[:, 0:6, :], ALU.add, [t("lnq", 0)] + Tm, [t("lnq", 0)])
        self.tt(yT, yT, zsT, ALU.mult, [t("lnq", 0)] + Ig, [t("lnq", 0)])
        sqs = v3(self.lnq[1][:, 0:96])
        self.act(sqs, yT, AF.Square, [t("lnq", 0)], [t("lnq", 1)])
        b0 = self.bank(0)
        for c in range(6):
            self.mm(b0[:, 0:NS], ones, sqs[:, c, :], c == 0, c == 5, [t("cst"), t("lnq", 1)], [self.pT(0)])
        rs = self.dtt[:, 64:64 + NS]
        self.act(rs, b0[:, 0:NS], AF.Sqrt, [self.pT(0)], Bt("dtt"), bias=768.0 * EPS)
        P.op("dve", lambda e: e.reciprocal(out=rs, in_=rs), Bt("dtt"), Bt("dtt"))
        for c in range(6):
            self.stt(self.mixT[:, 12 + c, 0:NS], yT[:, c, :], self.gS[:, l, c:c + 1], rs, ALU.mult, ALU.mult,
                     [t("lnq", 0)] + Bt("dtt") + PR, [t("SmixT", 12 + c)])

        self.phase_out(l, None, nco=NS, x32=x32, xb=self.xTb)


    def build(self, sample=True, n_pass=NPASS, n_layers=DEPTH):
        self.declare_io()
        self.alloc()
        self.prologue()
        for p in range(n_pass):
            for l in range(n_layers):
                self.job(l, p)
        if sample:
            self.P.fence()
            for l in range(n_layers):
                self.sample_job(l)
        self.P.emit()
        self.st.close()
        return self.nc


def make_consts():
    c = np.zeros((128, K_END), np.float32)
    c[:, K_ID:K_ID + 128] = np.eye(128, dtype=np.float32)
    s = np.arange(128)[:, None]
    q = np.arange(128)[None, :]
    c[:, K_U:K_U + 128] = (s <= q).astype(np.float32)
    c[:, K_ONE:K_ONE + 128] = 1.0
    i = np.arange(128)[:, None]
    j = np.arange(256)[None, :]
    band = (j >= i) & (j <= i + 128)
    c[:, K_MG:K_MG + 256] = np.where(band, 0.0, NEG)
    c[:, K_MF:K_MF + 256] = np.where(band & (j >= 128), 0.0, NEG)
    c[:, K_POS:K_POS + 16] = np.arange(128)[:, None] + 128.0 * np.arange(16)[None, :]
    c[:, K_INV:K_INV + 8] = (500000.0 ** (-np.arange(8, dtype=np.float32) / 8.0)).astype(np.float32)[None, :]
    for e in range(12):
        c[e, K_EX + e * 64:K_EX + (e + 1) * 64] = 1.0
    c[:, K_HM] = (np.arange(128) < 4)
    c[:, K_HM + 1] = (np.arange(128) >= 4)
    return c


WNAMES = ["w_in", "w_out", "att_sinks", "lru_conv_w", "lru_conv_b", "lru_wa", "lru_ba", "lru_wx", "lru_bx",
          "lru_lambda", "ssd_conv_w", "ssd_conv_b", "ssd_dt_bias", "ssd_a_log", "ssd_d", "ssd_norm_g", "ln_g", "ln_b"]


def make_in_maps(inputs, n=8):
    f = lambda a: np.ascontiguousarray(np.asarray(a, dtype=np.float32))
    shared = {k: f(inputs[k]) for k in WNAMES}
    shared["consts"] = make_consts()
    maps = []
    for i in range(n):
        s = slice(i * NS, (i + 1) * NS)
        m = dict(shared)
        m["x_prompt"] = f(inputs["x_prompt"][i])
        m["x_sample"] = f(np.asarray(inputs["x_sample"])[s, 0, :])
        m["cache_swa_k"] = f(np.asarray(inputs["cache_swa_k"])[:, s].reshape(DEPTH, NS, 128, 128))
        m["cache_swa_v"] = f(np.asarray(inputs["cache_swa_v"])[:, s].reshape(DEPTH, NS, 128, 128))
        m["state_lru_conv"] = f(np.asarray(inputs["state_lru_conv"])[:, s])
        m["state_lru_h"] = f(np.asarray(inputs["state_lru_h"])[:, s])
        m["state_ssd_conv"] = f(np.asarray(inputs["state_ssd_conv"])[:, s])
        m["state_ssd_h"] = f(np.asarray(inputs["state_ssd_h"])[:, s])
        maps.append(m)
    return maps


def kernel(**inputs):
    nc = Builder().build()
    res = run_bass_kernel_spmd(nc, make_in_maps(inputs), core_ids=list(range(8)))
    R = res.results
    cat = lambda k, ax: np.concatenate([np.asarray(r[k]) for r in R], axis=ax)
    stk = lambda k: np.stack([np.asarray(r[k]) for r in R], axis=1)
    y_prompt = np.stack([np.asarray(r["y_prompt"]) for r in R], 0)
    y_sample = cat("y_sample", 0).reshape(128, 1, D)
    p_swa_k = stk("p_swa_k").reshape(DEPTH, 8, 128, 2, 64)
    p_swa_v = stk("p_swa_v").reshape(DEPTH, 8, 128, 2, 64)
    p_lru_conv = stk("p_lru_conv")
    p_lru_h = stk("p_lru_h")
    p_ssd_conv = stk("p_ssd_conv")
    p_ssd_h = stk("p_ssd_h").reshape(DEPTH, 8, 12, 64, 128)
    s_swa_k = cat("s_swa_k", 1).reshape(DEPTH, 128, 128, 2, 64)
    s_swa_v = cat("s_swa_v", 1).reshape(DEPTH, 128, 128, 2, 64)
    s_lru_conv = cat("s_lru_conv", 1)
    s_lru_h = cat("s_lru_h", 1)
    s_ssd_conv = cat("s_ssd_conv", 1)
    s_ssd_h = cat("s_ssd_h", 1).reshape(DEPTH, 128, 12, 64, 128)
    outs = (y_prompt, y_sample, p_swa_k, p_swa_v, p_lru_conv, p_lru_h, p_ssd_conv, p_ssd_h,
            s_swa_k, s_swa_v, s_lru_conv, s_lru_h, s_ssd_conv, s_ssd_h)
    return tuple(np.ascontiguousarray(o, dtype=np.float32) for o in outs)
```

```python
import contextlib
import math
import numpy as np
import concourse.bass as bass
import concourse.mybir as mybir
from concourse.bass_utils import run_bass_kernel_spmd

F32 = mybir.dt.float32
BF16 = mybir.dt.bfloat16
I32 = mybir.dt.int32
AF = mybir.ActivationFunctionType
ALU = mybir.AluOpType
AX = mybir.AxisListType

D = 1024
SEQ = 2048
DEPTH = 4
NS = 16
DIN = 4876
C_Q, C_K, C_V, C_GA, C_XL, C_GL, C_Z, C_XBC, C_DT = 0, 512, 640, 768, 1280, 2048, 2816, 3584, 4864
SCALE = 64 ** -0.5
ALPHA = (2.0 * DEPTH) ** 0.25
EPS = 1e-5
H = 512
NT = H // 128
NPASS = SEQ // H
PAST = 8192.0
NEG = -30000.0
import os
SSD_STOP = int(os.environ.get('SSD_STOP', '0'))
PIN_ENGS = set(os.environ.get('PIN_ENGS', '').split(','))
UNPIN_LINES = set(int(x) for x in os.environ.get('UNPIN_LINES', '').split(',') if x)

K_ID, K_U, K_ONE, K_MG, K_MF, K_POS, K_INV, K_HM, K_EX, K_END = 0, 128, 256, 384, 640, 896, 912, 920, 922, 922 + 768


class T:
    __slots__ = ("name", "lw", "rd", "excl")

    def __init__(self, name):
        self.name = name
        self.lw = None
        self.rd = []
        self.excl = (len(name) > 0 and name[0] == "ps")


class Op:
    __slots__ = ("eng", "fn", "deps", "signal", "val", "is_dma", "dsem", "dval", "odeps", "cost", "idx", "prio", "fin", "nbytes", "tbl")

    def __init__(self, eng, fn, is_dma):
        self.eng = eng
        self.fn = fn
        self.odeps = []
        self.tbl = None
        self.cost = 0.3
        self.nbytes = 0
        self.deps = []
        self.signal = False
        self.val = None
        self.is_dma = is_dma
        self.dsem = None
        self.dval = None


class Prog:
    ENGS = ("pe", "act", "dve", "pool", "sp")

    def __init__(self, nc, n_dma_sems=24):
        self.nc = nc
        self.ops = []
        self.tiles = {}
        self.n_dma_sems = n_dma_sems
        self.last_fence = {}
        self.last_pe = None
        self.last_on = {}
        self.do_schedule = os.environ.get('SCHED', '1') == '1'
        self.sched_window = int(os.environ.get('SCHEDW', '0'))

    def t(self, *key):
        if key not in self.tiles:
            self.tiles[key] = T(key)
        return self.tiles[key]

    def op(self, eng, fn, reads=(), writes=(), dma=False, cost=0.3, nbytes=0):
        o = Op(eng, fn, dma)
        o.cost = cost
        o.nbytes = nbytes
        o.idx = len(self.ops)
        lf = self.last_fence.get(eng)
        if lf is not None:
            o.odeps.append(lf)
        if eng in PIN_ENGS:
            import sys as _sys
            f = _sys._getframe(1)
            while f.f_code.co_name in ("act", "acopy", "vcopy", "tt", "ts", "stt", "mm", "tr", "dma", "op", "<lambda>"):
                f = f.f_back
            if f.f_lineno not in UNPIN_LINES:
                lo = self.last_on.get(eng)
                if lo is not None:
                    o.odeps.append(lo)
                self.last_on[eng] = o
        if eng == "pe":
            pin = (cost < 0)
            if pin:
                o.cost = cost = -cost
            lp = self.last_pe
            if lp is not None and (pin or lp[1]):
                o.odeps.append(lp[0])
            self.last_pe = (o, pin)
        deps = set()
        for t in reads:
            if t.lw is not None:
                deps.add(t.lw)
            if t.excl:
                for r in t.rd:
                    if r.eng != eng:
                        deps.add(r)
        for t in writes:
            if t.lw is not None:
                deps.add(t.lw)
            for r in t.rd:
                deps.add(r)
        for t in reads:
            t.rd.append(o)
        for t in writes:
            t.lw = o
            t.rd = []
        for d in deps:
            if d is o:
                continue
            if (not d.is_dma) and (not dma) and d.eng == "pe" and eng == "pe":
                o.odeps.append(d)
                continue
            o.deps.append(d)
            d.signal = True
        self.ops.append(o)
        return o

    def dma(self, eng, out, in_, reads=(), writes=(), **kw):
        nb = 1
        for d in out.shape:
            nb *= d
        nb *= mybir.dt.size(out.dtype)
        return self.op(eng, lambda e: e.dma_start(out=out, in_=in_, **kw), reads, writes, dma=True,
                       cost=(0.08 if eng != "pool" else 0.6), nbytes=nb)

    def fence(self):
        allt = list(self.tiles.values())
        ft = self.t("__fence__")
        first = True
        for e in ("pe", "act", "dve", "pool", "sp"):
            o = self.op(e, lambda eng: eng.nop(), [], (allt + [ft]) if first else [ft], cost=0.05)
            self.last_fence[e] = o
            first = False

    def schedule(self):
        import heapq
        ops = self.ops
        n = len(ops)
        succ = [[] for _ in range(n)]
        npred = [0] * n
        for o in ops:
            ps = set(id(d) for d in o.deps) | set(id(d) for d in o.odeps)
            seen = set()
            for d in list(o.deps) + list(o.odeps):
                if id(d) in seen:
                    continue
                seen.add(id(d))
                succ[d.idx].append(o.idx)
                npred[o.idx] += 1
        dur = [0.0] * n
        for o in ops:
            dur[o.idx] = o.cost + (2.0 + o.nbytes / 250e3 if o.is_dma else 0.0)
        prio = [0.0] * n
        for i in range(n - 1, -1, -1):
            m = 0.0
            for j in succ[i]:
                if prio[j] > m:
                    m = prio[j]
            prio[i] = m + dur[i]
        ready = {e: [] for e in self.ENGS}
        ready_t = [0.0] * n
        for o in ops:
            if npred[o.idx] == 0:
                heapq.heappush(ready[o.eng], (-prio[o.idx], o.idx))
        free = {e: 0.0 for e in self.ENGS}
        order = {e: [] for e in self.ENGS}
        fin = [0.0] * n
        cur_tbl = None
        SETS_OF = {"E": ("0", "6"), "A": ("0",), "L": ("6",), "S": ("3",), "G": ("2",), "U": ("18",), "N": ("9",)}

        def tbl_ok(cur, f):
            return f is None or cur is None or any(x in cur for x in SETS_OF[f])

        def tbl_next(cur, f):
            if cur is None:
                return SETS_OF[f]
            inter = tuple(x for x in SETS_OF[f] if x in cur)
            return inter if inter else SETS_OF[f]
        dma_free = 0.0
        done = 0
        while done < n:
            best = None
            for e in self.ENGS:
                if not ready[e]:
                    continue
                cand = ready[e][0]
                i = cand[1]
                st = max(free[e], ready_t[i])
                if best is None or st < best[0]:
                    best = (st, e)
            st, e = best
            heap = ready[e]
            pick = None
            tmp = []
            fallback = None
            while heap:
                c = heapq.heappop(heap)
                if ready_t[c[1]] <= max(free[e], st) + 1e-9:
                    if e != "act" or tbl_ok(cur_tbl, ops[c[1]].tbl):
                        pick = c
                        break
                    if fallback is None:
                        fallback = c
                        if len(tmp) > 24:
                            break
                        continue
                tmp.append(c)
                if len(tmp) > 64:
                    break
            if pick is None and fallback is not None:
                pick = fallback
                fallback = None
            if fallback is not None:
                tmp.append(fallback)
            for c in tmp:
                heapq.heappush(heap, c)
            if pick is None:
                pick = heapq.heappop(heap)
            i = pick[1]
            o = ops[i]
            start = max(free[e], ready_t[i])
            if e == "act" and o.tbl is not None:
                if not tbl_ok(cur_tbl, o.tbl):
                    start += 1.3
                cur_tbl = tbl_next(cur_tbl, o.tbl)
            if o.is_dma:
                free[e] = start + o.cost
                t0 = max(start + o.cost, dma_free)
                dma_free = t0 + o.nbytes / 250e3
                fin[i] = dma_free + 2.0
            else:
                free[e] = start + o.cost
                fin[i] = free[e]
            order[e].append(o)
            done += 1
            for j in succ[i]:
                lat = 0.0 if (ops[j].eng == e and not o.is_dma) else 0.15
                if fin[i] + lat > ready_t[j]:
                    ready_t[j] = fin[i] + lat
                npred[j] -= 1
                if npred[j] == 0:
                    heapq.heappush(ready[ops[j].eng], (-prio[j], j))
        self.est_time = max(fin) if n else 0.0
        if self.sched_window > 0:
            return self.schedule_window(succ)
        return order

    def schedule_window(self, succ):
        ops = self.ops
        n = len(ops)
        W = self.sched_window
        npred = [0] * n
        for i in range(n):
            for j in succ[i]:
                npred[j] += 1
        orig = {e: [o.idx for o in ops if o.eng == e] for e in self.ENGS}
        pos = {e: 0 for e in self.ENGS}
        taken = [False] * n
        order = {e: [] for e in self.ENGS}
        done = 0
        while done < n:
            progressed = False
            for e in self.ENGS:
                lst = orig[e]
                while pos[e] < len(lst) and taken[lst[pos[e]]]:
                    pos[e] += 1
                for k in range(pos[e], min(pos[e] + W, len(lst))):
                    i = lst[k]
                    if not taken[i] and npred[i] == 0:
                        taken[i] = True
                        order[e].append(ops[i])
                        for j in succ[i]:
                            npred[j] -= 1
                        done += 1
                        progressed = True
                        break
            assert progressed
        return order

    def emit(self):
        nc = self.nc
        cnt = {e: 0 for e in self.ENGS}
        dma_rr = {e: 0 for e in self.ENGS}
        dma_cnt = {}
        if self.do_schedule:
            per = self.schedule()
        else:
            per = {e: [o for o in self.ops if o.eng == e] for e in self.ENGS}
        for o in [o for e in self.ENGS for o in per[e]]:
            if o.is_dma:
                k = dma_rr[o.eng] % self.n_dma_sems
                dma_rr[o.eng] += 1
                key = (o.eng, k)
                dma_cnt[key] = dma_cnt.get(key, 0) + 16
                o.dsem = key
                o.dval = dma_cnt[key]
            elif o.signal:
                cnt[o.eng] += 1
                o.val = cnt[o.eng]
        with contextlib.ExitStack() as st:
            sems = {e: st.enter_context(nc.semaphore("s_" + e)) for e in ("pe", "act", "dve", "pool")}
            dsems = {}
            for e in self.ENGS:
                for k in range(min(self.n_dma_sems, dma_rr[e])):
                    dsems[(e, k)] = st.enter_context(nc.semaphore("d_%s_%d" % (e, k)))
            block = st.enter_context(nc.Block())

            def run(ename):
                def body(eng):
                    waited = {}
                    for o in per[ename]:
                        need = {}
                        for d in o.deps:
                            if d.is_dma:
                                s, v, sk = dsems[d.dsem], d.dval, ("d",) + d.dsem
                            else:
                                s, v, sk = sems[d.eng], d.val, ("c", d.eng)
                            if need.get(sk, (None, 0))[1] < v:
                                need[sk] = (s, v)
                        if o.is_dma and o.dval > 16:
                            sk = ("d",) + o.dsem
                            if need.get(sk, (None, 0))[1] < o.dval - 16:
                                need[sk] = (dsems[o.dsem], o.dval - 16)
                        for sk, (s, v) in need.items():
                            if waited.get(sk, 0) >= v:
                                continue
                            eng.wait_ge(s, v)
                            waited[sk] = v
                        ins = o.fn(eng)
                        if o.is_dma:
                            ins.then_inc(dsems[o.dsem], 16)
                        elif o.signal:
                            ins.then_inc(sems[ename], 1)
                    if ename == "sp":
                        for key, v in dma_cnt.items():
                            eng.wait_ge(dsems[key], v)
                        for e in ("pe", "act", "dve", "pool"):
                            if cnt[e]:
                                eng.wait_ge(sems[e], cnt[e])
                return body

            block.tensor(run("pe"))
            block.scalar(run("act"))
            block.vector(run("dve"))
            block.gpsimd(run("pool"))
            block.sync(run("sp"))


def bc(ap, shape, axis):
    return ap.unsqueeze(axis).to_broadcast(shape)


class Builder:
    def __init__(self, dbg=None):
        self.dbg = dbg or {}
        self.nc = nc = bass.Bass("TRN2", target_bir_lowering=False)
        self.P = Prog(nc)
        self.st = contextlib.ExitStack()
        self.wrr = 0

    def sb(self, name, shape, dt=F32):
        return self.st.enter_context(self.nc.sbuf_tensor(name, shape, dt))

    def din(self, name, shape, dt=F32):
        return self.nc.dram_tensor(name, shape, dt, kind="ExternalInput").ap()

    def dout(self, name, shape, dt=F32):
        return self.nc.dram_tensor(name, shape, dt, kind="ExternalOutput").ap()

    def declare_io(self):
        L = DEPTH
        self.x_prompt = self.din("x_prompt", [SEQ, D])
        self.x_sample = self.din("x_sample", [NS, D])
        self.cache_k = self.din("cache_swa_k", [L, NS, 128, 128])
        self.cache_v = self.din("cache_swa_v", [L, NS, 128, 128])
        self.st_lru_conv = self.din("state_lru_conv", [L, NS, 3, 768])
        self.st_lru_h = self.din("state_lru_h", [L, NS, 768])
        self.st_ssd_conv = self.din("state_ssd_conv", [L, NS, 3, 1280])
        self.st_ssd_h = self.din("state_ssd_h", [L, NS, 12, 64, 128])
        self.w_in = self.din("w_in", [L, D, DIN])
        self.w_out = self.din("w_out", [L, 2048, D])
        self.att_sinks = self.din("att_sinks", [L, 8])
        self.lru_conv_w = self.din("lru_conv_w", [L, 4, 768])
        self.lru_conv_b = self.din("lru_conv_b", [L, 768])
        self.lru_wa = self.din("lru_wa", [L, 8, 96, 96])
        self.lru_ba = self.din("lru_ba", [L, 768])
        self.lru_wx = self.din("lru_wx", [L, 8, 96, 96])
        self.lru_bx = self.din("lru_bx", [L, 768])
        self.lru_lambda = self.din("lru_lambda", [L, 768])
        self.ssd_conv_w = self.din("ssd_conv_w", [L, 4, 1280])
        self.ssd_conv_b = self.din("ssd_conv_b", [L, 1280])
        self.ssd_dt_bias = self.din("ssd_dt_bias", [L, 12])
        self.ssd_a_log = self.din("ssd_a_log", [L, 12])
        self.ssd_d = self.din("ssd_d", [L, 12])
        self.ssd_norm_g = self.din("ssd_norm_g", [L, 768])
        self.ln_g = self.din("ln_g", [L, D])
        self.ln_b = self.din("ln_b", [L, D])
        self.consts = self.din("consts", [128, K_END])
        self.y_prompt = self.dout("y_prompt", [SEQ, D])
        self.y_sample = self.dout("y_sample", [NS, D])
        self.p_swa_k = self.dout("p_swa_k", [L, 128, 128])
        self.p_swa_v = self.dout("p_swa_v", [L, 128, 128])
        self.p_lru_conv = self.dout("p_lru_conv", [L, 3, 768])
        self.p_lru_h = self.dout("p_lru_h", [L, 768])
        self.p_ssd_conv = self.dout("p_ssd_conv", [L, 3, 1280])
        self.p_ssd_h = self.dout("p_ssd_h", [L, 768, 128])
        self.s_swa_k = self.dout("s_swa_k", [L, NS, 128, 128])
        self.s_swa_v = self.dout("s_swa_v", [L, NS, 128, 128])
        self.s_lru_conv = self.dout("s_lru_conv", [L, NS, 3, 768])
        self.s_lru_h = self.dout("s_lru_h", [L, NS, 768])
        self.s_ssd_conv = self.dout("s_ssd_conv", [L, NS, 3, 1280])
        self.s_ssd_h = self.dout("s_ssd_h", [L, NS, 768, 128])
        self.w_in_bf = self.nc.dram_tensor("w_in_bf", [L, D, DIN], BF16).ap()
        self.w_out_bf = self.nc.dram_tensor("w_out_bf", [L, 2048, D], BF16).ap()
        self.dbg_out = {k: self.dout("dbg_" + k, list(v)) for k, v in self.dbg.items()}

    def tap(self, name, ap, reads):
        if name in self.dbg_out:
            self.P.dma("pool", self.dbg_out[name], ap, reads=reads)

    def alloc(self):
        sb = self.sb
        self.cst = sb("cst", [128, K_END])
        self.ident_bf = sb("ident_bf", [128, 128], BF16)
        self.maskbf = sb("maskbf", [128, 2, 256], BF16)
        self.cosT = sb("cosT", [128, 16, 8]); self.sinT = sb("sinT", [128, 16, 8]); self.nsinT = sb("nsinT", [128, 16, 8])
        self.ropeS = sb("ropeS", [128, 3, 8])
        self.xT32 = sb("xT32", [128, 8, H])
        self.xTb = sb("xTb", [128, 8, H], BF16)
        self.mixT = sb("mixT", [128, 18, H], BF16)
        self.wbuf = [sb("wbuf%d" % i, [128, 8, 128], BF16) for i in range(3)]
        self.wqkv = sb("wqkv", [128, 8, 768], BF16)
        self.kT = sb("kT", [128, 5 * 128], BF16)
        self.vtm = sb("vtm", [128, 5, 128], BF16)
        self.ck = sb("ck", [128, DEPTH, 128], BF16); self.cv = sb("cv", [128, DEPTH, 128], BF16)
        self.hist_l = sb("hist_l", [128, DEPTH, 8, 3]); self.hist_s = sb("hist_s", [128, DEPTH, 10, 3])
        self.hcar = sb("hcar", [128, DEPTH, 8])
        self.hT32 = sb("hT32", [128, DEPTH, 768]); self.hTb = sb("hTb", [128, 768], BF16)
        self.lp = sb("lp", [128, DEPTH, 8, 8])
        self.la = sb("la", [128, DEPTH, 8, 4])
        self.spp = sb("spp", [128, DEPTH, 10, 5])
        self.gS = sb("gS", [128, DEPTH, 6])
        self.lnp = sb("lnp", [128, DEPTH, 8, 2])
        self.sinkb = sb("sinkb", [128, DEPTH, 8]); self.nsinkb = sb("nsinkb", [128, DEPTH, 8])
        self.Ab = sb("Ab", [128, DEPTH, 12]); self.Db = sb("Db", [128, DEPTH, 12])
        self.dtb = sb("dtb", [128, DEPTH])
        self.wab = sb("wab", [128, 8, 96], BF16); self.wxb = sb("wxb", [128, 8, 96], BF16)
        self.pstage = sb("pstage", [8, 1280])
        self.adc = sb("adc", [128, DEPTH, 2]); self.sinkc = sb("sinkc", [128, DEPTH, 2])
        self.qkv32 = [sb("qkv32_%d" % i, [128, 768]) for i in range(2)]
        self.qkbf = sb("qkbf", [128, 640], BF16)
        self.ropet = sb("ropet", [128, 2, 10, 16])
        self.qT = sb("qT", [128, 4, H], BF16)
        self.G = [sb("G%d" % i, [128, H]) for i in range(2)]
        self.Pm = [sb("Pm%d" % i, [128, 512], BF16) for i in range(2)]
        self.PT = [sb("PT%d" % i, [128, 512], BF16) for i in range(2)]
        self.ast = [sb("ast%d" % i, [128, 8, 2]) for i in range(2)]
        self.xpad = sb("xpad", [128, H + 3]); self.xl = sb("xl", [128, H]); self.xlb = sb("xlb", [128, H], BF16)
        self.rr = sb("rr", [128, H]); self.ig = sb("ig", [128, H]); self.EE = sb("EE", [128, H])
        self.zs = sb("zs", [128, 6, H]); self.xsT = sb("xsT", [128, 6, H], BF16)
        self.BT = sb("BT", [128, 2, H], BF16); self.CT = sb("CT", [128, 2, H], BF16)
        self.dtT = sb("dtT", [128, H]); self.dtt = sb("dtt", [128, H])
        self.xstm = sb("xstm", [128, 768], BF16); self.Btm = sb("Btm", [128, 256], BF16)
        self.tsm = sb("tsm", [128, 8, 12])
        self.aU = sb("aU", [128, 12, 128]); self.Dm = sb("Dm", [128, 12, 128])
        self.Ebc = sb("Ebc", [128, 12, 128], BF16); self.GT = sb("GT", [128, 12, 128], BF16)
        self.CE = sb("CE", [128, 12, 128], BF16); self.CBm = sb("CBm", [128, 2, 128])
        self.xdt = sb("xdt", [128, 768], BF16); self.xdd = sb("xdd", [128, 768], BF16); self.xsD = sb("xsD", [128, 768], BF16)
        self.yg = sb("yg", [128, 6, 128]); self.sq = sb("sq", [128, 6, 128]); self.rstd = sb("rstd", [128, 128])
        self.lnq = [sb("lnq%d" % i, [128, H]) for i in range(2)]
        self.mean = sb("mean", [128, H]); self.lrs = sb("lrs", [128, H])
        self.iost = sb("iost", [128, D])
        self.ps = self.st.enter_context(self.nc.psum_tensor("ps", [128, 4096], F32))

    def bank(self, b, n=1):
        return self.ps[:, b * 512:(b + n) * 512]

    def bankbf(self, b):
        return self.ps[:, b * 512:(b + 1) * 512].bitcast(BF16)

    def pT(self, b):
        return self.P.t("ps", b)

    @staticmethod
    def _fs(ap):
        n = 1
        for d in ap.shape[1:]:
            n *= d
        return n

    def _c(self, eng, out, in_=None):
        F = self._fs(out)
        ps = (in_ is not None and str(in_.space).lower().find("psum") >= 0)
        if eng == "act":
            return (224 + F) / 1200.0
        if eng == "pool":
            return (100 + 2 * F) / 1200.0
        return ((120 if ps else 60) + F) / 960.0

    TBL = {AF.Exp: "E", AF.Tanh: "A", AF.Ln: "L", AF.Sqrt: "S", AF.Sigmoid: "G", AF.Silu: "U", AF.Sin: "N"}

    def act(self, out, in_, func, r, w, **kw):
        o = self.P.op("act", lambda e: e.activation(out=out, in_=in_, func=func, **kw), r, w, cost=self._c("act", out))
        o.tbl = self.TBL.get(func)
        return o

    def acopy(self, out, in_, r, w):
        return self.P.op("act", lambda e: e.copy(out=out, in_=in_), r, w, cost=self._c("act", out))

    def vcopy(self, out, in_, r, w, eng="dve"):
        return self.P.op(eng, lambda e: e.tensor_copy(out=out, in_=in_), r, w, cost=self._c(eng, out, in_))

    def tt(self, out, a, b, op, r, w, eng="dve"):
        return self.P.op(eng, lambda e: e.tensor_tensor(out=out, in0=a, in1=b, op=op), r, w, cost=self._c(eng, out, a))

    def ts(self, out, a, s1, s2, op0, op1, r, w, eng="dve"):
        c = self._c(eng, out, a)
        if op1 is None:
            return self.P.op(eng, lambda e: e.tensor_scalar(out=out, in0=a, scalar1=s1, scalar2=None, op0=op0), r, w, cost=c)
        return self.P.op(eng, lambda e: e.tensor_scalar(out=out, in0=a, scalar1=s1, scalar2=s2, op0=op0, op1=op1), r, w, cost=c)

    def stt(self, out, a, s, b, op0, op1, r, w, eng="dve"):
        return self.P.op(eng, lambda e: e.scalar_tensor_tensor(out=out, in0=a, scalar=s, in1=b, op0=op0, op1=op1), r, w,
                         cost=self._c(eng, out, a))

    def mm(self, out, lhsT, rhs, start, stop, r, w):
        N = self._fs(out)
        k = 4.0 if rhs.dtype == F32 else 1.0
        c = (max(N, 64) * k / 2400.0 + 0.035)
        return self.P.op("pe", lambda e: e.matmul(out, lhsT=lhsT, rhs=rhs, start=start, stop=stop), r, w,
                         cost=(-c if k > 1 else c))

    def tr(self, out, in_, ident, r, w):
        N = self._fs(in_)
        k = 4.0 if in_.dtype == F32 else 1.0
        c = (max(N, 64) * k / 2400.0 + 0.06)
        return self.P.op("pe", lambda e: e.transpose(out, in_, ident), r, w, cost=(-c if k > 1 else c))

    def next_wbuf(self):
        k = self.wrr % 3
        self.wrr += 1
        return k

    def win_T(self, l):
        return [self.P.t("winbf", l, i) for i in range(8)]

    def wout_T(self, l):
        return [self.P.t("woutbf", l, i) for i in range(16)]

    def load_win(self, l, pieces):
        k = self.next_wbuf()
        src = self.w_in_bf[l].rearrange("(kc p) c -> p kc c", p=128)
        for (c0, n, d0) in pieces:
            self.P.dma("sp", self.wbuf[k][:, :, d0:d0 + n], src[:, :, c0:c0 + n],
                       reads=self.win_T(l), writes=[self.P.t("wbuf", k)])
        return k

    def inproj(self, l, k, M, psout, psT, ncols=H, x=None, xT=None):
        x = self.xTb if x is None else x
        xT = [self.P.t("xTb", kc) for kc in range(8)] if xT is None else xT
        for kc in range(8):
            self.mm(psout[0:M, 0:ncols], self.wbuf[k][:, kc, 0:M], x[:, kc, 0:ncols], kc == 0, kc == 7,
                    [self.P.t("wbuf", k)] + xT, [psT])

    def prologue(self):
        P, t = self.P, self.P.t
        cst = self.cst
        P.dma("sp", cst[:], self.consts, writes=[t("cst")])
        for l in range(DEPTH):
            for i in range(8):
                P.dma("pool", self.w_in_bf[l, i * 128:(i + 1) * 128, :], self.w_in[l, i * 128:(i + 1) * 128, :],
                      writes=[t("winbf", l, i)])
            for i in range(16):
                P.dma("pool", self.w_out_bf[l, i * 128:(i + 1) * 128, :], self.w_out[l, i * 128:(i + 1) * 128, :],
                      writes=[t("woutbf", l, i)])
        self.vcopy(self.ident_bf[:], cst[:, K_ID:K_ID + 128], [t("cst")], [t("ident_bf")])
        self.vcopy(self.maskbf[:, 0, :], cst[:, K_MG:K_MG + 256], [t("cst")], [t("maskbf")])
        self.vcopy(self.maskbf[:, 1, :], cst[:, K_MF:K_MF + 256], [t("cst")], [t("maskbf")])
        for buf, nm in ((self.ck, "ck"), (self.cv, "cv")):
            P.op("dve", lambda e, buf=buf: e.memset(buf[:], 0.0), [], [t(nm, l) for l in range(DEPTH)])
        P.op("dve", lambda e: e.memset(self.hist_l[:], 0.0), [], [t("hist_l", l) for l in range(DEPTH)])
        P.op("dve", lambda e: e.memset(self.hist_s[:], 0.0), [], [t("hist_s", l) for l in range(DEPTH)])
        P.op("dve", lambda e: e.memset(self.hcar[:], 0.0), [], [t("hcar", l) for l in range(DEPTH)])
        P.op("dve", lambda e: e.memset(self.hT32[:], 0.0), [], [t("hT32", l) for l in range(DEPTH)])
        self.rope_tables()
        for l in range(DEPTH):
            self.layer_params(l)

    def sincos(self, ang, n, outs, rd):
        P, t = self.P, self.P.t
        tmp = self.iost
        c1 = float(np.float32(2 * np.pi)); c2 = float(2 * np.pi - c1)
        for j, (shift, out) in enumerate(((0.5 * np.pi, outs[0]), (0.0, outs[1]))):
            a = tmp[:, 0:n]; kf = tmp[:, n:2 * n]; ki = tmp[:, 2 * n:3 * n].bitcast(I32); m = tmp[:, 3 * n:4 * n]
            W = [t("iost")]
            self.ts(a, ang, float(shift), None, ALU.add, None, rd + W, W)
            self.ts(kf, a, float(1.0 / (2 * np.pi)), None, ALU.mult, None, W, W)
            self.vcopy(ki, kf, W, W)
            self.vcopy(kf, ki, W, W)
            self.stt(a, kf, -c1, a, ALU.mult, ALU.add, W, W)
            self.stt(a, kf, -c2, a, ALU.mult, ALU.add, W, W)
            self.ts(m, a, float(np.pi), float(-2 * np.pi), ALU.is_gt, ALU.mult, W, W)
            self.tt(a, a, m, ALU.add, W, W)
            self.ts(m, a, float(-np.pi), float(2 * np.pi), ALU.is_lt, ALU.mult, W, W)
            self.tt(a, a, m, ALU.add, W, W)
            self.act(out, a, AF.Sin, W, [t("rope")])
        self.ts(outs[2], outs[1], -1.0, None, ALU.mult, None, [t("rope")], [t("rope")])

    def rope_tables(self):
        t = self.P.t
        cst = self.cst
        ang = self.iost[:, 512:640]
        self.tt(ang.rearrange("p (a b) -> p a b", b=8), bc(cst[:, K_POS:K_POS + 16], [128, 16, 8], 2),
                bc(cst[:, K_INV:K_INV + 8], [128, 16, 8], 1), ALU.mult, [t("cst")], [t("iost")])
        f = lambda x: x[:].rearrange("p a b -> p (a b)")
        self.sincos(ang, 128, (f(self.cosT), f(self.sinT), f(self.nsinT)), [t("iost")])
        angs = self.iost[:, 640:648]
        self.ts(angs, cst[:, K_INV:K_INV + 8], PAST, None, ALU.mult, None, [t("cst")], [t("iost")])
        self.sincos(angs, 8, (self.ropeS[:, 0, :], self.ropeS[:, 1, :], self.ropeS[:, 2, :]), [t("iost")])

    def layer_params(self, l):
        P, t = self.P, self.P.t
        cst = self.cst
        idf = cst[:, K_ID:K_ID + 128]
        stg = self.pstage
        S = [t("pstage")]
        W = [t("par", l)]
        P.dma("sp", stg[0:4, 0:768], self.lru_conv_w[l], writes=S)
        for i, src in enumerate((self.lru_conv_b, self.lru_ba, self.lru_bx, self.lru_lambda)):
            P.dma("sp", stg[4 + i:5 + i, 0:768], src[l:l + 1, :], writes=S)
        pb = self.bank(0)
        for n in range(8):
            self.tr(pb[0:96, n * 8:(n + 1) * 8], stg[0:8, n * 96:(n + 1) * 96], idf[0:8, 0:8], S + [t("cst")], [self.pT(0)])
        self.vcopy(self.lp[0:96, l, :, :], pb[0:96, 0:64].rearrange("p (a b) -> p a b", b=8), [self.pT(0)], W)
        la = self.la
        self.act(la[0:96, l, :, 0], self.lp[0:96, l, :, 7], AF.Exp, W, W, scale=-1.0)
        self.act(la[0:96, l, :, 0], la[0:96, l, :, 0], AF.Ln, W, W, bias=1.0)
        self.ts(la[0:96, l, :, 1], la[0:96, l, :, 0], -8.0, None, ALU.mult, None, W, W)
        self.ts(la[0:96, l, :, 0], la[0:96, l, :, 0], -4.0, None, ALU.mult, None, W, W)
        self.ts(la[0:96, l, :, 2:4], self.lp[0:96, l, :, 5:7], 0.5, None, ALU.mult, None, W, W)
        P.dma("sp", stg[0:4, 0:1280], self.ssd_conv_w[l], writes=S)
        P.dma("sp", stg[4:5, 0:1280], self.ssd_conv_b[l:l + 1, :], writes=S)
        pb = self.bank(1)
        for c in range(10):
            self.tr(pb[:, c * 5:(c + 1) * 5], stg[0:5, c * 128:(c + 1) * 128], idf[0:5, 0:5], S + [t("cst")], [self.pT(1)])
        self.ts(self.spp[:, l, :, :], pb[:, 0:50].rearrange("p (a b) -> p a b", b=5), 0.5, None, ALU.mult, None, [self.pT(1)], W)
        P.dma("sp", stg[0:1, 0:768], self.ssd_norm_g[l:l + 1, :], writes=S)
        P.dma("sp", stg[1:2, 0:1024], self.ln_g[l:l + 1, :], writes=S)
        P.dma("sp", stg[2:3, 0:1024], self.ln_b[l:l + 1, :], writes=S)
        pb = self.bank(2)
        for c in range(6):
            self.tr(pb[:, c:c + 1], stg[0:1, c * 128:(c + 1) * 128], idf[0:1, 0:1], S + [t("cst")], [self.pT(2)])
        self.ts(self.gS[:, l, :], pb[:, 0:6], float(math.sqrt(768.0)), None, ALU.mult, None, [self.pT(2)], W)
        P.dma("sp", stg[0:1, 0:1024], self.ln_g[l:l + 1, :], writes=S)
        P.dma("sp", stg[1:2, 0:1024], self.ln_b[l:l + 1, :], writes=S)
        pb = self.bank(3)
        for c in range(8):
            self.tr(pb[:, 2 * c:2 * c + 2], stg[0:2, c * 128:(c + 1) * 128], idf[0:2, 0:2], S + [t("cst")], [self.pT(3)])
        self.vcopy(self.lnp[:, l, :, :], pb[:, 0:16].rearrange("p (a b) -> p a b", b=2), [self.pT(3)], W)
        P.dma("sp", self.sinkb[:, l, :], self.att_sinks[l:l + 1, :].partition_broadcast(128), writes=W)
        self.ts(self.nsinkb[:, l, :], self.sinkb[:, l, :], -1.0, None, ALU.mult, None, W, W)
        P.dma("sp", self.Ab[:, l, :], self.ssd_a_log[l:l + 1, :].partition_broadcast(128), writes=W)
        self.act(self.Ab[:, l, :], self.Ab[:, l, :], AF.Exp, W, W)
        self.ts(self.Ab[:, l, :], self.Ab[:, l, :], -1.0, None, ALU.mult, None, W, W)
        P.dma("sp", self.Db[:, l, :], self.ssd_d[l:l + 1, :].partition_broadcast(128), writes=W)
        P.dma("sp", self.dtb[0:12, l:l + 1], self.ssd_dt_bias[l].rearrange("(a b) -> a b", b=1), writes=W)
        P.dma("sp", self.adc[0:12, l, 0:1], self.ssd_a_log[l].rearrange("(a b) -> a b", b=1), writes=W)
        self.act(self.adc[0:12, l, 0:1], self.adc[0:12, l, 0:1], AF.Exp, W, W)
        self.ts(self.adc[0:12, l, 0:1], self.adc[0:12, l, 0:1], -1.0, None, ALU.mult, None, W, W)
        P.dma("sp", self.adc[0:12, l, 1:2], self.ssd_d[l].rearrange("(a b) -> a b", b=1), writes=W)
        P.dma("sp", self.sinkc[0:8, l, 0:1], self.att_sinks[l].rearrange("(a b) -> a b", b=1), writes=W)
        self.ts(self.sinkc[0:8, l, 1:2], self.sinkc[0:8, l, 0:1], -1.0, None, ALU.mult, None, W, W)

    def job(self, l, p):
        ph = getattr(self, "phases", "xjqalso")
        if "x" in ph: self.load_x(l, p)
        if "j" in ph: self.job_prologue(l, p)
        if "q" in ph: self.phase_qkv(l, p)
        if "a" in ph: self.phase_att(l, p)
        if "l" in ph: self.phase_lru(l, p)
        if "s" in ph: self.phase_ssd(l, p)
        if l == 0 and p == 0:
            self.tap("mixT", self.mixT[:], [self.P.t("mixT", ec, tl) for ec in range(18) for tl in range(NT)])
        if "o" in ph: self.phase_out(l, p)

    def load_x(self, l, p):
        P, t = self.P, self.P.t
        if l != 0:
            return
        idf = self.cst[:, K_ID:K_ID + 128]
        for tl in range(NT):
            gt = p * NT + tl
            P.dma("sp", self.iost[:], self.x_prompt[gt * 128:(gt + 1) * 128, :], writes=[t("iost")])
            for hb in range(2):
                pb = self.bank(hb)
                for j in range(4):
                    dc = hb * 4 + j
                    self.tr(pb[:, j * 128:(j + 1) * 128], self.iost[:, dc * 128:(dc + 1) * 128], idf,
                            [t("iost"), t("cst")], [self.pT(hb)])
                dst = self.xT32[:, hb * 4:hb * 4 + 4, tl * 128:(tl + 1) * 128]
                self.vcopy(dst, pb.rearrange("p (a b) -> p a b", b=128), [self.pT(hb)],
                           [t("xT32", dc) for dc in range(hb * 4, hb * 4 + 4)])
        for dc in range(8):
            self.acopy(self.xTb[:, dc, :], self.xT32[:, dc, :], [t("xT32", dc)], [t("xTb", dc)])
        if p == 0:
            self.tap("xT_in", self.xT32[:], [t("xT32", dc) for dc in range(8)])

    def job_prologue(self, l, p):
        P, t = self.P, self.P.t
        self.vcopy(self.kT[:, 0:128], self.ck[:, l, :], [t("ck", l)], [t("kT", 0)], eng="pool")
        self.vcopy(self.vtm[:, 0, :], self.cv[:, l, :], [t("cv", l)], [t("v", 0)], eng="pool")
        self.vcopy(self.hTb[:], self.hT32[:, l, :], [t("hT32", l)], [t("hTb")], eng="pool")
        P.dma("pool", self.wab[0:96, :, :], self.lru_wa[l].rearrange("n c d -> c n d"), writes=[t("wab")])
        P.dma("pool", self.wxb[0:96, :, :], self.lru_wx[l].rearrange("n c d -> c n d"), writes=[t("wxb")])
        src = self.w_in_bf[l].rearrange("(kc p) c -> p kc c", p=128)
        P.dma("sp", self.wqkv[:], src[:, :, 0:768], reads=self.win_T(l), writes=[t("wqkv")])

    def phase_qkv(self, l, p):
        P, t = self.P, self.P.t
        xT = [t("xTb", kc) for kc in range(8)]
        last = (p == NPASS - 1)
        for tl in range(NT):
            gt = p * NT + tl
            pa, pb_, pc = (0, 1, 4) if tl % 2 == 0 else (2, 3, 5)
            A, B = self.bank(pa), self.bank(pb_)
            for kc in range(8):
                lhs = self.xTb[:, kc, tl * 128:(tl + 1) * 128]
                self.mm(A, lhs, self.wqkv[:, kc, 0:512], kc == 0, kc == 7, xT + [t("wqkv")], [self.pT(pa)])
                self.mm(B[:, 0:256], lhs, self.wqkv[:, kc, 512:768], kc == 0, kc == 7, xT + [t("wqkv")], [self.pT(pb_)])
            q32 = self.qkv32[tl % 2]
            Q = [t("qkv32", tl % 2)]
            self.acopy(q32[:, 0:512], A, [self.pT(pa)], Q)
            self.acopy(q32[:, 512:768], B[:, 0:256], [self.pT(pb_)], Q)
            hv = q32[:, 0:640].rearrange("p (h d) -> p h d", d=64)
            x1, x2 = hv[:, :, 0:8], hv[:, :, 8:16]
            cs = bc(self.cosT[:, gt, :], [128, 10, 8], 1)
            sn = bc(self.sinT[:, gt, :], [128, 10, 8], 1)
            ns = bc(self.nsinT[:, gt, :], [128, 10, 8], 1)
            R = [t("ropet")]
            ra, rb = self.ropet[:, 0, :, :], self.ropet[:, 1, :, :]
            self.tt(rb[:, :, 0:8], x2, ns, ALU.mult, Q + [t("rope")], R)
            self.tt(rb[:, :, 8:16], x1, sn, ALU.mult, Q + [t("rope")], R)
            self.tt(ra[:, :, 0:8], x1, cs, ALU.mult, Q + [t("rope")], R)
            self.tt(ra[:, :, 8:16], x2, cs, ALU.mult, Q + [t("rope")], R)
            self.tt(hv[:, :, 0:16], ra, rb, ALU.add, R, Q)
            if l == 0 and p == 0 and tl == 1:
                self.tap("qkv_t1", q32[:], Q)
            if last and tl == NT - 1:
                P.dma("pool", self.p_swa_k[l], q32[:, 512:640], reads=Q)
                P.dma("pool", self.p_swa_v[l], q32[:, 640:768], reads=Q)
            self.acopy(self.qkbf[:, 0:512].rearrange("p (c w d) -> p c w d", c=4, w=2),
                       q32[:, 0:512].rearrange("p (w c d) -> p c w d", w=2, c=4), Q, [t("qkbf")])
            self.acopy(self.qkbf[:, 512:640], q32[:, 512:640], Q, [t("qkbf")])
            self.vcopy(self.vtm[:, 1 + tl, :], q32[:, 640:768], Q, [t("v", 1 + tl)], eng="pool")
            C = self.bankbf(pc)
            for j in range(5):
                self.tr(C[:, j * 128:(j + 1) * 128], self.qkbf[:, j * 128:(j + 1) * 128], self.ident_bf[:],
                        [t("qkbf"), t("ident_bf")], [self.pT(pc)])
            self.vcopy(self.qT[:, :, tl * 128:(tl + 1) * 128], C[:, 0:512].rearrange("p (c q) -> p c q", q=128),
                       [self.pT(pc)], [t("qT", tl)])
            self.vcopy(self.kT[:, (1 + tl) * 128:(2 + tl) * 128], C[:, 512:640], [self.pT(pc)], [t("kT", 1 + tl)])
        self.vcopy(self.ck[:, l, :], self.kT[:, 512:640], [t("kT", 4)], [t("ck", l)], eng="pool")
        self.vcopy(self.cv[:, l, :], self.vtm[:, 4, :], [t("v", 4)], [t("cv", l)], eng="pool")

    def phase_att(self, l, p):
        P, t = self.P, self.P.t
        cnt = 0
        for c in range(4):
            k = self.load_win(l, [(C_GA + c * 64, 64, 0), (C_GA + (c + 4) * 64, 64, 64)])
            gb = c % 2
            pg = self.bank(gb)
            self.inproj(l, k, 128, pg, self.pT(gb))
            G = self.G[gb]
            self.act(G[:], pg, AF.Tanh, [self.pT(gb)], [t("G", gb)], scale=0.5)
            self.stt(G[:], G[:], 1.0, pg, ALU.add, ALU.mult, [self.pT(gb), t("G", gb)], [t("G", gb)])
            for tl in range(NT):
                gt = p * NT + tl
                i = cnt % 2
                cnt += 1
                bS, bP, bO = 2 + i, 4 + i, 6 + i
                S = self.bank(bS)
                mk = self.maskbf[:, 1 if gt == 0 else 0, :]
                for h in range(2):
                    rows = slice(h * 64, (h + 1) * 64)
                    self.mm(S[:, h * 256:(h + 1) * 256], self.qT[rows, c, tl * 128:(tl + 1) * 128],
                            self.kT[rows, tl * 128:tl * 128 + 256], True, False,
                            [t("qT", tl), t("kT", tl), t("kT", tl + 1)], [self.pT(bS)])
                    self.mm(S[:, h * 256:(h + 1) * 256], self.ident_bf[:], mk, False, True,
                            [t("ident_bf"), t("maskbf")], [self.pT(bS)])
                st = self.ast[i]
                A = [t("ast", i)]
                mx, negm, ssum, es, den, rden = (st[:, j, :] for j in range(6))
                P.op("dve", lambda e, mx=mx, S=S: e.reduce_max(out=mx, in_=S.rearrange("p (h k) -> p h k", k=256), axis=AX.X),
                     [self.pT(bS)], A)
                hsel = self.nsinkb[:, l, c:c + 5:4]
                self.stt(negm, mx, -SCALE, hsel, ALU.mult, ALU.min, A + [t("par", l)], A)
                Pm = self.Pm[i]
                for h in range(2):
                    self.act(Pm[:, h * 256:(h + 1) * 256], S[:, h * 256:(h + 1) * 256], AF.Exp,
                             [self.pT(bS)] + A, [t("Pm", i)] + A, scale=SCALE, bias=negm[:, h:h + 1], accum_out=ssum[:, h:h + 1])
                self.tt(es, negm, self.sinkb[:, l, c:c + 5:4], ALU.add, A + [t("par", l)], A)
                self.act(es, es, AF.Exp, A, A)
                self.tt(den, ssum, es, ALU.add, A, A)
                P.op("dve", lambda e, rden=rden, den=den: e.reciprocal(out=rden, in_=den), A, A)
                pv = Pm[:].rearrange("p (h k) -> p h k", k=256)
                self.tt(pv, pv, bc(rden, [128, 2, 256], 2), ALU.mult, [t("Pm", i)] + A, [t("Pm", i)])
                PTp = self.bankbf(bP)
                for h in range(2):
                    for kb in range(2):
                        j = h * 2 + kb
                        self.tr(PTp[:, j * 128:(j + 1) * 128], Pm[:, h * 256 + kb * 128:h * 256 + (kb + 1) * 128],
                                self.ident_bf[:], [t("Pm", i), t("ident_bf")], [self.pT(bP)])
                PT = self.PT[i]
                self.acopy(PT[:], PTp[:, 0:512], [self.pT(bP)], [t("PT", i)])
                O = self.bank(bO)
                for h in range(2):
                    for kb in range(2):
                        j = h * 2 + kb
                        self.mm(O[h * 64:(h + 1) * 64, 0:128], self.vtm[:, tl + kb, h * 64:(h + 1) * 64],
                                PT[:, j * 128:(j + 1) * 128], kb == 0, kb == 1,
                                [t("v", tl + kb), t("PT", i)], [self.pT(bO)])
                self.stt(self.mixT[:, c, tl * 128:(tl + 1) * 128], O[:, 0:128], 0.5, G[:, tl * 128:(tl + 1) * 128], ALU.mult, ALU.mult,
                         [self.pT(bO), t("G", gb)], [t("mixT", c, tl)])

    def conv4(self, out, xpad, par, n, M, rd, wr):
        self.act(out[0:M, :], xpad[0:M, 0:H], AF.Identity, rd, wr, scale=par[0:M, 0:1], bias=par[0:M, 4:5])
        for k in range(1, 4):
            self.stt(out[0:M, :], xpad[0:M, k:k + H], par[0:M, k:k + 1], out[0:M, :], ALU.mult, ALU.add, rd + wr, wr)

    def phase_lru(self, l, p):
        P, t = self.P, self.P.t
        last = (p == NPASS - 1)
        for n in range(8):
            kx = self.load_win(l, [(C_XL + n * 96, 96, 0)])
            kg = self.load_win(l, [(C_GL + n * 96, 96, 0)])
            b0 = 0 if n % 2 == 0 else 4
            px, pg, pr, pi = (self.bank(b0 + j) for j in range(4))
            pxT, pgT, prT, piT = (self.pT(b0 + j) for j in range(4))
            self.inproj(l, kx, 96, px, pxT)
            self.inproj(l, kg, 96, pg, pgT)
            par = self.lp[:, l, n, :]
            PR = [t("par", l)]
            xp, xl, xlb, rr, ig, EE = self.xpad, self.xl, self.xlb, self.rr, self.ig, self.EE
            self.vcopy(xp[0:96, 0:3], self.hist_l[0:96, l, n, :], [t("hist_l", l)], [t("xpad")])
            self.acopy(xp[0:96, 3:3 + H], px[0:96, :], [pxT], [t("xpad")])
            self.vcopy(self.hist_l[0:96, l, n, :], xp[0:96, H:H + 3], [t("xpad")], [t("hist_l", l)])
            self.conv4(xl, xp, par, n, 96, [t("xpad")] + PR, [t("xl")])
            self.acopy(xlb[0:96, :], xl[0:96, :], [t("xl")], [t("xlb")])
            self.mm(pr[0:96, :], self.wab[0:96, n, :], xlb[0:96, :], True, True, [t("wab"), t("xlb")], [prT])
            self.mm(pi[0:96, :], self.wxb[0:96, n, :], xlb[0:96, :], True, True, [t("wxb"), t("xlb")], [piT])
            lac = self.la[0:96, l, n, :]
            self.act(rr[0:96, :], pr[0:96, :], AF.Tanh, [prT] + PR, [t("rr")], scale=0.5, bias=lac[:, 2:3])
            self.act(ig[0:96, :], pi[0:96, :], AF.Tanh, [piT] + PR, [t("ig")], scale=0.5, bias=lac[:, 3:4])
            self.act(EE[0:96, :], rr[0:96, :], AF.Exp, [t("rr")] + PR, [t("EE")], scale=lac[:, 1:2], bias=lac[:, 1:2])
            self.act(rr[0:96, :], rr[0:96, :], AF.Exp, [t("rr")] + PR, [t("rr")], scale=lac[:, 0:1], bias=lac[:, 0:1])
            self.act(EE[0:96, :], EE[0:96, :], AF.Sqrt, [t("EE")], [t("EE")], scale=-0.25, bias=0.25)
            self.stt(ig[0:96, :], ig[0:96, :], 1.0, xl[0:96, :], ALU.add, ALU.mult, [t("ig"), t("xl")], [t("ig")])
            self.tt(ig[0:96, :], ig[0:96, :], EE[0:96, :], ALU.mult, [t("ig"), t("EE")], [t("ig")])
            P.op("dve", lambda e, n=n: e.tensor_tensor_scan(out=EE[0:96, :], data0=rr[0:96, :], data1=ig[0:96, :],
                                                            initial=self.hcar[0:96, l, n:n + 1], op0=ALU.mult, op1=ALU.add),
                 [t("rr"), t("ig"), t("hcar", l)], [t("EE")])
            self.vcopy(self.hcar[0:96, l, n:n + 1], EE[0:96, H - 1:H], [t("EE")], [t("hcar", l)])
            self.act(xl[0:96, :], pg[0:96, :], AF.Tanh, [pgT], [t("xl")], scale=0.5)
            self.stt(xl[0:96, :], xl[0:96, :], 1.0, pg[0:96, :], ALU.add, ALU.mult, [pgT, t("xl")], [t("xl")])
            self.stt(self.mixT[0:96, 4 + n, :], EE[0:96, :], 0.5, xl[0:96, :], ALU.mult, ALU.mult, [t("EE"), t("xl")],
                    [t("mixT", 4 + n, tl) for tl in range(NT)])
        if last:
            for k3 in range(3):
                P.dma("pool", self.p_lru_conv[l, k3].rearrange("(n p) -> p n", p=96), self.hist_l[0:96, l, :, k3],
                      reads=[t("hist_l", l)], allow_slow_non_contiguous=True)
            P.dma("pool", self.p_lru_h[l].rearrange("(n p) -> p n", p=96), self.hcar[0:96, l, :],
                  reads=[t("hcar", l)], allow_slow_non_contiguous=True)

    def phase_ssd(self, l, p):
        P, t = self.P, self.P.t
        last = (p == NPASS - 1)
        cst = self.cst
        idf = cst[:, K_ID:K_ID + 128]
        U = cst[:, K_U:K_U + 128]
        ones = cst[:, K_ONE:K_ONE + 128]
        PR = [t("par", l)]
        nb = 0
        for c in range(6):
            k = self.load_win(l, [(C_Z + c * 128, 128, 0)])
            b = nb % 2; nb += 1
            pb = self.bank(b); pbT = self.pT(b)
            self.inproj(l, k, 128, pb, pbT)
            self.act(self.zs[:, c, :], pb, AF.Tanh, [pbT], [t("zs", c)], scale=0.5)
            self.stt(self.zs[:, c, :], self.zs[:, c, :], 1.0, pb, ALU.add, ALU.mult, [pbT, t("zs", c)], [t("zs", c)])
        xp, acc = self.xpad, self.xl
        for c in range(10):
            k = self.load_win(l, [(C_XBC + c * 128, 128, 0)])
            b = nb % 2; nb += 1
            pb = self.bank(b); pbT = self.pT(b)
            self.inproj(l, k, 128, pb, pbT)
            self.vcopy(xp[:, 0:3], self.hist_s[:, l, c, :], [t("hist_s", l)], [t("xpad")])
            self.acopy(xp[:, 3:3 + H], pb, [pbT], [t("xpad")])
            self.vcopy(self.hist_s[:, l, c, :], xp[:, H:H + 3], [t("xpad")], [t("hist_s", l)])
            self.conv4(acc, xp, self.spp[:, l, c, :], c, 128, [t("xpad")] + PR, [t("xl")])
            if c < 6:
                dst, dT = self.xsT[:, c, :], t("xsT", c)
            elif c < 8:
                dst, dT = self.BT[:, c - 6, :], t("BT", c - 6)
            else:
                dst, dT = self.CT[:, c - 8, :], t("CT", c - 8)
            self.act(self.rr[:], acc[:], AF.Tanh, [t("xl")], [t("rr")])
            self.stt(dst, self.rr[:], 1.0, acc[:], ALU.add, ALU.mult, [t("rr"), t("xl")], [dT])
        k = self.load_win(l, [(C_DT, 12, 0)])
        b = nb % 2; nb += 1
        pb = self.bank(b); pbT = self.pT(b)
        self.inproj(l, k, 12, pb, pbT)
        u, v = self.dtT[0:12, :], self.dtt[0:12, :]
        self.act(u, pb[0:12, :], AF.Identity, [pbT] + PR, [t("dtT")], bias=self.dtb[0:12, l:l + 1])
        self.act(v, u, AF.Abs, [t("dtT")], [t("dtt")])
        self.act(v, v, AF.Exp, [t("dtt")], [t("dtt")], scale=-1.0)
        self.act(v, v, AF.Ln, [t("dtt")], [t("dtt")], bias=1.0)
        self.stt(u, u, 0.0, v, ALU.max, ALU.add, [t("dtT"), t("dtt")], [t("dtT")])
        if last:
            for k3 in range(3):
                P.dma("pool", self.p_ssd_conv[l, k3].rearrange("(c p) -> p c", p=128), self.hist_s[:, l, :, k3],
                      reads=[t("hist_s", l)], allow_slow_non_contiguous=True)
        if SSD_STOP == 1: return
        tsm = self.tsm
        dt_tm, a_tm, acs_tm, cd, tmp, dec, dtdec = (tsm[:, j, :] for j in range(7))
        TS = [t("tsm")]
        pacs = self.bank(3, 3)
        pacsT = [self.pT(3), self.pT(4), self.pT(5)]
        py = self.bank(6, 2)
        pyT = [self.pT(6), self.pT(7)]
        for tl in range(NT):
            sl = slice(tl * 128, (tl + 1) * 128)
            p0 = self.bankbf(0)
            for c in range(6):
                self.tr(p0[:, c * 128:(c + 1) * 128], self.xsT[:, c, sl], self.ident_bf[:], [t("xsT", c), t("ident_bf")], [self.pT(0)])
            self.acopy(self.xstm[:], p0[:, 0:768], [self.pT(0)], [t("xstm")])
            p1b = self.bankbf(1)
            p1 = self.bank(1)
            for g in range(2):
                self.tr(p1b[:, g * 128:(g + 1) * 128], self.BT[:, g, sl], self.ident_bf[:], [t("BT", g), t("ident_bf")], [self.pT(1)])
            p2 = self.bank(2)
            self.tr(p2[:, 256:268], self.dtT[0:12, sl], idf[0:12, 0:12], [t("dtT"), t("cst")], [self.pT(2)])
            self.acopy(self.Btm[:], p1b[:, 0:256], [self.pT(1)], [t("Btm")])
            self.vcopy(dt_tm, p2[:, 256:268], [self.pT(2)], TS)
            self.tt(a_tm, dt_tm, self.Ab[:, l, :], ALU.mult, TS + PR, TS)
            for g in range(2):
                self.mm(p2[:, g * 128:(g + 1) * 128], self.BT[:, g, sl], self.CT[:, g, sl], True, True,
                        [t("BT", g), t("CT", g)], [self.pT(2)])
            self.tt(self.CBm[:], p2[:, 0:256].rearrange("p (g q) -> p g q", q=128), bc(U, [128, 2, 128], 1), ALU.mult,
                    [self.pT(2), t("cst")], [t("CBm")])
            if SSD_STOP == 2: return
            self.tt(self.aU[:], bc(U, [128, 12, 128], 1), bc(a_tm, [128, 12, 128], 2), ALU.mult, TS + [t("cst")], [t("aU")])
            aUf = self.aU[:].rearrange("p e q -> p (e q)")
            for j in range(3):
                self.mm(pacs[:, j * 512:(j + 1) * 512], ones, aUf[:, j * 512:(j + 1) * 512], True, True,
                        [t("cst"), t("aU")], [pacsT[j]])
            self.mm(p2[:, 300:312], U, a_tm, True, True, [t("cst")] + TS, [self.pT(2)])
            self.vcopy(acs_tm, p2[:, 300:312], [self.pT(2)], TS)
            pav = pacs.rearrange("p (e q) -> p e q", q=128)
            for e in range(12):
                self.ts(self.Dm[:, e, :], pav[:, e, :], acs_tm[:, e:e + 1], 0.0, ALU.subtract, ALU.min,
                        [pacsT[e // 4]] + TS, [t("Dm")])
            if SSD_STOP == 3: return
            self.act(self.Dm[:], self.Dm[:], AF.Exp, [t("Dm")], [t("Dm")])
            self.tt(self.GT[:].rearrange("p (g e) q -> p g e q", g=2), self.Dm[:].rearrange("p (g e) q -> p g e q", g=2),
                    bc(self.CBm[:], [128, 2, 6, 128], 2), ALU.mult, [t("Dm"), t("CBm")], [t("GT")])
            self.act(self.Ebc[:], pav, AF.Exp, pacsT, [t("Ebc")])
            self.act(cd, pav[:, :, 127], AF.Exp, pacsT, TS)
            self.tt(tmp, pav[:, :, 127], acs_tm, ALU.subtract, pacsT + TS, TS)
            self.act(dec, tmp, AF.Exp, TS, TS)
            self.tt(dtdec, dt_tm, dec, ALU.mult, TS, TS)
            self.tt(self.CE[:].rearrange("p (g e) q -> p g e q", g=2), self.Ebc[:].rearrange("p (g e) q -> p g e q", g=2),
                    bc(self.CT[:, :, sl], [128, 2, 6, 128], 2), ALU.mult, [t("Ebc"), t("CT", 0), t("CT", 1)], [t("CE")])
            if SSD_STOP == 4: return
            xs3 = self.xstm[:].rearrange("p (e d) -> p e d", d=64)
            f3 = lambda x: x[:].rearrange("p (e d) -> p e d", d=64)
            self.tt(f3(self.xdt), xs3, bc(dt_tm, [128, 12, 64], 2), ALU.mult, [t("xstm")] + TS, [t("xdt")])
            self.tt(f3(self.xdd), xs3, bc(dtdec, [128, 12, 64], 2), ALU.mult, [t("xstm")] + TS, [t("xdd")])
            self.tt(f3(self.xsD), xs3, bc(self.Db[:, l, :], [128, 12, 64], 2), ALU.mult, [t("xstm")] + PR, [t("xsD")])
            for e in range(12):
                o = py[(e % 2) * 64:(e % 2) * 64 + 64, (e // 2) * 128:(e // 2) * 128 + 128]
                es_ = slice(e * 64, (e + 1) * 64)
                W = [pyT[(e // 2) // 4]]
                self.mm(o, self.xdt[:, es_], self.GT[:, e, :], True, False, [t("xdt"), t("GT")], W)
                self.mm(o, self.hTb[:, es_], self.CE[:, e, :], False, False, [t("hTb"), t("CE")], W)
                self.mm(o, self.xsD[:, es_], self.ident_bf[:], False, True, [t("xsD"), t("ident_bf")], W)
            if SSD_STOP == 5: return
            for (c0, c1, g) in ((0, 384, 0), (384, 512, 1), (512, 768, 1)):
                self.mm(pacs[:, c0:c1], self.Btm[:, g * 128:(g + 1) * 128], self.xdd[:, c0:c1], True, True,
                        [t("Btm"), t("xdd")], pacsT[0:2])
            h3 = self.hT32[:, l, :].rearrange("p (e d) -> p e d", d=64)
            self.tt(h3, h3, bc(cd, [128, 12, 64], 2), ALU.mult, [t("hT32", l)] + TS, [t("hT32", l)])
            self.tt(self.hT32[:, l, :], self.hT32[:, l, :], pacs[:, 0:768], ALU.add, [t("hT32", l)] + pacsT[0:2], [t("hT32", l)])
            self.acopy(self.hTb[:], self.hT32[:, l, :], [t("hT32", l)], [t("hTb")])
            if SSD_STOP == 6: return
            self.stt(self.yg[:], py[:, 0:768].rearrange("p (c q) -> p c q", q=128), 0.5, self.zs[:, :, sl], ALU.mult, ALU.mult,
                     pyT + [t("zs", c) for c in range(6)], [t("yg")])
            self.act(self.sq[:], self.yg[:], AF.Square, [t("yg")], [t("sq")])
            p0f = self.bank(0)
            for c in range(6):
                self.mm(p0f[:, 0:128], ones, self.sq[:, c, :], c == 0, c == 5, [t("cst"), t("sq")], [self.pT(0)])
            self.act(self.rstd[:], p0f[:, 0:128], AF.Sqrt, [self.pT(0)], [t("rstd")], bias=768.0 * EPS)
            P.op("dve", lambda e: e.reciprocal(out=self.rstd[:], in_=self.rstd[:]), [t("rstd")], [t("rstd")])
            for c in range(6):
                self.stt(self.mixT[:, 12 + c, sl], self.yg[:, c, :], self.gS[:, l, c:c + 1], self.rstd[:], ALU.mult, ALU.mult,
                         [t("yg"), t("rstd")] + PR, [t("mixT", 12 + c, tl)])
        if last:
            for hb in range(2):
                pb = self.bank(hb)
                for j in range(3):
                    c = hb * 3 + j
                    self.tr(pb[:, j * 128:(j + 1) * 128], self.hT32[:, l, c * 128:(c + 1) * 128], idf, [t("hT32", l), t("cst")], [self.pT(hb)])
                self.vcopy(self.iost[:, hb * 384:(hb + 1) * 384], pb[:, 0:384], [self.pT(hb)], [t("iost")])
            P.dma("pool", self.p_ssd_h[l].rearrange("(c p) n -> p c n", p=128), self.iost[:, 0:768].rearrange("p (c n) -> p c n", n=128),
                  reads=[t("iost")])

    def phase_out(self, l, p, nco=H, x32=None, xb=None, tag=""):
        P, t = self.P, self.P.t
        cst = self.cst
        idf = cst[:, K_ID:K_ID + 128]
        ones = cst[:, K_ONE:K_ONE + 128]
        PR = [t("par", l)]
        src = self.w_out_bf[l]
        x32 = self.xT32 if x32 is None else x32
        xb = self.xTb if xb is None else xb
        smp = (nco != H)
        XT = (lambda dc: t("SxT32", dc)) if smp else (lambda dc: t("xT32", dc))
        XB = (lambda dc: t("SxTb", dc)) if smp else (lambda dc: t("xTb", dc))
        MT = (lambda ec: [t("SmixT", ec)]) if smp else (lambda ec: [t("mixT", ec, tl) for tl in range(NT)])
        bk = lambda i: self.bank(i)[:, 0:nco]
        for ec in range(18):
            k = self.next_wbuf()
            wv = self.wbuf[k][:].rearrange("p a b -> p (a b)")
            if ec < 4:
                R = 128
                pieces = [(ec * 64, 64, 0), ((ec + 4) * 64, 64, 64)]
            elif ec < 12:
                R = 96
                pieces = [(512 + (ec - 4) * 96, 96, 0)]
            else:
                R = 128
                pieces = [(1280 + (ec - 12) * 128, 128, 0)]
            for (r0, n, d0) in pieces:
                P.dma("sp", wv[d0:d0 + n, :], src[r0:r0 + n, :], reads=self.wout_T(l), writes=[t("wbuf", k)])
            for dc in range(8):
                self.mm(bk(dc), wv[0:R, dc * 128:(dc + 1) * 128], self.mixT[0:R, ec, 0:nco], ec == 0, ec == 17,
                        [t("wbuf", k)] + MT(ec), [self.pT(dc)])
        for dc in range(8):
            X = [XT(dc)]
            self.stt(x32[:, dc, :], x32[:, dc, :], ALPHA, bk(dc), ALU.mult, ALU.add, X + [self.pT(dc)], X)
        for dc in range(8):
            X = [XT(dc)]
            q = self.lnq[dc % 2]
            self.act(q[:, 0:nco], x32[:, dc, :], AF.Square, X, [t("lnq", dc % 2)])
            self.mm(bk(0), ones, x32[:, dc, :], dc == 0, dc == 7, [t("cst")] + X, [self.pT(0)])
            self.mm(bk(1), ones, q[:, 0:nco], dc == 0, dc == 7, [t("cst"), t("lnq", dc % 2)], [self.pT(1)])
        M, Rs = [t("mean")], [t("lrs")]
        mean, lrs = self.mean[:, 0:nco], self.lrs[:, 0:nco]
        self.ts(mean, bk(0), 1.0 / D, None, ALU.mult, None, [self.pT(0)], M)
        self.tt(lrs, mean, mean, ALU.mult, M, Rs)
        self.stt(lrs, bk(1), 1.0 / D, lrs, ALU.mult, ALU.subtract, [self.pT(1)] + Rs, Rs)
        self.act(lrs, lrs, AF.Sqrt, Rs, Rs, bias=EPS)
        P.op("dve", lambda e: e.reciprocal(out=lrs, in_=lrs), Rs, Rs)
        for dc in range(8):
            X = [XT(dc)]
            xv = x32[:, dc, :]
            self.tt(xv, xv, mean, ALU.subtract, X + M, X)
            self.tt(xv, xv, lrs, ALU.mult, X + Rs, X)
            self.act(xv, xv, AF.Identity, X + PR, X, scale=self.lnp[:, l, dc, 0:1], bias=self.lnp[:, l, dc, 1:2])
            if l < DEPTH - 1:
                self.acopy(xb[:, dc, 0:nco], xv, X, [XB(dc)])
        if l == 0 and p == 0:
            self.tap("x_l0p0", self.xT32[:], [t("xT32", dc) for dc in range(8)])
        if smp:
            if l == DEPTH - 1:
                pb = self.bank(2, 2)
                for dc in range(8):
                    self.tr(pb[0:NS, dc * 128:(dc + 1) * 128], x32[:, dc, :], idf, [XT(dc), t("cst")], [self.pT(2 + dc // 4)])
                self.vcopy(self.iost[0:NS, :], pb[0:NS, :], [self.pT(2), self.pT(3)], [t("iost")])
                P.dma("pool", self.y_sample, self.iost[0:NS, :], reads=[t("iost")])
        elif l == DEPTH - 1:
            for tl in range(NT):
                gt = p * NT + tl
                for hb in range(2):
                    pb = self.bank(2 + hb)
                    for j in range(4):
                        dc = hb * 4 + j
                        self.tr(pb[:, j * 128:(j + 1) * 128], self.xT32[:, dc, tl * 128:(tl + 1) * 128], idf,
                                [t("xT32", dc), t("cst")], [self.pT(2 + hb)])
                    self.vcopy(self.iost[:, hb * 512:(hb + 1) * 512], pb, [self.pT(2 + hb)], [t("iost")])
                P.dma("pool", self.y_prompt[gt * 128:(gt + 1) * 128, :], self.iost[:], reads=[t("iost")])

    def sample_job(self, l):
        P, t = self.P, self.P.t
        cst = self.cst
        idf = cst[:, K_ID:K_ID + 128]
        ones = cst[:, K_ONE:K_ONE + 128]
        i16 = idf[0:NS, 0:NS]
        PR = [t("par", l)]
        x32 = self.rstd[:].rearrange("p (a b) -> p a b", b=NS)
        xb = self.xTb
        XB = [t("SxTb", kc) for kc in range(8)]
        XT = [t("SxT32", dc) for dc in range(8)]
        Bt = lambda n: [t("B", n)]
        f2 = lambda ap: ap.rearrange("p a b -> p (a b)")

        def fm(k, M, out, bT):
            for kc in range(8):
                self.mm(out, self.wbuf[k][:, kc, 0:M], xb[:, kc, 0:NS], kc == 0, kc == 7, [t("wbuf", k)] + XB, [bT])

        def tm(k, M, out, bT):
            for kc in range(8):
                self.mm(out, xb[:, kc, 0:NS], self.wbuf[k][:, kc, 0:M], kc == 0, kc == 7, [t("wbuf", k)] + XB, [bT])

        if l == 0:
            P.dma("sp", self.iost[0:NS, :], self.x_sample, writes=[t("iost")])
            pb = self.bank(0)
            for dc in range(8):
                self.tr(pb[:, dc * NS:(dc + 1) * NS], self.iost[0:NS, dc * 128:(dc + 1) * 128], i16, [t("iost"), t("cst")], [self.pT(0)])
            self.vcopy(x32, pb[:, 0:128].rearrange("p (a b) -> p a b", b=NS), [self.pT(0)], XT)
            for dc in range(8):
                self.acopy(xb[:, dc, 0:NS], x32[:, dc, :], [XT[dc]], [XB[dc]])
        P.dma("pool", self.wab[0:96, :, :], self.lru_wa[l].rearrange("n c d -> c n d"), writes=[t("wab")])
        P.dma("pool", self.wxb[0:96, :, :], self.lru_wx[l].rearrange("n c d -> c n d"), writes=[t("wxb")])

        b0, b1 = self.bank(0), self.bank(1)
        for j in range(6):
            k = self.load_win(l, [(j * 128, 128, 0)])
            if j < 4:
                tm(k, 128, b0[0:NS, j * 128:(j + 1) * 128], self.pT(0))
            else:
                tm(k, 128, b1[0:NS, (j - 4) * 128:(j - 3) * 128], self.pT(1))
        q32 = self.qkv32[0]
        Q = Bt("q0")
        self.acopy(q32[0:NS, 0:512], b0[0:NS, :], [self.pT(0)], Q)
        self.acopy(q32[0:NS, 512:768], b1[0:NS, 0:256], [self.pT(1)], Q)
        hv = q32[0:NS, 0:640].rearrange("p (h d) -> p h d", d=64)
        x1, x2 = hv[:, :, 0:8], hv[:, :, 8:16]
        cs = bc(self.ropeS[0:NS, 0, :], [NS, 10, 8], 1)
        sn = bc(self.ropeS[0:NS, 1, :], [NS, 10, 8], 1)
        ns = bc(self.ropeS[0:NS, 2, :], [NS, 10, 8], 1)
        R = Bt("ropet")
        ra, rb = self.ropet[0:NS, 0, :, :], self.ropet[0:NS, 1, :, :]
        self.tt(rb[:, :, 0:8], x2, ns, ALU.mult, Q + [t("rope")], R)
        self.tt(rb[:, :, 8:16], x1, sn, ALU.mult, Q + [t("rope")], R)
        self.tt(ra[:, :, 0:8], x1, cs, ALU.mult, Q + [t("rope")], R)
        self.tt(ra[:, :, 8:16], x2, cs, ALU.mult, Q + [t("rope")], R)
        self.tt(hv[:, :, 0:16], ra, rb, ALU.add, R, Q)
        P.dma("pool", self.s_swa_k[l][:, 0:127, :], self.cache_k[l][:, 1:128, :])
        P.dma("pool", self.s_swa_v[l][:, 0:127, :], self.cache_v[l][:, 1:128, :])
        P.dma("pool", self.s_swa_k[l][:, 127, :], q32[0:NS, 512:640], reads=Q)
        P.dma("pool", self.s_swa_v[l][:, 127, :], q32[0:NS, 640:768], reads=Q)
        b2 = self.bank(2)
        for c in range(4):
            k = self.load_win(l, [(C_GA + c * 64, 64, 0), (C_GA + (c + 4) * 64, 64, 64)])
            fm(k, 128, b2[:, c * NS:(c + 1) * NS], self.pT(2))
        GsT = self.G[0][:, 0:64].rearrange("p (c b) -> p c b", b=NS)
        self.act(f2(GsT), b2[:, 0:64], AF.Silu, [self.pT(2)], Bt("G0"))
        kbf = self.qkbf[0:NS, 512:640]
        vnb = self.qkbf[0:NS, 0:128]
        self.acopy(kbf, q32[0:NS, 512:640], Q, Bt("qkbf"))
        self.acopy(vnb, q32[0:NS, 640:768], Q, Bt("qkbf"))
        qz = f2(self.CE[:])[0:NS, 0:1024].rearrange("p (h f) -> p h f", f=128)
        P.op("dve", lambda e: e.memset(f2(self.CE[:])[0:NS, 0:1024], 0.0), [], Bt("CE"))
        qh = q32[0:NS, 0:512].rearrange("p (h d) -> p h d", d=64)
        self.vcopy(qz[:, 0:4, 0:64], qh[:, 0:4, :], Q + Bt("CE"), Bt("CE"))
        self.vcopy(qz[:, 4:8, 64:128], qh[:, 4:8, :], Q + Bt("CE"), Bt("CE"))
        p3 = self.bankbf(3)
        ib16 = self.ident_bf[0:NS, 0:NS]
        for h in range(8):
            self.tr(p3[:, h * NS:(h + 1) * NS], qz[:, h, :], ib16, Bt("CE") + [t("ident_bf")], [self.pT(3)])
        self.tr(p3[:, 128:128 + NS], kbf, ib16, Bt("qkbf") + [t("ident_bf")], [self.pT(3)])
        qblk = self.PT[0][:, 0:128].rearrange("p (b h) -> p b h", h=8)
        knT = self.PT[0][:, 128:128 + NS]
        self.vcopy(qblk.rearrange("p b h -> p h b"), p3[:, 0:128].rearrange("p (h b) -> p h b", b=NS), [self.pT(3)], Bt("PT0"))
        self.vcopy(knT, p3[:, 128:128 + NS], [self.pT(3)], Bt("PT0"))
        st8 = self.wqkv[:].rearrange("p a b -> p (a b)").bitcast(F32)[:, 0:2048].rearrange("p (b f) -> p b f", f=128)
        P.dma("sp", st8, self.cache_k[l].rearrange("b k f -> k b f"), writes=Bt("st8"))
        for b in range(NS):
            self.tr(self.bank(4 + b // 4)[:, (b % 4) * 128:(b % 4 + 1) * 128], st8[:, b, :], idf, Bt("st8") + [t("cst")], [self.pT(4 + b // 4)])
        KcT = f2(self.xsT[:])[:, 0:2048].rearrange("p (b k) -> p b k", k=128)
        for i in range(4):
            self.acopy(f2(KcT[:, 4 * i:4 * i + 4, :]), self.bank(4 + i), [self.pT(4 + i)], Bt("xsT"))
        for b in range(NS):
            self.mm(self.bank(b // 4)[0:8, (b % 4) * 128:(b % 4 + 1) * 128], qblk[:, b, :], KcT[:, b, :], True, True,
                    Bt("PT0") + Bt("xsT"), [self.pT(b // 4)])
        b4 = self.bank(4)
        for b in range(NS):
            self.mm(b4[0:8, b:b + 1], qblk[:, b, :], knT[:, b:b + 1], True, True, Bt("PT0"), [self.pT(4)])
        stt_ = f2(self.tsm[:])[0:8, 0:96].rearrange("p (j b) -> p j b", b=NS)
        mx, negm, ssum, pnew, es, rden = (stt_[:, j, :] for j in range(6))
        A = Bt("tsm")
        S4 = self.ps[0:8, 0:2048].rearrange("p (b k) -> p b k", k=128)
        ST = [self.pT(i) for i in range(4)]
        P.op("dve", lambda e: e.reduce_max(out=mx, in_=S4, axis=AX.X), ST, A)
        self.tt(mx, mx, b4[0:8, 0:NS], ALU.max, A + [self.pT(4)], A)
        self.stt(negm, mx, -SCALE, self.sinkc[0:8, l, 1:2].to_broadcast([8, NS]), ALU.mult, ALU.min, A + PR, A)
        Pms = f2(self.Dm[:]).bitcast(BF16)[0:8, 0:2048]
        for b in range(NS):
            self.act(Pms[:, b * 128:(b + 1) * 128], S4[:, b, :], AF.Exp, [self.pT(b // 4)] + A, Bt("Dm") + A,
                     scale=SCALE, bias=negm[:, b:b + 1], accum_out=ssum[:, b:b + 1])
        self.stt(pnew, b4[0:8, 0:NS], SCALE, negm, ALU.mult, ALU.add, A + [self.pT(4)], A)
        self.act(pnew, pnew, AF.Exp, A, A)
        self.act(es, negm, AF.Exp, A + PR, A, bias=self.sinkc[0:8, l, 0:1])
        self.tt(ssum, ssum, pnew, ALU.add, A, A)
        self.tt(ssum, ssum, es, ALU.add, A, A)
        P.op("dve", lambda e: e.reciprocal(out=rden, in_=ssum), A, A)
        pv = Pms.rearrange("p (b k) -> p b k", k=128)
        self.tt(pv, pv, bc(rden, [8, NS, 128], 2), ALU.mult, Bt("Dm") + A, Bt("Dm"))
        self.tt(pnew, pnew, rden, ALU.mult, A, A)
        p5 = self.bankbf(5)
        for b in range(NS):
            self.tr(p5[:, b * 8:(b + 1) * 8], Pms[:, b * 128:(b + 1) * 128], self.ident_bf[0:8, 0:8], Bt("Dm") + [t("ident_bf")], [self.pT(5)])
        PTs = self.Pm[0][:, 0:128].rearrange("p (b h) -> p b h", h=8)
        self.acopy(f2(PTs), p5[:, 0:128], [self.pT(5)], Bt("Pm0"))
        b6 = self.bank(6)
        self.tr(b6[0:NS, 0:8], pnew, idf[0:8, 0:8], A + [t("cst")], [self.pT(6)])
        pnt = f2(self.ast[1][:])[0:NS, 0:8]
        self.vcopy(pnt, b6[0:NS, 0:8], [self.pT(6)], Bt("ast1"))
        psel = self.PT[1][0:NS, 0:128].rearrange("p (b h) -> p b h", h=8)
        self.tt(psel, bc(pnt, [NS, NS, 8], 1), bc(i16, [NS, NS, 8], 2), ALU.mult, Bt("ast1") + [t("cst")], Bt("PT1"))
        P.dma("sp", st8, self.cache_v[l].rearrange("b k f -> k b f"), writes=Bt("st8"))
        Vc = f2(self.qT[:]).rearrange("p (b f) -> p b f", f=128)
        self.vcopy(f2(Vc), f2(st8), Bt("st8"), Bt("qT"))
        b7 = self.bank(7)
        oT = b7[:, 0:128].rearrange("p (b h) -> p b h", h=8)
        for b in range(NS):
            self.mm(b7[:, b * 8:(b + 1) * 8], Vc[:, b, :], PTs[:, b, :], True, False, Bt("qT") + Bt("Pm0"), [self.pT(7)])
            self.mm(b7[:, b * 8:(b + 1) * 8], vnb, psel[:, b, :], False, True, Bt("qkbf") + Bt("PT1"), [self.pT(7)])
        for c in range(4):
            for w in range(2):
                rows = slice(w * 64, (w + 1) * 64)
                self.tt(self.mixT[rows, c, 0:NS], oT[rows, :, c + 4 * w], GsT[rows, c, :], ALU.mult,
                        [self.pT(7)] + Bt("G0"), [t("SmixT", c)])

        zf = f2(self.zs[:])
        stc = zf[0:NS, 0:2304]
        sth = zf[0:NS, 2304:3072]
        P.dma("sp", stc.rearrange("p (k f) -> p k f", f=768), self.st_lru_conv[l], writes=Bt("zs"))
        P.dma("sp", sth, self.st_lru_h[l], writes=Bt("zs"))
        P.dma("pool", self.s_lru_conv[l][:, 0:2, :], self.st_lru_conv[l][:, 1:3, :])
        b0, b1 = self.bank(0), self.bank(1)
        for n in range(8):
            for k3 in range(3):
                j = n * 3 + k3
                self.tr(b0[0:96, j * NS:(j + 1) * NS], stc[:, k3 * 768 + n * 96:k3 * 768 + (n + 1) * 96], i16, Bt("zs") + [t("cst")], [self.pT(0)])
            self.tr(b1[0:96, n * NS:(n + 1) * NS], sth[:, n * 96:(n + 1) * 96], i16, Bt("zs") + [t("cst")], [self.pT(1)])
        aUf = f2(self.aU[:])
        hs = aUf[0:96, 0:384].rearrange("p (n k b) -> p n k b", k=3, b=NS)
        h0T = aUf[0:96, 384:512].rearrange("p (n b) -> p n b", b=NS)
        self.vcopy(aUf[0:96, 0:384], b0[0:96, 0:384], [self.pT(0)], Bt("aU"))
        self.vcopy(aUf[0:96, 384:512], b1[0:96, 0:128], [self.pT(1)], Bt("aU"))
        b2, b5 = self.bank(2), self.bank(5)
        for n in range(8):
            kx = self.load_win(l, [(C_XL + n * 96, 96, 0)])
            fm(kx, 96, b2[0:96, n * NS:(n + 1) * NS], self.pT(2))
            tm(kx, 96, self.bank(3 + n // 4)[0:NS, (n % 4) * 128:(n % 4) * 128 + 96], self.pT(3 + n // 4))
            kg = self.load_win(l, [(C_GL + n * 96, 96, 0)])
            fm(kg, 96, b5[0:96, n * NS:(n + 1) * NS], self.pT(5))
        xltm = self.qkv32[1][0:NS, 0:768]
        self.vcopy(xltm.rearrange("p (n c) -> p n c", c=96),
                   self.ps[0:NS, 3 * 512:5 * 512].rearrange("p (n c) -> p n c", c=128)[:, :, 0:96], [self.pT(3), self.pT(4)], Bt("q1"))
        P.dma("pool", self.s_lru_conv[l][:, 2, :], xltm, reads=Bt("q1"))
        v3 = lambda ap: ap.rearrange("p (n b) -> p n b", b=NS)
        lpb = lambda j: bc(self.lp[0:96, l, :, j], [96, 8, NS], 2)
        acc = v3(self.xl[0:96, 0:128]); tmp = v3(self.xlb[0:96, 0:256].bitcast(F32))
        Xl, Tm = Bt("xl"), Bt("xlb")
        self.tt(acc, v3(b2[0:96, 0:128]), lpb(3), ALU.mult, [self.pT(2)] + PR, Xl)
        self.tt(acc, acc, lpb(4), ALU.add, Xl + PR, Xl)
        for k3 in range(3):
            self.tt(tmp, hs[:, :, k3, :], lpb(k3), ALU.mult, Bt("aU") + PR, Tm)
            self.tt(acc, acc, tmp, ALU.add, Xl + Tm, Xl)
        xlbf = self.dtt[0:96, 0:64].bitcast(BF16)
        self.acopy(xlbf, self.xl[0:96, 0:128], Xl, Bt("dtt"))
        b6, b7 = self.bank(6), self.bank(7)
        for n in range(8):
            self.mm(b6[0:96, n * NS:(n + 1) * NS], self.wab[0:96, n, :], xlbf[:, n * NS:(n + 1) * NS], True, True, [t("wab")] + Bt("dtt"), [self.pT(6)])
            self.mm(b7[0:96, n * NS:(n + 1) * NS], self.wxb[0:96, n, :], xlbf[:, n * NS:(n + 1) * NS], True, True, [t("wxb")] + Bt("dtt"), [self.pT(7)])
        rr, ig, EE = v3(self.rr[0:96, 0:128]), v3(self.ig[0:96, 0:128]), v3(self.EE[0:96, 0:128])
        Rr, Ig, Ee = Bt("rr"), Bt("ig"), Bt("EE")
        self.tt(rr, v3(b6[0:96, 0:128]), lpb(5), ALU.add, [self.pT(6)] + PR, Rr)
        self.act(rr, rr, AF.Sigmoid, Rr, Rr)
        self.tt(ig, v3(b7[0:96, 0:128]), lpb(6), ALU.add, [self.pT(7)] + PR, Ig)
        self.act(ig, ig, AF.Sigmoid, Ig, Ig)
        self.tt(EE, rr, bc(self.la[0:96, l, :, 1], [96, 8, NS], 2), ALU.mult, Rr + PR, Ee)
        self.act(EE, EE, AF.Exp, Ee, Ee, scale=2.0)
        self.act(EE, EE, AF.Sqrt, Ee, Ee, scale=-1.0, bias=1.0)
        self.tt(rr, rr, bc(self.la[0:96, l, :, 1], [96, 8, NS], 2), ALU.mult, Rr + PR, Rr)
        self.act(rr, rr, AF.Exp, Rr, Rr)
        self.tt(ig, ig, acc, ALU.mult, Ig + Xl, Ig)
        self.tt(ig, ig, EE, ALU.mult, Ig + Ee, Ig)
        self.tt(EE, rr, h0T, ALU.mult, Rr + Bt("aU"), Ee)
        self.tt(EE, EE, ig, ALU.add, Ee + Ig, Ee)
        for n in range(8):
            self.tr(self.bank(3 + n // 4)[0:NS, (n % 4) * 128:(n % 4) * 128 + 96], self.EE[0:96, n * NS:(n + 1) * NS], idf[0:96, 0:96],
                    Ee + [t("cst")], [self.pT(3 + n // 4)])
        h1tm = f2(self.yg[:])[0:NS, 0:768]
        self.vcopy(h1tm.rearrange("p (n c) -> p n c", c=96),
                   self.ps[0:NS, 3 * 512:5 * 512].rearrange("p (n c) -> p n c", c=128)[:, :, 0:96], [self.pT(3), self.pT(4)], Bt("yg"))
        P.dma("pool", self.s_lru_h[l], h1tm, reads=Bt("yg"))
        sg = v3(self.dtT[0:96, 0:128])
        self.act(sg, v3(b5[0:96, 0:128]), AF.Silu, [self.pT(5)], Bt("dtT"))
        self.tt(self.mixT[0:96, 4:12, 0:NS], EE, sg, ALU.mult, Ee + Bt("dtT"), [t("SmixT", 4 + n) for n in range(8)])

        Dmf = f2(self.Dm[:])
        srcs = (zf[0:NS, 0:1280], zf[0:NS, 1280:2560], Dmf[0:NS, 0:1280])
        srcT = (Bt("zs"), Bt("zs"), Bt("Dm"))
        for k3 in range(3):
            P.dma("sp", srcs[k3], self.st_ssd_conv[l][:, k3, :], writes=srcT[k3])
        P.dma("pool", self.s_ssd_conv[l][:, 0:2, :], self.st_ssd_conv[l][:, 1:3, :])
        b0 = self.bank(0)
        for c in range(10):
            for k3 in range(3):
                j = c * 3 + k3
                self.tr(b0[:, j * NS:(j + 1) * NS], srcs[k3][:, c * 128:(c + 1) * 128], i16, srcT[k3] + [t("cst")], [self.pT(0)])
        sqf = f2(self.sq[:])
        hss = sqf[:, 0:480].rearrange("p (c k b) -> p c k b", k=3, b=NS)
        self.vcopy(sqf[:, 0:480], b0[:, 0:480], [self.pT(0)], Bt("sq"))
        b1 = self.bank(1)
        for c in range(10):
            k = self.load_win(l, [(C_XBC + c * 128, 128, 0)])
            fm(k, 128, b1[:, c * NS:(c + 1) * NS], self.pT(1))
            tm(k, 128, self.bank(2 + c // 4)[0:NS, (c % 4) * 128:(c % 4 + 1) * 128], self.pT(2 + c // 4))
        xbtm = aUf[0:NS, 0:1280]
        self.vcopy(xbtm, self.ps[0:NS, 2 * 512:2 * 512 + 1280], [self.pT(2), self.pT(3), self.pT(4)], Bt("aU"))
        P.dma("pool", self.s_ssd_conv[l][:, 2, :], xbtm, reads=Bt("aU"))
        b5, b6 = self.bank(5), self.bank(6)
        for c in range(6):
            k = self.load_win(l, [(C_Z + c * 128, 128, 0)])
            fm(k, 128, b5[:, c * NS:(c + 1) * NS], self.pT(5))
        k = self.load_win(l, [(C_DT, 12, 0)])
        fm(k, 12, b6[0:12, 0:NS], self.pT(6))
        spb = lambda j: bc(self.spp[:, l, :, j], [128, 10, NS], 2)
        acc = v3(self.xl[:, 0:160]); tmp = v3(self.xlb[:, 0:320].bitcast(F32))
        self.tt(acc, v3(b1[:, 0:160]), spb(3), ALU.mult, [self.pT(1)] + PR, Xl)
        self.tt(acc, acc, spb(4), ALU.add, Xl + PR, Xl)
        for k3 in range(3):
            self.tt(tmp, hss[:, :, k3, :], spb(k3), ALU.mult, Bt("sq") + PR, Tm)
            self.tt(acc, acc, tmp, ALU.add, Xl + Tm, Xl)
        xc = v3(self.rr[:, 0:160])
        self.act(xc, acc, AF.Silu, Xl, Rr, scale=2.0)
        xsTs, BsT, CsT = xc[:, 0:6, :], xc[:, 6:8, :], xc[:, 8:10, :]
        zsT = v3(self.ig[:, 0:96])
        self.act(zsT, v3(b5[:, 0:96]), AF.Silu, [self.pT(5)], Ig)
        u, v = self.dtT[0:12, 0:NS], self.dtt[0:12, 0:NS]
        self.act(u, b6[0:12, 0:NS], AF.Identity, [self.pT(6)] + PR, Bt("dtT"), bias=self.dtb[0:12, l:l + 1])
        self.act(v, u, AF.Abs, Bt("dtT"), Bt("dtt"))
        self.act(v, v, AF.Exp, Bt("dtt"), Bt("dtt"), scale=-1.0)
        self.act(v, v, AF.Ln, Bt("dtt"), Bt("dtt"), bias=1.0)
        self.stt(u, u, 0.0, v, ALU.max, ALU.add, Bt("dtT") + Bt("dtt"), Bt("dtT"))
        Ex = cst[0:12, K_EX:K_EX + 768]
        b7 = self.bank(7)
        for c in range(6):
            Exc = Ex[:, c * 128:(c + 1) * 128]
            self.mm(b7[:, c * NS:(c + 1) * NS], Exc, u, True, True, [t("cst")] + Bt("dtT"), [self.pT(7)])
            self.mm(b7[:, 96 + c:97 + c], Exc, self.adc[0:12, l, 0:1], True, True, [t("cst")] + PR, [self.pT(7)])
            self.mm(b7[:, 104 + c:105 + c], Exc, self.adc[0:12, l, 1:2], True, True, [t("cst")] + PR, [self.pT(7)])
        ex = self.EE[:, 0:112]
        self.vcopy(ex, b7[:, 0:112], [self.pT(7)], Ee)
        dtx = v3(ex[:, 0:96]); Ax = ex[:, 96:102]; Dx = ex[:, 104:110]
        decT = v3(self.mean[:, 0:96]); x0T = v3(self.lrs[:, 0:96])
        self.tt(decT, dtx, bc(Ax, [128, 6, NS], 2), ALU.mult, Ee, [t("mean")])
        self.act(decT, decT, AF.Exp, [t("mean")], [t("mean")])
        self.tt(x0T, xsTs, dtx, ALU.mult, Rr + Ee, [t("lrs")])
        yT = v3(self.lnq[0][:, 0:96])
        hbuf = [f2(self.xT32[:])[:, i * 2048:(i + 1) * 2048].rearrange("p (b n) -> p b n", n=128) for i in range(2)]
        st8v = st8
        Bbc = self.ps[:, 0:2048].rearrange("p (b n) -> p b n", n=128)
        Cbc = self.ps[:, 2048:4096].rearrange("p (b n) -> p b n", n=128)
        idb = bc(idf, [128, NS, 128], 1)
        for g in range(2):
            self.tt(st8v, bc(BsT[:, g, :], [128, NS, 128], 2), idb, ALU.mult, Rr + [t("cst")], Bt("st8"))
            for j in range(4):
                self.mm(self.bank(j), ones, f2(st8v)[:, j * 512:(j + 1) * 512], True, True, [t("cst")] + Bt("st8"), [self.pT(j)])
            self.tt(st8v, bc(CsT[:, g, :], [128, NS, 128], 2), idb, ALU.mult, Rr + [t("cst")], Bt("st8"))
            for j in range(4):
                self.mm(self.bank(4 + j), ones, f2(st8v)[:, j * 512:(j + 1) * 512], True, True, [t("cst")] + Bt("st8"), [self.pT(4 + j)])
            for c in range(3 * g, 3 * g + 3):
                hb = hbuf[c % 2]
                Hb = [t("B", "hb", c % 2)]
                P.dma("sp", hb, self.st_ssd_h[l][:, 2 * c:2 * c + 2].rearrange("b e p n -> (e p) b n"), writes=Hb)
                self.tt(hb, hb, bc(decT[:, c, :], [128, NS, 128], 2), ALU.mult, Hb + [t("mean")], Hb)
                self.tt(st8v, Bbc, bc(x0T[:, c, :], [128, NS, 128], 2), ALU.mult, [self.pT(j) for j in range(4)] + [t("lrs")], Bt("st8"))
                self.tt(hb, hb, st8v, ALU.add, Hb + Bt("st8"), Hb)
                P.dma("pool", self.s_ssd_h[l][:, c * 128:(c + 1) * 128, :].rearrange("b f n -> f b n"), hb, reads=Hb)
                self.tt(st8v, hb, Cbc, ALU.mult, Hb + [self.pT(4 + j) for j in range(4)], Bt("st8"))
                P.op("dve", lambda e, c=c: e.reduce_sum(out=yT[:, c, :], in_=st8v, axis=AX.X), Bt("st8"), [t("lnq", 0)])
        self.tt(tmp[:, 0:6, :], xsTs, bc(Dx, [128, 6, NS], 2), ALU.mult, Rr + Ee, Tm)
        self.tt(yT, yT, tmp[:, 0:6, :], ALU.add, [t("lnq", 0)] + Tm, [t("lnq", 0)])
        self.tt(yT, yT, zsT, ALU.mult, [t("lnq", 0)] + Ig, [t("lnq", 0)])
        sqs = v3(self.lnq[1][:, 0:96])
        self.act(sqs, yT, AF.Square, [t("lnq", 0)], [t("lnq", 1)])
        b0 = self.bank(0)
        for c in range(6):
            self.mm(b0[:, 0:NS], ones, sqs[:, c, :], c == 0, c == 5, [t("cst"), t("lnq", 1)], [self.pT(0)])
        rs = self.dtt[:, 64:64 + NS]
        self.act(rs, b0[:, 0:NS], AF.Sqrt, [self.pT(0)], Bt("dtt"), bias=768.0 * EPS)
        P.op("dve", lambda e: e.reciprocal(out=rs, in_=rs), Bt("dtt"), Bt("dtt"))
        for c in range(6):
            self.stt(self.mixT[:, 12 + c, 0:NS], yT[:, c, :], self.gS[:, l, c:c + 1], rs, ALU.mult, ALU.mult,
                     [t("lnq", 0)] + Bt("dtt") + PR, [t("SmixT", 12 + c)])

        self.phase_out(l, None, nco=NS, x32=x32, xb=self.xTb)


    def build(self, sample=True, n_pass=NPASS, n_layers=DEPTH):
        self.declare_io()
        self.alloc()
        self.prologue()
        for p in range(n_pass):
            for l in range(n_layers):
                self.job(l, p)
        if sample:
            self.P.fence()
            for l in range(n_layers):
                self.sample_job(l)
        self.P.emit()
        self.st.close()
        return self.nc


def make_consts():
    c = np.zeros((128, K_END), np.float32)
    c[:, K_ID:K_ID + 128] = np.eye(128, dtype=np.float32)
    s = np.arange(128)[:, None]
    q = np.arange(128)[None, :]
    c[:, K_U:K_U + 128] = (s <= q).astype(np.float32)
    c[:, K_ONE:K_ONE + 128] = 1.0
    i = np.arange(128)[:, None]
    j = np.arange(256)[None, :]
    band = (j >= i) & (j <= i + 128)
    c[:, K_MG:K_MG + 256] = np.where(band, 0.0, NEG)
    c[:, K_MF:K_MF + 256] = np.where(band & (j >= 128), 0.0, NEG)
    c[:, K_POS:K_POS + 16] = np.arange(128)[:, None] + 128.0 * np.arange(16)[None, :]
    c[:, K_INV:K_INV + 8] = (500000.0 ** (-np.arange(8, dtype=np.float32) / 8.0)).astype(np.float32)[None, :]
    for e in range(12):
        c[e, K_EX + e * 64:K_EX + (e + 1) * 64] = 1.0
    c[:, K_HM] = (np.arange(128) < 4)
    c[:, K_HM + 1] = (np.arange(128) >= 4)
    return c


WNAMES = ["w_in", "w_out", "att_sinks", "lru_conv_w", "lru_conv_b", "lru_wa", "lru_ba", "lru_wx", "lru_bx",
          "lru_lambda", "ssd_conv_w", "ssd_conv_b", "ssd_dt_bias", "ssd_a_log", "ssd_d", "ssd_norm_g", "ln_g", "ln_b"]


def make_in_maps(inputs, n=8):
    f = lambda a: np.ascontiguousarray(np.asarray(a, dtype=np.float32))
    shared = {k: f(inputs[k]) for k in WNAMES}
    shared["consts"] = make_consts()
    maps = []
    for i in range(n):
        s = slice(i * NS, (i + 1) * NS)
        m = dict(shared)
        m["x_prompt"] = f(inputs["x_prompt"][i])
        m["x_sample"] = f(np.asarray(inputs["x_sample"])[s, 0, :])
        m["cache_swa_k"] = f(np.asarray(inputs["cache_swa_k"])[:, s].reshape(DEPTH, NS, 128, 128))
        m["cache_swa_v"] = f(np.asarray(inputs["cache_swa_v"])[:, s].reshape(DEPTH, NS, 128, 128))
        m["state_lru_conv"] = f(np.asarray(inputs["state_lru_conv"])[:, s])
        m["state_lru_h"] = f(np.asarray(inputs["state_lru_h"])[:, s])
        m["state_ssd_conv"] = f(np.asarray(inputs["state_ssd_conv"])[:, s])
        m["state_ssd_h"] = f(np.asarray(inputs["state_ssd_h"])[:, s])
        maps.append(m)
    return maps


def kernel(**inputs):
    nc = Builder().build()
    res = run_bass_kernel_spmd(nc, make_in_maps(inputs), core_ids=list(range(8)))
    R = res.results
    cat = lambda k, ax: np.concatenate([np.asarray(r[k]) for r in R], axis=ax)
    stk = lambda k: np.stack([np.asarray(r[k]) for r in R], axis=1)
    y_prompt = np.stack([np.asarray(r["y_prompt"]) for r in R], 0)
    y_sample = cat("y_sample", 0).reshape(128, 1, D)
    p_swa_k = stk("p_swa_k").reshape(DEPTH, 8, 128, 2, 64)
    p_swa_v = stk("p_swa_v").reshape(DEPTH, 8, 128, 2, 64)
    p_lru_conv = stk("p_lru_conv")
    p_lru_h = stk("p_lru_h")
    p_ssd_conv = stk("p_ssd_conv")
    p_ssd_h = stk("p_ssd_h").reshape(DEPTH, 8, 12, 64, 128)
    s_swa_k = cat("s_swa_k", 1).reshape(DEPTH, 128, 128, 2, 64)
    s_swa_v = cat("s_swa_v", 1).reshape(DEPTH, 128, 128, 2, 64)
    s_lru_conv = cat("s_lru_conv", 1)
    s_lru_h = cat("s_lru_h", 1)
    s_ssd_conv = cat("s_ssd_conv", 1)
    s_ssd_h = cat("s_ssd_h", 1).reshape(DEPTH, 128, 12, 64, 128)
    outs = (y_prompt, y_sample, p_swa_k, p_swa_v, p_lru_conv, p_lru_h, p_ssd_conv, p_ssd_h,
            s_swa_k, s_swa_v, s_lru_conv, s_lru_h, s_ssd_conv, s_ssd_h)
    return tuple(np.ascontiguousarray(o, dtype=np.float32) for o in outs)
```

```python
import contextlib
import math
import numpy as np
import concourse.bass as bass
import concourse.mybir as mybir
from concourse.bass_utils import run_bass_kernel_spmd

F32 = mybir.dt.float32
BF16 = mybir.dt.bfloat16
I32 = mybir.dt.int32
AF = mybir.ActivationFunctionType
ALU = mybir.AluOpType
AX = mybir.AxisListType

D = 1024
SEQ = 2048
DEPTH = 4
NS = 16
DIN = 4876
C_Q, C_K, C_V, C_GA, C_XL, C_GL, C_Z, C_XBC, C_DT = 0, 512, 640, 768, 1280, 2048, 2816, 3584, 4864
SCALE = 64 ** -0.5
ALPHA = (2.0 * DEPTH) ** 0.25
EPS = 1e-5
H = 512
NT = H // 128
NPASS = SEQ // H
PAST = 8192.0
NEG = -30000.0
import os
SSD_STOP = int(os.environ.get('SSD_STOP', '0'))
PIN_ENGS = set(os.environ.get('PIN_ENGS', '').split(','))
UNPIN_LINES = set(int(x) for x in os.environ.get('UNPIN_LINES', '').split(',') if x)

K_ID, K_U, K_ONE, K_MG, K_MF, K_POS, K_INV, K_HM, K_EX, K_END = 0, 128, 256, 384, 640, 896, 912, 920, 922, 922 + 768


class T:
    __slots__ = ("name", "lw", "rd", "excl")

    def __init__(self, name):
        self.name = name
        self.lw = None
        self.rd = []
        self.excl = (len(name) > 0 and name[0] == "ps")


class Op:
    __slots__ = ("eng", "fn", "deps", "signal", "val", "is_dma", "dsem", "dval", "odeps", "cost", "idx", "prio", "fin", "nbytes", "tbl")

    def __init__(self, eng, fn, is_dma):
        self.eng = eng
        self.fn = fn
        self.odeps = []
        self.tbl = None
        self.cost = 0.3
        self.nbytes = 0
        self.deps = []
        self.signal = False
        self.val = None
        self.is_dma = is_dma
        self.dsem = None
        self.dval = None


class Prog:
    ENGS = ("pe", "act", "dve", "pool", "sp")

    def __init__(self, nc, n_dma_sems=24):
        self.nc = nc
        self.ops = []
        self.tiles = {}
        self.n_dma_sems = n_dma_sems
        self.last_fence = {}
        self.last_pe = None
        self.last_on = {}
        self.do_schedule = os.environ.get('SCHED', '1') == '1'
        self.sched_window = int(os.environ.get('SCHEDW', '0'))

    def t(self, *key):
        if key not in self.tiles:
            self.tiles[key] = T(key)
        return self.tiles[key]

    def op(self, eng, fn, reads=(), writes=(), dma=False, cost=0.3, nbytes=0):
        o = Op(eng, fn, dma)
        o.cost = cost
        o.nbytes = nbytes
        o.idx = len(self.ops)
        lf = self.last_fence.get(eng)
        if lf is not None:
            o.odeps.append(lf)
        if eng in PIN_ENGS:
            import sys as _sys
            f = _sys._getframe(1)
            while f.f_code.co_name in ("act", "acopy", "vcopy", "tt", "ts", "stt", "mm", "tr", "dma", "op", "<lambda>"):
                f = f.f_back
            if f.f_lineno not in UNPIN_LINES:
                lo = self.last_on.get(eng)
                if lo is not None:
                    o.odeps.append(lo)
                self.last_on[eng] = o
        if eng == "pe":
            pin = (cost < 0)
            if pin:
                o.cost = cost = -cost
            lp = self.last_pe
            if lp is not None and (pin or lp[1]):
                o.odeps.append(lp[0])
            self.last_pe = (o, pin)
        deps = set()
        for t in reads:
            if t.lw is not None:
                deps.add(t.lw)
            if t.excl:
                for r in t.rd:
                    if r.eng != eng:
                        deps.add(r)
        for t in writes:
            if t.lw is not None:
                deps.add(t.lw)
            for r in t.rd:
                deps.add(r)
        for t in reads:
            t.rd.append(o)
        for t in writes:
            t.lw = o
            t.rd = []
        for d in deps:
            if d is o:
                continue
            if (not d.is_dma) and (not dma) and d.eng == "pe" and eng == "pe":
                o.odeps.append(d)
                continue
            o.deps.append(d)
            d.signal = True
        self.ops.append(o)
        return o

    def dma(self, eng, out, in_, reads=(), writes=(), **kw):
        nb = 1
        for d in out.shape:
            nb *= d
        nb *= mybir.dt.size(out.dtype)
        return self.op(eng, lambda e: e.dma_start(out=out, in_=in_, **kw), reads, writes, dma=True,
                       cost=(0.08 if eng != "pool" else 0.6), nbytes=nb)

    def fence(self):
        allt = list(self.tiles.values())
        ft = self.t("__fence__")
        first = True
        for e in ("pe", "act", "dve", "pool", "sp"):
            o = self.op(e, lambda eng: eng.nop(), [], (allt + [ft]) if first else [ft], cost=0.05)
            self.last_fence[e] = o
            first = False

    def schedule(self):
        import heapq
        ops = self.ops
        n = len(ops)
        succ = [[] for _ in range(n)]
        npred = [0] * n
        for o in ops:
            ps = set(id(d) for d in o.deps) | set(id(d) for d in o.odeps)
            seen = set()
            for d in list(o.deps) + list(o.odeps):
                if id(d) in seen:
                    continue
                seen.add(id(d))
                succ[d.idx].append(o.idx)
                npred[o.idx] += 1
        dur = [0.0] * n
        for o in ops:
            dur[o.idx] = o.cost + (2.0 + o.nbytes / 250e3 if o.is_dma else 0.0)
        prio = [0.0] * n
        for i in range(n - 1, -1, -1):
            m = 0.0
            for j in succ[i]:
                if prio[j] > m:
                    m = prio[j]
            prio[i] = m + dur[i]
        ready = {e: [] for e in self.ENGS}
        ready_t = [0.0] * n
        for o in ops:
            if npred[o.idx] == 0:
                heapq.heappush(ready[o.eng], (-prio[o.idx], o.idx))
        free = {e: 0.0 for e in self.ENGS}
        order = {e: [] for e in self.ENGS}
        fin = [0.0] * n
        cur_tbl = None
        SETS_OF = {"E": ("0", "6"), "A": ("0",), "L": ("6",), "S": ("3",), "G": ("2",), "U": ("18",), "N": ("9",)}

        def tbl_ok(cur, f):
            return f is None or cur is None or any(x in cur for x in SETS_OF[f])

        def tbl_next(cur, f):
            if cur is None:
                return SETS_OF[f]
            inter = tuple(x for x in SETS_OF[f] if x in cur)
            return inter if inter else SETS_OF[f]
        dma_free = 0.0
        done = 0
        while done < n:
            best = None
            for e in self.ENGS:
                if not ready[e]:
                    continue
                cand = ready[e][0]
                i = cand[1]
                st = max(free[e], ready_t[i])
                if best is None or st < best[0]:
                    best = (st, e)
            st, e = best
            heap = ready[e]
            pick = None
            tmp = []
            fallback = None
            while heap:
                c = heapq.heappop(heap)
                if ready_t[c[1]] <= max(free[e], st) + 1e-9:
                    if e != "act" or tbl_ok(cur_tbl, ops[c[1]].tbl):
                        pick = c
                        break
                    if fallback is None:
                        fallback = c
                        if len(tmp) > 24:
                            break
                        continue
                tmp.append(c)
                if len(tmp) > 64:
                    break
            if pick is None and fallback is not None:
                pick = fallback
                fallback = None
            if fallback is not None:
                tmp.append(fallback)
            for c in tmp:
                heapq.heappush(heap, c)
            if pick is None:
                pick = heapq.heappop(heap)
            i = pick[1]
            o = ops[i]
            start = max(free[e], ready_t[i])
            if e == "act" and o.tbl is not None:
                if not tbl_ok(cur_tbl, o.tbl):
                    start += 1.3
                cur_tbl = tbl_next(cur_tbl, o.tbl)
            if o.is_dma:
                free[e] = start + o.cost
                t0 = max(start + o.cost, dma_free)
                dma_free = t0 + o.nbytes / 250e3
                fin[i] = dma_free + 2.0
            else:
                free[e] = start + o.cost
                fin[i] = free[e]
            order[e].append(o)
            done += 1
            for j in succ[i]:
                lat = 0.0 if (ops[j].eng == e and not o.is_dma) else 0.15
                if fin[i] + lat > ready_t[j]:
                    ready_t[j] = fin[i] + lat
                npred[j] -= 1
                if npred[j] == 0:
                    heapq.heappush(ready[ops[j].eng], (-prio[j], j))
        self.est_time = max(fin) if n else 0.0
        if self.sched_window > 0:
            return self.schedule_window(succ)
        return order

    def schedule_window(self, succ):
        ops = self.ops
        n = len(ops)
        W = self.sched_window
        npred = [0] * n
        for i in range(n):
            for j in succ[i]:
                npred[j] += 1
        orig = {e: [o.idx for o in ops if o.eng == e] for e in self.ENGS}
        pos = {e: 0 for e in self.ENGS}
        taken = [False] * n
        order = {e: [] for e in self.ENGS}
        done = 0
        while done < n:
            progressed = False
            for e in self.ENGS:
                lst = orig[e]
                while pos[e] < len(lst) and taken[lst[pos[e]]]:
                    pos[e] += 1
                for k in range(pos[e], min(pos[e] + W, len(lst))):
                    i = lst[k]
                    if not taken[i] and npred[i] == 0:
                        taken[i] = True
                        order[e].append(ops[i])
                        for j in succ[i]:
                            npred[j] -= 1
                        done += 1
                        progressed = True
                        break
            assert progressed
        return order

    def emit(self):
        nc = self.nc
        cnt = {e: 0 for e in self.ENGS}
        dma_rr = {e: 0 for e in self.ENGS}
        dma_cnt = {}
        if self.do_schedule:
            per = self.schedule()
        else:
            per = {e: [o for o in self.ops if o.eng == e] for e in self.ENGS}
        for o in [o for e in self.ENGS for o in per[e]]:
            if o.is_dma:
                k = dma_rr[o.eng] % self.n_dma_sems
                dma_rr[o.eng] += 1
                key = (o.eng, k)
                dma_cnt[key] = dma_cnt.get(key, 0) + 16
                o.dsem = key
                o.dval = dma_cnt[key]
            elif o.signal:
                cnt[o.eng] += 1
                o.val = cnt[o.eng]
        with contextlib.ExitStack() as st:
            sems = {e: st.enter_context(nc.semaphore("s_" + e)) for e in ("pe", "act", "dve", "pool")}
            dsems = {}
            for e in self.ENGS:
                for k in range(min(self.n_dma_sems, dma_rr[e])):
                    dsems[(e, k)] = st.enter_context(nc.semaphore("d_%s_%d" % (e, k)))
            block = st.enter_context(nc.Block())

            def run(ename):
                def body(eng):
                    waited = {}
                    for o in per[ename]:
                        need = {}
                        for d in o.deps:
                            if d.is_dma:
                                s, v, sk = dsems[d.dsem], d.dval, ("d",) + d.dsem
                            else:
                                s, v, sk = sems[d.eng], d.val, ("c", d.eng)
                            if need.get(sk, (None, 0))[1] < v:
                                need[sk] = (s, v)
                        if o.is_dma and o.dval > 16:
                            sk = ("d",) + o.dsem
                            if need.get(sk, (None, 0))[1] < o.dval - 16:
                                need[sk] = (dsems[o.dsem], o.dval - 16)
                        for sk, (s, v) in need.items():
                            if waited.get(sk, 0) >= v:
                                continue
                            eng.wait_ge(s, v)
                            waited[sk] = v
                        ins = o.fn(eng)
                        if o.is_dma:
                            ins.then_inc(dsems[o.dsem], 16)
                        elif o.signal:
                            ins.then_inc(sems[ename], 1)
                    if ename == "sp":
                        for key, v in dma_cnt.items():
                            eng.wait_ge(dsems[key], v)
                        for e in ("pe", "act", "dve", "pool"):
                            if cnt[e]:
                                eng.wait_ge(sems[e], cnt[e])
                return body

            block.tensor(run("pe"))
            block.scalar(run("act"))
            block.vector(run("dve"))
            block.gpsimd(run("pool"))
            block.sync(run("sp"))


def bc(ap, shape, axis):
    return ap.unsqueeze(axis).to_broadcast(shape)


class Builder:
    def __init__(self, dbg=None):
        self.dbg = dbg or {}
        self.nc = nc = bass.Bass("TRN2", target_bir_lowering=False)
        self.P = Prog(nc)
        self.st = contextlib.ExitStack()
        self.wrr = 0

    def sb(self, name, shape, dt=F32):
        return self.st.enter_context(self.nc.sbuf_tensor(name, shape, dt))

    def din(self, name, shape, dt=F32):
        return self.nc.dram_tensor(name, shape, dt, kind="ExternalInput").ap()

    def dout(self, name, shape, dt=F32):
        return self.nc.dram_tensor(name, shape, dt, kind="ExternalOutput").ap()

    def declare_io(self):
        L = DEPTH
        self.x_prompt = self.din("x_prompt", [SEQ, D])
        self.x_sample = self.din("x_sample", [NS, D])
        self.cache_k = self.din("cache_swa_k", [L, NS, 128, 128])
        self.cache_v = self.din("cache_swa_v", [L, NS, 128, 128])
        self.st_lru_conv = self.din("state_lru_conv", [L, NS, 3, 768])
        self.st_lru_h = self.din("state_lru_h", [L, NS, 768])
        self.st_ssd_conv = self.din("state_ssd_conv", [L, NS, 3, 1280])
        self.st_ssd_h = self.din("state_ssd_h", [L, NS, 12, 64, 128])
        self.w_in = self.din("w_in", [L, D, DIN])
        self.w_out = self.din("w_out", [L, 2048, D])
        self.att_sinks = self.din("att_sinks", [L, 8])
        self.lru_conv_w = self.din("lru_conv_w", [L, 4, 768])
        self.lru_conv_b = self.din("lru_conv_b", [L, 768])
        self.lru_wa = self.din("lru_wa", [L, 8, 96, 96])
        self.lru_ba = self.din("lru_ba", [L, 768])
        self.lru_wx = self.din("lru_wx", [L, 8, 96, 96])
        self.lru_bx = self.din("lru_bx", [L, 768])
        self.lru_lambda = self.din("lru_lambda", [L, 768])
        self.ssd_conv_w = self.din("ssd_conv_w", [L, 4, 1280])
        self.ssd_conv_b = self.din("ssd_conv_b", [L, 1280])
        self.ssd_dt_bias = self.din("ssd_dt_bias", [L, 12])
        self.ssd_a_log = self.din("ssd_a_log", [L, 12])
        self.ssd_d = self.din("ssd_d", [L, 12])
        self.ssd_norm_g = self.din("ssd_norm_g", [L, 768])
        self.ln_g = self.din("ln_g", [L, D])
        self.ln_b = self.din("ln_b", [L, D])
        self.consts = self.din("consts", [128, K_END])
        self.y_prompt = self.dout("y_prompt", [SEQ, D])
        self.y_sample = self.dout("y_sample", [NS, D])
        self.p_swa_k = self.dout("p_swa_k", [L, 128, 128])
        self.p_swa_v = self.dout("p_swa_v", [L, 128, 128])
        self.p_lru_conv = self.dout("p_lru_conv", [L, 3, 768])
        self.p_lru_h = self.dout("p_lru_h", [L, 768])
        self.p_ssd_conv = self.dout("p_ssd_conv", [L, 3, 1280])
        self.p_ssd_h = self.dout("p_ssd_h", [L, 768, 128])
        self.s_swa_k = self.dout("s_swa_k", [L, NS, 128, 128])
        self.s_swa_v = self.dout("s_swa_v", [L, NS, 128, 128])
        self.s_lru_conv = self.dout("s_lru_conv", [L, NS, 3, 768])
        self.s_lru_h = self.dout("s_lru_h", [L, NS, 768])
        self.s_ssd_conv = self.dout("s_ssd_conv", [L, NS, 3, 1280])
        self.s_ssd_h = self.dout("s_ssd_h", [L, NS, 768, 128])
        self.w_in_bf = self.nc.dram_tensor("w_in_bf", [L, D, DIN], BF16).ap()
        self.w_out_bf = self.nc.dram_tensor("w_out_bf", [L, 2048, D], BF16).ap()
        self.dbg_out = {k: self.dout("dbg_" + k, list(v)) for k, v in self.dbg.items()}

    def tap(self, name, ap, reads):
        if name in self.dbg_out:
            self.P.dma("pool", self.dbg_out[name], ap, reads=reads)

    def alloc(self):
        sb = self.sb
        self.cst = sb("cst", [128, K_END])
        self.ident_bf = sb("ident_bf", [128, 128], BF16)
        self.maskbf = sb("maskbf", [128, 2, 256], BF16)
        self.cosT = sb("cosT", [128, 16, 8]); self.sinT = sb("sinT", [128, 16, 8]); self.nsinT = sb("nsinT", [128, 16, 8])
        self.ropeS = sb("ropeS", [128, 3, 8])
        self.xT32 = sb("xT32", [128, 8, H])
        self.xTb = sb("xTb", [128, 8, H], BF16)
        self.mixT = sb("mixT", [128, 18, H], BF16)
        self.wbuf = [sb("wbuf%d" % i, [128, 8, 128], BF16) for i in range(6)]
        self.wpool = {"att": [0], "lru": [1, 2], "ssd": [3, 4], "out": [5, 0, 1], "all": [0, 1, 2, 3, 4, 5]}
        self.wcnt = {k: 0 for k in self.wpool}
        self.wqkv = sb("wqkv", [128, 8, 768], BF16)
        self.kT = sb("kT", [128, 5 * 128], BF16)
        self.vtm = sb("vtm", [128, 5, 128], BF16)
        self.ck = sb("ck", [128, DEPTH, 128], BF16); self.cv = sb("cv", [128, DEPTH, 128], BF16)
        self.hist_l = sb("hist_l", [128, DEPTH, 8, 3]); self.hist_s = sb("hist_s", [128, DEPTH, 10, 3])
        self.hcar = sb("hcar", [128, DEPTH, 8])
        self.hT32 = sb("hT32", [128, DEPTH, 768]); self.hTb = sb("hTb", [128, 768], BF16)
        self.lp = sb("lp", [128, DEPTH, 8, 8])
        self.la = sb("la", [128, DEPTH, 8, 4])
        self.spp = sb("spp", [128, DEPTH, 10, 5])
        self.gS = sb("gS", [128, DEPTH, 6])
        self.lnp = sb("lnp", [128, DEPTH, 8, 2])
        self.sinkb = sb("sinkb", [128, DEPTH, 8]); self.nsinkb = sb("nsinkb", [128, DEPTH, 8])
        self.Ab = sb("Ab", [128, DEPTH, 12]); self.Db = sb("Db", [128, DEPTH, 12])
        self.dtb = sb("dtb", [128, DEPTH])
        self.wab = sb("wab", [128, 8, 96], BF16); self.wxb = sb("wxb", [128, 8, 96], BF16)
        self.pstage = self.xT32[:].rearrange("p a b -> p (a b)")[0:8, 0:1280]
        self.adc = sb("adc", [128, DEPTH, 2]); self.sinkc = sb("sinkc", [128, DEPTH, 2])
        self.qkv32 = [sb("qkv32_%d" % i, [128, 768]) for i in range(2)]
        self.qkbf = sb("qkbf", [128, 640], BF16)
        self.ropet = sb("ropet", [128, 2, 10, 16])
        self.qT = sb("qT", [128, 4, H], BF16)
        self.G = [sb("G%d" % i, [128, H]) for i in range(2)]
        self.Pm = [sb("Pm%d" % i, [128, 512], BF16) for i in range(2)]
        self.PT = [sb("PT%d" % i, [128, 512], BF16) for i in range(2)]
        self.ast = [sb("ast%d" % i, [128, 8, 2]) for i in range(2)]
        self.xpad2 = sb("xpad2", [128, H + 3])
        self.xpad = sb("xpad", [128, H + 3]); self.xl = sb("xl", [128, H]); self.xlb = sb("xlb", [128, H], BF16)
        self.rr = sb("rr", [128, H]); self.ig = sb("ig", [128, H]); self.EE = sb("EE", [128, H])
        self.zs = sb("zs", [128, 6, H]); self.xsT = sb("xsT", [128, 6, H], BF16)
        self.BT = sb("BT", [128, 2, H], BF16); self.CT = sb("CT", [128, 2, H], BF16)
        self.dtT = sb("dtT", [128, H]); self.dtt = sb("dtt", [128, H])
        self.xstm = sb("xstm", [128, 768], BF16); self.Btm = sb("Btm", [128, 256], BF16)
        self.tsm = sb("tsm", [128, 8, 12])
        self.aU = sb("aU", [128, 12, 128]); self.Dm = sb("Dm", [128, 12, 128])
        self.Ebc = sb("Ebc", [128, 12, 128], BF16); self.GT = sb("GT", [128, 12, 128], BF16)
        self.CE = sb("CE", [128, 12, 128], BF16); self.CBm = sb("CBm", [128, 2, 128])
        self.xdt = sb("xdt", [128, 768], BF16); self.xdd = sb("xdd", [128, 768], BF16); self.xsD = sb("xsD", [128, 768], BF16)
        self.yg = sb("yg", [128, 6, 128]); self.sq = sb("sq", [128, 6, 128]); self.rstd = sb("rstd", [128, 128])
        self.lnq = [sb("lnq%d" % i, [128, H]) for i in range(2)]
        self.mean = sb("mean", [128, H]); self.lrs = sb("lrs", [128, H])
        self.iost = sb("iost", [128, D])
        self.ps = self.st.enter_context(self.nc.psum_tensor("ps", [128, 4096], F32))

    def bank(self, b, n=1):
        return self.ps[:, b * 512:(b + n) * 512]

    def bankbf(self, b):
        return self.ps[:, b * 512:(b + 1) * 512].bitcast(BF16)

    def pT(self, b):
        return self.P.t("ps", b)

    @staticmethod
    def _fs(ap):
        n = 1
        for d in ap.shape[1:]:
            n *= d
        return n

    def _c(self, eng, out, in_=None):
        F = self._fs(out)
        ps = (in_ is not None and str(in_.space).lower().find("psum") >= 0)
        if eng == "act":
            return (224 + F) / 1200.0
        if eng == "pool":
            return (100 + 2 * F) / 1200.0
        return 1.2 * ((120 if ps else 60) + F) / 960.0

    TBL = {AF.Exp: "E", AF.Tanh: "A", AF.Ln: "L", AF.Sqrt: "S", AF.Sigmoid: "G", AF.Silu: "U", AF.Sin: "N"}

    def act(self, out, in_, func, r, w, **kw):
        o = self.P.op("act", lambda e: e.activation(out=out, in_=in_, func=func, **kw), r, w, cost=self._c("act", out))
        o.tbl = self.TBL.get(func)
        return o

    def acopy(self, out, in_, r, w):
        return self.P.op("act", lambda e: e.copy(out=out, in_=in_), r, w, cost=self._c("act", out))

    def vcopy(self, out, in_, r, w, eng="dve"):
        return self.P.op(eng, lambda e: e.tensor_copy(out=out, in_=in_), r, w, cost=self._c(eng, out, in_))

    def tt(self, out, a, b, op, r, w, eng="dve"):
        return self.P.op(eng, lambda e: e.tensor_tensor(out=out, in0=a, in1=b, op=op), r, w, cost=self._c(eng, out, a))

    def ts(self, out, a, s1, s2, op0, op1, r, w, eng="dve"):
        c = self._c(eng, out, a)
        if op1 is None:
            return self.P.op(eng, lambda e: e.tensor_scalar(out=out, in0=a, scalar1=s1, scalar2=None, op0=op0), r, w, cost=c)
        return self.P.op(eng, lambda e: e.tensor_scalar(out=out, in0=a, scalar1=s1, scalar2=s2, op0=op0, op1=op1), r, w, cost=c)

    def stt(self, out, a, s, b, op0, op1, r, w, eng="dve"):
        return self.P.op(eng, lambda e: e.scalar_tensor_tensor(out=out, in0=a, scalar=s, in1=b, op0=op0, op1=op1), r, w,
                         cost=self._c(eng, out, a))

    def mm(self, out, lhsT, rhs, start, stop, r, w):
        N = self._fs(out)
        k = 4.0 if rhs.dtype == F32 else 1.0
        c = 1.4 * (max(N, 64) * k / 2400.0 + 0.035)
        return self.P.op("pe", lambda e: e.matmul(out, lhsT=lhsT, rhs=rhs, start=start, stop=stop), r, w,
                         cost=(-c if k > 1 else c))

    def tr(self, out, in_, ident, r, w):
        N = self._fs(in_)
        k = 4.0 if in_.dtype == F32 else 1.0
        c = 1.4 * (max(N, 64) * k / 2400.0 + 0.06)
        return self.P.op("pe", lambda e: e.transpose(out, in_, ident), r, w, cost=(-c if k > 1 else c))

    def next_wbuf(self, pool="all"):
        lst = self.wpool[pool]
        k = lst[self.wcnt[pool] % len(lst)]
        self.wcnt[pool] += 1
        return k

    def win_T(self, l):
        return [self.P.t("winbf", l, i) for i in range(8)]

    def wout_T(self, l):
        return [self.P.t("woutbf", l, i) for i in range(16)]

    def load_win(self, l, pieces, pool="all"):
        k = self.next_wbuf(pool)
        src = self.w_in_bf[l].rearrange("(kc p) c -> p kc c", p=128)
        for (c0, n, d0) in pieces:
            self.P.dma("sp", self.wbuf[k][:, :, d0:d0 + n], src[:, :, c0:c0 + n],
                       reads=self.win_T(l), writes=[self.P.t("wbuf", k)])
        return k

    def inproj(self, l, k, M, psout, psT, ncols=H, x=None, xT=None):
        x = self.xTb if x is None else x
        xT = [self.P.t("xTb", kc) for kc in range(8)] if xT is None else xT
        for kc in range(8):
            self.mm(psout[0:M, 0:ncols], self.wbuf[k][:, kc, 0:M], x[:, kc, 0:ncols], kc == 0, kc == 7,
                    [self.P.t("wbuf", k)] + xT, [psT])

    def prologue(self):
        P, t = self.P, self.P.t
        cst = self.cst
        P.dma("sp", cst[:], self.consts, writes=[t("cst")])
        for l in range(DEPTH):
            for i in range(8):
                P.dma("pool", self.w_in_bf[l, i * 128:(i + 1) * 128, :], self.w_in[l, i * 128:(i + 1) * 128, :],
                      writes=[t("winbf", l, i)])
            for i in range(16):
                P.dma("pool", self.w_out_bf[l, i * 128:(i + 1) * 128, :], self.w_out[l, i * 128:(i + 1) * 128, :],
                      writes=[t("woutbf", l, i)])
        self.vcopy(self.ident_bf[:], cst[:, K_ID:K_ID + 128], [t("cst")], [t("ident_bf")])
        self.vcopy(self.maskbf[:, 0, :], cst[:, K_MG:K_MG + 256], [t("cst")], [t("maskbf")])
        self.vcopy(self.maskbf[:, 1, :], cst[:, K_MF:K_MF + 256], [t("cst")], [t("maskbf")])
        for buf, nm in ((self.ck, "ck"), (self.cv, "cv")):
            P.op("dve", lambda e, buf=buf: e.memset(buf[:], 0.0), [], [t(nm, l) for l in range(DEPTH)])
        P.op("dve", lambda e: e.memset(self.hist_l[:], 0.0), [], [t("hist_l", l) for l in range(DEPTH)])
        P.op("dve", lambda e: e.memset(self.hist_s[:], 0.0), [], [t("hist_s", l) for l in range(DEPTH)])
        P.op("dve", lambda e: e.memset(self.hcar[:], 0.0), [], [t("hcar", l) for l in range(DEPTH)])
        P.op("dve", lambda e: e.memset(self.hT32[:], 0.0), [], [t("hT32", l, g) for l in range(DEPTH) for g in range(2)])
        self.rope_tables()
        for l in range(DEPTH):
            self.layer_params(l)

    def sincos(self, ang, n, outs, rd):
        P, t = self.P, self.P.t
        tmp = self.iost
        c1 = float(np.float32(2 * np.pi)); c2 = float(2 * np.pi - c1)
        for j, (shift, out) in enumerate(((0.5 * np.pi, outs[0]), (0.0, outs[1]))):
            a = tmp[:, 0:n]; kf = tmp[:, n:2 * n]; ki = tmp[:, 2 * n:3 * n].bitcast(I32); m = tmp[:, 3 * n:4 * n]
            W = [t("iost")]
            self.ts(a, ang, float(shift), None, ALU.add, None, rd + W, W)
            self.ts(kf, a, float(1.0 / (2 * np.pi)), None, ALU.mult, None, W, W)
            self.vcopy(ki, kf, W, W)
            self.vcopy(kf, ki, W, W)
            self.stt(a, kf, -c1, a, ALU.mult, ALU.add, W, W)
            self.stt(a, kf, -c2, a, ALU.mult, ALU.add, W, W)
            self.ts(m, a, float(np.pi), float(-2 * np.pi), ALU.is_gt, ALU.mult, W, W)
            self.tt(a, a, m, ALU.add, W, W)
            self.ts(m, a, float(-np.pi), float(2 * np.pi), ALU.is_lt, ALU.mult, W, W)
            self.tt(a, a, m, ALU.add, W, W)
            self.act(out, a, AF.Sin, W, [t("rope")])
        self.ts(outs[2], outs[1], -1.0, None, ALU.mult, None, [t("rope")], [t("rope")])

    def rope_tables(self):
        t = self.P.t
        cst = self.cst
        ang = self.iost[:, 512:640]
        self.tt(ang.rearrange("p (a b) -> p a b", b=8), bc(cst[:, K_POS:K_POS + 16], [128, 16, 8], 2),
                bc(cst[:, K_INV:K_INV + 8], [128, 16, 8], 1), ALU.mult, [t("cst")], [t("iost")])
        f = lambda x: x[:].rearrange("p a b -> p (a b)")
        self.sincos(ang, 128, (f(self.cosT), f(self.sinT), f(self.nsinT)), [t("iost")])
        angs = self.iost[:, 640:648]
        self.ts(angs, cst[:, K_INV:K_INV + 8], PAST, None, ALU.mult, None, [t("cst")], [t("iost")])
        self.sincos(angs, 8, (self.ropeS[:, 0, :], self.ropeS[:, 1, :], self.ropeS[:, 2, :]), [t("iost")])

    def layer_params(self, l):
        P, t = self.P, self.P.t
        cst = self.cst
        idf = cst[:, K_ID:K_ID + 128]
        stg = self.pstage
        S = [t("xT32", dc) for dc in range(8)]
        W = [t("par", l)]
        P.dma("sp", stg[0:4, 0:768], self.lru_conv_w[l], writes=S)
        for i, src in enumerate((self.lru_conv_b, self.lru_ba, self.lru_bx, self.lru_lambda)):
            P.dma("sp", stg[4 + i:5 + i, 0:768], src[l:l + 1, :], writes=S)
        pb = self.bank(0)
        for n in range(8):
            self.tr(pb[0:96, n * 8:(n + 1) * 8], stg[0:8, n * 96:(n + 1) * 96], idf[0:8, 0:8], S + [t("cst")], [self.pT(0)])
        self.vcopy(self.lp[0:96, l, :, :], pb[0:96, 0:64].rearrange("p (a b) -> p a b", b=8), [self.pT(0)], W)
        la = self.la
        self.act(la[0:96, l, :, 0], self.lp[0:96, l, :, 7], AF.Exp, W, W, scale=-1.0)
        self.act(la[0:96, l, :, 0], la[0:96, l, :, 0], AF.Ln, W, W, bias=1.0)
        self.ts(la[0:96, l, :, 1], la[0:96, l, :, 0], -8.0, None, ALU.mult, None, W, W)
        self.ts(la[0:96, l, :, 0], la[0:96, l, :, 0], -4.0, None, ALU.mult, None, W, W)
        self.ts(la[0:96, l, :, 2:4], self.lp[0:96, l, :, 5:7], 0.5, None, ALU.mult, None, W, W)
        P.dma("sp", stg[0:4, 0:1280], self.ssd_conv_w[l], writes=S)
        P.dma("sp", stg[4:5, 0:1280], self.ssd_conv_b[l:l + 1, :], writes=S)
        pb = self.bank(1)
        for c in range(10):
            self.tr(pb[:, c * 5:(c + 1) * 5], stg[0:5, c * 128:(c + 1) * 128], idf[0:5, 0:5], S + [t("cst")], [self.pT(1)])
        self.ts(self.spp[:, l, :, :], pb[:, 0:50].rearrange("p (a b) -> p a b", b=5), 0.5, None, ALU.mult, None, [self.pT(1)], W)
        P.dma("sp", stg[0:1, 0:768], self.ssd_norm_g[l:l + 1, :], writes=S)
        P.dma("sp", stg[1:2, 0:1024], self.ln_g[l:l + 1, :], writes=S)
        P.dma("sp", stg[2:3, 0:1024], self.ln_b[l:l + 1, :], writes=S)
        pb = self.bank(2)
        for c in range(6):
            self.tr(pb[:, c:c + 1], stg[0:1, c * 128:(c + 1) * 128], idf[0:1, 0:1], S + [t("cst")], [self.pT(2)])
        self.ts(self.gS[:, l, :], pb[:, 0:6], float(math.sqrt(768.0)), None, ALU.mult, None, [self.pT(2)], W)
        P.dma("sp", stg[0:1, 0:1024], self.ln_g[l:l + 1, :], writes=S)
        P.dma("sp", stg[1:2, 0:1024], self.ln_b[l:l + 1, :], writes=S)
        pb = self.bank(3)
        for c in range(8):
            self.tr(pb[:, 2 * c:2 * c + 2], stg[0:2, c * 128:(c + 1) * 128], idf[0:2, 0:2], S + [t("cst")], [self.pT(3)])
        self.vcopy(self.lnp[:, l, :, :], pb[:, 0:16].rearrange("p (a b) -> p a b", b=2), [self.pT(3)], W)
        P.dma("sp", self.sinkb[:, l, :], self.att_sinks[l:l + 1, :].partition_broadcast(128), writes=W)
        self.ts(self.nsinkb[:, l, :], self.sinkb[:, l, :], -1.0, None, ALU.mult, None, W, W)
        P.dma("sp", self.Ab[:, l, :], self.ssd_a_log[l:l + 1, :].partition_broadcast(128), writes=W)
        self.act(self.Ab[:, l, :], self.Ab[:, l, :], AF.Exp, W, W)
        self.ts(self.Ab[:, l, :], self.Ab[:, l, :], -1.0, None, ALU.mult, None, W, W)
        P.dma("sp", self.Db[:, l, :], self.ssd_d[l:l + 1, :].partition_broadcast(128), writes=W)
        P.dma("sp", self.dtb[0:12, l:l + 1], self.ssd_dt_bias[l].rearrange("(a b) -> a b", b=1), writes=W)
        P.dma("sp", self.adc[0:12, l, 0:1], self.ssd_a_log[l].rearrange("(a b) -> a b", b=1), writes=W)
        self.act(self.adc[0:12, l, 0:1], self.adc[0:12, l, 0:1], AF.Exp, W, W)
        self.ts(self.adc[0:12, l, 0:1], self.adc[0:12, l, 0:1], -1.0, None, ALU.mult, None, W, W)
        P.dma("sp", self.adc[0:12, l, 1:2], self.ssd_d[l].rearrange("(a b) -> a b", b=1), writes=W)
        P.dma("sp", self.sinkc[0:8, l, 0:1], self.att_sinks[l].rearrange("(a b) -> a b", b=1), writes=W)
        self.ts(self.sinkc[0:8, l, 1:2], self.sinkc[0:8, l, 0:1], -1.0, None, ALU.mult, None, W, W)

    def job(self, l, p):
        ph = getattr(self, "phases", "xjqalso")
        if "x" in ph: self.load_x(l, p)
        if "j" in ph: self.job_prologue(l, p)
        if "q" in ph: self.phase_qkv(l, p)
        if "a" in ph: self.phase_att(l, p)
        if "s" in ph: self.phase_ssd(l, p)
        if "l" in ph: self.phase_lru(l, p)
        if "s" in ph: self.phase_ssd_tiles(l, p)
        if l == 0 and p == 0:
            self.tap("mixT", self.mixT[:], [self.P.t("mixT", ec, tl) for ec in range(18) for tl in range(NT)])
        if "o" in ph: self.phase_out(l, p)

    def load_x(self, l, p):
        P, t = self.P, self.P.t
        if l != 0:
            return
        idf = self.cst[:, K_ID:K_ID + 128]
        for tl in range(NT):
            gt = p * NT + tl
            P.dma("sp", self.iost[:], self.x_prompt[gt * 128:(gt + 1) * 128, :], writes=[t("iost")])
            for hb in range(2):
                pb = self.bank(hb)
                for j in range(4):
                    dc = hb * 4 + j
                    self.tr(pb[:, j * 128:(j + 1) * 128], self.iost[:, dc * 128:(dc + 1) * 128], idf,
                            [t("iost"), t("cst")], [self.pT(hb)])
                dst = self.xT32[:, hb * 4:hb * 4 + 4, tl * 128:(tl + 1) * 128]
                self.vcopy(dst, pb.rearrange("p (a b) -> p a b", b=128), [self.pT(hb)],
                           [t("xT32", dc) for dc in range(hb * 4, hb * 4 + 4)])
        for dc in range(8):
            self.acopy(self.xTb[:, dc, :], self.xT32[:, dc, :], [t("xT32", dc)], [t("xTb", dc)])
        if p == 0:
            self.tap("xT_in", self.xT32[:], [t("xT32", dc) for dc in range(8)])

    def job_prologue(self, l, p):
        P, t = self.P, self.P.t
        self.vcopy(self.kT[:, 0:128], self.ck[:, l, :], [t("ck", l)], [t("kT", 0)], eng="pool")
        self.vcopy(self.vtm[:, 0, :], self.cv[:, l, :], [t("cv", l)], [t("v", 0)], eng="pool")
        self.vcopy(self.hTb[:], self.hT32[:, l, :], [t("hT32", l, 0), t("hT32", l, 1)], [t("hTb", 0), t("hTb", 1)], eng="pool")
        P.dma("pool", self.wab[0:96, :, :], self.lru_wa[l].rearrange("n c d -> c n d"), writes=[t("wab")])
        P.dma("pool", self.wxb[0:96, :, :], self.lru_wx[l].rearrange("n c d -> c n d"), writes=[t("wxb")])
        src = self.w_in_bf[l].rearrange("(kc p) c -> p kc c", p=128)
        P.dma("sp", self.wqkv[:], src[:, :, 0:768], reads=self.win_T(l), writes=[t("wqkv", j) for j in range(6)])

    def phase_qkv(self, l, p):
        P, t = self.P, self.P.t
        xT = [t("xTb", kc) for kc in range(8)]
        last = (p == NPASS - 1)
        for tl in range(NT):
            gt = p * NT + tl
            pa, pb_, pc = (0, 1, 4) if tl % 2 == 0 else (2, 3, 5)
            A, B = self.bank(pa), self.bank(pb_)
            for kc in range(8):
                lhs = self.xTb[:, kc, tl * 128:(tl + 1) * 128]
                WQ = [t("wqkv", j) for j in range(6)]
                self.mm(A, lhs, self.wqkv[:, kc, 0:512], kc == 0, kc == 7, xT + WQ, [self.pT(pa)])
                self.mm(B[:, 0:256], lhs, self.wqkv[:, kc, 512:768], kc == 0, kc == 7, xT + WQ, [self.pT(pb_)])
            q32 = self.qkv32[tl % 2]
            Q = [t("qkv32", tl % 2)]
            self.acopy(q32[:, 0:512], A, [self.pT(pa)], Q)
            self.acopy(q32[:, 512:768], B[:, 0:256], [self.pT(pb_)], Q)
            hv = q32[:, 0:640].rearrange("p (h d) -> p h d", d=64)
            x1, x2 = hv[:, :, 0:8], hv[:, :, 8:16]
            cs = bc(self.cosT[:, gt, :], [128, 10, 8], 1)
            sn = bc(self.sinT[:, gt, :], [128, 10, 8], 1)
            ns = bc(self.nsinT[:, gt, :], [128, 10, 8], 1)
            R = [t("ropet")]
            ra, rb = self.ropet[:, 0, :, :], self.ropet[:, 1, :, :]
            self.tt(rb[:, :, 0:8], x2, ns, ALU.mult, Q + [t("rope")], R)
            self.tt(rb[:, :, 8:16], x1, sn, ALU.mult, Q + [t("rope")], R)
            self.tt(ra[:, :, 0:8], x1, cs, ALU.mult, Q + [t("rope")], R)
            self.tt(ra[:, :, 8:16], x2, cs, ALU.mult, Q + [t("rope")], R)
            self.tt(hv[:, :, 0:16], ra, rb, ALU.add, R, Q)
            if l == 0 and p == 0 and tl == 1:
                self.tap("qkv_t1", q32[:], Q)
            if last and tl == NT - 1:
                P.dma("pool", self.p_swa_k[l], q32[:, 512:640], reads=Q)
                P.dma("pool", self.p_swa_v[l], q32[:, 640:768], reads=Q)
            self.acopy(self.qkbf[:, 0:512].rearrange("p (c w d) -> p c w d", c=4, w=2),
                       q32[:, 0:512].rearrange("p (w c d) -> p c w d", w=2, c=4), Q, [t("qkbf")])
            self.acopy(self.qkbf[:, 512:640], q32[:, 512:640], Q, [t("qkbf")])
            self.vcopy(self.vtm[:, 1 + tl, :], q32[:, 640:768], Q, [t("v", 1 + tl)], eng="pool")
            C = self.bankbf(pc)
            for j in range(5):
                self.tr(C[:, j * 128:(j + 1) * 128], self.qkbf[:, j * 128:(j + 1) * 128], self.ident_bf[:],
                        [t("qkbf"), t("ident_bf")], [self.pT(pc)])
            self.vcopy(self.qT[:, :, tl * 128:(tl + 1) * 128], C[:, 0:512].rearrange("p (c q) -> p c q", q=128),
                       [self.pT(pc)], [t("qT", tl)])
            self.vcopy(self.kT[:, (1 + tl) * 128:(2 + tl) * 128], C[:, 512:640], [self.pT(pc)], [t("kT", 1 + tl)])
        self.vcopy(self.ck[:, l, :], self.kT[:, 512:640], [t("kT", 4)], [t("ck", l)], eng="pool")
        self.vcopy(self.cv[:, l, :], self.vtm[:, 4, :], [t("v", 4)], [t("cv", l)], eng="pool")

    def phase_att(self, l, p):
        P, t = self.P, self.P.t
        cnt = 0
        for c in range(4):
            k = self.load_win(l, [(C_GA + c * 64, 64, 0), (C_GA + (c + 4) * 64, 64, 64)], pool="att")
            gb = c % 2
            pg = self.bank(3)
            self.inproj(l, k, 128, pg, self.pT(3))
            G = self.G[gb]
            self.act(G[:], pg, AF.Tanh, [self.pT(3)], [t("G", gb)], scale=0.5)
            self.stt(G[:], G[:], 1.0, pg, ALU.add, ALU.mult, [self.pT(3), t("G", gb)], [t("G", gb)])
            for tl in range(NT):
                gt = p * NT + tl
                i = cnt % 2
                cnt += 1
                bS, bP, bO = 1, 2, 3
                S = self.bank(bS)
                mk = self.maskbf[:, 1 if gt == 0 else 0, :]
                for h in range(2):
                    rows = slice(h * 64, (h + 1) * 64)
                    self.mm(S[:, h * 256:(h + 1) * 256], self.qT[rows, c, tl * 128:(tl + 1) * 128],
                            self.kT[rows, tl * 128:tl * 128 + 256], True, False,
                            [t("qT", tl), t("kT", tl), t("kT", tl + 1)], [self.pT(bS)])
                    self.mm(S[:, h * 256:(h + 1) * 256], self.ident_bf[:], mk, False, True,
                            [t("ident_bf"), t("maskbf")], [self.pT(bS)])
                st = self.ast[i]
                A = [t("ast", i)]
                mx, negm, ssum, es, den, rden = (st[:, j, :] for j in range(6))
                P.op("dve", lambda e, mx=mx, S=S: e.reduce_max(out=mx, in_=S.rearrange("p (h k) -> p h k", k=256), axis=AX.X),
                     [self.pT(bS)], A)
                hsel = self.nsinkb[:, l, c:c + 5:4]
                self.stt(negm, mx, -SCALE, hsel, ALU.mult, ALU.min, A + [t("par", l)], A)
                Pm = self.Pm[i]
                for h in range(2):
                    self.act(Pm[:, h * 256:(h + 1) * 256], S[:, h * 256:(h + 1) * 256], AF.Exp,
                             [self.pT(bS)] + A, [t("Pm", i)] + A, scale=SCALE, bias=negm[:, h:h + 1], accum_out=ssum[:, h:h + 1])
                self.tt(es, negm, self.sinkb[:, l, c:c + 5:4], ALU.add, A + [t("par", l)], A)
                self.act(es, es, AF.Exp, A, A)
                self.tt(den, ssum, es, ALU.add, A, A)
                P.op("dve", lambda e, rden=rden, den=den: e.reciprocal(out=rden, in_=den), A, A)
                pv = Pm[:].rearrange("p (h k) -> p h k", k=256)
                self.tt(pv, pv, bc(rden, [128, 2, 256], 2), ALU.mult, [t("Pm", i)] + A, [t("Pm", i)])
                PTp = self.bankbf(bP)
                for h in range(2):
                    for kb in range(2):
                        j = h * 2 + kb
                        self.tr(PTp[:, j * 128:(j + 1) * 128], Pm[:, h * 256 + kb * 128:h * 256 + (kb + 1) * 128],
                                self.ident_bf[:], [t("Pm", i), t("ident_bf")], [self.pT(bP)])
                PT = self.PT[i]
                self.acopy(PT[:], PTp[:, 0:512], [self.pT(bP)], [t("PT", i)])
                O = self.bank(bO)
                for h in range(2):
                    for kb in range(2):
                        j = h * 2 + kb
                        self.mm(O[h * 64:(h + 1) * 64, 0:128], self.vtm[:, tl + kb, h * 64:(h + 1) * 64],
                                PT[:, j * 128:(j + 1) * 128], kb == 0, kb == 1,
                                [t("v", tl + kb), t("PT", i)], [self.pT(bO)])
                self.stt(self.mixT[:, c, tl * 128:(tl + 1) * 128], O[:, 0:128], 0.5, G[:, tl * 128:(tl + 1) * 128], ALU.mult, ALU.mult,
                         [self.pT(bO), t("G", gb)], [t("mixT", c, tl)])

    def conv4(self, out, xpad, par, n, M, rd, wr):
        self.act(out[0:M, :], xpad[0:M, 0:H], AF.Identity, rd, wr, scale=par[0:M, 0:1], bias=par[0:M, 4:5])
        for k in range(1, 4):
            self.stt(out[0:M, :], xpad[0:M, k:k + H], par[0:M, k:k + 1], out[0:M, :], ALU.mult, ALU.add, rd + wr, wr)

    def phase_lru(self, l, p):
        P, t = self.P, self.P.t
        last = (p == NPASS - 1)
        for n in range(8):
            kx = self.load_win(l, [(C_XL + n * 96, 96, 0)], pool="lru")
            kg = self.load_win(l, [(C_GL + n * 96, 96, 0)], pool="lru")
            px, pg, pr, pi = self.bank(5), self.bank(6), self.bank(5), self.bank(7)
            pxT, pgT, prT, piT = self.pT(5), self.pT(6), self.pT(5), self.pT(7)
            self.inproj(l, kx, 96, px, pxT)
            self.inproj(l, kg, 96, pg, pgT)
            par = self.lp[:, l, n, :]
            PR = [t("par", l)]
            xp, xl, xlb, rr, ig, EE = self.xpad, self.xl, self.xlb, self.rr, self.ig, self.EE
            self.vcopy(xp[0:96, 0:3], self.hist_l[0:96, l, n, :], [t("hist_l", l)], [t("xpad")])
            self.acopy(xp[0:96, 3:3 + H], px[0:96, :], [pxT], [t("xpad")])
            self.vcopy(self.hist_l[0:96, l, n, :], xp[0:96, H:H + 3], [t("xpad")], [t("hist_l", l)])
            self.conv4(xl, xp, par, n, 96, [t("xpad")] + PR, [t("xl")])
            self.acopy(xlb[0:96, :], xl[0:96, :], [t("xl")], [t("xlb")])
            self.mm(pr[0:96, :], self.wab[0:96, n, :], xlb[0:96, :], True, True, [t("wab"), t("xlb")], [prT])
            self.mm(pi[0:96, :], self.wxb[0:96, n, :], xlb[0:96, :], True, True, [t("wxb"), t("xlb")], [piT])
            lac = self.la[0:96, l, n, :]
            self.act(rr[0:96, :], pr[0:96, :], AF.Tanh, [prT] + PR, [t("rr")], scale=0.5, bias=lac[:, 2:3])
            self.act(ig[0:96, :], pi[0:96, :], AF.Tanh, [piT] + PR, [t("ig")], scale=0.5, bias=lac[:, 3:4])
            self.act(EE[0:96, :], rr[0:96, :], AF.Exp, [t("rr")] + PR, [t("EE")], scale=lac[:, 1:2], bias=lac[:, 1:2])
            self.act(rr[0:96, :], rr[0:96, :], AF.Exp, [t("rr")] + PR, [t("rr")], scale=lac[:, 0:1], bias=lac[:, 0:1])
            self.act(EE[0:96, :], EE[0:96, :], AF.Sqrt, [t("EE")], [t("EE")], scale=-0.25, bias=0.25)
            self.stt(ig[0:96, :], ig[0:96, :], 1.0, xl[0:96, :], ALU.add, ALU.mult, [t("ig"), t("xl")], [t("ig")])
            self.tt(ig[0:96, :], ig[0:96, :], EE[0:96, :], ALU.mult, [t("ig"), t("EE")], [t("ig")])
            P.op("dve", lambda e, n=n: e.tensor_tensor_scan(out=EE[0:96, :], data0=rr[0:96, :], data1=ig[0:96, :],
                                                            initial=self.hcar[0:96, l, n:n + 1], op0=ALU.mult, op1=ALU.add),
                 [t("rr"), t("ig"), t("hcar", l)], [t("EE")])
            self.vcopy(self.hcar[0:96, l, n:n + 1], EE[0:96, H - 1:H], [t("EE")], [t("hcar", l)])
            self.act(xl[0:96, :], pg[0:96, :], AF.Tanh, [pgT], [t("xl")], scale=0.5)
            self.stt(xl[0:96, :], xl[0:96, :], 1.0, pg[0:96, :], ALU.add, ALU.mult, [pgT, t("xl")], [t("xl")])
            self.stt(self.mixT[0:96, 4 + n, :], EE[0:96, :], 0.5, xl[0:96, :], ALU.mult, ALU.mult, [t("EE"), t("xl")],
                    [t("mixT", 4 + n, tl) for tl in range(NT)])
        if last:
            for k3 in range(3):
                P.dma("pool", self.p_lru_conv[l, k3].rearrange("(n p) -> p n", p=96), self.hist_l[0:96, l, :, k3],
                      reads=[t("hist_l", l)], allow_slow_non_contiguous=True)
            P.dma("pool", self.p_lru_h[l].rearrange("(n p) -> p n", p=96), self.hcar[0:96, l, :],
                  reads=[t("hcar", l)], allow_slow_non_contiguous=True)

    def phase_ssd(self, l, p):
        P, t = self.P, self.P.t
        last = (p == NPASS - 1)
        cst = self.cst
        idf = cst[:, K_ID:K_ID + 128]
        U = cst[:, K_U:K_U + 128]
        ones = cst[:, K_ONE:K_ONE + 128]
        PR = [t("par", l)]
        nb = 0
        for c in range(6):
            k = self.load_win(l, [(C_Z + c * 128, 128, 0)], pool="ssd")
            b = (0, 4)[c % 2]
            pb = self.bank(b); pbT = self.pT(b)
            self.inproj(l, k, 128, pb, pbT)
            self.act(self.zs[:, c, :], pb, AF.Tanh, [pbT], [t("zs", c)], scale=0.5)
            self.stt(self.zs[:, c, :], self.zs[:, c, :], 1.0, pb, ALU.add, ALU.mult, [pbT, t("zs", c)], [t("zs", c)])
        xp, acc, tnh = self.xpad2, self.lnq[0], self.lnq[1]
        for c in range(10):
            k = self.load_win(l, [(C_XBC + c * 128, 128, 0)], pool="ssd")
            b = (0, 4)[c % 2]
            pb = self.bank(b); pbT = self.pT(b)
            self.inproj(l, k, 128, pb, pbT)
            self.vcopy(xp[:, 0:3], self.hist_s[:, l, c, :], [t("hist_s", l)], [t("xpad2")])
            self.acopy(xp[:, 3:3 + H], pb, [pbT], [t("xpad2")])
            self.vcopy(self.hist_s[:, l, c, :], xp[:, H:H + 3], [t("xpad2")], [t("hist_s", l)])
            self.conv4(acc, xp, self.spp[:, l, c, :], c, 128, [t("xpad2")] + PR, [t("lnq", 0)])
            if c < 6:
                dst, dT = self.xsT[:, c, :], t("xsT", c)
            elif c < 8:
                dst, dT = self.BT[:, c - 6, :], t("BT", c - 6)
            else:
                dst, dT = self.CT[:, c - 8, :], t("CT", c - 8)
            self.act(tnh[:], acc[:], AF.Tanh, [t("lnq", 0)], [t("lnq", 1)])
            self.stt(dst, tnh[:], 1.0, acc[:], ALU.add, ALU.mult, [t("lnq", 1), t("lnq", 0)], [dT])
        k = self.load_win(l, [(C_DT, 12, 0)], pool="ssd")
        b = 4
        pb = self.bank(b); pbT = self.pT(b)
        self.inproj(l, k, 12, pb, pbT)
        u, v = self.dtT[0:12, :], self.dtt[0:12, :]
        self.act(u, pb[0:12, :], AF.Identity, [pbT] + PR, [t("dtT")], bias=self.dtb[0:12, l:l + 1])
        self.act(v, u, AF.Abs, [t("dtT")], [t("dtt")])
        self.act(v, v, AF.Exp, [t("dtt")], [t("dtt")], scale=-1.0)
        self.act(v, v, AF.Ln, [t("dtt")], [t("dtt")], bias=1.0)
        self.stt(u, u, 0.0, v, ALU.max, ALU.add, [t("dtT"), t("dtt")], [t("dtT")])
        if last:
            for k3 in range(3):
                P.dma("pool", self.p_ssd_conv[l, k3].rearrange("(c p) -> p c", p=128), self.hist_s[:, l, :, k3],
                      reads=[t("hist_s", l)], allow_slow_non_contiguous=True)

    def phase_ssd_tiles(self, l, p):
        P, t = self.P, self.P.t
        last = (p == NPASS - 1)
        cst = self.cst
        idf = cst[:, K_ID:K_ID + 128]
        U = cst[:, K_U:K_U + 128]
        ones = cst[:, K_ONE:K_ONE + 128]
        PR = [t("par", l)]
        tsm = self.tsm
        dt_tm, a_tm, acs_tm, cd, tmp, dec, dtdec = (tsm[:, j, :] for j in range(7))
        TS = [t("tsm")]
        b1 = self.bank(1)
        for tl in range(NT):
            sl = slice(tl * 128, (tl + 1) * 128)
            self.tr(b1[:, 0:12], self.dtT[0:12, sl], idf[0:12, 0:12], [t("dtT"), t("cst")], [self.pT(1)])
            self.vcopy(dt_tm, b1[:, 0:12], [self.pT(1)], TS)
            self.tt(a_tm, dt_tm, self.Ab[:, l, :], ALU.mult, TS + PR, TS)
            self.mm(b1[:, 16:28], U, a_tm, True, True, [t("cst")] + TS, [self.pT(1)])
            self.vcopy(acs_tm, b1[:, 16:28], [self.pT(1)], TS)
            for g in range(2):
                hs = slice(6 * g, 6 * g + 6)
                cs_ = slice(384 * g, 384 * (g + 1))
                TSg = [t("tsm", g)]
                p0 = self.bankbf(0)
                for j in range(3):
                    c = 3 * g + j
                    self.tr(p0[:, j * 128:(j + 1) * 128], self.xsT[:, c, sl], self.ident_bf[:], [t("xsT", c), t("ident_bf")], [self.pT(0)])
                self.tr(p0[:, 384:512], self.BT[:, g, sl], self.ident_bf[:], [t("BT", g), t("ident_bf")], [self.pT(0)])
                self.acopy(self.xstm[:, cs_], p0[:, 0:384], [self.pT(0)], [t("xstm", g)])
                self.acopy(self.Btm[:, g * 128:(g + 1) * 128], p0[:, 384:512], [self.pT(0)], [t("Btm", g)])
                cb = b1[:, 128 * (g + 1):128 * (g + 2)]
                self.mm(cb, self.BT[:, g, sl], self.CT[:, g, sl], True, True, [t("BT", g), t("CT", g)], [self.pT(1)])
                self.tt(self.CBm[:, g, :], cb, U, ALU.mult, [self.pT(1), t("cst")], [t("CBm", g)])
                aUg = self.aU[:, hs, :]
                self.tt(aUg, bc(U, [128, 6, 128], 1), bc(a_tm[:, hs], [128, 6, 128], 2), ALU.mult, TS + [t("cst")], [t("aU", g)])
                A0 = (2, 5)[g]
                YB = (4, 7)[g]
                pacs = self.ps[:, A0 * 512:A0 * 512 + 768]
                pacsT = [self.pT(A0), self.pT(A0 + 1)]
                aUf = aUg.rearrange("p e q -> p (e q)")
                self.mm(pacs[:, 0:512], ones, aUf[:, 0:512], True, True, [t("cst"), t("aU", g)], [self.pT(A0)])
                self.mm(pacs[:, 512:768], ones, aUf[:, 512:768], True, True, [t("cst"), t("aU", g)], [self.pT(A0 + 1)])
                pav = pacs.rearrange("p (e q) -> p e q", q=128)
                Dmg = self.Dm[:, hs, :]
                for j in range(6):
                    e = 6 * g + j
                    self.ts(Dmg[:, j, :], pav[:, j, :], acs_tm[:, e:e + 1], 0.0, ALU.subtract, ALU.min,
                            [self.pT(A0 + j // 4)] + TS, [t("Dm", g)])
                self.act(Dmg, Dmg, AF.Exp, [t("Dm", g)], [t("Dm", g)])
                self.tt(self.GT[:, hs, :], Dmg, bc(self.CBm[:, g, :], [128, 6, 128], 1), ALU.mult, [t("Dm", g), t("CBm", g)], [t("GT", g)])
                self.act(self.Ebc[:, hs, :], pav, AF.Exp, pacsT, [t("Ebc", g)])
                cdg, tmpg, decg, dtdecg = cd[:, hs], tmp[:, hs], dec[:, hs], dtdec[:, hs]
                self.act(cdg, pav[:, :, 127], AF.Exp, pacsT, TSg)
                self.tt(tmpg, pav[:, :, 127], acs_tm[:, hs], ALU.subtract, pacsT + TS, TSg)
                self.act(decg, tmpg, AF.Exp, TSg, TSg)
                self.tt(dtdecg, dt_tm[:, hs], decg, ALU.mult, TS + TSg, TSg)
                self.tt(self.CE[:, hs, :], self.Ebc[:, hs, :], bc(self.CT[:, g, sl], [128, 6, 128], 1), ALU.mult,
                        [t("Ebc", g), t("CT", g)], [t("CE", g)])
                xs3 = self.xstm[:, cs_].rearrange("p (e d) -> p e d", d=64)
                g3 = lambda x: x[:, cs_].rearrange("p (e d) -> p e d", d=64)
                self.tt(g3(self.xdt), xs3, bc(dt_tm[:, hs], [128, 6, 64], 2), ALU.mult, [t("xstm", g)] + TS, [t("xdt", g)])
                self.tt(g3(self.xdd), xs3, bc(dtdecg, [128, 6, 64], 2), ALU.mult, [t("xstm", g)] + TSg, [t("xdd", g)])
                self.tt(g3(self.xsD), xs3, bc(self.Db[:, l, hs], [128, 6, 64], 2), ALU.mult, [t("xstm", g)] + PR, [t("xsD", g)])
                py = self.bank(YB)
                for j in range(6):
                    e = 6 * g + j
                    o = py[(e % 2) * 64:(e % 2) * 64 + 64, (j // 2) * 128:(j // 2) * 128 + 128]
                    es_ = slice(e * 64, (e + 1) * 64)
                    self.mm(o, self.xdt[:, es_], self.GT[:, e, :], True, False, [t("xdt", g), t("GT", g)], [self.pT(YB)])
                    self.mm(o, self.hTb[:, es_], self.CE[:, e, :], False, False, [t("hTb", g), t("CE", g)], [self.pT(YB)])
                    self.mm(o, self.xsD[:, es_], self.ident_bf[:], False, True, [t("xsD", g), t("ident_bf")], [self.pT(YB)])
                pst = self.bank(A0)[:, 0:384]
                self.mm(pst, self.Btm[:, g * 128:(g + 1) * 128], self.xdd[:, cs_], True, True, [t("Btm", g), t("xdd", g)], [self.pT(A0)])
                HT = [t("hT32", l, g)]
                h3 = self.hT32[:, l, cs_].rearrange("p (e d) -> p e d", d=64)
                self.tt(h3, h3, bc(cdg, [128, 6, 64], 2), ALU.mult, HT + TSg, HT)
                self.tt(self.hT32[:, l, cs_], self.hT32[:, l, cs_], pst, ALU.add, HT + [self.pT(A0)], HT)
                self.acopy(self.hTb[:, cs_], self.hT32[:, l, cs_], HT, [t("hTb", g)])
                self.stt(self.yg[:, 3 * g:3 * g + 3, :], py[:, 0:384].rearrange("p (c q) -> p c q", q=128), 0.5,
                         self.zs[:, 3 * g:3 * g + 3, sl], ALU.mult, ALU.mult,
                         [self.pT(YB)] + [t("zs", c) for c in range(3 * g, 3 * g + 3)], [t("yg", g)])
                self.act(self.sq[:, 3 * g:3 * g + 3, :], self.yg[:, 3 * g:3 * g + 3, :], AF.Square, [t("yg", g)], [t("sq", g)])
            ssb = b1[:, 384:512]
            for c in range(6):
                self.mm(ssb, ones, self.sq[:, c, :], c == 0, c == 5, [t("cst"), t("sq", c // 3)], [self.pT(1)])
            self.act(self.rstd[:], ssb, AF.Ln, [self.pT(1)], [t("rstd")], bias=768.0 * EPS)
            self.act(self.rstd[:], self.rstd[:], AF.Exp, [t("rstd")], [t("rstd")], scale=-0.5)
            for c in range(6):
                self.stt(self.mixT[:, 12 + c, sl], self.yg[:, c, :], self.gS[:, l, c:c + 1], self.rstd[:], ALU.mult, ALU.mult,
                         [t("yg", c // 3), t("rstd")] + PR, [t("mixT", 12 + c, tl)])
        if last:
            for hb in range(2):
                pb = self.bank(hb)
                for j in range(3):
                    c = hb * 3 + j
                    self.tr(pb[:, j * 128:(j + 1) * 128], self.hT32[:, l, c * 128:(c + 1) * 128], idf, [t("hT32", l, 0), t("hT32", l, 1), t("cst")], [self.pT(hb)])
                self.vcopy(self.iost[:, hb * 384:(hb + 1) * 384], pb[:, 0:384], [self.pT(hb)], [t("iost")])
            P.dma("pool", self.p_ssd_h[l].rearrange("(c p) n -> p c n", p=128), self.iost[:, 0:768].rearrange("p (c n) -> p c n", n=128),
                  reads=[t("iost")])

    def phase_out(self, l, p, nco=H, x32=None, xb=None, tag=""):
        P, t = self.P, self.P.t
        cst = self.cst
        idf = cst[:, K_ID:K_ID + 128]
        ones = cst[:, K_ONE:K_ONE + 128]
        PR = [t("par", l)]
        src = self.w_out_bf[l]
        x32 = self.xT32 if x32 is None else x32
        xb = self.xTb if xb is None else xb
        smp = (nco != H)
        XT = (lambda dc: t("SxT32", dc)) if smp else (lambda dc: t("xT32", dc))
        XB = (lambda dc: t("SxTb", dc)) if smp else (lambda dc: t("xTb", dc))
        MT = (lambda ec: [t("SmixT", ec)]) if smp else (lambda ec: [t("mixT", ec, tl) for tl in range(NT)])
        bk = lambda i: self.bank(i)[:, 0:nco]
        wqf = self.wqkv[:].rearrange("p a b -> p (a b)")
        for ec in range(18):
            if nco == H and ec % 7 != 6:
                j = ec % 7
                wv = wqf[:, j * 1024:(j + 1) * 1024]
                WT = t("wqkv", j)
            else:
                k = self.next_wbuf("all" if nco != H else "out")
                wv = self.wbuf[k][:].rearrange("p a b -> p (a b)")
                WT = t("wbuf", k)
            if ec < 4:
                R = 128
                pieces = [(ec * 64, 64, 0), ((ec + 4) * 64, 64, 64)]
            elif ec < 12:
                R = 96
                pieces = [(512 + (ec - 4) * 96, 96, 0)]
            else:
                R = 128
                pieces = [(1280 + (ec - 12) * 128, 128, 0)]
            for (r0, n, d0) in pieces:
                P.dma("sp", wv[d0:d0 + n, :], src[r0:r0 + n, :], reads=self.wout_T(l), writes=[WT])
            for dc in range(8):
                self.mm(bk(dc), wv[0:R, dc * 128:(dc + 1) * 128], self.mixT[0:R, ec, 0:nco], ec == 0, ec == 17,
                        [WT] + MT(ec), [self.pT(dc)])
        for dc in range(8):
            X = [XT(dc)]
            self.stt(x32[:, dc, :], x32[:, dc, :], ALPHA, bk(dc), ALU.mult, ALU.add, X + [self.pT(dc)], X)
        for dc in range(8):
            X = [XT(dc)]
            q = self.lnq[dc % 2]
            self.act(q[:, 0:nco], x32[:, dc, :], AF.Square, X, [t("lnq", dc % 2)])
            self.mm(bk(0), ones, x32[:, dc, :], dc == 0, dc == 7, [t("cst")] + X, [self.pT(0)])
            self.mm(bk(1), ones, q[:, 0:nco], dc == 0, dc == 7, [t("cst"), t("lnq", dc % 2)], [self.pT(1)])
        M, Rs = [t("mean")], [t("lrs")]
        mean, lrs = self.mean[:, 0:nco], self.lrs[:, 0:nco]
        self.ts(mean, bk(0), 1.0 / D, None, ALU.mult, None, [self.pT(0)], M)
        self.tt(lrs, mean, mean, ALU.mult, M, Rs)
        self.stt(lrs, bk(1), 1.0 / D, lrs, ALU.mult, ALU.subtract, [self.pT(1)] + Rs, Rs)
        self.act(lrs, lrs, AF.Sqrt, Rs, Rs, bias=EPS)
        P.op("dve", lambda e: e.reciprocal(out=lrs, in_=lrs), Rs, Rs)
        for dc in range(8):
            X = [XT(dc)]
            xv = x32[:, dc, :]
            self.tt(xv, xv, mean, ALU.subtract, X + M, X)
            self.tt(xv, xv, lrs, ALU.mult, X + Rs, X)
            self.act(xv, xv, AF.Identity, X + PR, X, scale=self.lnp[:, l, dc, 0:1], bias=self.lnp[:, l, dc, 1:2])
            if l < DEPTH - 1:
                self.acopy(xb[:, dc, 0:nco], xv, X, [XB(dc)])
        if l == 0 and p == 0:
            self.tap("x_l0p0", self.xT32[:], [t("xT32", dc) for dc in range(8)])
        if smp:
            if l == DEPTH - 1:
                pb = self.bank(2, 2)
                for dc in range(8):
                    self.tr(pb[0:NS, dc * 128:(dc + 1) * 128], x32[:, dc, :], idf, [XT(dc), t("cst")], [self.pT(2 + dc // 4)])
                self.vcopy(self.iost[0:NS, :], pb[0:NS, :], [self.pT(2), self.pT(3)], [t("iost")])
                P.dma("pool", self.y_sample, self.iost[0:NS, :], reads=[t("iost")])
        elif l == DEPTH - 1:
            for tl in range(NT):
                gt = p * NT + tl
                for hb in range(2):
                    pb = self.bank(2 + hb)
                    for j in range(4):
                        dc = hb * 4 + j
                        self.tr(pb[:, j * 128:(j + 1) * 128], self.xT32[:, dc, tl * 128:(tl + 1) * 128], idf,
                                [t("xT32", dc), t("cst")], [self.pT(2 + hb)])
                    self.vcopy(self.iost[:, hb * 512:(hb + 1) * 512], pb, [self.pT(2 + hb)], [t("iost")])
                P.dma("pool", self.y_prompt[gt * 128:(gt + 1) * 128, :], self.iost[:], reads=[t("iost")])

    def sample_job(self, l):
        P, t = self.P, self.P.t
        cst = self.cst
        idf = cst[:, K_ID:K_ID + 128]
        ones = cst[:, K_ONE:K_ONE + 128]
        i16 = idf[0:NS, 0:NS]
        PR = [t("par", l)]
        x32 = self.rstd[:].rearrange("p (a b) -> p a b", b=NS)
        xb = self.xTb
        XB = [t("SxTb", kc) for kc in range(8)]
        XT = [t("SxT32", dc) for dc in range(8)]
        Bt = lambda n: [t("B", n)]
        f2 = lambda ap: ap.rearrange("p a b -> p (a b)")

        def fm(k, M, out, bT):
            for kc in range(8):
                self.mm(out, self.wbuf[k][:, kc, 0:M], xb[:, kc, 0:NS], kc == 0, kc == 7, [t("wbuf", k)] + XB, [bT])

        def tm(k, M, out, bT):
            for kc in range(8):
                self.mm(out, xb[:, kc, 0:NS], self.wbuf[k][:, kc, 0:M], kc == 0, kc == 7, [t("wbuf", k)] + XB, [bT])

        if l == 0:
            P.dma("sp", self.iost[0:NS, :], self.x_sample, writes=[t("iost")])
            pb = self.bank(0)
            for dc in range(8):
                self.tr(pb[:, dc * NS:(dc + 1) * NS], self.iost[0:NS, dc * 128:(dc + 1) * 128], i16, [t("iost"), t("cst")], [self.pT(0)])
            self.vcopy(x32, pb[:, 0:128].rearrange("p (a b) -> p a b", b=NS), [self.pT(0)], XT)
            for dc in range(8):
                self.acopy(xb[:, dc, 0:NS], x32[:, dc, :], [XT[dc]], [XB[dc]])
        P.dma("pool", self.wab[0:96, :, :], self.lru_wa[l].rearrange("n c d -> c n d"), writes=[t("wab")])
        P.dma("pool", self.wxb[0:96, :, :], self.lru_wx[l].rearrange("n c d -> c n d"), writes=[t("wxb")])

        b0, b1 = self.bank(0), self.bank(1)
        for j in range(6):
            k = self.load_win(l, [(j * 128, 128, 0)])
            if j < 4:
                tm(k, 128, b0[0:NS, j * 128:(j + 1) * 128], self.pT(0))
            else:
                tm(k, 128, b1[0:NS, (j - 4) * 128:(j - 3) * 128], self.pT(1))
        q32 = self.qkv32[0]
        Q = Bt("q0")
        self.acopy(q32[0:NS, 0:512], b0[0:NS, :], [self.pT(0)], Q)
        self.acopy(q32[0:NS, 512:768], b1[0:NS, 0:256], [self.pT(1)], Q)
        hv = q32[0:NS, 0:640].rearrange("p (h d) -> p h d", d=64)
        x1, x2 = hv[:, :, 0:8], hv[:, :, 8:16]
        cs = bc(self.ropeS[0:NS, 0, :], [NS, 10, 8], 1)
        sn = bc(self.ropeS[0:NS, 1, :], [NS, 10, 8], 1)
        ns = bc(self.ropeS[0:NS, 2, :], [NS, 10, 8], 1)
        R = Bt("ropet")
        ra, rb = self.ropet[0:NS, 0, :, :], self.ropet[0:NS, 1, :, :]
        self.tt(rb[:, :, 0:8], x2, ns, ALU.mult, Q + [t("rope")], R)
        self.tt(rb[:, :, 8:16], x1, sn, ALU.mult, Q + [t("rope")], R)
        self.tt(ra[:, :, 0:8], x1, cs, ALU.mult, Q + [t("rope")], R)
        self.tt(ra[:, :, 8:16], x2, cs, ALU.mult, Q + [t("rope")], R)
        self.tt(hv[:, :, 0:16], ra, rb, ALU.add, R, Q)
        P.dma("pool", self.s_swa_k[l][:, 0:127, :], self.cache_k[l][:, 1:128, :])
        P.dma("pool", self.s_swa_v[l][:, 0:127, :], self.cache_v[l][:, 1:128, :])
        P.dma("pool", self.s_swa_k[l][:, 127, :], q32[0:NS, 512:640], reads=Q)
        P.dma("pool", self.s_swa_v[l][:, 127, :], q32[0:NS, 640:768], reads=Q)
        b2 = self.bank(2)
        for c in range(4):
            k = self.load_win(l, [(C_GA + c * 64, 64, 0), (C_GA + (c + 4) * 64, 64, 64)])
            fm(k, 128, b2[:, c * NS:(c + 1) * NS], self.pT(2))
        GsT = self.G[0][:, 0:64].rearrange("p (c b) -> p c b", b=NS)
        self.act(f2(GsT), b2[:, 0:64], AF.Silu, [self.pT(2)], Bt("G0"))
        kbf = self.qkbf[0:NS, 512:640]
        vnb = self.qkbf[0:NS, 0:128]
        self.acopy(kbf, q32[0:NS, 512:640], Q, Bt("qkbf"))
        self.acopy(vnb, q32[0:NS, 640:768], Q, Bt("qkbf"))
        qz = f2(self.CE[:])[0:NS, 0:1024].rearrange("p (h f) -> p h f", f=128)
        P.op("dve", lambda e: e.memset(f2(self.CE[:])[0:NS, 0:1024], 0.0), [], Bt("CE"))
        qh = q32[0:NS, 0:512].rearrange("p (h d) -> p h d", d=64)
        self.vcopy(qz[:, 0:4, 0:64], qh[:, 0:4, :], Q + Bt("CE"), Bt("CE"))
        self.vcopy(qz[:, 4:8, 64:128], qh[:, 4:8, :], Q + Bt("CE"), Bt("CE"))
        p3 = self.bankbf(3)
        ib16 = self.ident_bf[0:NS, 0:NS]
        for h in range(8):
            self.tr(p3[:, h * NS:(h + 1) * NS], qz[:, h, :], ib16, Bt("CE") + [t("ident_bf")], [self.pT(3)])
        self.tr(p3[:, 128:128 + NS], kbf, ib16, Bt("qkbf") + [t("ident_bf")], [self.pT(3)])
        qblk = self.PT[0][:, 0:128].rearrange("p (b h) -> p b h", h=8)
        knT = self.PT[0][:, 128:128 + NS]
        self.vcopy(qblk.rearrange("p b h -> p h b"), p3[:, 0:128].rearrange("p (h b) -> p h b", b=NS), [self.pT(3)], Bt("PT0"))
        self.vcopy(knT, p3[:, 128:128 + NS], [self.pT(3)], Bt("PT0"))
        st8 = self.wqkv[:].rearrange("p a b -> p (a b)").bitcast(F32)[:, 0:2048].rearrange("p (b f) -> p b f", f=128)
        P.dma("sp", st8, self.cache_k[l].rearrange("b k f -> k b f"), writes=Bt("st8"))
        for b in range(NS):
            self.tr(self.bank(4 + b // 4)[:, (b % 4) * 128:(b % 4 + 1) * 128], st8[:, b, :], idf, Bt("st8") + [t("cst")], [self.pT(4 + b // 4)])
        KcT = f2(self.xsT[:])[:, 0:2048].rearrange("p (b k) -> p b k", k=128)
        for i in range(4):
            self.acopy(f2(KcT[:, 4 * i:4 * i + 4, :]), self.bank(4 + i), [self.pT(4 + i)], Bt("xsT"))
        for b in range(NS):
            self.mm(self.bank(b // 4)[0:8, (b % 4) * 128:(b % 4 + 1) * 128], qblk[:, b, :], KcT[:, b, :], True, True,
                    Bt("PT0") + Bt("xsT"), [self.pT(b // 4)])
        b4 = self.bank(4)
        for b in range(NS):
            self.mm(b4[0:8, b:b + 1], qblk[:, b, :], knT[:, b:b + 1], True, True, Bt("PT0"), [self.pT(4)])
        stt_ = f2(self.tsm[:])[0:8, 0:96].rearrange("p (j b) -> p j b", b=NS)
        mx, negm, ssum, pnew, es, rden = (stt_[:, j, :] for j in range(6))
        A = Bt("tsm")
        S4 = self.ps[0:8, 0:2048].rearrange("p (b k) -> p b k", k=128)
        ST = [self.pT(i) for i in range(4)]
        P.op("dve", lambda e: e.reduce_max(out=mx, in_=S4, axis=AX.X), ST, A)
        self.tt(mx, mx, b4[0:8, 0:NS], ALU.max, A + [self.pT(4)], A)
        self.stt(negm, mx, -SCALE, self.sinkc[0:8, l, 1:2].to_broadcast([8, NS]), ALU.mult, ALU.min, A + PR, A)
        Pms = f2(self.Dm[:]).bitcast(BF16)[0:8, 0:2048]
        for b in range(NS):
            self.act(Pms[:, b * 128:(b + 1) * 128], S4[:, b, :], AF.Exp, [self.pT(b // 4)] + A, Bt("Dm") + A,
                     scale=SCALE, bias=negm[:, b:b + 1], accum_out=ssum[:, b:b + 1])
        self.stt(pnew, b4[0:8, 0:NS], SCALE, negm, ALU.mult, ALU.add, A + [self.pT(4)], A)
        self.act(pnew, pnew, AF.Exp, A, A)
        self.act(es, negm, AF.Exp, A + PR, A, bias=self.sinkc[0:8, l, 0:1])
        self.tt(ssum, ssum, pnew, ALU.add, A, A)
        self.tt(ssum, ssum, es, ALU.add, A, A)
        P.op("dve", lambda e: e.reciprocal(out=rden, in_=ssum), A, A)
        pv = Pms.rearrange("p (b k) -> p b k", k=128)
        self.tt(pv, pv, bc(rden, [8, NS, 128], 2), ALU.mult, Bt("Dm") + A, Bt("Dm"))
        self.tt(pnew, pnew, rden, ALU.mult, A, A)
        p5 = self.bankbf(5)
        for b in range(NS):
            self.tr(p5[:, b * 8:(b + 1) * 8], Pms[:, b * 128:(b + 1) * 128], self.ident_bf[0:8, 0:8], Bt("Dm") + [t("ident_bf")], [self.pT(5)])
        PTs = self.Pm[0][:, 0:128].rearrange("p (b h) -> p b h", h=8)
        self.acopy(f2(PTs), p5[:, 0:128], [self.pT(5)], Bt("Pm0"))
        b6 = self.bank(6)
        self.tr(b6[0:NS, 0:8], pnew, idf[0:8, 0:8], A + [t("cst")], [self.pT(6)])
        pnt = f2(self.ast[1][:])[0:NS, 0:8]
        self.vcopy(pnt, b6[0:NS, 0:8], [self.pT(6)], Bt("ast1"))
        psel = self.PT[1][0:NS, 0:128].rearrange("p (b h) -> p b h", h=8)
        self.tt(psel, bc(pnt, [NS, NS, 8], 1), bc(i16, [NS, NS, 8], 2), ALU.mult, Bt("ast1") + [t("cst")], Bt("PT1"))
        P.dma("sp", st8, self.cache_v[l].rearrange("b k f -> k b f"), writes=Bt("st8"))
        Vc = f2(self.qT[:]).rearrange("p (b f) -> p b f", f=128)
        self.vcopy(f2(Vc), f2(st8), Bt("st8"), Bt("qT"))
        b7 = self.bank(7)
        oT = b7[:, 0:128].rearrange("p (b h) -> p b h", h=8)
        for b in range(NS):
            self.mm(b7[:, b * 8:(b + 1) * 8], Vc[:, b, :], PTs[:, b, :], True, False, Bt("qT") + Bt("Pm0"), [self.pT(7)])
            self.mm(b7[:, b * 8:(b + 1) * 8], vnb, psel[:, b, :], False, True, Bt("qkbf") + Bt("PT1"), [self.pT(7)])
        for c in range(4):
            for w in range(2):
                rows = slice(w * 64, (w + 1) * 64)
                self.tt(self.mixT[rows, c, 0:NS], oT[rows, :, c + 4 * w], GsT[rows, c, :], ALU.mult,
                        [self.pT(7)] + Bt("G0"), [t("SmixT", c)])

        zf = f2(self.zs[:])
        stc = zf[0:NS, 0:2304]
        sth = zf[0:NS, 2304:3072]
        P.dma("sp", stc.rearrange("p (k f) -> p k f", f=768), self.st_lru_conv[l], writes=Bt("zs"))
        P.dma("sp", sth, self.st_lru_h[l], writes=Bt("zs"))
        P.dma("pool", self.s_lru_conv[l][:, 0:2, :], self.st_lru_conv[l][:, 1:3, :])
        b0, b1 = self.bank(0), self.bank(1)
        for n in range(8):
            for k3 in range(3):
                j = n * 3 + k3
                self.tr(b0[0:96, j * NS:(j + 1) * NS], stc[:, k3 * 768 + n * 96:k3 * 768 + (n + 1) * 96], i16, Bt("zs") + [t("cst")], [self.pT(0)])
            self.tr(b1[0:96, n * NS:(n + 1) * NS], sth[:, n * 96:(n + 1) * 96], i16, Bt("zs") + [t("cst")], [self.pT(1)])
        aUf = f2(self.aU[:])
        hs = aUf[0:96, 0:384].rearrange("p (n k b) -> p n k b", k=3, b=NS)
        h0T = aUf[0:96, 384:512].rearrange("p (n b) -> p n b", b=NS)
        self.vcopy(aUf[0:96, 0:384], b0[0:96, 0:384], [self.pT(0)], Bt("aU"))
        self.vcopy(aUf[0:96, 384:512], b1[0:96, 0:128], [self.pT(1)], Bt("aU"))
        b2, b5 = self.bank(2), self.bank(5)
        for n in range(8):
            kx = self.load_win(l, [(C_XL + n * 96, 96, 0)])
            fm(kx, 96, b2[0:96, n * NS:(n + 1) * NS], self.pT(2))
            tm(kx, 96, self.bank(3 + n // 4)[0:NS, (n % 4) * 128:(n % 4) * 128 + 96], self.pT(3 + n // 4))
            kg = self.load_win(l, [(C_GL + n * 96, 96, 0)])
            fm(kg, 96, b5[0:96, n * NS:(n + 1) * NS], self.pT(5))
        xltm = self.qkv32[1][0:NS, 0:768]
        self.vcopy(xltm.rearrange("p (n c) -> p n c", c=96),
                   self.ps[0:NS, 3 * 512:5 * 512].rearrange("p (n c) -> p n c", c=128)[:, :, 0:96], [self.pT(3), self.pT(4)], Bt("q1"))
        P.dma("pool", self.s_lru_conv[l][:, 2, :], xltm, reads=Bt("q1"))
        v3 = lambda ap: ap.rearrange("p (n b) -> p n b", b=NS)
        lpb = lambda j: bc(self.lp[0:96, l, :, j], [96, 8, NS], 2)
        acc = v3(self.xl[0:96, 0:128]); tmp = v3(self.xlb[0:96, 0:256].bitcast(F32))
        Xl, Tm = Bt("xl"), Bt("xlb")
        self.tt(acc, v3(b2[0:96, 0:128]), lpb(3), ALU.mult, [self.pT(2)] + PR, Xl)
        self.tt(acc, acc, lpb(4), ALU.add, Xl + PR, Xl)
        for k3 in range(3):
            self.tt(tmp, hs[:, :, k3, :], lpb(k3), ALU.mult, Bt("aU") + PR, Tm)
            self.tt(acc, acc, tmp, ALU.add, Xl + Tm, Xl)
        xlbf = self.dtt[0:96, 0:64].bitcast(BF16)
        self.acopy(xlbf, self.xl[0:96, 0:128], Xl, Bt("dtt"))
        b6, b7 = self.bank(6), self.bank(7)
        for n in range(8):
            self.mm(b6[0:96, n * NS:(n + 1) * NS], self.wab[0:96, n, :], xlbf[:, n * NS:(n + 1) * NS], True, True, [t("wab")] + Bt("dtt"), [self.pT(6)])
            self.mm(b7[0:96, n * NS:(n + 1) * NS], self.wxb[0:96, n, :], xlbf[:, n * NS:(n + 1) * NS], True, True, [t("wxb")] + Bt("dtt"), [self.pT(7)])
        rr, ig, EE = v3(self.rr[0:96, 0:128]), v3(self.ig[0:96, 0:128]), v3(self.EE[0:96, 0:128])
        Rr, Ig, Ee = Bt("rr"), Bt("ig"), Bt("EE")
        self.tt(rr, v3(b6[0:96, 0:128]), lpb(5), ALU.add, [self.pT(6)] + PR, Rr)
        self.act(rr, rr, AF.Sigmoid, Rr, Rr)
        self.tt(ig, v3(b7[0:96, 0:128]), lpb(6), ALU.add, [self.pT(7)] + PR, Ig)
        self.act(ig, ig, AF.Sigmoid, Ig, Ig)
        self.tt(EE, rr, bc(self.la[0:96, l, :, 1], [96, 8, NS], 2), ALU.mult, Rr + PR, Ee)
        self.act(EE, EE, AF.Exp, Ee, Ee, scale=2.0)
        self.act(EE, EE, AF.Sqrt, Ee, Ee, scale=-1.0, bias=1.0)
        self.tt(rr, rr, bc(self.la[0:96, l, :, 1], [96, 8, NS], 2), ALU.mult, Rr + PR, Rr)
        self.act(rr, rr, AF.Exp, Rr, Rr)
        self.tt(ig, ig, acc, ALU.mult, Ig + Xl, Ig)
        self.tt(ig, ig, EE, ALU.mult, Ig + Ee, Ig)
        self.tt(EE, rr, h0T, ALU.mult, Rr + Bt("aU"), Ee)
        self.tt(EE, EE, ig, ALU.add, Ee + Ig, Ee)
        for n in range(8):
            self.tr(self.bank(3 + n // 4)[0:NS, (n % 4) * 128:(n % 4) * 128 + 96], self.EE[0:96, n * NS:(n + 1) * NS], idf[0:96, 0:96],
                    Ee + [t("cst")], [self.pT(3 + n // 4)])
        h1tm = f2(self.yg[:])[0:NS, 0:768]
        self.vcopy(h1tm.rearrange("p (n c) -> p n c", c=96),
                   self.ps[0:NS, 3 * 512:5 * 512].rearrange("p (n c) -> p n c", c=128)[:, :, 0:96], [self.pT(3), self.pT(4)], Bt("yg"))
        P.dma("pool", self.s_lru_h[l], h1tm, reads=Bt("yg"))
        sg = v3(self.dtT[0:96, 0:128])
        self.act(sg, v3(b5[0:96, 0:128]), AF.Silu, [self.pT(5)], Bt("dtT"))
        self.tt(self.mixT[0:96, 4:12, 0:NS], EE, sg, ALU.mult, Ee + Bt("dtT"), [t("SmixT", 4 + n) for n in range(8)])

        Dmf = f2(self.Dm[:])
        srcs = (zf[0:NS, 0:1280], zf[0:NS, 1280:2560], Dmf[0:NS, 0:1280])
        srcT = (Bt("zs"), Bt("zs"), Bt("Dm"))
        for k3 in range(3):
            P.dma("sp", srcs[k3], self.st_ssd_conv[l][:, k3, :], writes=srcT[k3])
        P.dma("pool", self.s_ssd_conv[l][:, 0:2, :], self.st_ssd_conv[l][:, 1:3, :])
        b0 = self.bank(0)
        for c in range(10):
            for k3 in range(3):
                j = c * 3 + k3
                self.tr(b0[:, j * NS:(j + 1) * NS], srcs[k3][:, c * 128:(c + 1) * 128], i16, srcT[k3] + [t("cst")], [self.pT(0)])
        sqf = f2(self.sq[:])
        hss = sqf[:, 0:480].rearrange("p (c k b) -> p c k b", k=3, b=NS)
        self.vcopy(sqf[:, 0:480], b0[:, 0:480], [self.pT(0)], Bt("sq"))
        b1 = self.bank(1)
        for c in range(10):
            k = self.load_win(l, [(C_XBC + c * 128, 128, 0)])
            fm(k, 128, b1[:, c * NS:(c + 1) * NS], self.pT(1))
            tm(k, 128, self.bank(2 + c // 4)[0:NS, (c % 4) * 128:(c % 4 + 1) * 128], self.pT(2 + c // 4))
        xbtm = aUf[0:NS, 0:1280]
        self.vcopy(xbtm, self.ps[0:NS, 2 * 512:2 * 512 + 1280], [self.pT(2), self.pT(3), self.pT(4)], Bt("aU"))
        P.dma("pool", self.s_ssd_conv[l][:, 2, :], xbtm, reads=Bt("aU"))
        b5, b6 = self.bank(5), self.bank(6)
        for c in range(6):
            k = self.load_win(l, [(C_Z + c * 128, 128, 0)])
            fm(k, 128, b5[:, c * NS:(c + 1) * NS], self.pT(5))
        k = self.load_win(l, [(C_DT, 12, 0)])
        fm(k, 12, b6[0:12, 0:NS], self.pT(6))
        spb = lambda j: bc(self.spp[:, l, :, j], [128, 10, NS], 2)
        acc = v3(self.xl[:, 0:160]); tmp = v3(self.xlb[:, 0:320].bitcast(F32))
        self.tt(acc, v3(b1[:, 0:160]), spb(3), ALU.mult, [self.pT(1)] + PR, Xl)
        self.tt(acc, acc, spb(4), ALU.add, Xl + PR, Xl)
        for k3 in range(3):
            self.tt(tmp, hss[:, :, k3, :], spb(k3), ALU.mult, Bt("sq") + PR, Tm)
            self.tt(acc, acc, tmp, ALU.add, Xl + Tm, Xl)
        xc = v3(self.rr[:, 0:160])
        self.act(xc, acc, AF.Silu, Xl, Rr, scale=2.0)
        xsTs, BsT, CsT = xc[:, 0:6, :], xc[:, 6:8, :], xc[:, 8:10, :]
        zsT = v3(self.ig[:, 0:96])
        self.act(zsT, v3(b5[:, 0:96]), AF.Silu, [self.pT(5)], Ig)
        u, v = self.dtT[0:12, 0:NS], self.dtt[0:12, 0:NS]
        self.act(u, b6[0:12, 0:NS], AF.Identity, [self.pT(6)] + PR, Bt("dtT"), bias=self.dtb[0:12, l:l + 1])
        self.act(v, u, AF.Abs, Bt("dtT"), Bt("dtt"))
        self.act(v, v, AF.Exp, Bt("dtt"), Bt("dtt"), scale=-1.0)
        self.act(v, v, AF.Ln, Bt("dtt"), Bt("dtt"), bias=1.0)
        self.stt(u, u, 0.0, v, ALU.max, ALU.add, Bt("dtT") + Bt("dtt"), Bt("dtT"))
        Ex = cst[0:12, K_EX:K_EX + 768]
        b7 = self.bank(7)
        for c in range(6):
            Exc = Ex[:, c * 128:(c + 1) * 128]
            self.mm(b7[:, c * NS:(c + 1) * NS], Exc, u, True, True, [t("cst")] + Bt("dtT"), [self.pT(7)])
            self.mm(b7[:, 96 + c:97 + c], Exc, self.adc[0:12, l, 0:1], True, True, [t("cst")] + PR, [self.pT(7)])
            self.mm(b7[:, 104 + c:105 + c], Exc, self.adc[0:12, l, 1:2], True, True, [t("cst")] + PR, [self.pT(7)])
        ex = self.EE[:, 0:112]
        self.vcopy(ex, b7[:, 0:112], [self.pT(7)], Ee)
        dtx = v3(ex[:, 0:96]); Ax = ex[:, 96:102]; Dx = ex[:, 104:110]
        decT = v3(self.mean[:, 0:96]); x0T = v3(self.lrs[:, 0:96])
        self.tt(decT, dtx, bc(Ax, [128, 6, NS], 2), ALU.mult, Ee, [t("mean")])
        self.act(decT, decT, AF.Exp, [t("mean")], [t("mean")])
        self.tt(x0T, xsTs, dtx, ALU.mult, Rr + Ee, [t("lrs")])
        yT = v3(self.lnq[0][:, 0:96])
        hbuf = [f2(self.xT32[:])[:, i * 2048:(i + 1) * 2048].rearrange("p (b n) -> p b n", n=128) for i in range(2)]
        st8v = st8
        Bbc = self.ps[:, 0:2048].rearrange("p (b n) -> p b n", n=128)
        Cbc = self.ps[:, 2048:4096].rearrange("p (b n) -> p b n", n=128)
        idb = bc(idf, [128, NS, 128], 1)
        for g in range(2):
            self.tt(st8v, bc(BsT[:, g, :], [128, NS, 128], 2), idb, ALU.mult, Rr + [t("cst")], Bt("st8"))
            for j in range(4):
                self.mm(self.bank(j), ones, f2(st8v)[:, j * 512:(j + 1) * 512], True, True, [t("cst")] + Bt("st8"), [self.pT(j)])
            self.tt(st8v, bc(CsT[:, g, :], [128, NS, 128], 2), idb, ALU.mult, Rr + [t("cst")], Bt("st8"))
            for j in range(4):
                self.mm(self.bank(4 + j), ones, f2(st8v)[:, j * 512:(j + 1) * 512], True, True, [t("cst")] + Bt("st8"), [self.pT(4 + j)])
            for c in range(3 * g, 3 * g + 3):
                hb = hbuf[c % 2]
                Hb = [t("B", "hb", c % 2)]
                P.dma("sp", hb, self.st_ssd_h[l][:, 2 * c:2 * c + 2].rearrange("b e p n -> (e p) b n"), writes=Hb)
                self.tt(hb, hb, bc(decT[:, c, :], [128, NS, 128], 2), ALU.mult, Hb + [t("mean")], Hb)
                self.tt(st8v, Bbc, bc(x0T[:, c, :], [128, NS, 128], 2), ALU.mult, [self.pT(j) for j in range(4)] + [t("lrs")], Bt("st8"))
                self.tt(hb, hb, st8v, ALU.add, Hb + Bt("st8"), Hb)
                P.dma("pool", self.s_ssd_h[l][:, c * 128:(c + 1) * 128, :].rearrange("b f n -> f b n"), hb, reads=Hb)
                self.tt(st8v, hb, Cbc, ALU.mult, Hb + [self.pT(4 + j) for j in range(4)], Bt("st8"))
                P.op("dve", lambda e, c=c: e.reduce_sum(out=yT[:, c, :], in_=st8v, axis=AX.X), Bt("st8"), [t("lnq", 0)])
        self.tt(tmp[:, 0:6, :], xsTs, bc(Dx, [128, 6, NS], 2), ALU.mult, Rr + Ee, Tm)
        self.tt(yT, yT, tmp[:, 0:6, :], ALU.add, [t("lnq", 0)] + Tm, [t("lnq", 0)])
        self.tt(yT, yT, zsT, ALU.mult, [t("lnq", 0)] + Ig, [t("lnq", 0)])
        sqs = v3(self.lnq[1][:, 0:96])
        self.act(sqs, yT, AF.Square, [t("lnq", 0)], [t("lnq", 1)])
        b0 = self.bank(0)
        for c in range(6):
            self.mm(b0[:, 0:NS], ones, sqs[:, c, :], c == 0, c == 5, [t("cst"), t("lnq", 1)], [self.pT(0)])
        rs = self.dtt[:, 64:64 + NS]
        self.act(rs, b0[:, 0:NS], AF.Sqrt, [self.pT(0)], Bt("dtt"), bias=768.0 * EPS)
        P.op("dve", lambda e: e.reciprocal(out=rs, in_=rs), Bt("dtt"), Bt("dtt"))
        for c in range(6):
            self.stt(self.mixT[:, 12 + c, 0:NS], yT[:, c, :], self.gS[:, l, c:c + 1], rs, ALU.mult, ALU.mult,
                     [t("lnq", 0)] + Bt("dtt") + PR, [t("SmixT", 12 + c)])

        self.phase_out(l, None, nco=NS, x32=x32, xb=self.xTb)


    def build(self, sample=True, n_pass=NPASS, n_layers=DEPTH):
        self.declare_io()
        self.alloc()
        self.prologue()
        for p in range(n_pass):
            for l in range(n_layers):
                self.job(l, p)
        if sample:
            self.P.fence()
            for l in range(n_layers):
                self.sample_job(l)
        self.P.emit()
        self.st.close()
        return self.nc


def make_consts():
    c = np.zeros((128, K_END), np.float32)
    c[:, K_ID:K_ID + 128] = np.eye(128, dtype=np.float32)
    s = np.arange(128)[:, None]
    q = np.arange(128)[None, :]
    c[:, K_U:K_U + 128] = (s <= q).astype(np.float32)
    c[:, K_ONE:K_ONE + 128] = 1.0
    i = np.arange(128)[:, None]
    j = np.arange(256)[None, :]
    band = (j >= i) & (j <= i + 128)
    c[:, K_MG:K_MG + 256] = np.where(band, 0.0, NEG)
    c[:, K_MF:K_MF + 256] = np.where(band & (j >= 128), 0.0, NEG)
    c[:, K_POS:K_POS + 16] = np.arange(128)[:, None] + 128.0 * np.arange(16)[None, :]
    c[:, K_INV:K_INV + 8] = (500000.0 ** (-np.arange(8, dtype=np.float32) / 8.0)).astype(np.float32)[None, :]
    for e in range(12):
        c[e, K_EX + e * 64:K_EX + (e + 1) * 64] = 1.0
    c[:, K_HM] = (np.arange(128) < 4)
    c[:, K_HM + 1] = (np.arange(128) >= 4)
    return c


WNAMES = ["w_in", "w_out", "att_sinks", "lru_conv_w", "lru_conv_b", "lru_wa", "lru_ba", "lru_wx", "lru_bx",
          "lru_lambda", "ssd_conv_w", "ssd_conv_b", "ssd_dt_bias", "ssd_a_log", "ssd_d", "ssd_norm_g", "ln_g", "ln_b"]


def make_in_maps(inputs, n=8):
    f = lambda a: np.ascontiguousarray(np.asarray(a, dtype=np.float32))
    shared = {k: f(inputs[k]) for k in WNAMES}
    shared["consts"] = make_consts()
    maps = []
    for i in range(n):
        s = slice(i * NS, (i + 1) * NS)
        m = dict(shared)
        m["x_prompt"] = f(inputs["x_prompt"][i])
        m["x_sample"] = f(np.asarray(inputs["x_sample"])[s, 0, :])
        m["cache_swa_k"] = f(np.asarray(inputs["cache_swa_k"])[:, s].reshape(DEPTH, NS, 128, 128))
        m["cache_swa_v"] = f(np.asarray(inputs["cache_swa_v"])[:, s].reshape(DEPTH, NS, 128, 128))
        m["state_lru_conv"] = f(np.asarray(inputs["state_lru_conv"])[:, s])
        m["state_lru_h"] = f(np.asarray(inputs["state_lru_h"])[:, s])
        m["state_ssd_conv"] = f(np.asarray(inputs["state_ssd_conv"])[:, s])
        m["state_ssd_h"] = f(np.asarray(inputs["state_ssd_h"])[:, s])
        maps.append(m)
    return maps


def kernel(**inputs):
    nc = Builder().build()
    res = run_bass_kernel_spmd(nc, make_in_maps(inputs), core_ids=list(range(8)))
    R = res.results
    cat = lambda k, ax: np.concatenate([np.asarray(r[k]) for r in R], axis=ax)
    stk = lambda k: np.stack([np.asarray(r[k]) for r in R], axis=1)
    y_prompt = np.stack([np.asarray(r["y_prompt"]) for r in R], 0)
    y_sample = cat("y_sample", 0).reshape(128, 1, D)
    p_swa_k = stk("p_swa_k").reshape(DEPTH, 8, 128, 2, 64)
    p_swa_v = stk("p_swa_v").reshape(DEPTH, 8, 128, 2, 64)
    p_lru_conv = stk("p_lru_conv")
    p_lru_h = stk("p_lru_h")
    p_ssd_conv = stk("p_ssd_conv")
    p_ssd_h = stk("p_ssd_h").reshape(DEPTH, 8, 12, 64, 128)
    s_swa_k = cat("s_swa_k", 1).reshape(DEPTH, 128, 128, 2, 64)
    s_swa_v = cat("s_swa_v", 1).reshape(DEPTH, 128, 128, 2, 64)
    s_lru_conv = cat("s_lru_conv", 1)
    s_lru_h = cat("s_lru_h", 1)
    s_ssd_conv = cat("s_ssd_conv", 1)
    s_ssd_h = cat("s_ssd_h", 1).reshape(DEPTH, 128, 12, 64, 128)
    outs = (y_prompt, y_sample, p_swa_k, p_swa_v, p_lru_conv, p_lru_h, p_ssd_conv, p_ssd_h,
            s_swa_k, s_swa_v, s_lru_conv, s_lru_h, s_ssd_conv, s_ssd_h)
    return tuple(np.ascontiguousarray(o, dtype=np.float32) for o in outs)
```

```python
import contextlib
import math
import numpy as np
import concourse.bass as bass
import concourse.mybir as mybir
from concourse.bass_utils import run_bass_kernel_spmd

F32 = mybir.dt.float32
BF16 = mybir.dt.bfloat16
I32 = mybir.dt.int32
AF = mybir.ActivationFunctionType
ALU = mybir.AluOpType
AX = mybir.AxisListType

D = 1024
SEQ = 2048
DEPTH = 4
NS = 16
DIN = 4876
C_Q, C_K, C_V, C_GA, C_XL, C_GL, C_Z, C_XBC, C_DT = 0, 512, 640, 768, 1280, 2048, 2816, 3584, 4864
SCALE = 64 ** -0.5
ALPHA = (2.0 * DEPTH) ** 0.25
EPS = 1e-5
H = 512
NT = H // 128
NPASS = SEQ // H
PAST = 8192.0
NEG = -30000.0
import os
SSD_STOP = int(os.environ.get('SSD_STOP', '0'))
PIN_ENGS = set(os.environ.get('PIN_ENGS', '').split(','))
UNPIN_LINES = set(int(x) for x in os.environ.get('UNPIN_LINES', '').split(',') if x)

K_ID, K_U, K_ONE, K_MG, K_MF, K_POS, K_INV, K_HM, K_EX, K_END = 0, 128, 256, 384, 640, 896, 912, 920, 922, 922 + 768


class T:
    __slots__ = ("name", "lw", "rd", "excl")

    def __init__(self, name):
        self.name = name
        self.lw = None
        self.rd = []
        self.excl = (len(name) > 0 and name[0] == "ps")


class Op:
    __slots__ = ("eng", "fn", "deps", "signal", "val", "is_dma", "dsem", "dval", "odeps", "cost", "idx", "prio", "fin", "nbytes", "tbl")

    def __init__(self, eng, fn, is_dma):
        self.eng = eng
        self.fn = fn
        self.odeps = []
        self.tbl = None
        self.cost = 0.3
        self.nbytes = 0
        self.deps = []
        self.signal = False
        self.val = None
        self.is_dma = is_dma
        self.dsem = None
        self.dval = None


class Prog:
    ENGS = ("pe", "act", "dve", "pool", "sp")

    def __init__(self, nc, n_dma_sems=24):
        self.nc = nc
        self.ops = []
        self.tiles = {}
        self.n_dma_sems = n_dma_sems
        self.last_fence = {}
        self.last_pe = None
        self.last_on = {}
        self.do_schedule = os.environ.get('SCHED', '1') == '1'
        self.sched_window = int(os.environ.get('SCHEDW', '0'))

    def t(self, *key):
        if key not in self.tiles:
            self.tiles[key] = T(key)
        return self.tiles[key]

    def op(self, eng, fn, reads=(), writes=(), dma=False, cost=0.3, nbytes=0):
        o = Op(eng, fn, dma)
        o.cost = cost
        o.nbytes = nbytes
        o.idx = len(self.ops)
        lf = self.last_fence.get(eng)
        if lf is not None:
            o.odeps.append(lf)
        if eng in PIN_ENGS:
            import sys as _sys
            f = _sys._getframe(1)
            while f.f_code.co_name in ("act", "acopy", "vcopy", "tt", "ts", "stt", "mm", "tr", "dma", "op", "<lambda>"):
                f = f.f_back
            if f.f_lineno not in UNPIN_LINES:
                lo = self.last_on.get(eng)
                if lo is not None:
                    o.odeps.append(lo)
                self.last_on[eng] = o
        if eng == "pe":
            pin = (cost < 0)
            if pin:
                o.cost = cost = -cost
            lp = self.last_pe
            if lp is not None and (pin or lp[1]):
                o.odeps.append(lp[0])
            self.last_pe = (o, pin)
        deps = set()
        for t in reads:
            if t.lw is not None:
                deps.add(t.lw)
            if t.excl:
                for r in t.rd:
                    if r.eng != eng:
                        deps.add(r)
        for t in writes:
            if t.lw is not None:
                deps.add(t.lw)
            for r in t.rd:
                deps.add(r)
        for t in reads:
            t.rd.append(o)
        for t in writes:
            t.lw = o
            t.rd = []
        for d in deps:
            if d is o:
                continue
            if (not d.is_dma) and (not dma) and d.eng == "pe" and eng == "pe":
                o.odeps.append(d)
                continue
            o.deps.append(d)
            d.signal = True
        self.ops.append(o)
        return o

    def dma(self, eng, out, in_, reads=(), writes=(), **kw):
        nb = 1
        for d in out.shape:
            nb *= d
        nb *= mybir.dt.size(out.dtype)
        return self.op(eng, lambda e: e.dma_start(out=out, in_=in_, **kw), reads, writes, dma=True,
                       cost=(0.08 if eng != "pool" else 0.6), nbytes=nb)

    def fence(self):
        allt = list(self.tiles.values())
        ft = self.t("__fence__")
        first = True
        for e in ("pe", "act", "dve", "pool", "sp"):
            o = self.op(e, lambda eng: eng.nop(), [], (allt + [ft]) if first else [ft], cost=0.05)
            self.last_fence[e] = o
            first = False

    def schedule(self):
        import heapq
        ops = self.ops
        n = len(ops)
        succ = [[] for _ in range(n)]
        npred = [0] * n
        for o in ops:
            ps = set(id(d) for d in o.deps) | set(id(d) for d in o.odeps)
            seen = set()
            for d in list(o.deps) + list(o.odeps):
                if id(d) in seen:
                    continue
                seen.add(id(d))
                succ[d.idx].append(o.idx)
                npred[o.idx] += 1
        dur = [0.0] * n
        for o in ops:
            dur[o.idx] = o.cost + (2.0 + o.nbytes / 250e3 if o.is_dma else 0.0)
        prio = [0.0] * n
        for i in range(n - 1, -1, -1):
            m = 0.0
            for j in succ[i]:
                if prio[j] > m:
                    m = prio[j]
            prio[i] = m + dur[i]
        ready = {e: [] for e in self.ENGS}
        ready_t = [0.0] * n
        for o in ops:
            if npred[o.idx] == 0:
                heapq.heappush(ready[o.eng], (-prio[o.idx], o.idx))
        free = {e: 0.0 for e in self.ENGS}
        order = {e: [] for e in self.ENGS}
        fin = [0.0] * n
        cur_tbl = None
        SETS_OF = {"E": ("0", "6"), "A": ("0",), "L": ("6",), "S": ("3",), "G": ("2",), "U": ("18",), "N": ("9",)}

        def tbl_ok(cur, f):
            return f is None or cur is None or any(x in cur for x in SETS_OF[f])

        def tbl_next(cur, f):
            if cur is None:
                return SETS_OF[f]
            inter = tuple(x for x in SETS_OF[f] if x in cur)
            return inter if inter else SETS_OF[f]
        dma_free = 0.0
        done = 0
        while done < n:
            best = None
            for e in self.ENGS:
                if not ready[e]:
                    continue
                cand = ready[e][0]
                i = cand[1]
                st = max(free[e], ready_t[i])
                if best is None or st < best[0]:
                    best = (st, e)
            st, e = best
            heap = ready[e]
            pick = None
            tmp = []
            fallback = None
            while heap:
                c = heapq.heappop(heap)
                if ready_t[c[1]] <= max(free[e], st) + 1e-9:
                    if e != "act" or tbl_ok(cur_tbl, ops[c[1]].tbl):
                        pick = c
                        break
                    if fallback is None:
                        fallback = c
                        if len(tmp) > 24:
                            break
                        continue
                tmp.append(c)
                if len(tmp) > 64:
                    break
            if pick is None and fallback is not None:
                pick = fallback
                fallback = None
            if fallback is not None:
                tmp.append(fallback)
            for c in tmp:
                heapq.heappush(heap, c)
            if pick is None:
                pick = heapq.heappop(heap)
            i = pick[1]
            o = ops[i]
            start = max(free[e], ready_t[i])
            if e == "act" and o.tbl is not None:
                if not tbl_ok(cur_tbl, o.tbl):
                    start += 1.3
                cur_tbl = tbl_next(cur_tbl, o.tbl)
            if o.is_dma:
                free[e] = start + o.cost
                t0 = max(start + o.cost, dma_free)
                dma_free = t0 + o.nbytes / 250e3
                fin[i] = dma_free + 2.0
            else:
                free[e] = start + o.cost
                fin[i] = free[e]
            order[e].append(o)
            done += 1
            for j in succ[i]:
                lat = 0.0 if (ops[j].eng == e and not o.is_dma) else 0.15
                if fin[i] + lat > ready_t[j]:
                    ready_t[j] = fin[i] + lat
                npred[j] -= 1
                if npred[j] == 0:
                    heapq.heappush(ready[ops[j].eng], (-prio[j], j))
        self.est_time = max(fin) if n else 0.0
        if self.sched_window > 0:
            return self.schedule_window(succ)
        return order

    def schedule_window(self, succ):
        ops = self.ops
        n = len(ops)
        W = self.sched_window
        npred = [0] * n
        for i in range(n):
            for j in succ[i]:
                npred[j] += 1
        orig = {e: [o.idx for o in ops if o.eng == e] for e in self.ENGS}
        pos = {e: 0 for e in self.ENGS}
        taken = [False] * n
        order = {e: [] for e in self.ENGS}
        done = 0
        while done < n:
            progressed = False
            for e in self.ENGS:
                lst = orig[e]
                while pos[e] < len(lst) and taken[lst[pos[e]]]:
                    pos[e] += 1
                for k in range(pos[e], min(pos[e] + W, len(lst))):
                    i = lst[k]
                    if not taken[i] and npred[i] == 0:
                        taken[i] = True
                        order[e].append(ops[i])
                        for j in succ[i]:
                            npred[j] -= 1
                        done += 1
                        progressed = True
                        break
            assert progressed
        return order

    def emit(self):
        nc = self.nc
        cnt = {e: 0 for e in self.ENGS}
        dma_rr = {e: 0 for e in self.ENGS}
        dma_cnt = {}
        if self.do_schedule:
            per = self.schedule()
        else:
            per = {e: [o for o in self.ops if o.eng == e] for e in self.ENGS}
        for o in [o for e in self.ENGS for o in per[e]]:
            if o.is_dma:
                k = dma_rr[o.eng] % self.n_dma_sems
                dma_rr[o.eng] += 1
                key = (o.eng, k)
                dma_cnt[key] = dma_cnt.get(key, 0) + 16
                o.dsem = key
                o.dval = dma_cnt[key]
            elif o.signal:
                cnt[o.eng] += 1
                o.val = cnt[o.eng]
        with contextlib.ExitStack() as st:
            sems = {e: st.enter_context(nc.semaphore("s_" + e)) for e in ("pe", "act", "dve", "pool")}
            dsems = {}
            for e in self.ENGS:
                for k in range(min(self.n_dma_sems, dma_rr[e])):
                    dsems[(e, k)] = st.enter_context(nc.semaphore("d_%s_%d" % (e, k)))
            block = st.enter_context(nc.Block())

            def run(ename):
                def body(eng):
                    waited = {}
                    for o in per[ename]:
                        need = {}
                        for d in o.deps:
                            if d.is_dma:
                                s, v, sk = dsems[d.dsem], d.dval, ("d",) + d.dsem
                            else:
                                s, v, sk = sems[d.eng], d.val, ("c", d.eng)
                            if need.get(sk, (None, 0))[1] < v:
                                need[sk] = (s, v)
                        if o.is_dma and o.dval > 16:
                            sk = ("d",) + o.dsem
                            if need.get(sk, (None, 0))[1] < o.dval - 16:
                                need[sk] = (dsems[o.dsem], o.dval - 16)
                        for sk, (s, v) in need.items():
                            if waited.get(sk, 0) >= v:
                                continue
                            eng.wait_ge(s, v)
                            waited[sk] = v
                        ins = o.fn(eng)
                        if o.is_dma:
                            ins.then_inc(dsems[o.dsem], 16)
                        elif o.signal:
                            ins.then_inc(sems[ename], 1)
                    if ename == "sp":
                        for key, v in dma_cnt.items():
                            eng.wait_ge(dsems[key], v)
                        for e in ("pe", "act", "dve", "pool"):
                            if cnt[e]:
                                eng.wait_ge(sems[e], cnt[e])
                return body

            block.tensor(run("pe"))
            block.scalar(run("act"))
            block.vector(run("dve"))
            block.gpsimd(run("pool"))
            block.sync(run("sp"))


def bc(ap, shape, axis):
    return ap.unsqueeze(axis).to_broadcast(shape)


class Builder:
    def __init__(self, dbg=None):
        self.dbg = dbg or {}
        self.nc = nc = bass.Bass("TRN2", target_bir_lowering=False)
        self.P = Prog(nc)
        self.st = contextlib.ExitStack()
        self.wrr = 0

    def sb(self, name, shape, dt=F32):
        return self.st.enter_context(self.nc.sbuf_tensor(name, shape, dt))

    def din(self, name, shape, dt=F32):
        return self.nc.dram_tensor(name, shape, dt, kind="ExternalInput").ap()

    def dout(self, name, shape, dt=F32):
        return self.nc.dram_tensor(name, shape, dt, kind="ExternalOutput").ap()

    def declare_io(self):
        L = DEPTH
        self.x_prompt = self.din("x_prompt", [SEQ, D])
        self.x_sample = self.din("x_sample", [NS, D])
        self.cache_k = self.din("cache_swa_k", [L, NS, 128, 128])
        self.cache_v = self.din("cache_swa_v", [L, NS, 128, 128])
        self.st_lru_conv = self.din("state_lru_conv", [L, NS, 3, 768])
        self.st_lru_h = self.din("state_lru_h", [L, NS, 768])
        self.st_ssd_conv = self.din("state_ssd_conv", [L, NS, 3, 1280])
        self.st_ssd_h = self.din("state_ssd_h", [L, NS, 12, 64, 128])
        self.w_in = self.din("w_in", [L, D, DIN])
        self.w_out = self.din("w_out", [L, 2048, D])
        self.att_sinks = self.din("att_sinks", [L, 8])
        self.lru_conv_w = self.din("lru_conv_w", [L, 4, 768])
        self.lru_conv_b = self.din("lru_conv_b", [L, 768])
        self.lru_wa = self.din("lru_wa", [L, 8, 96, 96])
        self.lru_ba = self.din("lru_ba", [L, 768])
        self.lru_wx = self.din("lru_wx", [L, 8, 96, 96])
        self.lru_bx = self.din("lru_bx", [L, 768])
        self.lru_lambda = self.din("lru_lambda", [L, 768])
        self.ssd_conv_w = self.din("ssd_conv_w", [L, 4, 1280])
        self.ssd_conv_b = self.din("ssd_conv_b", [L, 1280])
        self.ssd_dt_bias = self.din("ssd_dt_bias", [L, 12])
        self.ssd_a_log = self.din("ssd_a_log", [L, 12])
        self.ssd_d = self.din("ssd_d", [L, 12])
        self.ssd_norm_g = self.din("ssd_norm_g", [L, 768])
        self.ln_g = self.din("ln_g", [L, D])
        self.ln_b = self.din("ln_b", [L, D])
        self.consts = self.din("consts", [128, K_END])
        self.y_prompt = self.dout("y_prompt", [SEQ, D])
        self.y_sample = self.dout("y_sample", [NS, D])
        self.p_swa_k = self.dout("p_swa_k", [L, 128, 128])
        self.p_swa_v = self.dout("p_swa_v", [L, 128, 128])
        self.p_lru_conv = self.dout("p_lru_conv", [L, 3, 768])
        self.p_lru_h = self.dout("p_lru_h", [L, 768])
        self.p_ssd_conv = self.dout("p_ssd_conv", [L, 3, 1280])
        self.p_ssd_h = self.dout("p_ssd_h", [L, 768, 128])
        self.s_swa_k = self.dout("s_swa_k", [L, NS, 128, 128])
        self.s_swa_v = self.dout("s_swa_v", [L, NS, 128, 128])
        self.s_lru_conv = self.dout("s_lru_conv", [L, NS, 3, 768])
        self.s_lru_h = self.dout("s_lru_h", [L, NS, 768])
        self.s_ssd_conv = self.dout("s_ssd_conv", [L, NS, 3, 1280])
        self.s_ssd_h = self.dout("s_ssd_h", [L, NS, 768, 128])
        self.w_in_bf = self.nc.dram_tensor("w_in_bf", [L, D, DIN], BF16).ap()
        self.w_out_bf = self.nc.dram_tensor("w_out_bf", [L, 2048, D], BF16).ap()
        self.dbg_out = {k: self.dout("dbg_" + k, list(v)) for k, v in self.dbg.items()}

    def tap(self, name, ap, reads):
        if name in self.dbg_out:
            self.P.dma("pool", self.dbg_out[name], ap, reads=reads)

    def alloc(self):
        sb = self.sb
        self.cst = sb("cst", [128, K_END])
        self.ident_bf = sb("ident_bf", [128, 128], BF16)
        self.maskbf = sb("maskbf", [128, 2, 256], BF16)
        self.cosT = sb("cosT", [128, 16, 8]); self.sinT = sb("sinT", [128, 16, 8]); self.nsinT = sb("nsinT", [128, 16, 8])
        self.ropeS = sb("ropeS", [128, 3, 8])
        self.xT32 = sb("xT32", [128, 8, H])
        self.xTb = sb("xTb", [128, 8, H], BF16)
        self.mixT = sb("mixT", [128, 18, H], BF16)
        self.wbuf = [sb("wbuf%d" % i, [128, 8, 128], BF16) for i in range(6)]
        self.wpool = {"att": [0], "lru": [1, 2], "ssd": [3, 4], "out": [5, 0, 1], "all": [0, 1, 2, 3, 4, 5]}
        self.wcnt = {k: 0 for k in self.wpool}
        self.wqkv = sb("wqkv", [128, 8, 768], BF16)
        self.kT = sb("kT", [128, 5 * 128], BF16)
        self.vtm = sb("vtm", [128, 5, 128], BF16)
        self.ck = sb("ck", [128, DEPTH, 128], BF16); self.cv = sb("cv", [128, DEPTH, 128], BF16)
        self.hist_l = sb("hist_l", [128, DEPTH, 8, 3]); self.hist_s = sb("hist_s", [128, DEPTH, 10, 3])
        self.hcar = sb("hcar", [128, DEPTH, 8])
        self.hT32 = sb("hT32", [128, DEPTH, 768]); self.hTb = sb("hTb", [128, 768], BF16)
        self.lp = sb("lp", [128, DEPTH, 8, 8])
        self.la = sb("la", [128, DEPTH, 8, 4])
        self.spp = sb("spp", [128, DEPTH, 10, 5])
        self.gS = sb("gS", [128, DEPTH, 6])
        self.lnp = sb("lnp", [128, DEPTH, 8, 2])
        self.sinkb = sb("sinkb", [128, DEPTH, 8]); self.nsinkb = sb("nsinkb", [128, DEPTH, 8])
        self.Ab = sb("Ab", [128, DEPTH, 12]); self.Db = sb("Db", [128, DEPTH, 12])
        self.dtb = sb("dtb", [128, DEPTH])
        self.wab = sb("wab", [128, 8, 96], BF16); self.wxb = sb("wxb", [128, 8, 96], BF16)
        self.pstage = self.xT32[:].rearrange("p a b -> p (a b)")[0:8, 0:1280]
        self.adc = sb("adc", [128, DEPTH, 2]); self.sinkc = sb("sinkc", [128, DEPTH, 2])
        self.qkv32 = [sb("qkv32_%d" % i, [128, 768]) for i in range(2)]
        self.qkbf = sb("qkbf", [128, 640], BF16)
        self.ropet = sb("ropet", [128, 2, 10, 16])
        self.qT = sb("qT", [128, 4, H], BF16)
        self.G = [sb("G%d" % i, [128, H]) for i in range(2)]
        self.Pm = [sb("Pm%d" % i, [128, 512], BF16) for i in range(2)]
        self.PT = [sb("PT%d" % i, [128, 512], BF16) for i in range(2)]
        self.ast = [sb("ast%d" % i, [128, 8, 2]) for i in range(2)]
        self.xpad2 = sb("xpad2", [128, H + 3])
        self.xpad = sb("xpad", [128, H + 3]); self.xl = sb("xl", [128, H]); self.xlb = sb("xlb", [128, H], BF16)
        self.rr = sb("rr", [128, H]); self.ig = sb("ig", [128, H]); self.EE = sb("EE", [128, H])
        self.zs = sb("zs", [128, 6, H]); self.xsT = sb("xsT", [128, 6, H], BF16)
        self.BT = sb("BT", [128, 2, H], BF16); self.CT = sb("CT", [128, 2, H], BF16)
        self.dtT = sb("dtT", [128, H]); self.dtt = sb("dtt", [128, H])
        self.xstm = sb("xstm", [128, 768], BF16); self.Btm = sb("Btm", [128, 256], BF16)
        self.tsm = sb("tsm", [128, 8, 12])
        self.aU = sb("aU", [128, 12, 128]); self.Dm = sb("Dm", [128, 12, 128])
        self.Ebc = sb("Ebc", [128, 12, 128], BF16); self.GT = sb("GT", [128, 12, 128], BF16)
        self.CE = sb("CE", [128, 12, 128], BF16); self.CBm = sb("CBm", [128, 2, 128])
        self.xdt = sb("xdt", [128, 768], BF16); self.xdd = sb("xdd", [128, 768], BF16); self.xsD = sb("xsD", [128, 768], BF16)
        self.yg = sb("yg", [128, 6, 128]); self.sq = sb("sq", [128, 6, 128]); self.rstd = sb("rstd", [128, 128])
        self.lnq = [sb("lnq%d" % i, [128, H]) for i in range(2)]
        self.mean = sb("mean", [128, H]); self.lrs = sb("lrs", [128, H])
        self.iost = sb("iost", [128, D])
        self.ps = self.st.enter_context(self.nc.psum_tensor("ps", [128, 4096], F32))

    def bank(self, b, n=1):
        return self.ps[:, b * 512:(b + n) * 512]

    def bankbf(self, b):
        return self.ps[:, b * 512:(b + 1) * 512].bitcast(BF16)

    def pT(self, b):
        return self.P.t("ps", b)

    @staticmethod
    def _fs(ap):
        n = 1
        for d in ap.shape[1:]:
            n *= d
        return n

    def _c(self, eng, out, in_=None):
        F = self._fs(out)
        ps = (in_ is not None and str(in_.space).lower().find("psum") >= 0)
        if eng == "act":
            return (224 + F) / 1200.0
        if eng == "pool":
            return (100 + 2 * F) / 1200.0
        return 1.2 * ((120 if ps else 60) + F) / 960.0

    TBL = {AF.Exp: "E", AF.Tanh: "A", AF.Ln: "L", AF.Sqrt: "S", AF.Sigmoid: "G", AF.Silu: "U", AF.Sin: "N"}

    def act(self, out, in_, func, r, w, **kw):
        o = self.P.op("act", lambda e: e.activation(out=out, in_=in_, func=func, **kw), r, w, cost=self._c("act", out))
        o.tbl = self.TBL.get(func)
        return o

    def acopy(self, out, in_, r, w):
        return self.P.op("act", lambda e: e.copy(out=out, in_=in_), r, w, cost=self._c("act", out))

    def vcopy(self, out, in_, r, w, eng="dve"):
        return self.P.op(eng, lambda e: e.tensor_copy(out=out, in_=in_), r, w, cost=self._c(eng, out, in_))

    def tt(self, out, a, b, op, r, w, eng="dve"):
        return self.P.op(eng, lambda e: e.tensor_tensor(out=out, in0=a, in1=b, op=op), r, w, cost=self._c(eng, out, a))

    def ts(self, out, a, s1, s2, op0, op1, r, w, eng="dve"):
        c = self._c(eng, out, a)
        if op1 is None:
            return self.P.op(eng, lambda e: e.tensor_scalar(out=out, in0=a, scalar1=s1, scalar2=None, op0=op0), r, w, cost=c)
        return self.P.op(eng, lambda e: e.tensor_scalar(out=out, in0=a, scalar1=s1, scalar2=s2, op0=op0, op1=op1), r, w, cost=c)

    def stt(self, out, a, s, b, op0, op1, r, w, eng="dve"):
        return self.P.op(eng, lambda e: e.scalar_tensor_tensor(out=out, in0=a, scalar=s, in1=b, op0=op0, op1=op1), r, w,
                         cost=self._c(eng, out, a))

    def mm(self, out, lhsT, rhs, start, stop, r, w):
        N = self._fs(out)
        k = 4.0 if rhs.dtype == F32 else 1.0
        c = 1.4 * (max(N, 64) * k / 2400.0 + 0.035)
        return self.P.op("pe", lambda e: e.matmul(out, lhsT=lhsT, rhs=rhs, start=start, stop=stop), r, w,
                         cost=(-c if k > 1 else c))

    def tr(self, out, in_, ident, r, w):
        N = self._fs(in_)
        k = 4.0 if in_.dtype == F32 else 1.0
        c = 1.4 * (max(N, 64) * k / 2400.0 + 0.06)
        return self.P.op("pe", lambda e: e.transpose(out, in_, ident), r, w, cost=(-c if k > 1 else c))

    def flush_casts(self, l):
        for (dst, src, tl) in self.pending_casts.get(l, []):
            self.P.dma("pool", dst, src, writes=[tl])
        self.pending_casts[l] = []

    def pace_cast(self, thr):
        l = getattr(self, "cast_layer", None)
        if l is None:
            return
        if l == 1 and self.pending_casts.get(0):
            l = 0
        if not self.pending_casts.get(l):
            return
        dst, src, tl = self.pending_casts[l].pop(0)
        self.P.dma("pool", dst, src, reads=[thr], writes=[tl])

    def thr_tile(self):
        self.thr_n = getattr(self, "thr_n", 0) + 1
        return self.P.t("thr", self.thr_n)

    def next_wbuf(self, pool="all"):
        lst = self.wpool[pool]
        k = lst[self.wcnt[pool] % len(lst)]
        self.wcnt[pool] += 1
        return k

    def win_T(self, l):
        return [self.P.t("winbf", l, i) for i in range(8)]

    def wout_T(self, l):
        return [self.P.t("woutbf", l, i) for i in range(16)]

    def load_win(self, l, pieces, pool="all"):
        k = self.next_wbuf(pool)
        src = self.w_in_bf[l].rearrange("(kc p) c -> p kc c", p=128)
        for (c0, n, d0) in pieces:
            th = self.thr_tile()
            self.P.dma("sp", self.wbuf[k][:, :, d0:d0 + n], src[:, :, c0:c0 + n],
                       reads=self.win_T(l), writes=[self.P.t("wbuf", k), th])
            self.pace_cast(th)
        return k

    def inproj(self, l, k, M, psout, psT, ncols=H, x=None, xT=None):
        x = self.xTb if x is None else x
        xT = [self.P.t("xTb", kc) for kc in range(8)] if xT is None else xT
        for kc in range(8):
            self.mm(psout[0:M, 0:ncols], self.wbuf[k][:, kc, 0:M], x[:, kc, 0:ncols], kc == 0, kc == 7,
                    [self.P.t("wbuf", k)] + xT, [psT])

    def prologue(self):
        P, t = self.P, self.P.t
        cst = self.cst
        P.dma("sp", cst[:], self.consts, writes=[t("cst")])
        self.pending_casts = {l: [] for l in range(DEPTH)}
        for l in range(DEPTH):
            for i in range(8):
                self.pending_casts[l].append((self.w_in_bf[l, i * 128:(i + 1) * 128, :], self.w_in[l, i * 128:(i + 1) * 128, :], t("winbf", l, i)))
            for i in range(16):
                self.pending_casts[l].append((self.w_out_bf[l, i * 128:(i + 1) * 128, :], self.w_out[l, i * 128:(i + 1) * 128, :], t("woutbf", l, i)))
        for (dst, src, tl) in self.pending_casts[0][:8]:
            P.dma("pool", dst, src, writes=[tl])
        self.pending_casts[0] = self.pending_casts[0][8:]
        self.vcopy(self.ident_bf[:], cst[:, K_ID:K_ID + 128], [t("cst")], [t("ident_bf")])
        self.vcopy(self.maskbf[:, 0, :], cst[:, K_MG:K_MG + 256], [t("cst")], [t("maskbf")])
        self.vcopy(self.maskbf[:, 1, :], cst[:, K_MF:K_MF + 256], [t("cst")], [t("maskbf")])
        for buf, nm in ((self.ck, "ck"), (self.cv, "cv")):
            P.op("dve", lambda e, buf=buf: e.memset(buf[:], 0.0), [], [t(nm, l) for l in range(DEPTH)])
        P.op("dve", lambda e: e.memset(self.hist_l[:], 0.0), [], [t("hist_l", l) for l in range(DEPTH)])
        P.op("dve", lambda e: e.memset(self.hist_s[:], 0.0), [], [t("hist_s", l) for l in range(DEPTH)])
        P.op("dve", lambda e: e.memset(self.hcar[:], 0.0), [], [t("hcar", l) for l in range(DEPTH)])
        P.op("dve", lambda e: e.memset(self.hT32[:], 0.0), [], [t("hT32", l, g) for l in range(DEPTH) for g in range(2)])
        self.rope_tables()
        for l in range(DEPTH):
            self.layer_params(l)

    def sincos(self, ang, n, outs, rd):
        P, t = self.P, self.P.t
        tmp = self.iost
        c1 = float(np.float32(2 * np.pi)); c2 = float(2 * np.pi - c1)
        for j, (shift, out) in enumerate(((0.5 * np.pi, outs[0]), (0.0, outs[1]))):
            a = tmp[:, 0:n]; kf = tmp[:, n:2 * n]; ki = tmp[:, 2 * n:3 * n].bitcast(I32); m = tmp[:, 3 * n:4 * n]
            W = [t("iost")]
            self.ts(a, ang, float(shift), None, ALU.add, None, rd + W, W)
            self.ts(kf, a, float(1.0 / (2 * np.pi)), None, ALU.mult, None, W, W)
            self.vcopy(ki, kf, W, W)
            self.vcopy(kf, ki, W, W)
            self.stt(a, kf, -c1, a, ALU.mult, ALU.add, W, W)
            self.stt(a, kf, -c2, a, ALU.mult, ALU.add, W, W)
            self.ts(m, a, float(np.pi), float(-2 * np.pi), ALU.is_gt, ALU.mult, W, W)
            self.tt(a, a, m, ALU.add, W, W)
            self.ts(m, a, float(-np.pi), float(2 * np.pi), ALU.is_lt, ALU.mult, W, W)
            self.tt(a, a, m, ALU.add, W, W)
            self.act(out, a, AF.Sin, W, [t("rope")])
        self.ts(outs[2], outs[1], -1.0, None, ALU.mult, None, [t("rope")], [t("rope")])

    def rope_tables(self):
        t = self.P.t
        cst = self.cst
        ang = self.iost[:, 512:640]
        self.tt(ang.rearrange("p (a b) -> p a b", b=8), bc(cst[:, K_POS:K_POS + 16], [128, 16, 8], 2),
                bc(cst[:, K_INV:K_INV + 8], [128, 16, 8], 1), ALU.mult, [t("cst")], [t("iost")])
        f = lambda x: x[:].rearrange("p a b -> p (a b)")
        self.sincos(ang, 128, (f(self.cosT), f(self.sinT), f(self.nsinT)), [t("iost")])
        angs = self.iost[:, 640:648]
        self.ts(angs, cst[:, K_INV:K_INV + 8], PAST, None, ALU.mult, None, [t("cst")], [t("iost")])
        self.sincos(angs, 8, (self.ropeS[:, 0, :], self.ropeS[:, 1, :], self.ropeS[:, 2, :]), [t("iost")])

    def layer_params(self, l):
        P, t = self.P, self.P.t
        cst = self.cst
        idf = cst[:, K_ID:K_ID + 128]
        stg = self.pstage
        S = [t("xT32", dc) for dc in range(8)]
        W = [t("par", l)]
        P.dma("sp", stg[0:4, 0:768], self.lru_conv_w[l], writes=S)
        for i, src in enumerate((self.lru_conv_b, self.lru_ba, self.lru_bx, self.lru_lambda)):
            P.dma("sp", stg[4 + i:5 + i, 0:768], src[l:l + 1, :], writes=S)
        pb = self.bank(0)
        for n in range(8):
            self.tr(pb[0:96, n * 8:(n + 1) * 8], stg[0:8, n * 96:(n + 1) * 96], idf[0:8, 0:8], S + [t("cst")], [self.pT(0)])
        self.vcopy(self.lp[0:96, l, :, :], pb[0:96, 0:64].rearrange("p (a b) -> p a b", b=8), [self.pT(0)], W)
        la = self.la
        self.act(la[0:96, l, :, 0], self.lp[0:96, l, :, 7], AF.Exp, W, W, scale=-1.0)
        self.act(la[0:96, l, :, 0], la[0:96, l, :, 0], AF.Ln, W, W, bias=1.0)
        self.ts(la[0:96, l, :, 1], la[0:96, l, :, 0], -8.0, None, ALU.mult, None, W, W)
        self.ts(la[0:96, l, :, 0], la[0:96, l, :, 0], -4.0, None, ALU.mult, None, W, W)
        self.ts(la[0:96, l, :, 2:4], self.lp[0:96, l, :, 5:7], 0.5, None, ALU.mult, None, W, W)
        P.dma("sp", stg[0:4, 0:1280], self.ssd_conv_w[l], writes=S)
        P.dma("sp", stg[4:5, 0:1280], self.ssd_conv_b[l:l + 1, :], writes=S)
        pb = self.bank(1)
        for c in range(10):
            self.tr(pb[:, c * 5:(c + 1) * 5], stg[0:5, c * 128:(c + 1) * 128], idf[0:5, 0:5], S + [t("cst")], [self.pT(1)])
        self.ts(self.spp[:, l, :, :], pb[:, 0:50].rearrange("p (a b) -> p a b", b=5), 0.5, None, ALU.mult, None, [self.pT(1)], W)
        P.dma("sp", stg[0:1, 0:768], self.ssd_norm_g[l:l + 1, :], writes=S)
        P.dma("sp", stg[1:2, 0:1024], self.ln_g[l:l + 1, :], writes=S)
        P.dma("sp", stg[2:3, 0:1024], self.ln_b[l:l + 1, :], writes=S)
        pb = self.bank(2)
        for c in range(6):
            self.tr(pb[:, c:c + 1], stg[0:1, c * 128:(c + 1) * 128], idf[0:1, 0:1], S + [t("cst")], [self.pT(2)])
        self.ts(self.gS[:, l, :], pb[:, 0:6], float(math.sqrt(768.0)), None, ALU.mult, None, [self.pT(2)], W)
        P.dma("sp", stg[0:1, 0:1024], self.ln_g[l:l + 1, :], writes=S)
        P.dma("sp", stg[1:2, 0:1024], self.ln_b[l:l + 1, :], writes=S)
        pb = self.bank(3)
        for c in range(8):
            self.tr(pb[:, 2 * c:2 * c + 2], stg[0:2, c * 128:(c + 1) * 128], idf[0:2, 0:2], S + [t("cst")], [self.pT(3)])
        self.vcopy(self.lnp[:, l, :, :], pb[:, 0:16].rearrange("p (a b) -> p a b", b=2), [self.pT(3)], W)
        P.dma("sp", self.sinkb[:, l, :], self.att_sinks[l:l + 1, :].partition_broadcast(128), writes=W)
        self.ts(self.nsinkb[:, l, :], self.sinkb[:, l, :], -1.0, None, ALU.mult, None, W, W)
        P.dma("sp", self.Ab[:, l, :], self.ssd_a_log[l:l + 1, :].partition_broadcast(128), writes=W)
        self.act(self.Ab[:, l, :], self.Ab[:, l, :], AF.Exp, W, W)
        self.ts(self.Ab[:, l, :], self.Ab[:, l, :], -1.0, None, ALU.mult, None, W, W)
        P.dma("sp", self.Db[:, l, :], self.ssd_d[l:l + 1, :].partition_broadcast(128), writes=W)
        P.dma("sp", self.dtb[0:12, l:l + 1], self.ssd_dt_bias[l].rearrange("(a b) -> a b", b=1), writes=W)
        P.dma("sp", self.adc[0:12, l, 0:1], self.ssd_a_log[l].rearrange("(a b) -> a b", b=1), writes=W)
        self.act(self.adc[0:12, l, 0:1], self.adc[0:12, l, 0:1], AF.Exp, W, W)
        self.ts(self.adc[0:12, l, 0:1], self.adc[0:12, l, 0:1], -1.0, None, ALU.mult, None, W, W)
        P.dma("sp", self.adc[0:12, l, 1:2], self.ssd_d[l].rearrange("(a b) -> a b", b=1), writes=W)
        P.dma("sp", self.sinkc[0:8, l, 0:1], self.att_sinks[l].rearrange("(a b) -> a b", b=1), writes=W)
        self.ts(self.sinkc[0:8, l, 1:2], self.sinkc[0:8, l, 0:1], -1.0, None, ALU.mult, None, W, W)

    def job(self, l, p):
        ph = getattr(self, "phases", "xjqalso")
        self.cast_layer = (l + 1) if (p == 0 and l + 1 < DEPTH) else None
        if "x" in ph: self.load_x(l, p)
        if "j" in ph: self.job_prologue(l, p)
        if "q" in ph: self.phase_qkv(l, p)
        if "a" in ph: self.phase_att(l, p)
        if "s" in ph: self.phase_ssd(l, p)
        if "l" in ph: self.phase_lru(l, p)
        if "s" in ph: self.phase_ssd_tiles(l, p)
        if l == 0 and p == 0:
            self.tap("mixT", self.mixT[:], [self.P.t("mixT", ec, tl) for ec in range(18) for tl in range(NT)])
        if "o" in ph: self.phase_out(l, p)
        if self.cast_layer is not None:
            self.flush_casts(0)
            self.flush_casts(self.cast_layer)
        self.cast_layer = None

    def load_x(self, l, p):
        P, t = self.P, self.P.t
        if l != 0:
            return
        idf = self.cst[:, K_ID:K_ID + 128]
        for tl in range(NT):
            gt = p * NT + tl
            P.dma("sp", self.iost[:], self.x_prompt[gt * 128:(gt + 1) * 128, :], writes=[t("iost")])
            for hb in range(2):
                pb = self.bank(hb)
                for j in range(4):
                    dc = hb * 4 + j
                    self.tr(pb[:, j * 128:(j + 1) * 128], self.iost[:, dc * 128:(dc + 1) * 128], idf,
                            [t("iost"), t("cst")], [self.pT(hb)])
                dst = self.xT32[:, hb * 4:hb * 4 + 4, tl * 128:(tl + 1) * 128]
                self.vcopy(dst, pb.rearrange("p (a b) -> p a b", b=128), [self.pT(hb)],
                           [t("xT32", dc) for dc in range(hb * 4, hb * 4 + 4)])
        for dc in range(8):
            self.acopy(self.xTb[:, dc, :], self.xT32[:, dc, :], [t("xT32", dc)], [t("xTb", dc)])
        if p == 0:
            self.tap("xT_in", self.xT32[:], [t("xT32", dc) for dc in range(8)])

    def job_prologue(self, l, p):
        P, t = self.P, self.P.t
        self.vcopy(self.kT[:, 0:128], self.ck[:, l, :], [t("ck", l)], [t("kT", 0)], eng="pool")
        self.vcopy(self.vtm[:, 0, :], self.cv[:, l, :], [t("cv", l)], [t("v", 0)], eng="pool")
        self.vcopy(self.hTb[:], self.hT32[:, l, :], [t("hT32", l, 0), t("hT32", l, 1)], [t("hTb", 0), t("hTb", 1)], eng="pool")
        P.dma("pool", self.wab[0:96, :, :], self.lru_wa[l].rearrange("n c d -> c n d"), writes=[t("wab")])
        P.dma("pool", self.wxb[0:96, :, :], self.lru_wx[l].rearrange("n c d -> c n d"), writes=[t("wxb")])
        src = self.w_in_bf[l].rearrange("(kc p) c -> p kc c", p=128)
        P.dma("sp", self.wqkv[:], src[:, :, 0:768], reads=self.win_T(l), writes=[t("wqkv", j) for j in range(6)])

    def phase_qkv(self, l, p):
        P, t = self.P, self.P.t
        xT = [t("xTb", kc) for kc in range(8)]
        last = (p == NPASS - 1)
        for tl in range(NT):
            gt = p * NT + tl
            pa, pb_, pc = (0, 1, 4) if tl % 2 == 0 else (2, 3, 5)
            A, B = self.bank(pa), self.bank(pb_)
            for kc in range(8):
                lhs = self.xTb[:, kc, tl * 128:(tl + 1) * 128]
                WQ = [t("wqkv", j) for j in range(6)]
                self.mm(A, lhs, self.wqkv[:, kc, 0:512], kc == 0, kc == 7, xT + WQ, [self.pT(pa)])
                self.mm(B[:, 0:256], lhs, self.wqkv[:, kc, 512:768], kc == 0, kc == 7, xT + WQ, [self.pT(pb_)])
            q32 = self.qkv32[tl % 2]
            Q = [t("qkv32", tl % 2)]
            self.acopy(q32[:, 0:512], A, [self.pT(pa)], Q)
            self.acopy(q32[:, 512:768], B[:, 0:256], [self.pT(pb_)], Q)
            hv = q32[:, 0:640].rearrange("p (h d) -> p h d", d=64)
            x1, x2 = hv[:, :, 0:8], hv[:, :, 8:16]
            cs = bc(self.cosT[:, gt, :], [128, 10, 8], 1)
            sn = bc(self.sinT[:, gt, :], [128, 10, 8], 1)
            ns = bc(self.nsinT[:, gt, :], [128, 10, 8], 1)
            R = [t("ropet")]
            ra, rb = self.ropet[:, 0, :, :], self.ropet[:, 1, :, :]
            self.tt(rb[:, :, 0:8], x2, ns, ALU.mult, Q + [t("rope")], R)
            self.tt(rb[:, :, 8:16], x1, sn, ALU.mult, Q + [t("rope")], R)
            self.tt(ra[:, :, 0:8], x1, cs, ALU.mult, Q + [t("rope")], R)
            self.tt(ra[:, :, 8:16], x2, cs, ALU.mult, Q + [t("rope")], R)
            self.tt(hv[:, :, 0:16], ra, rb, ALU.add, R, Q)
            if l == 0 and p == 0 and tl == 1:
                self.tap("qkv_t1", q32[:], Q)
            if last and tl == NT - 1:
                P.dma("pool", self.p_swa_k[l], q32[:, 512:640], reads=Q)
                P.dma("pool", self.p_swa_v[l], q32[:, 640:768], reads=Q)
            self.acopy(self.qkbf[:, 0:512].rearrange("p (c w d) -> p c w d", c=4, w=2),
                       q32[:, 0:512].rearrange("p (w c d) -> p c w d", w=2, c=4), Q, [t("qkbf")])
            self.acopy(self.qkbf[:, 512:640], q32[:, 512:640], Q, [t("qkbf")])
            self.vcopy(self.vtm[:, 1 + tl, :], q32[:, 640:768], Q, [t("v", 1 + tl)], eng="pool")
            C = self.bankbf(pc)
            for j in range(5):
                self.tr(C[:, j * 128:(j + 1) * 128], self.qkbf[:, j * 128:(j + 1) * 128], self.ident_bf[:],
                        [t("qkbf"), t("ident_bf")], [self.pT(pc)])
            self.vcopy(self.qT[:, :, tl * 128:(tl + 1) * 128], C[:, 0:512].rearrange("p (c q) -> p c q", q=128),
                       [self.pT(pc)], [t("qT", tl)])
            self.vcopy(self.kT[:, (1 + tl) * 128:(2 + tl) * 128], C[:, 512:640], [self.pT(pc)], [t("kT", 1 + tl)])
        self.vcopy(self.ck[:, l, :], self.kT[:, 512:640], [t("kT", 4)], [t("ck", l)], eng="pool")
        self.vcopy(self.cv[:, l, :], self.vtm[:, 4, :], [t("v", 4)], [t("cv", l)], eng="pool")

    def phase_att(self, l, p):
        P, t = self.P, self.P.t
        cnt = 0
        for c in range(4):
            k = self.load_win(l, [(C_GA + c * 64, 64, 0), (C_GA + (c + 4) * 64, 64, 64)], pool="att")
            gb = c % 2
            pg = self.bank(3)
            self.inproj(l, k, 128, pg, self.pT(3))
            G = self.G[gb]
            self.act(G[:], pg, AF.Tanh, [self.pT(3)], [t("G", gb)], scale=0.5)
            self.stt(G[:], G[:], 1.0, pg, ALU.add, ALU.mult, [self.pT(3), t("G", gb)], [t("G", gb)])
            for tl in range(NT):
                gt = p * NT + tl
                i = cnt % 2
                cnt += 1
                bS, bP, bO = 1, 2, 3
                S = self.bank(bS)
                mk = self.maskbf[:, 1 if gt == 0 else 0, :]
                for h in range(2):
                    rows = slice(h * 64, (h + 1) * 64)
                    self.mm(S[:, h * 256:(h + 1) * 256], self.qT[rows, c, tl * 128:(tl + 1) * 128],
                            self.kT[rows, tl * 128:tl * 128 + 256], True, False,
                            [t("qT", tl), t("kT", tl), t("kT", tl + 1)], [self.pT(bS)])
                    self.mm(S[:, h * 256:(h + 1) * 256], self.ident_bf[:], mk, False, True,
                            [t("ident_bf"), t("maskbf")], [self.pT(bS)])
                st = self.ast[i]
                A = [t("ast", i)]
                mx, negm, ssum, es, den, rden = (st[:, j, :] for j in range(6))
                P.op("dve", lambda e, mx=mx, S=S: e.reduce_max(out=mx, in_=S.rearrange("p (h k) -> p h k", k=256), axis=AX.X),
                     [self.pT(bS)], A)
                hsel = self.nsinkb[:, l, c:c + 5:4]
                self.stt(negm, mx, -SCALE, hsel, ALU.mult, ALU.min, A + [t("par", l)], A)
                Pm = self.Pm[i]
                for h in range(2):
                    self.act(Pm[:, h * 256:(h + 1) * 256], S[:, h * 256:(h + 1) * 256], AF.Exp,
                             [self.pT(bS)] + A, [t("Pm", i)] + A, scale=SCALE, bias=negm[:, h:h + 1], accum_out=ssum[:, h:h + 1])
                self.tt(es, negm, self.sinkb[:, l, c:c + 5:4], ALU.add, A + [t("par", l)], A)
                self.act(es, es, AF.Exp, A, A)
                self.tt(den, ssum, es, ALU.add, A, A)
                P.op("dve", lambda e, rden=rden, den=den: e.reciprocal(out=rden, in_=den), A, A)
                pv = Pm[:].rearrange("p (h k) -> p h k", k=256)
                self.tt(pv, pv, bc(rden, [128, 2, 256], 2), ALU.mult, [t("Pm", i)] + A, [t("Pm", i)])
                PTp = self.bankbf(bP)
                for h in range(2):
                    for kb in range(2):
                        j = h * 2 + kb
                        self.tr(PTp[:, j * 128:(j + 1) * 128], Pm[:, h * 256 + kb * 128:h * 256 + (kb + 1) * 128],
                                self.ident_bf[:], [t("Pm", i), t("ident_bf")], [self.pT(bP)])
                PT = self.PT[i]
                self.acopy(PT[:], PTp[:, 0:512], [self.pT(bP)], [t("PT", i)])
                O = self.bank(bO)
                for h in range(2):
                    for kb in range(2):
                        j = h * 2 + kb
                        self.mm(O[h * 64:(h + 1) * 64, 0:128], self.vtm[:, tl + kb, h * 64:(h + 1) * 64],
                                PT[:, j * 128:(j + 1) * 128], kb == 0, kb == 1,
                                [t("v", tl + kb), t("PT", i)], [self.pT(bO)])
                self.stt(self.mixT[:, c, tl * 128:(tl + 1) * 128], O[:, 0:128], 0.5, G[:, tl * 128:(tl + 1) * 128], ALU.mult, ALU.mult,
                         [self.pT(bO), t("G", gb)], [t("mixT", c, tl)])

    def conv4(self, out, xpad, par, n, M, rd, wr):
        self.act(out[0:M, :], xpad[0:M, 0:H], AF.Identity, rd, wr, scale=par[0:M, 0:1], bias=par[0:M, 4:5])
        for k in range(1, 4):
            self.stt(out[0:M, :], xpad[0:M, k:k + H], par[0:M, k:k + 1], out[0:M, :], ALU.mult, ALU.add, rd + wr, wr)

    def phase_lru(self, l, p):
        P, t = self.P, self.P.t
        last = (p == NPASS - 1)
        for n in range(8):
            kx = self.load_win(l, [(C_XL + n * 96, 96, 0)], pool="lru")
            kg = self.load_win(l, [(C_GL + n * 96, 96, 0)], pool="lru")
            px, pg, pr, pi = self.bank(5), self.bank(6), self.bank(5), self.bank(7)
            pxT, pgT, prT, piT = self.pT(5), self.pT(6), self.pT(5), self.pT(7)
            self.inproj(l, kx, 96, px, pxT)
            self.inproj(l, kg, 96, pg, pgT)
            par = self.lp[:, l, n, :]
            PR = [t("par", l)]
            xp, xl, xlb, rr, ig, EE = self.xpad, self.xl, self.xlb, self.rr, self.ig, self.EE
            self.vcopy(xp[0:96, 0:3], self.hist_l[0:96, l, n, :], [t("hist_l", l)], [t("xpad")])
            self.acopy(xp[0:96, 3:3 + H], px[0:96, :], [pxT], [t("xpad")])
            self.vcopy(self.hist_l[0:96, l, n, :], xp[0:96, H:H + 3], [t("xpad")], [t("hist_l", l)])
            self.conv4(xl, xp, par, n, 96, [t("xpad")] + PR, [t("xl")])
            self.acopy(xlb[0:96, :], xl[0:96, :], [t("xl")], [t("xlb")])
            self.mm(pr[0:96, :], self.wab[0:96, n, :], xlb[0:96, :], True, True, [t("wab"), t("xlb")], [prT])
            self.mm(pi[0:96, :], self.wxb[0:96, n, :], xlb[0:96, :], True, True, [t("wxb"), t("xlb")], [piT])
            lac = self.la[0:96, l, n, :]
            self.act(rr[0:96, :], pr[0:96, :], AF.Tanh, [prT] + PR, [t("rr")], scale=0.5, bias=lac[:, 2:3])
            self.act(ig[0:96, :], pi[0:96, :], AF.Tanh, [piT] + PR, [t("ig")], scale=0.5, bias=lac[:, 3:4])
            self.act(EE[0:96, :], rr[0:96, :], AF.Exp, [t("rr")] + PR, [t("EE")], scale=lac[:, 1:2], bias=lac[:, 1:2])
            self.act(rr[0:96, :], rr[0:96, :], AF.Exp, [t("rr")] + PR, [t("rr")], scale=lac[:, 0:1], bias=lac[:, 0:1])
            self.act(EE[0:96, :], EE[0:96, :], AF.Sqrt, [t("EE")], [t("EE")], scale=-0.25, bias=0.25)
            self.stt(ig[0:96, :], ig[0:96, :], 1.0, xl[0:96, :], ALU.add, ALU.mult, [t("ig"), t("xl")], [t("ig")])
            self.tt(ig[0:96, :], ig[0:96, :], EE[0:96, :], ALU.mult, [t("ig"), t("EE")], [t("ig")])
            P.op("dve", lambda e, n=n: e.tensor_tensor_scan(out=EE[0:96, :], data0=rr[0:96, :], data1=ig[0:96, :],
                                                            initial=self.hcar[0:96, l, n:n + 1], op0=ALU.mult, op1=ALU.add),
                 [t("rr"), t("ig"), t("hcar", l)], [t("EE")])
            self.vcopy(self.hcar[0:96, l, n:n + 1], EE[0:96, H - 1:H], [t("EE")], [t("hcar", l)])
            self.act(xl[0:96, :], pg[0:96, :], AF.Tanh, [pgT], [t("xl")], scale=0.5)
            self.stt(xl[0:96, :], xl[0:96, :], 1.0, pg[0:96, :], ALU.add, ALU.mult, [pgT, t("xl")], [t("xl")])
            self.stt(self.mixT[0:96, 4 + n, :], EE[0:96, :], 0.5, xl[0:96, :], ALU.mult, ALU.mult, [t("EE"), t("xl")],
                    [t("mixT", 4 + n, tl) for tl in range(NT)])
        if last:
            for k3 in range(3):
                P.dma("pool", self.p_lru_conv[l, k3].rearrange("(n p) -> p n", p=96), self.hist_l[0:96, l, :, k3],
                      reads=[t("hist_l", l)], allow_slow_non_contiguous=True)
            P.dma("pool", self.p_lru_h[l].rearrange("(n p) -> p n", p=96), self.hcar[0:96, l, :],
                  reads=[t("hcar", l)], allow_slow_non_contiguous=True)

    def phase_ssd(self, l, p):
        P, t = self.P, self.P.t
        last = (p == NPASS - 1)
        cst = self.cst
        idf = cst[:, K_ID:K_ID + 128]
        U = cst[:, K_U:K_U + 128]
        ones = cst[:, K_ONE:K_ONE + 128]
        PR = [t("par", l)]
        nb = 0
        for c in range(6):
            k = self.load_win(l, [(C_Z + c * 128, 128, 0)], pool="ssd")
            b = (0, 4)[c % 2]
            pb = self.bank(b); pbT = self.pT(b)
            self.inproj(l, k, 128, pb, pbT)
            self.act(self.zs[:, c, :], pb, AF.Tanh, [pbT], [t("zs", c)], scale=0.5)
            self.stt(self.zs[:, c, :], self.zs[:, c, :], 1.0, pb, ALU.add, ALU.mult, [pbT, t("zs", c)], [t("zs", c)])
        xp, acc, tnh = self.xpad2, self.lnq[0], self.lnq[1]
        for c in range(10):
            k = self.load_win(l, [(C_XBC + c * 128, 128, 0)], pool="ssd")
            b = (0, 4)[c % 2]
            pb = self.bank(b); pbT = self.pT(b)
            self.inproj(l, k, 128, pb, pbT)
            self.vcopy(xp[:, 0:3], self.hist_s[:, l, c, :], [t("hist_s", l)], [t("xpad2")])
            self.acopy(xp[:, 3:3 + H], pb, [pbT], [t("xpad2")])
            self.vcopy(self.hist_s[:, l, c, :], xp[:, H:H + 3], [t("xpad2")], [t("hist_s", l)])
            self.conv4(acc, xp, self.spp[:, l, c, :], c, 128, [t("xpad2")] + PR, [t("lnq", 0)])
            if c < 6:
                dst, dT = self.xsT[:, c, :], t("xsT", c)
            elif c < 8:
                dst, dT = self.BT[:, c - 6, :], t("BT", c - 6)
            else:
                dst, dT = self.CT[:, c - 8, :], t("CT", c - 8)
            self.act(tnh[:], acc[:], AF.Tanh, [t("lnq", 0)], [t("lnq", 1)])
            self.stt(dst, tnh[:], 1.0, acc[:], ALU.add, ALU.mult, [t("lnq", 1), t("lnq", 0)], [dT])
        k = self.load_win(l, [(C_DT, 12, 0)], pool="ssd")
        b = 4
        pb = self.bank(b); pbT = self.pT(b)
        self.inproj(l, k, 12, pb, pbT)
        u, v = self.dtT[0:12, :], self.dtt[0:12, :]
        self.act(u, pb[0:12, :], AF.Identity, [pbT] + PR, [t("dtT")], bias=self.dtb[0:12, l:l + 1])
        self.act(v, u, AF.Abs, [t("dtT")], [t("dtt")])
        self.act(v, v, AF.Exp, [t("dtt")], [t("dtt")], scale=-1.0)
        self.act(v, v, AF.Ln, [t("dtt")], [t("dtt")], bias=1.0)
        self.stt(u, u, 0.0, v, ALU.max, ALU.add, [t("dtT"), t("dtt")], [t("dtT")])
        if last:
            for k3 in range(3):
                P.dma("pool", self.p_ssd_conv[l, k3].rearrange("(c p) -> p c", p=128), self.hist_s[:, l, :, k3],
                      reads=[t("hist_s", l)], allow_slow_non_contiguous=True)

    def phase_ssd_tiles(self, l, p):
        P, t = self.P, self.P.t
        last = (p == NPASS - 1)
        cst = self.cst
        idf = cst[:, K_ID:K_ID + 128]
        U = cst[:, K_U:K_U + 128]
        ones = cst[:, K_ONE:K_ONE + 128]
        PR = [t("par", l)]
        tsm = self.tsm
        dt_tm, a_tm, acs_tm, cd, tmp, dec, dtdec = (tsm[:, j, :] for j in range(7))
        TS = [t("tsm")]
        b1 = self.bank(1)
        for tl in range(NT):
            sl = slice(tl * 128, (tl + 1) * 128)
            self.tr(b1[:, 0:12], self.dtT[0:12, sl], idf[0:12, 0:12], [t("dtT"), t("cst")], [self.pT(1)])
            self.vcopy(dt_tm, b1[:, 0:12], [self.pT(1)], TS)
            self.tt(a_tm, dt_tm, self.Ab[:, l, :], ALU.mult, TS + PR, TS)
            self.mm(b1[:, 16:28], U, a_tm, True, True, [t("cst")] + TS, [self.pT(1)])
            self.vcopy(acs_tm, b1[:, 16:28], [self.pT(1)], TS)
            for g in range(2):
                hs = slice(6 * g, 6 * g + 6)
                cs_ = slice(384 * g, 384 * (g + 1))
                TSg = [t("tsm", g)]
                p0 = self.bankbf(0)
                for j in range(3):
                    c = 3 * g + j
                    self.tr(p0[:, j * 128:(j + 1) * 128], self.xsT[:, c, sl], self.ident_bf[:], [t("xsT", c), t("ident_bf")], [self.pT(0)])
                self.tr(p0[:, 384:512], self.BT[:, g, sl], self.ident_bf[:], [t("BT", g), t("ident_bf")], [self.pT(0)])
                self.acopy(self.xstm[:, cs_], p0[:, 0:384], [self.pT(0)], [t("xstm", g)])
                self.acopy(self.Btm[:, g * 128:(g + 1) * 128], p0[:, 384:512], [self.pT(0)], [t("Btm", g)])
                cb = b1[:, 128 * (g + 1):128 * (g + 2)]
                self.mm(cb, self.BT[:, g, sl], self.CT[:, g, sl], True, True, [t("BT", g), t("CT", g)], [self.pT(1)])
                self.tt(self.CBm[:, g, :], cb, U, ALU.mult, [self.pT(1), t("cst")], [t("CBm", g)])
                aUg = self.aU[:, hs, :]
                self.tt(aUg, bc(U, [128, 6, 128], 1), bc(a_tm[:, hs], [128, 6, 128], 2), ALU.mult, TS + [t("cst")], [t("aU", g)])
                A0 = (2, 5)[g]
                YB = (4, 7)[g]
                pacs = self.ps[:, A0 * 512:A0 * 512 + 768]
                pacsT = [self.pT(A0), self.pT(A0 + 1)]
                aUf = aUg.rearrange("p e q -> p (e q)")
                self.mm(pacs[:, 0:512], ones, aUf[:, 0:512], True, True, [t("cst"), t("aU", g)], [self.pT(A0)])
                self.mm(pacs[:, 512:768], ones, aUf[:, 512:768], True, True, [t("cst"), t("aU", g)], [self.pT(A0 + 1)])
                pav = pacs.rearrange("p (e q) -> p e q", q=128)
                Dmg = self.Dm[:, hs, :]
                for j in range(6):
                    e = 6 * g + j
                    self.ts(Dmg[:, j, :], pav[:, j, :], acs_tm[:, e:e + 1], 0.0, ALU.subtract, ALU.min,
                            [self.pT(A0 + j // 4)] + TS, [t("Dm", g)])
                self.act(Dmg, Dmg, AF.Exp, [t("Dm", g)], [t("Dm", g)])
                self.tt(self.GT[:, hs, :], Dmg, bc(self.CBm[:, g, :], [128, 6, 128], 1), ALU.mult, [t("Dm", g), t("CBm", g)], [t("GT", g)])
                self.act(self.Ebc[:, hs, :], pav, AF.Exp, pacsT, [t("Ebc", g)])
                cdg, tmpg, decg, dtdecg = cd[:, hs], tmp[:, hs], dec[:, hs], dtdec[:, hs]
                self.act(cdg, pav[:, :, 127], AF.Exp, pacsT, TSg)
                self.tt(tmpg, pav[:, :, 127], acs_tm[:, hs], ALU.subtract, pacsT + TS, TSg)
                self.act(decg, tmpg, AF.Exp, TSg, TSg)
                self.tt(dtdecg, dt_tm[:, hs], decg, ALU.mult, TS + TSg, TSg)
                self.tt(self.CE[:, hs, :], self.Ebc[:, hs, :], bc(self.CT[:, g, sl], [128, 6, 128], 1), ALU.mult,
                        [t("Ebc", g), t("CT", g)], [t("CE", g)])
                xs3 = self.xstm[:, cs_].rearrange("p (e d) -> p e d", d=64)
                g3 = lambda x: x[:, cs_].rearrange("p (e d) -> p e d", d=64)
                self.tt(g3(self.xdt), xs3, bc(dt_tm[:, hs], [128, 6, 64], 2), ALU.mult, [t("xstm", g)] + TS, [t("xdt", g)])
                self.tt(g3(self.xdd), xs3, bc(dtdecg, [128, 6, 64], 2), ALU.mult, [t("xstm", g)] + TSg, [t("xdd", g)])
                self.tt(g3(self.xsD), xs3, bc(self.Db[:, l, hs], [128, 6, 64], 2), ALU.mult, [t("xstm", g)] + PR, [t("xsD", g)])
                py = self.bank(YB)
                for j in range(6):
                    e = 6 * g + j
                    o = py[(e % 2) * 64:(e % 2) * 64 + 64, (j // 2) * 128:(j // 2) * 128 + 128]
                    es_ = slice(e * 64, (e + 1) * 64)
                    self.mm(o, self.xdt[:, es_], self.GT[:, e, :], True, False, [t("xdt", g), t("GT", g)], [self.pT(YB)])
                    self.mm(o, self.hTb[:, es_], self.CE[:, e, :], False, False, [t("hTb", g), t("CE", g)], [self.pT(YB)])
                    self.mm(o, self.xsD[:, es_], self.ident_bf[:], False, True, [t("xsD", g), t("ident_bf")], [self.pT(YB)])
                pst = self.bank(A0)[:, 0:384]
                self.mm(pst, self.Btm[:, g * 128:(g + 1) * 128], self.xdd[:, cs_], True, True, [t("Btm", g), t("xdd", g)], [self.pT(A0)])
                HT = [t("hT32", l, g)]
                h3 = self.hT32[:, l, cs_].rearrange("p (e d) -> p e d", d=64)
                self.tt(h3, h3, bc(cdg, [128, 6, 64], 2), ALU.mult, HT + TSg, HT)
                self.tt(self.hT32[:, l, cs_], self.hT32[:, l, cs_], pst, ALU.add, HT + [self.pT(A0)], HT)
                self.acopy(self.hTb[:, cs_], self.hT32[:, l, cs_], HT, [t("hTb", g)])
                self.stt(self.yg[:, 3 * g:3 * g + 3, :], py[:, 0:384].rearrange("p (c q) -> p c q", q=128), 0.5,
                         self.zs[:, 3 * g:3 * g + 3, sl], ALU.mult, ALU.mult,
                         [self.pT(YB)] + [t("zs", c) for c in range(3 * g, 3 * g + 3)], [t("yg", g)])
                self.act(self.sq[:, 3 * g:3 * g + 3, :], self.yg[:, 3 * g:3 * g + 3, :], AF.Square, [t("yg", g)], [t("sq", g)])
            ssb = b1[:, 384:512]
            for c in range(6):
                self.mm(ssb, ones, self.sq[:, c, :], c == 0, c == 5, [t("cst"), t("sq", c // 3)], [self.pT(1)])
            self.act(self.rstd[:], ssb, AF.Ln, [self.pT(1)], [t("rstd")], bias=768.0 * EPS)
            self.act(self.rstd[:], self.rstd[:], AF.Exp, [t("rstd")], [t("rstd")], scale=-0.5)
            for c in range(6):
                self.stt(self.mixT[:, 12 + c, sl], self.yg[:, c, :], self.gS[:, l, c:c + 1], self.rstd[:], ALU.mult, ALU.mult,
                         [t("yg", c // 3), t("rstd")] + PR, [t("mixT", 12 + c, tl)])
        if last:
            for hb in range(2):
                pb = self.bank(hb)
                for j in range(3):
                    c = hb * 3 + j
                    self.tr(pb[:, j * 128:(j + 1) * 128], self.hT32[:, l, c * 128:(c + 1) * 128], idf, [t("hT32", l, 0), t("hT32", l, 1), t("cst")], [self.pT(hb)])
                self.vcopy(self.iost[:, hb * 384:(hb + 1) * 384], pb[:, 0:384], [self.pT(hb)], [t("iost")])
            P.dma("pool", self.p_ssd_h[l].rearrange("(c p) n -> p c n", p=128), self.iost[:, 0:768].rearrange("p (c n) -> p c n", n=128),
                  reads=[t("iost")])

    def phase_out(self, l, p, nco=H, x32=None, xb=None, tag=""):
        P, t = self.P, self.P.t
        cst = self.cst
        idf = cst[:, K_ID:K_ID + 128]
        ones = cst[:, K_ONE:K_ONE + 128]
        PR = [t("par", l)]
        src = self.w_out_bf[l]
        x32 = self.xT32 if x32 is None else x32
        xb = self.xTb if xb is None else xb
        smp = (nco != H)
        XT = (lambda dc: t("SxT32", dc)) if smp else (lambda dc: t("xT32", dc))
        XB = (lambda dc: t("SxTb", dc)) if smp else (lambda dc: t("xTb", dc))
        MT = (lambda ec: [t("SmixT", ec)]) if smp else (lambda ec: [t("mixT", ec, tl) for tl in range(NT)])
        bk = lambda i: self.bank(i)[:, 0:nco]
        wqf = self.wqkv[:].rearrange("p a b -> p (a b)")
        for ec in range(18):
            if nco == H and ec % 7 != 6:
                j = ec % 7
                wv = wqf[:, j * 1024:(j + 1) * 1024]
                WT = t("wqkv", j)
            else:
                k = self.next_wbuf("all" if nco != H else "out")
                wv = self.wbuf[k][:].rearrange("p a b -> p (a b)")
                WT = t("wbuf", k)
            if ec < 4:
                R = 128
                pieces = [(ec * 64, 64, 0), ((ec + 4) * 64, 64, 64)]
            elif ec < 12:
                R = 96
                pieces = [(512 + (ec - 4) * 96, 96, 0)]
            else:
                R = 128
                pieces = [(1280 + (ec - 12) * 128, 128, 0)]
            for (r0, n, d0) in pieces:
                th = self.thr_tile()
                P.dma("sp", wv[d0:d0 + n, :], src[r0:r0 + n, :], reads=self.wout_T(l), writes=[WT, th])
                self.pace_cast(th)
            for dc in range(8):
                self.mm(bk(dc), wv[0:R, dc * 128:(dc + 1) * 128], self.mixT[0:R, ec, 0:nco], ec == 0, ec == 17,
                        [WT] + MT(ec), [self.pT(dc)])
        for dc in range(8):
            X = [XT(dc)]
            self.stt(x32[:, dc, :], x32[:, dc, :], ALPHA, bk(dc), ALU.mult, ALU.add, X + [self.pT(dc)], X)
        for dc in range(8):
            X = [XT(dc)]
            q = self.lnq[dc % 2]
            self.act(q[:, 0:nco], x32[:, dc, :], AF.Square, X, [t("lnq", dc % 2)])
            self.mm(bk(0), ones, x32[:, dc, :], dc == 0, dc == 7, [t("cst")] + X, [self.pT(0)])
            self.mm(bk(1), ones, q[:, 0:nco], dc == 0, dc == 7, [t("cst"), t("lnq", dc % 2)], [self.pT(1)])
        M, Rs = [t("mean")], [t("lrs")]
        mean, lrs = self.mean[:, 0:nco], self.lrs[:, 0:nco]
        self.ts(mean, bk(0), 1.0 / D, None, ALU.mult, None, [self.pT(0)], M)
        self.tt(lrs, mean, mean, ALU.mult, M, Rs)
        self.stt(lrs, bk(1), 1.0 / D, lrs, ALU.mult, ALU.subtract, [self.pT(1)] + Rs, Rs)
        self.act(lrs, lrs, AF.Sqrt, Rs, Rs, bias=EPS)
        P.op("dve", lambda e: e.reciprocal(out=lrs, in_=lrs), Rs, Rs)
        for dc in range(8):
            X = [XT(dc)]
            xv = x32[:, dc, :]
            self.tt(xv, xv, mean, ALU.subtract, X + M, X)
            self.tt(xv, xv, lrs, ALU.mult, X + Rs, X)
            self.act(xv, xv, AF.Identity, X + PR, X, scale=self.lnp[:, l, dc, 0:1], bias=self.lnp[:, l, dc, 1:2])
            if l < DEPTH - 1:
                self.acopy(xb[:, dc, 0:nco], xv, X, [XB(dc)])
        if l == 0 and p == 0:
            self.tap("x_l0p0", self.xT32[:], [t("xT32", dc) for dc in range(8)])
        if smp:
            if l == DEPTH - 1:
                pb = self.bank(2, 2)
                for dc in range(8):
                    self.tr(pb[0:NS, dc * 128:(dc + 1) * 128], x32[:, dc, :], idf, [XT(dc), t("cst")], [self.pT(2 + dc // 4)])
                self.vcopy(self.iost[0:NS, :], pb[0:NS, :], [self.pT(2), self.pT(3)], [t("iost")])
                P.dma("pool", self.y_sample, self.iost[0:NS, :], reads=[t("iost")])
        elif l == DEPTH - 1:
            for tl in range(NT):
                gt = p * NT + tl
                for hb in range(2):
                    pb = self.bank(2 + hb)
                    for j in range(4):
                        dc = hb * 4 + j
                        self.tr(pb[:, j * 128:(j + 1) * 128], self.xT32[:, dc, tl * 128:(tl + 1) * 128], idf,
                                [t("xT32", dc), t("cst")], [self.pT(2 + hb)])
                    self.vcopy(self.iost[:, hb * 512:(hb + 1) * 512], pb, [self.pT(2 + hb)], [t("iost")])
                P.dma("pool", self.y_prompt[gt * 128:(gt + 1) * 128, :], self.iost[:], reads=[t("iost")])

    def sample_job(self, l):
        P, t = self.P, self.P.t
        cst = self.cst
        idf = cst[:, K_ID:K_ID + 128]
        ones = cst[:, K_ONE:K_ONE + 128]
        i16 = idf[0:NS, 0:NS]
        PR = [t("par", l)]
        x32 = self.rstd[:].rearrange("p (a b) -> p a b", b=NS)
        xb = self.xTb
        XB = [t("SxTb", kc) for kc in range(8)]
        XT = [t("SxT32", dc) for dc in range(8)]
        Bt = lambda n: [t("B", n)]
        f2 = lambda ap: ap.rearrange("p a b -> p (a b)")

        def fm(k, M, out, bT):
            for kc in range(8):
                self.mm(out, self.wbuf[k][:, kc, 0:M], xb[:, kc, 0:NS], kc == 0, kc == 7, [t("wbuf", k)] + XB, [bT])

        def tm(k, M, out, bT):
            for kc in range(8):
                self.mm(out, xb[:, kc, 0:NS], self.wbuf[k][:, kc, 0:M], kc == 0, kc == 7, [t("wbuf", k)] + XB, [bT])

        if l == 0:
            P.dma("sp", self.iost[0:NS, :], self.x_sample, writes=[t("iost")])
            pb = self.bank(0)
            for dc in range(8):
                self.tr(pb[:, dc * NS:(dc + 1) * NS], self.iost[0:NS, dc * 128:(dc + 1) * 128], i16, [t("iost"), t("cst")], [self.pT(0)])
            self.vcopy(x32, pb[:, 0:128].rearrange("p (a b) -> p a b", b=NS), [self.pT(0)], XT)
            for dc in range(8):
                self.acopy(xb[:, dc, 0:NS], x32[:, dc, :], [XT[dc]], [XB[dc]])
        P.dma("pool", self.wab[0:96, :, :], self.lru_wa[l].rearrange("n c d -> c n d"), writes=[t("wab")])
        P.dma("pool", self.wxb[0:96, :, :], self.lru_wx[l].rearrange("n c d -> c n d"), writes=[t("wxb")])

        b0, b1 = self.bank(0), self.bank(1)
        for j in range(6):
            k = self.load_win(l, [(j * 128, 128, 0)])
            if j < 4:
                tm(k, 128, b0[0:NS, j * 128:(j + 1) * 128], self.pT(0))
            else:
                tm(k, 128, b1[0:NS, (j - 4) * 128:(j - 3) * 128], self.pT(1))
        q32 = self.qkv32[0]
        Q = Bt("q0")
        self.acopy(q32[0:NS, 0:512], b0[0:NS, :], [self.pT(0)], Q)
        self.acopy(q32[0:NS, 512:768], b1[0:NS, 0:256], [self.pT(1)], Q)
        hv = q32[0:NS, 0:640].rearrange("p (h d) -> p h d", d=64)
        x1, x2 = hv[:, :, 0:8], hv[:, :, 8:16]
        cs = bc(self.ropeS[0:NS, 0, :], [NS, 10, 8], 1)
        sn = bc(self.ropeS[0:NS, 1, :], [NS, 10, 8], 1)
        ns = bc(self.ropeS[0:NS, 2, :], [NS, 10, 8], 1)
        R = Bt("ropet")
        ra, rb = self.ropet[0:NS, 0, :, :], self.ropet[0:NS, 1, :, :]
        self.tt(rb[:, :, 0:8], x2, ns, ALU.mult, Q + [t("rope")], R)
        self.tt(rb[:, :, 8:16], x1, sn, ALU.mult, Q + [t("rope")], R)
        self.tt(ra[:, :, 0:8], x1, cs, ALU.mult, Q + [t("rope")], R)
        self.tt(ra[:, :, 8:16], x2, cs, ALU.mult, Q + [t("rope")], R)
        self.tt(hv[:, :, 0:16], ra, rb, ALU.add, R, Q)
        P.dma("pool", self.s_swa_k[l][:, 0:127, :], self.cache_k[l][:, 1:128, :])
        P.dma("pool", self.s_swa_v[l][:, 0:127, :], self.cache_v[l][:, 1:128, :])
        P.dma("pool", self.s_swa_k[l][:, 127, :], q32[0:NS, 512:640], reads=Q)
        P.dma("pool", self.s_swa_v[l][:, 127, :], q32[0:NS, 640:768], reads=Q)
        b2 = self.bank(2)
        for c in range(4):
            k = self.load_win(l, [(C_GA + c * 64, 64, 0), (C_GA + (c + 4) * 64, 64, 64)])
            fm(k, 128, b2[:, c * NS:(c + 1) * NS], self.pT(2))
        GsT = self.G[0][:, 0:64].rearrange("p (c b) -> p c b", b=NS)
        self.act(f2(GsT), b2[:, 0:64], AF.Silu, [self.pT(2)], Bt("G0"))
        kbf = self.qkbf[0:NS, 512:640]
        vnb = self.qkbf[0:NS, 0:128]
        self.acopy(kbf, q32[0:NS, 512:640], Q, Bt("qkbf"))
        self.acopy(vnb, q32[0:NS, 640:768], Q, Bt("qkbf"))
        qz = f2(self.CE[:])[0:NS, 0:1024].rearrange("p (h f) -> p h f", f=128)
        P.op("dve", lambda e: e.memset(f2(self.CE[:])[0:NS, 0:1024], 0.0), [], Bt("CE"))
        qh = q32[0:NS, 0:512].rearrange("p (h d) -> p h d", d=64)
        self.vcopy(qz[:, 0:4, 0:64], qh[:, 0:4, :], Q + Bt("CE"), Bt("CE"))
        self.vcopy(qz[:, 4:8, 64:128], qh[:, 4:8, :], Q + Bt("CE"), Bt("CE"))
        p3 = self.bankbf(3)
        ib16 = self.ident_bf[0:NS, 0:NS]
        for h in range(8):
            self.tr(p3[:, h * NS:(h + 1) * NS], qz[:, h, :], ib16, Bt("CE") + [t("ident_bf")], [self.pT(3)])
        self.tr(p3[:, 128:128 + NS], kbf, ib16, Bt("qkbf") + [t("ident_bf")], [self.pT(3)])
        qblk = self.PT[0][:, 0:128].rearrange("p (b h) -> p b h", h=8)
        knT = self.PT[0][:, 128:128 + NS]
        self.vcopy(qblk.rearrange("p b h -> p h b"), p3[:, 0:128].rearrange("p (h b) -> p h b", b=NS), [self.pT(3)], Bt("PT0"))
        self.vcopy(knT, p3[:, 128:128 + NS], [self.pT(3)], Bt("PT0"))
        st8 = self.wqkv[:].rearrange("p a b -> p (a b)").bitcast(F32)[:, 0:2048].rearrange("p (b f) -> p b f", f=128)
        P.dma("sp", st8, self.cache_k[l].rearrange("b k f -> k b f"), writes=Bt("st8"))
        for b in range(NS):
            self.tr(self.bank(4 + b // 4)[:, (b % 4) * 128:(b % 4 + 1) * 128], st8[:, b, :], idf, Bt("st8") + [t("cst")], [self.pT(4 + b // 4)])
        KcT = f2(self.xsT[:])[:, 0:2048].rearrange("p (b k) -> p b k", k=128)
        for i in range(4):
            self.acopy(f2(KcT[:, 4 * i:4 * i + 4, :]), self.bank(4 + i), [self.pT(4 + i)], Bt("xsT"))
        for b in range(NS):
            self.mm(self.bank(b // 4)[0:8, (b % 4) * 128:(b % 4 + 1) * 128], qblk[:, b, :], KcT[:, b, :], True, True,
                    Bt("PT0") + Bt("xsT"), [self.pT(b // 4)])
        b4 = self.bank(4)
        for b in range(NS):
            self.mm(b4[0:8, b:b + 1], qblk[:, b, :], knT[:, b:b + 1], True, True, Bt("PT0"), [self.pT(4)])
        stt_ = f2(self.tsm[:])[0:8, 0:96].rearrange("p (j b) -> p j b", b=NS)
        mx, negm, ssum, pnew, es, rden = (stt_[:, j, :] for j in range(6))
        A = Bt("tsm")
        S4 = self.ps[0:8, 0:2048].rearrange("p (b k) -> p b k", k=128)
        ST = [self.pT(i) for i in range(4)]
        P.op("dve", lambda e: e.reduce_max(out=mx, in_=S4, axis=AX.X), ST, A)
        self.tt(mx, mx, b4[0:8, 0:NS], ALU.max, A + [self.pT(4)], A)
        self.stt(negm, mx, -SCALE, self.sinkc[0:8, l, 1:2].to_broadcast([8, NS]), ALU.mult, ALU.min, A + PR, A)
        Pms = f2(self.Dm[:]).bitcast(BF16)[0:8, 0:2048]
        for b in range(NS):
            self.act(Pms[:, b * 128:(b + 1) * 128], S4[:, b, :], AF.Exp, [self.pT(b // 4)] + A, Bt("Dm") + A,
                     scale=SCALE, bias=negm[:, b:b + 1], accum_out=ssum[:, b:b + 1])
        self.stt(pnew, b4[0:8, 0:NS], SCALE, negm, ALU.mult, ALU.add, A + [self.pT(4)], A)
        self.act(pnew, pnew, AF.Exp, A, A)
        self.act(es, negm, AF.Exp, A + PR, A, bias=self.sinkc[0:8, l, 0:1])
        self.tt(ssum, ssum, pnew, ALU.add, A, A)
        self.tt(ssum, ssum, es, ALU.add, A, A)
        P.op("dve", lambda e: e.reciprocal(out=rden, in_=ssum), A, A)
        pv = Pms.rearrange("p (b k) -> p b k", k=128)
        self.tt(pv, pv, bc(rden, [8, NS, 128], 2), ALU.mult, Bt("Dm") + A, Bt("Dm"))
        self.tt(pnew, pnew, rden, ALU.mult, A, A)
        p5 = self.bankbf(5)
        for b in range(NS):
            self.tr(p5[:, b * 8:(b + 1) * 8], Pms[:, b * 128:(b + 1) * 128], self.ident_bf[0:8, 0:8], Bt("Dm") + [t("ident_bf")], [self.pT(5)])
        PTs = self.Pm[0][:, 0:128].rearrange("p (b h) -> p b h", h=8)
        self.acopy(f2(PTs), p5[:, 0:128], [self.pT(5)], Bt("Pm0"))
        b6 = self.bank(6)
        self.tr(b6[0:NS, 0:8], pnew, idf[0:8, 0:8], A + [t("cst")], [self.pT(6)])
        pnt = f2(self.ast[1][:])[0:NS, 0:8]
        self.vcopy(pnt, b6[0:NS, 0:8], [self.pT(6)], Bt("ast1"))
        psel = self.PT[1][0:NS, 0:128].rearrange("p (b h) -> p b h", h=8)
        self.tt(psel, bc(pnt, [NS, NS, 8], 1), bc(i16, [NS, NS, 8], 2), ALU.mult, Bt("ast1") + [t("cst")], Bt("PT1"))
        P.dma("sp", st8, self.cache_v[l].rearrange("b k f -> k b f"), writes=Bt("st8"))
        Vc = f2(self.qT[:]).rearrange("p (b f) -> p b f", f=128)
        self.vcopy(f2(Vc), f2(st8), Bt("st8"), Bt("qT"))
        b7 = self.bank(7)
        oT = b7[:, 0:128].rearrange("p (b h) -> p b h", h=8)
        for b in range(NS):
            self.mm(b7[:, b * 8:(b + 1) * 8], Vc[:, b, :], PTs[:, b, :], True, False, Bt("qT") + Bt("Pm0"), [self.pT(7)])
            self.mm(b7[:, b * 8:(b + 1) * 8], vnb, psel[:, b, :], False, True, Bt("qkbf") + Bt("PT1"), [self.pT(7)])
        for c in range(4):
            for w in range(2):
                rows = slice(w * 64, (w + 1) * 64)
                self.tt(self.mixT[rows, c, 0:NS], oT[rows, :, c + 4 * w], GsT[rows, c, :], ALU.mult,
                        [self.pT(7)] + Bt("G0"), [t("SmixT", c)])

        zf = f2(self.zs[:])
        stc = zf[0:NS, 0:2304]
        sth = zf[0:NS, 2304:3072]
        P.dma("sp", stc.rearrange("p (k f) -> p k f", f=768), self.st_lru_conv[l], writes=Bt("zs"))
        P.dma("sp", sth, self.st_lru_h[l], writes=Bt("zs"))
        P.dma("pool", self.s_lru_conv[l][:, 0:2, :], self.st_lru_conv[l][:, 1:3, :])
        b0, b1 = self.bank(0), self.bank(1)
        for n in range(8):
            for k3 in range(3):
                j = n * 3 + k3
                self.tr(b0[0:96, j * NS:(j + 1) * NS], stc[:, k3 * 768 + n * 96:k3 * 768 + (n + 1) * 96], i16, Bt("zs") + [t("cst")], [self.pT(0)])
            self.tr(b1[0:96, n * NS:(n + 1) * NS], sth[:, n * 96:(n + 1) * 96], i16, Bt("zs") + [t("cst")], [self.pT(1)])
        aUf = f2(self.aU[:])
        hs = aUf[0:96, 0:384].rearrange("p (n k b) -> p n k b", k=3, b=NS)
        h0T = aUf[0:96, 384:512].rearrange("p (n b) -> p n b", b=NS)
        self.vcopy(aUf[0:96, 0:384], b0[0:96, 0:384], [self.pT(0)], Bt("aU"))
        self.vcopy(aUf[0:96, 384:512], b1[0:96, 0:128], [self.pT(1)], Bt("aU"))
        b2, b5 = self.bank(2), self.bank(5)
        for n in range(8):
            kx = self.load_win(l, [(C_XL + n * 96, 96, 0)])
            fm(kx, 96, b2[0:96, n * NS:(n + 1) * NS], self.pT(2))
            tm(kx, 96, self.bank(3 + n // 4)[0:NS, (n % 4) * 128:(n % 4) * 128 + 96], self.pT(3 + n // 4))
            kg = self.load_win(l, [(C_GL + n * 96, 96, 0)])
            fm(kg, 96, b5[0:96, n * NS:(n + 1) * NS], self.pT(5))
        xltm = self.qkv32[1][0:NS, 0:768]
        self.vcopy(xltm.rearrange("p (n c) -> p n c", c=96),
                   self.ps[0:NS, 3 * 512:5 * 512].rearrange("p (n c) -> p n c", c=128)[:, :, 0:96], [self.pT(3), self.pT(4)], Bt("q1"))
        P.dma("pool", self.s_lru_conv[l][:, 2, :], xltm, reads=Bt("q1"))
        v3 = lambda ap: ap.rearrange("p (n b) -> p n b", b=NS)
        lpb = lambda j: bc(self.lp[0:96, l, :, j], [96, 8, NS], 2)
        acc = v3(self.xl[0:96, 0:128]); tmp = v3(self.xlb[0:96, 0:256].bitcast(F32))
        Xl, Tm = Bt("xl"), Bt("xlb")
        self.tt(acc, v3(b2[0:96, 0:128]), lpb(3), ALU.mult, [self.pT(2)] + PR, Xl)
        self.tt(acc, acc, lpb(4), ALU.add, Xl + PR, Xl)
        for k3 in range(3):
            self.tt(tmp, hs[:, :, k3, :], lpb(k3), ALU.mult, Bt("aU") + PR, Tm)
            self.tt(acc, acc, tmp, ALU.add, Xl + Tm, Xl)
        xlbf = self.dtt[0:96, 0:64].bitcast(BF16)
        self.acopy(xlbf, self.xl[0:96, 0:128], Xl, Bt("dtt"))
        b6, b7 = self.bank(6), self.bank(7)
        for n in range(8):
            self.mm(b6[0:96, n * NS:(n + 1) * NS], self.wab[0:96, n, :], xlbf[:, n * NS:(n + 1) * NS], True, True, [t("wab")] + Bt("dtt"), [self.pT(6)])
            self.mm(b7[0:96, n * NS:(n + 1) * NS], self.wxb[0:96, n, :], xlbf[:, n * NS:(n + 1) * NS], True, True, [t("wxb")] + Bt("dtt"), [self.pT(7)])
        rr, ig, EE = v3(self.rr[0:96, 0:128]), v3(self.ig[0:96, 0:128]), v3(self.EE[0:96, 0:128])
        Rr, Ig, Ee = Bt("rr"), Bt("ig"), Bt("EE")
        self.tt(rr, v3(b6[0:96, 0:128]), lpb(5), ALU.add, [self.pT(6)] + PR, Rr)
        self.act(rr, rr, AF.Sigmoid, Rr, Rr)
        self.tt(ig, v3(b7[0:96, 0:128]), lpb(6), ALU.add, [self.pT(7)] + PR, Ig)
        self.act(ig, ig, AF.Sigmoid, Ig, Ig)
        self.tt(EE, rr, bc(self.la[0:96, l, :, 1], [96, 8, NS], 2), ALU.mult, Rr + PR, Ee)
        self.act(EE, EE, AF.Exp, Ee, Ee, scale=2.0)
        self.act(EE, EE, AF.Sqrt, Ee, Ee, scale=-1.0, bias=1.0)
        self.tt(rr, rr, bc(self.la[0:96, l, :, 1], [96, 8, NS], 2), ALU.mult, Rr + PR, Rr)
        self.act(rr, rr, AF.Exp, Rr, Rr)
        self.tt(ig, ig, acc, ALU.mult, Ig + Xl, Ig)
        self.tt(ig, ig, EE, ALU.mult, Ig + Ee, Ig)
        self.tt(EE, rr, h0T, ALU.mult, Rr + Bt("aU"), Ee)
        self.tt(EE, EE, ig, ALU.add, Ee + Ig, Ee)
        for n in range(8):
            self.tr(self.bank(3 + n // 4)[0:NS, (n % 4) * 128:(n % 4) * 128 + 96], self.EE[0:96, n * NS:(n + 1) * NS], idf[0:96, 0:96],
                    Ee + [t("cst")], [self.pT(3 + n // 4)])
        h1tm = f2(self.yg[:])[0:NS, 0:768]
        self.vcopy(h1tm.rearrange("p (n c) -> p n c", c=96),
                   self.ps[0:NS, 3 * 512:5 * 512].rearrange("p (n c) -> p n c", c=128)[:, :, 0:96], [self.pT(3), self.pT(4)], Bt("yg"))
        P.dma("pool", self.s_lru_h[l], h1tm, reads=Bt("yg"))
        sg = v3(self.dtT[0:96, 0:128])
        self.act(sg, v3(b5[0:96, 0:128]), AF.Silu, [self.pT(5)], Bt("dtT"))
        self.tt(self.mixT[0:96, 4:12, 0:NS], EE, sg, ALU.mult, Ee + Bt("dtT"), [t("SmixT", 4 + n) for n in range(8)])

        Dmf = f2(self.Dm[:])
        srcs = (zf[0:NS, 0:1280], zf[0:NS, 1280:2560], Dmf[0:NS, 0:1280])
        srcT = (Bt("zs"), Bt("zs"), Bt("Dm"))
        for k3 in range(3):
            P.dma("sp", srcs[k3], self.st_ssd_conv[l][:, k3, :], writes=srcT[k3])
        P.dma("pool", self.s_ssd_conv[l][:, 0:2, :], self.st_ssd_conv[l][:, 1:3, :])
        b0 = self.bank(0)
        for c in range(10):
            for k3 in range(3):
                j = c * 3 + k3
                self.tr(b0[:, j * NS:(j + 1) * NS], srcs[k3][:, c * 128:(c + 1) * 128], i16, srcT[k3] + [t("cst")], [self.pT(0)])
        sqf = f2(self.sq[:])
        hss = sqf[:, 0:480].rearrange("p (c k b) -> p c k b", k=3, b=NS)
        self.vcopy(sqf[:, 0:480], b0[:, 0:480], [self.pT(0)], Bt("sq"))
        b1 = self.bank(1)
        for c in range(10):
            k = self.load_win(l, [(C_XBC + c * 128, 128, 0)])
            fm(k, 128, b1[:, c * NS:(c + 1) * NS], self.pT(1))
            tm(k, 128, self.bank(2 + c // 4)[0:NS, (c % 4) * 128:(c % 4 + 1) * 128], self.pT(2 + c // 4))
        xbtm = aUf[0:NS, 0:1280]
        self.vcopy(xbtm, self.ps[0:NS, 2 * 512:2 * 512 + 1280], [self.pT(2), self.pT(3), self.pT(4)], Bt("aU"))
        P.dma("pool", self.s_ssd_conv[l][:, 2, :], xbtm, reads=Bt("aU"))
        b5, b6 = self.bank(5), self.bank(6)
        for c in range(6):
            k = self.load_win(l, [(C_Z + c * 128, 128, 0)])
            fm(k, 128, b5[:, c * NS:(c + 1) * NS], self.pT(5))
        k = self.load_win(l, [(C_DT, 12, 0)])
        fm(k, 12, b6[0:12, 0:NS], self.pT(6))
        spb = lambda j: bc(self.spp[:, l, :, j], [128, 10, NS], 2)
        acc = v3(self.xl[:, 0:160]); tmp = v3(self.xlb[:, 0:320].bitcast(F32))
        self.tt(acc, v3(b1[:, 0:160]), spb(3), ALU.mult, [self.pT(1)] + PR, Xl)
        self.tt(acc, acc, spb(4), ALU.add, Xl + PR, Xl)
        for k3 in range(3):
            self.tt(tmp, hss[:, :, k3, :], spb(k3), ALU.mult, Bt("sq") + PR, Tm)
            self.tt(acc, acc, tmp, ALU.add, Xl + Tm, Xl)
        xc = v3(self.rr[:, 0:160])
        self.act(xc, acc, AF.Silu, Xl, Rr, scale=2.0)
        xsTs, BsT, CsT = xc[:, 0:6, :], xc[:, 6:8, :], xc[:, 8:10, :]
        zsT = v3(self.ig[:, 0:96])
        self.act(zsT, v3(b5[:, 0:96]), AF.Silu, [self.pT(5)], Ig)
        u, v = self.dtT[0:12, 0:NS], self.dtt[0:12, 0:NS]
        self.act(u, b6[0:12, 0:NS], AF.Identity, [self.pT(6)] + PR, Bt("dtT"), bias=self.dtb[0:12, l:l + 1])
        self.act(v, u, AF.Abs, Bt("dtT"), Bt("dtt"))
        self.act(v, v, AF.Exp, Bt("dtt"), Bt("dtt"), scale=-1.0)
        self.act(v, v, AF.Ln, Bt("dtt"), Bt("dtt"), bias=1.0)
        self.stt(u, u, 0.0, v, ALU.max, ALU.add, Bt("dtT") + Bt("dtt"), Bt("dtT"))
        Ex = cst[0:12, K_EX:K_EX + 768]
        b7 = self.bank(7)
        for c in range(6):
            Exc = Ex[:, c * 128:(c + 1) * 128]
            self.mm(b7[:, c * NS:(c + 1) * NS], Exc, u, True, True, [t("cst")] + Bt("dtT"), [self.pT(7)])
            self.mm(b7[:, 96 + c:97 + c], Exc, self.adc[0:12, l, 0:1], True, True, [t("cst")] + PR, [self.pT(7)])
            self.mm(b7[:, 104 + c:105 + c], Exc, self.adc[0:12, l, 1:2], True, True, [t("cst")] + PR, [self.pT(7)])
        ex = self.EE[:, 0:112]
        self.vcopy(ex, b7[:, 0:112], [self.pT(7)], Ee)
        dtx = v3(ex[:, 0:96]); Ax = ex[:, 96:102]; Dx = ex[:, 104:110]
        decT = v3(self.mean[:, 0:96]); x0T = v3(self.lrs[:, 0:96])
        self.tt(decT, dtx, bc(Ax, [128, 6, NS], 2), ALU.mult, Ee, [t("mean")])
        self.act(decT, decT, AF.Exp, [t("mean")], [t("mean")])
        self.tt(x0T, xsTs, dtx, ALU.mult, Rr + Ee, [t("lrs")])
        yT = v3(self.lnq[0][:, 0:96])
        hbuf = [f2(self.xT32[:])[:, i * 2048:(i + 1) * 2048].rearrange("p (b n) -> p b n", n=128) for i in range(2)]
        st8v = st8
        Bbc = self.ps[:, 0:2048].rearrange("p (b n) -> p b n", n=128)
        Cbc = self.ps[:, 2048:4096].rearrange("p (b n) -> p b n", n=128)
        idb = bc(idf, [128, NS, 128], 1)
        for g in range(2):
            self.tt(st8v, bc(BsT[:, g, :], [128, NS, 128], 2), idb, ALU.mult, Rr + [t("cst")], Bt("st8"))
            for j in range(4):
                self.mm(self.bank(j), ones, f2(st8v)[:, j * 512:(j + 1) * 512], True, True, [t("cst")] + Bt("st8"), [self.pT(j)])
            self.tt(st8v, bc(CsT[:, g, :], [128, NS, 128], 2), idb, ALU.mult, Rr + [t("cst")], Bt("st8"))
            for j in range(4):
                self.mm(self.bank(4 + j), ones, f2(st8v)[:, j * 512:(j + 1) * 512], True, True, [t("cst")] + Bt("st8"), [self.pT(4 + j)])
            for c in range(3 * g, 3 * g + 3):
                hb = hbuf[c % 2]
                Hb = [t("B", "hb", c % 2)]
                P.dma("sp", hb, self.st_ssd_h[l][:, 2 * c:2 * c + 2].rearrange("b e p n -> (e p) b n"), writes=Hb)
                self.tt(hb, hb, bc(decT[:, c, :], [128, NS, 128], 2), ALU.mult, Hb + [t("mean")], Hb)
                self.tt(st8v, Bbc, bc(x0T[:, c, :], [128, NS, 128], 2), ALU.mult, [self.pT(j) for j in range(4)] + [t("lrs")], Bt("st8"))
                self.tt(hb, hb, st8v, ALU.add, Hb + Bt("st8"), Hb)
                P.dma("pool", self.s_ssd_h[l][:, c * 128:(c + 1) * 128, :].rearrange("b f n -> f b n"), hb, reads=Hb)
                self.tt(st8v, hb, Cbc, ALU.mult, Hb + [self.pT(4 + j) for j in range(4)], Bt("st8"))
                P.op("dve", lambda e, c=c: e.reduce_sum(out=yT[:, c, :], in_=st8v, axis=AX.X), Bt("st8"), [t("lnq", 0)])
        self.tt(tmp[:, 0:6, :], xsTs, bc(Dx, [128, 6, NS], 2), ALU.mult, Rr + Ee, Tm)
        self.tt(yT, yT, tmp[:, 0:6, :], ALU.add, [t("lnq", 0)] + Tm, [t("lnq", 0)])
        self.tt(yT, yT, zsT, ALU.mult, [t("lnq", 0)] + Ig, [t("lnq", 0)])
        sqs = v3(self.lnq[1][:, 0:96])
        self.act(sqs, yT, AF.Square, [t("lnq", 0)], [t("lnq", 1)])
        b0 = self.bank(0)
        for c in range(6):
            self.mm(b0[:, 0:NS], ones, sqs[:, c, :], c == 0, c == 5, [t("cst"), t("lnq", 1)], [self.pT(0)])
        rs = self.dtt[:, 64:64 + NS]
        self.act(rs, b0[:, 0:NS], AF.Sqrt, [self.pT(0)], Bt("dtt"), bias=768.0 * EPS)
        P.op("dve", lambda e: e.reciprocal(out=rs, in_=rs), Bt("dtt"), Bt("dtt"))
        for c in range(6):
            self.stt(self.mixT[:, 12 + c, 0:NS], yT[:, c, :], self.gS[:, l, c:c + 1], rs, ALU.mult, ALU.mult,
                     [t("lnq", 0)] + Bt("dtt") + PR, [t("SmixT", 12 + c)])

        self.phase_out(l, None, nco=NS, x32=x32, xb=self.xTb)


    def build(self, sample=True, n_pass=NPASS, n_layers=DEPTH):
        self.declare_io()
        self.alloc()
        self.prologue()
        for p in range(n_pass):
            for l in range(n_layers):
                self.job(l, p)
        for l in range(DEPTH):
            self.flush_casts(l)
        if sample:
            self.P.fence()
            for l in range(n_layers):
                self.sample_job(l)
        self.P.emit()
        self.st.close()
        return self.nc


def make_consts():
    c = np.zeros((128, K_END), np.float32)
    c[:, K_ID:K_ID + 128] = np.eye(128, dtype=np.float32)
    s = np.arange(128)[:, None]
    q = np.arange(128)[None, :]
    c[:, K_U:K_U + 128] = (s <= q).astype(np.float32)
    c[:, K_ONE:K_ONE + 128] = 1.0
    i = np.arange(128)[:, None]
    j = np.arange(256)[None, :]
    band = (j >= i) & (j <= i + 128)
    c[:, K_MG:K_MG + 256] = np.where(band, 0.0, NEG)
    c[:, K_MF:K_MF + 256] = np.where(band & (j >= 128), 0.0, NEG)
    c[:, K_POS:K_POS + 16] = np.arange(128)[:, None] + 128.0 * np.arange(16)[None, :]
    c[:, K_INV:K_INV + 8] = (500000.0 ** (-np.arange(8, dtype=np.float32) / 8.0)).astype(np.float32)[None, :]
    for e in range(12):
        c[e, K_EX + e * 64:K_EX + (e + 1) * 64] = 1.0
    c[:, K_HM] = (np.arange(128) < 4)
    c[:, K_HM + 1] = (np.arange(128) >= 4)
    return c


WNAMES = ["w_in", "w_out", "att_sinks", "lru_conv_w", "lru_conv_b", "lru_wa", "lru_ba", "lru_wx", "lru_bx",
          "lru_lambda", "ssd_conv_w", "ssd_conv_b", "ssd_dt_bias", "ssd_a_log", "ssd_d", "ssd_norm_g", "ln_g", "ln_b"]


def make_in_maps(inputs, n=8):
    f = lambda a: np.ascontiguousarray(np.asarray(a, dtype=np.float32))
    shared = {k: f(inputs[k]) for k in WNAMES}
    shared["consts"] = make_consts()
    maps = []
    for i in range(n):
        s = slice(i * NS, (i + 1) * NS)
        m = dict(shared)
        m["x_prompt"] = f(inputs["x_prompt"][i])
        m["x_sample"] = f(np.asarray(inputs["x_sample"])[s, 0, :])
        m["cache_swa_k"] = f(np.asarray(inputs["cache_swa_k"])[:, s].reshape(DEPTH, NS, 128, 128))
        m["cache_swa_v"] = f(np.asarray(inputs["cache_swa_v"])[:, s].reshape(DEPTH, NS, 128, 128))
        m["state_lru_conv"] = f(np.asarray(inputs["state_lru_conv"])[:, s])
        m["state_lru_h"] = f(np.asarray(inputs["state_lru_h"])[:, s])
        m["state_ssd_conv"] = f(np.asarray(inputs["state_ssd_conv"])[:, s])
        m["state_ssd_h"] = f(np.asarray(inputs["state_ssd_h"])[:, s])
        maps.append(m)
    return maps


def kernel(**inputs):
    nc = Builder().build()
    res = run_bass_kernel_spmd(nc, make_in_maps(inputs), core_ids=list(range(8)))
    R = res.results
    cat = lambda k, ax: np.concatenate([np.asarray(r[k]) for r in R], axis=ax)
    stk = lambda k: np.stack([np.asarray(r[k]) for r in R], axis=1)
    y_prompt = np.stack([np.asarray(r["y_prompt"]) for r in R], 0)
    y_sample = cat("y_sample", 0).reshape(128, 1, D)
    p_swa_k = stk("p_swa_k").reshape(DEPTH, 8, 128, 2, 64)
    p_swa_v = stk("p_swa_v").reshape(DEPTH, 8, 128, 2, 64)
    p_lru_conv = stk("p_lru_conv")
    p_lru_h = stk("p_lru_h")
    p_ssd_conv = stk("p_ssd_conv")
    p_ssd_h = stk("p_ssd_h").reshape(DEPTH, 8, 12, 64, 128)
    s_swa_k = cat("s_swa_k", 1).reshape(DEPTH, 128, 128, 2, 64)
    s_swa_v = cat("s_swa_v", 1).reshape(DEPTH, 128, 128, 2, 64)
    s_lru_conv = cat("s_lru_conv", 1)
    s_lru_h = cat("s_lru_h", 1)
    s_ssd_conv = cat("s_ssd_conv", 1)
    s_ssd_h = cat("s_ssd_h", 1).reshape(DEPTH, 128, 12, 64, 128)
    outs = (y_prompt, y_sample, p_swa_k, p_swa_v, p_lru_conv, p_lru_h, p_ssd_conv, p_ssd_h,
            s_swa_k, s_swa_v, s_lru_conv, s_lru_h, s_ssd_conv, s_ssd_h)
    return tuple(np.ascontiguousarray(o, dtype=np.float32) for o in outs)
```

```python
import contextlib
import math
import numpy as np
import concourse.bass as bass
import concourse.mybir as mybir
from concourse.bass_utils import run_bass_kernel_spmd

F32 = mybir.dt.float32
BF16 = mybir.dt.bfloat16
I32 = mybir.dt.int32
AF = mybir.ActivationFunctionType
ALU = mybir.AluOpType
AX = mybir.AxisListType

D = 1024
SEQ = 2048
DEPTH = 4
NS = 16
DIN = 4876
C_Q, C_K, C_V, C_GA, C_XL, C_GL, C_Z, C_XBC, C_DT = 0, 512, 640, 768, 1280, 2048, 2816, 3584, 4864
SCALE = 64 ** -0.5
ALPHA = (2.0 * DEPTH) ** 0.25
EPS = 1e-5
H = 512
NT = H // 128
NPASS = SEQ // H
PAST = 8192.0
NEG = -30000.0
import os
SSD_STOP = int(os.environ.get('SSD_STOP', '0'))
PIN_ENGS = set(os.environ.get('PIN_ENGS', '').split(','))
UNPIN_LINES = set(int(x) for x in os.environ.get('UNPIN_LINES', '').split(',') if x)

K_ID, K_U, K_ONE, K_MG, K_MF, K_POS, K_INV, K_HM, K_EX, K_END = 0, 128, 256, 384, 640, 896, 912, 920, 922, 922 + 768


class T:
    __slots__ = ("name", "lw", "rd", "excl")

    def __init__(self, name):
        self.name = name
        self.lw = None
        self.rd = []
        self.excl = (len(name) > 0 and name[0] == "ps")


class Op:
    __slots__ = ("eng", "fn", "deps", "signal", "val", "is_dma", "dsem", "dval", "odeps", "cost", "idx", "prio", "fin", "nbytes", "tbl")

    def __init__(self, eng, fn, is_dma):
        self.eng = eng
        self.fn = fn
        self.odeps = []
        self.tbl = None
        self.cost = 0.3
        self.nbytes = 0
        self.deps = []
        self.signal = False
        self.val = None
        self.is_dma = is_dma
        self.dsem = None
        self.dval = None


class Prog:
    ENGS = ("pe", "act", "dve", "pool", "sp")

    def __init__(self, nc, n_dma_sems=24):
        self.nc = nc
        self.ops = []
        self.tiles = {}
        self.n_dma_sems = n_dma_sems
        self.last_fence = {}
        self.last_pe = None
        self.last_on = {}
        self.do_schedule = os.environ.get('SCHED', '1') == '1'
        self.sched_window = int(os.environ.get('SCHEDW', '0'))

    def t(self, *key):
        if key not in self.tiles:
            self.tiles[key] = T(key)
        return self.tiles[key]

    def op(self, eng, fn, reads=(), writes=(), dma=False, cost=0.3, nbytes=0):
        o = Op(eng, fn, dma)
        o.cost = cost
        o.nbytes = nbytes
        o.idx = len(self.ops)
        lf = self.last_fence.get(eng)
        if lf is not None:
            o.odeps.append(lf)
        if eng in PIN_ENGS:
            import sys as _sys
            f = _sys._getframe(1)
            while f.f_code.co_name in ("act", "acopy", "vcopy", "tt", "ts", "stt", "mm", "tr", "dma", "op", "<lambda>"):
                f = f.f_back
            if f.f_lineno not in UNPIN_LINES:
                lo = self.last_on.get(eng)
                if lo is not None:
                    o.odeps.append(lo)
                self.last_on[eng] = o
        if eng == "pe":
            pin = (cost < 0)
            if pin:
                o.cost = cost = -cost
            lp = self.last_pe
            if lp is not None and (pin or lp[1]):
                o.odeps.append(lp[0])
            self.last_pe = (o, pin)
        deps = set()
        for t in reads:
            if t.lw is not None:
                deps.add(t.lw)
            if t.excl:
                for r in t.rd:
                    if r.eng != eng:
                        deps.add(r)
        for t in writes:
            if t.lw is not None:
                deps.add(t.lw)
            for r in t.rd:
                deps.add(r)
        for t in reads:
            t.rd.append(o)
        for t in writes:
            t.lw = o
            t.rd = []
        for d in deps:
            if d is o:
                continue
            if (not d.is_dma) and (not dma) and d.eng == "pe" and eng == "pe":
                o.odeps.append(d)
                continue
            o.deps.append(d)
            d.signal = True
        self.ops.append(o)
        return o

    def dma(self, eng, out, in_, reads=(), writes=(), **kw):
        nb = 1
        for d in out.shape:
            nb *= d
        nb *= mybir.dt.size(out.dtype)
        return self.op(eng, lambda e: e.dma_start(out=out, in_=in_, **kw), reads, writes, dma=True,
                       cost=(0.08 if eng != "pool" else 0.6), nbytes=nb)

    def fence(self):
        allt = list(self.tiles.values())
        ft = self.t("__fence__")
        first = True
        for e in ("pe", "act", "dve", "pool", "sp"):
            o = self.op(e, lambda eng: eng.nop(), [], (allt + [ft]) if first else [ft], cost=0.05)
            self.last_fence[e] = o
            first = False

    def schedule(self):
        import heapq
        ops = self.ops
        n = len(ops)
        succ = [[] for _ in range(n)]
        npred = [0] * n
        for o in ops:
            ps = set(id(d) for d in o.deps) | set(id(d) for d in o.odeps)
            seen = set()
            for d in list(o.deps) + list(o.odeps):
                if id(d) in seen:
                    continue
                seen.add(id(d))
                succ[d.idx].append(o.idx)
                npred[o.idx] += 1
        dur = [0.0] * n
        for o in ops:
            dur[o.idx] = o.cost + (2.0 + o.nbytes / 250e3 if o.is_dma else 0.0)
        prio = [0.0] * n
        for i in range(n - 1, -1, -1):
            m = 0.0
            for j in succ[i]:
                if prio[j] > m:
                    m = prio[j]
            prio[i] = m + dur[i]
        ready = {e: [] for e in self.ENGS}
        ready_t = [0.0] * n
        for o in ops:
            if npred[o.idx] == 0:
                heapq.heappush(ready[o.eng], (-prio[o.idx], o.idx))
        free = {e: 0.0 for e in self.ENGS}
        order = {e: [] for e in self.ENGS}
        fin = [0.0] * n
        cur_tbl = None
        SETS_OF = {"E": ("0", "6"), "A": ("0",), "L": ("6",), "S": ("3",), "G": ("2",), "U": ("18",), "N": ("9",)}

        def tbl_ok(cur, f):
            return f is None or cur is None or any(x in cur for x in SETS_OF[f])

        def tbl_next(cur, f):
            if cur is None:
                return SETS_OF[f]
            inter = tuple(x for x in SETS_OF[f] if x in cur)
            return inter if inter else SETS_OF[f]
        dma_free = 0.0
        done = 0
        while done < n:
            best = None
            for e in self.ENGS:
                if not ready[e]:
                    continue
                cand = ready[e][0]
                i = cand[1]
                st = max(free[e], ready_t[i])
                if best is None or st < best[0]:
                    best = (st, e)
            st, e = best
            heap = ready[e]
            pick = None
            tmp = []
            fallback = None
            while heap:
                c = heapq.heappop(heap)
                if ready_t[c[1]] <= max(free[e], st) + 1e-9:
                    if e != "act" or tbl_ok(cur_tbl, ops[c[1]].tbl):
                        pick = c
                        break
                    if fallback is None:
                        fallback = c
                        if len(tmp) > 24:
                            break
                        continue
                tmp.append(c)
                if len(tmp) > 64:
                    break
            if pick is None and fallback is not None:
                pick = fallback
                fallback = None
            if fallback is not None:
                tmp.append(fallback)
            for c in tmp:
                heapq.heappush(heap, c)
            if pick is None:
                pick = heapq.heappop(heap)
            i = pick[1]
            o = ops[i]
            start = max(free[e], ready_t[i])
            if e == "act" and o.tbl is not None:
                if not tbl_ok(cur_tbl, o.tbl):
                    start += 1.3
                cur_tbl = tbl_next(cur_tbl, o.tbl)
            if o.is_dma:
                free[e] = start + o.cost
                t0 = max(start + o.cost, dma_free)
                dma_free = t0 + o.nbytes / 250e3
                fin[i] = dma_free + 2.0
            else:
                free[e] = start + o.cost
                fin[i] = free[e]
            order[e].append(o)
            done += 1
            for j in succ[i]:
                lat = 0.0 if (ops[j].eng == e and not o.is_dma) else 0.15
                if fin[i] + lat > ready_t[j]:
                    ready_t[j] = fin[i] + lat
                npred[j] -= 1
                if npred[j] == 0:
                    heapq.heappush(ready[ops[j].eng], (-prio[j], j))
        self.est_time = max(fin) if n else 0.0
        if self.sched_window > 0:
            return self.schedule_window(succ)
        return order

    def schedule_window(self, succ):
        ops = self.ops
        n = len(ops)
        W = self.sched_window
        npred = [0] * n
        for i in range(n):
            for j in succ[i]:
                npred[j] += 1
        orig = {e: [o.idx for o in ops if o.eng == e] for e in self.ENGS}
        pos = {e: 0 for e in self.ENGS}
        taken = [False] * n
        order = {e: [] for e in self.ENGS}
        done = 0
        while done < n:
            progressed = False
            for e in self.ENGS:
                lst = orig[e]
                while pos[e] < len(lst) and taken[lst[pos[e]]]:
                    pos[e] += 1
                for k in range(pos[e], min(pos[e] + W, len(lst))):
                    i = lst[k]
                    if not taken[i] and npred[i] == 0:
                        taken[i] = True
                        order[e].append(ops[i])
                        for j in succ[i]:
                            npred[j] -= 1
                        done += 1
                        progressed = True
                        break
            assert progressed
        return order

    def emit(self):
        nc = self.nc
        cnt = {e: 0 for e in self.ENGS}
        dma_rr = {e: 0 for e in self.ENGS}
        dma_cnt = {}
        if self.do_schedule:
            per = self.schedule()
        else:
            per = {e: [o for o in self.ops if o.eng == e] for e in self.ENGS}
        for o in [o for e in self.ENGS for o in per[e]]:
            if o.is_dma:
                k = dma_rr[o.eng] % self.n_dma_sems
                dma_rr[o.eng] += 1
                key = (o.eng, k)
                dma_cnt[key] = dma_cnt.get(key, 0) + 16
                o.dsem = key
                o.dval = dma_cnt[key]
            elif o.signal:
                cnt[o.eng] += 1
                o.val = cnt[o.eng]
        with contextlib.ExitStack() as st:
            sems = {e: st.enter_context(nc.semaphore("s_" + e)) for e in ("pe", "act", "dve", "pool")}
            dsems = {}
            for e in self.ENGS:
                for k in range(min(self.n_dma_sems, dma_rr[e])):
                    dsems[(e, k)] = st.enter_context(nc.semaphore("d_%s_%d" % (e, k)))
            block = st.enter_context(nc.Block())

            def run(ename):
                def body(eng):
                    waited = {}
                    for o in per[ename]:
                        need = {}
                        for d in o.deps:
                            if d.is_dma:
                                s, v, sk = dsems[d.dsem], d.dval, ("d",) + d.dsem
                            else:
                                s, v, sk = sems[d.eng], d.val, ("c", d.eng)
                            if need.get(sk, (None, 0))[1] < v:
                                need[sk] = (s, v)
                        if o.is_dma and o.dval > 16:
                            sk = ("d",) + o.dsem
                            if need.get(sk, (None, 0))[1] < o.dval - 16:
                                need[sk] = (dsems[o.dsem], o.dval - 16)
                        for sk, (s, v) in need.items():
                            if waited.get(sk, 0) >= v:
                                continue
                            eng.wait_ge(s, v)
                            waited[sk] = v
                        ins = o.fn(eng)
                        if o.is_dma:
                            ins.then_inc(dsems[o.dsem], 16)
                        elif o.signal:
                            ins.then_inc(sems[ename], 1)
                    if ename == "sp":
                        for key, v in dma_cnt.items():
                            eng.wait_ge(dsems[key], v)
                        for e in ("pe", "act", "dve", "pool"):
                            if cnt[e]:
                                eng.wait_ge(sems[e], cnt[e])
                return body

            block.tensor(run("pe"))
            block.scalar(run("act"))
            block.vector(run("dve"))
            block.gpsimd(run("pool"))
            block.sync(run("sp"))


def bc(ap, shape, axis):
    return ap.unsqueeze(axis).to_broadcast(shape)


class Builder:
    def __init__(self, dbg=None):
        self.dbg = dbg or {}
        self.nc = nc = bass.Bass("TRN2", target_bir_lowering=False)
        self.P = Prog(nc)
        self.st = contextlib.ExitStack()
        self.wrr = 0

    def sb(self, name, shape, dt=F32):
        return self.st.enter_context(self.nc.sbuf_tensor(name, shape, dt))

    def din(self, name, shape, dt=F32):
        return self.nc.dram_tensor(name, shape, dt, kind="ExternalInput").ap()

    def dout(self, name, shape, dt=F32):
        return self.nc.dram_tensor(name, shape, dt, kind="ExternalOutput").ap()

    def declare_io(self):
        L = DEPTH
        self.x_prompt = self.din("x_prompt", [SEQ, D])
        self.x_sample = self.din("x_sample", [NS, D])
        self.cache_k = self.din("cache_swa_k", [L, NS, 128, 128])
        self.cache_v = self.din("cache_swa_v", [L, NS, 128, 128])
        self.st_lru_conv = self.din("state_lru_conv", [L, NS, 3, 768])
        self.st_lru_h = self.din("state_lru_h", [L, NS, 768])
        self.st_ssd_conv = self.din("state_ssd_conv", [L, NS, 3, 1280])
        self.st_ssd_h = self.din("state_ssd_h", [L, NS, 12, 64, 128])
        self.w_in = self.din("w_in", [L, D, DIN])
        self.w_out = self.din("w_out", [L, 2048, D])
        self.att_sinks = self.din("att_sinks", [L, 8])
        self.lru_conv_w = self.din("lru_conv_w", [L, 4, 768])
        self.lru_conv_b = self.din("lru_conv_b", [L, 768])
        self.lru_wa = self.din("lru_wa", [L, 8, 96, 96])
        self.lru_ba = self.din("lru_ba", [L, 768])
        self.lru_wx = self.din("lru_wx", [L, 8, 96, 96])
        self.lru_bx = self.din("lru_bx", [L, 768])
        self.lru_lambda = self.din("lru_lambda", [L, 768])
        self.ssd_conv_w = self.din("ssd_conv_w", [L, 4, 1280])
        self.ssd_conv_b = self.din("ssd_conv_b", [L, 1280])
        self.ssd_dt_bias = self.din("ssd_dt_bias", [L, 12])
        self.ssd_a_log = self.din("ssd_a_log", [L, 12])
        self.ssd_d = self.din("ssd_d", [L, 12])
        self.ssd_norm_g = self.din("ssd_norm_g", [L, 768])
        self.ln_g = self.din("ln_g", [L, D])
        self.ln_b = self.din("ln_b", [L, D])
        self.consts = self.din("consts", [128, K_END])
        self.y_prompt = self.dout("y_prompt", [SEQ, D])
        self.y_sample = self.dout("y_sample", [NS, D])
        self.p_swa_k = self.dout("p_swa_k", [L, 128, 128])
        self.p_swa_v = self.dout("p_swa_v", [L, 128, 128])
        self.p_lru_conv = self.dout("p_lru_conv", [L, 3, 768])
        self.p_lru_h = self.dout("p_lru_h", [L, 768])
        self.p_ssd_conv = self.dout("p_ssd_conv", [L, 3, 1280])
        self.p_ssd_h = self.dout("p_ssd_h", [L, 768, 128])
        self.s_swa_k = self.dout("s_swa_k", [L, NS, 128, 128])
        self.s_swa_v = self.dout("s_swa_v", [L, NS, 128, 128])
        self.s_lru_conv = self.dout("s_lru_conv", [L, NS, 3, 768])
        self.s_lru_h = self.dout("s_lru_h", [L, NS, 768])
        self.s_ssd_conv = self.dout("s_ssd_conv", [L, NS, 3, 1280])
        self.s_ssd_h = self.dout("s_ssd_h", [L, NS, 768, 128])
        self.w_in_bf = self.nc.dram_tensor("w_in_bf", [L, D, DIN], BF16).ap()
        self.w_out_bf = self.nc.dram_tensor("w_out_bf", [L, 2048, D], BF16).ap()
        self.dbg_out = {k: self.dout("dbg_" + k, list(v)) for k, v in self.dbg.items()}

    def tap(self, name, ap, reads):
        if name in self.dbg_out:
            self.P.dma("pool", self.dbg_out[name], ap, reads=reads)

    def alloc(self):
        sb = self.sb
        self.cst = sb("cst", [128, K_END])
        self.ident_bf = sb("ident_bf", [128, 128], BF16)
        self.maskbf = sb("maskbf", [128, 2, 256], BF16)
        self.cosT = sb("cosT", [128, 16, 8]); self.sinT = sb("sinT", [128, 16, 8]); self.nsinT = sb("nsinT", [128, 16, 8])
        self.ropeS = sb("ropeS", [128, 3, 8])
        self.xT32 = sb("xT32", [128, 8, H])
        self.xTb = sb("xTb", [128, 8, H], BF16)
        self.mixT = sb("mixT", [128, 18, H], BF16)
        self.wbuf = [sb("wbuf%d" % i, [128, 8, 128], BF16) for i in range(6)]
        self.wpool = {"att": [0], "lru": [1, 2], "ssd": [3, 4], "out": [5, 0, 1], "all": [0, 1, 2, 3, 4, 5]}
        self.wcnt = {k: 0 for k in self.wpool}
        self.wqkv = sb("wqkv", [128, 8, 768], BF16)
        self.kT = sb("kT", [128, 5 * 128], BF16)
        self.vtm = sb("vtm", [128, 5, 128], BF16)
        self.ck = sb("ck", [128, DEPTH, 128], BF16); self.cv = sb("cv", [128, DEPTH, 128], BF16)
        self.hist_l = sb("hist_l", [128, DEPTH, 8, 3]); self.hist_s = sb("hist_s", [128, DEPTH, 10, 3])
        self.hcar = sb("hcar", [128, DEPTH, 8])
        self.hT32 = sb("hT32", [128, DEPTH, 768]); self.hTb = sb("hTb", [128, 768], BF16)
        self.lp = sb("lp", [128, DEPTH, 8, 8])
        self.la = sb("la", [128, DEPTH, 8, 4])
        self.spp = sb("spp", [128, DEPTH, 10, 5])
        self.gS = sb("gS", [128, DEPTH, 6])
        self.lnp = sb("lnp", [128, DEPTH, 8, 2])
        self.sinkb = sb("sinkb", [128, DEPTH, 8]); self.nsinkb = sb("nsinkb", [128, DEPTH, 8])
        self.Ab = sb("Ab", [128, DEPTH, 12]); self.Db = sb("Db", [128, DEPTH, 12])
        self.dtb = sb("dtb", [128, DEPTH])
        self.wab = sb("wab", [128, 8, 96], BF16); self.wxb = sb("wxb", [128, 8, 96], BF16)
        self.pstage = self.xT32[:].rearrange("p a b -> p (a b)")[0:8, 0:1280]
        self.adc = sb("adc", [128, DEPTH, 2]); self.sinkc = sb("sinkc", [128, DEPTH, 2])
        self.qkv32 = [sb("qkv32_%d" % i, [128, 768]) for i in range(2)]
        self.qkbf = sb("qkbf", [128, 640], BF16)
        self.ropet = sb("ropet", [128, 2, 10, 16])
        self.qT = sb("qT", [128, 4, H], BF16)
        self.G = [sb("G%d" % i, [128, H]) for i in range(2)]
        self.Pm = [sb("Pm%d" % i, [128, 512], BF16) for i in range(2)]
        self.PT = [sb("PT%d" % i, [128, 512], BF16) for i in range(2)]
        self.ast = [sb("ast%d" % i, [128, 8, 2]) for i in range(2)]
        self.xpad2 = sb("xpad2", [128, H + 3])
        self.xpad = sb("xpad", [128, H + 3]); self.xl = sb("xl", [128, H]); self.xlb = sb("xlb", [128, H], BF16)
        self.rr = sb("rr", [128, H]); self.ig = sb("ig", [128, H]); self.EE = sb("EE", [128, H])
        self.zs = sb("zs", [128, 6, H]); self.xsT = sb("xsT", [128, 6, H], BF16)
        self.BT = sb("BT", [128, 2, H], BF16); self.CT = sb("CT", [128, 2, H], BF16)
        self.dtT = sb("dtT", [128, H]); self.dtt = sb("dtt", [128, H])
        self.xstm = sb("xstm", [128, 768], BF16); self.Btm = sb("Btm", [128, 256], BF16)
        self.tsm = sb("tsm", [128, 8, 12])
        self.aU = sb("aU", [128, 12, 128]); self.Dm = sb("Dm", [128, 12, 128])
        self.Ebc = sb("Ebc", [128, 12, 128], BF16); self.GT = sb("GT", [128, 12, 128], BF16)
        self.CE = sb("CE", [128, 12, 128], BF16); self.CBm = sb("CBm", [128, 2, 128])
        self.xdt = sb("xdt", [128, 768], BF16); self.xdd = sb("xdd", [128, 768], BF16); self.xsD = sb("xsD", [128, 768], BF16)
        self.yg = sb("yg", [128, 6, 128]); self.sq = sb("sq", [128, 6, 128]); self.rstd = sb("rstd", [128, 128])
        self.lnq = [sb("lnq%d" % i, [128, H]) for i in range(2)]
        self.mean = sb("mean", [128, H]); self.lrs = sb("lrs", [128, H])
        self.iost = sb("iost", [128, D])
        self.ps = self.st.enter_context(self.nc.psum_tensor("ps", [128, 4096], F32))

    def bank(self, b, n=1):
        return self.ps[:, b * 512:(b + n) * 512]

    def bankbf(self, b):
        return self.ps[:, b * 512:(b + 1) * 512].bitcast(BF16)

    def pT(self, b):
        return self.P.t("ps", b)

    @staticmethod
    def _fs(ap):
        n = 1
        for d in ap.shape[1:]:
            n *= d
        return n

    def _c(self, eng, out, in_=None):
        F = self._fs(out)
        ps = (in_ is not None and str(in_.space).lower().find("psum") >= 0)
        if eng == "act":
            return (224 + F) / 1200.0
        if eng == "pool":
            return (100 + 2 * F) / 1200.0
        return 1.2 * ((120 if ps else 60) + F) / 960.0

    TBL = {AF.Exp: "E", AF.Tanh: "A", AF.Ln: "L", AF.Sqrt: "S", AF.Sigmoid: "G", AF.Silu: "U", AF.Sin: "N"}

    def act(self, out, in_, func, r, w, **kw):
        o = self.P.op("act", lambda e: e.activation(out=out, in_=in_, func=func, **kw), r, w, cost=self._c("act", out))
        o.tbl = self.TBL.get(func)
        return o

    def acopy(self, out, in_, r, w):
        return self.P.op("act", lambda e: e.copy(out=out, in_=in_), r, w, cost=self._c("act", out))

    def vcopy(self, out, in_, r, w, eng="dve"):
        return self.P.op(eng, lambda e: e.tensor_copy(out=out, in_=in_), r, w, cost=self._c(eng, out, in_))

    def tt(self, out, a, b, op, r, w, eng="dve"):
        return self.P.op(eng, lambda e: e.tensor_tensor(out=out, in0=a, in1=b, op=op), r, w, cost=self._c(eng, out, a))

    def ts(self, out, a, s1, s2, op0, op1, r, w, eng="dve"):
        c = self._c(eng, out, a)
        if op1 is None:
            return self.P.op(eng, lambda e: e.tensor_scalar(out=out, in0=a, scalar1=s1, scalar2=None, op0=op0), r, w, cost=c)
        return self.P.op(eng, lambda e: e.tensor_scalar(out=out, in0=a, scalar1=s1, scalar2=s2, op0=op0, op1=op1), r, w, cost=c)

    def stt(self, out, a, s, b, op0, op1, r, w, eng="dve"):
        return self.P.op(eng, lambda e: e.scalar_tensor_tensor(out=out, in0=a, scalar=s, in1=b, op0=op0, op1=op1), r, w,
                         cost=self._c(eng, out, a))

    def mm(self, out, lhsT, rhs, start, stop, r, w):
        N = self._fs(out)
        k = 4.0 if rhs.dtype == F32 else 1.0
        c = 1.4 * (max(N, 64) * k / 2400.0 + 0.035)
        return self.P.op("pe", lambda e: e.matmul(out, lhsT=lhsT, rhs=rhs, start=start, stop=stop), r, w,
                         cost=(-c if k > 1 else c))

    def tr(self, out, in_, ident, r, w):
        N = self._fs(in_)
        k = 4.0 if in_.dtype == F32 else 1.0
        c = 1.4 * (max(N, 64) * k / 2400.0 + 0.06)
        return self.P.op("pe", lambda e: e.transpose(out, in_, ident), r, w, cost=(-c if k > 1 else c))

    def flush_casts(self, l):
        for (dst, src, tl) in self.pending_casts.get(l, []):
            self.P.dma("pool", dst, src, writes=[tl])
        self.pending_casts[l] = []

    def pace_cast(self, thr):
        l = getattr(self, "cast_layer", None)
        if l is None:
            return
        if l == 1 and self.pending_casts.get(0):
            l = 0
        if not self.pending_casts.get(l):
            return
        dst, src, tl = self.pending_casts[l].pop(0)
        self.P.dma("pool", dst, src, reads=[thr], writes=[tl])

    def thr_tile(self):
        self.thr_n = getattr(self, "thr_n", 0) + 1
        return self.P.t("thr", self.thr_n)

    def next_wbuf(self, pool="all"):
        lst = self.wpool[pool]
        k = lst[self.wcnt[pool] % len(lst)]
        self.wcnt[pool] += 1
        return k

    def win_T(self, l):
        return [self.P.t("winbf", l, i) for i in range(8)]

    def wout_T(self, l):
        return [self.P.t("woutbf", l, i) for i in range(16)]

    def load_win(self, l, pieces, pool="all"):
        k = self.next_wbuf(pool)
        src = self.w_in_bf[l].rearrange("(kc p) c -> p kc c", p=128)
        for (c0, n, d0) in pieces:
            th = self.thr_tile()
            self.P.dma("sp", self.wbuf[k][:, :, d0:d0 + n], src[:, :, c0:c0 + n],
                       reads=self.win_T(l), writes=[self.P.t("wbuf", k), th])
            self.pace_cast(th)
        return k

    def inproj(self, l, k, M, psout, psT, ncols=H, x=None, xT=None):
        x = self.xTb if x is None else x
        xT = [self.P.t("xTb", kc) for kc in range(8)] if xT is None else xT
        for kc in range(8):
            self.mm(psout[0:M, 0:ncols], self.wbuf[k][:, kc, 0:M], x[:, kc, 0:ncols], kc == 0, kc == 7,
                    [self.P.t("wbuf", k)] + xT, [psT])

    def prologue(self):
        P, t = self.P, self.P.t
        cst = self.cst
        P.dma("sp", cst[:], self.consts, writes=[t("cst")])
        self.pending_casts = {l: [] for l in range(DEPTH)}
        for l in range(DEPTH):
            for i in range(8):
                self.pending_casts[l].append((self.w_in_bf[l, i * 128:(i + 1) * 128, :], self.w_in[l, i * 128:(i + 1) * 128, :], t("winbf", l, i)))
            for i in range(16):
                self.pending_casts[l].append((self.w_out_bf[l, i * 128:(i + 1) * 128, :], self.w_out[l, i * 128:(i + 1) * 128, :], t("woutbf", l, i)))
        for (dst, src, tl) in self.pending_casts[0][:8]:
            P.dma("pool", dst, src, writes=[tl])
        self.pending_casts[0] = self.pending_casts[0][8:]
        self.vcopy(self.ident_bf[:], cst[:, K_ID:K_ID + 128], [t("cst")], [t("ident_bf")])
        self.vcopy(self.maskbf[:, 0, :], cst[:, K_MG:K_MG + 256], [t("cst")], [t("maskbf")])
        self.vcopy(self.maskbf[:, 1, :], cst[:, K_MF:K_MF + 256], [t("cst")], [t("maskbf")])
        for buf, nm in ((self.ck, "ck"), (self.cv, "cv")):
            P.op("dve", lambda e, buf=buf: e.memset(buf[:], 0.0), [], [t(nm, l) for l in range(DEPTH)])
        P.op("dve", lambda e: e.memset(self.hist_l[:], 0.0), [], [t("hist_l", l) for l in range(DEPTH)])
        P.op("dve", lambda e: e.memset(self.hist_s[:], 0.0), [], [t("hist_s", l) for l in range(DEPTH)])
        P.op("dve", lambda e: e.memset(self.hcar[:], 0.0), [], [t("hcar", l) for l in range(DEPTH)])
        P.op("dve", lambda e: e.memset(self.hT32[:], 0.0), [], [t("hT32", l, g) for l in range(DEPTH) for g in range(2)])
        self.rope_tables()
        for l in range(DEPTH):
            self.layer_params(l)

    def sincos(self, ang, n, outs, rd):
        P, t = self.P, self.P.t
        tmp = self.iost
        c1 = float(np.float32(2 * np.pi)); c2 = float(2 * np.pi - c1)
        for j, (shift, out) in enumerate(((0.5 * np.pi, outs[0]), (0.0, outs[1]))):
            a = tmp[:, 0:n]; kf = tmp[:, n:2 * n]; ki = tmp[:, 2 * n:3 * n].bitcast(I32); m = tmp[:, 3 * n:4 * n]
            W = [t("iost")]
            self.ts(a, ang, float(shift), None, ALU.add, None, rd + W, W)
            self.ts(kf, a, float(1.0 / (2 * np.pi)), None, ALU.mult, None, W, W)
            self.vcopy(ki, kf, W, W)
            self.vcopy(kf, ki, W, W)
            self.stt(a, kf, -c1, a, ALU.mult, ALU.add, W, W)
            self.stt(a, kf, -c2, a, ALU.mult, ALU.add, W, W)
            self.ts(m, a, float(np.pi), float(-2 * np.pi), ALU.is_gt, ALU.mult, W, W)
            self.tt(a, a, m, ALU.add, W, W)
            self.ts(m, a, float(-np.pi), float(2 * np.pi), ALU.is_lt, ALU.mult, W, W)
            self.tt(a, a, m, ALU.add, W, W)
            self.act(out, a, AF.Sin, W, [t("rope")])
        self.ts(outs[2], outs[1], -1.0, None, ALU.mult, None, [t("rope")], [t("rope")])

    def rope_tables(self):
        t = self.P.t
        cst = self.cst
        ang = self.iost[:, 512:640]
        self.tt(ang.rearrange("p (a b) -> p a b", b=8), bc(cst[:, K_POS:K_POS + 16], [128, 16, 8], 2),
                bc(cst[:, K_INV:K_INV + 8], [128, 16, 8], 1), ALU.mult, [t("cst")], [t("iost")])
        f = lambda x: x[:].rearrange("p a b -> p (a b)")
        self.sincos(ang, 128, (f(self.cosT), f(self.sinT), f(self.nsinT)), [t("iost")])
        angs = self.iost[:, 640:648]
        self.ts(angs, cst[:, K_INV:K_INV + 8], PAST, None, ALU.mult, None, [t("cst")], [t("iost")])
        self.sincos(angs, 8, (self.ropeS[:, 0, :], self.ropeS[:, 1, :], self.ropeS[:, 2, :]), [t("iost")])

    def layer_params(self, l):
        P, t = self.P, self.P.t
        cst = self.cst
        idf = cst[:, K_ID:K_ID + 128]
        stg = self.pstage
        S = [t("xT32", dc) for dc in range(8)]
        W = [t("par", l)]
        P.dma("sp", stg[0:4, 0:768], self.lru_conv_w[l], writes=S)
        for i, src in enumerate((self.lru_conv_b, self.lru_ba, self.lru_bx, self.lru_lambda)):
            P.dma("sp", stg[4 + i:5 + i, 0:768], src[l:l + 1, :], writes=S)
        pb = self.bank(0)
        for n in range(8):
            self.tr(pb[0:96, n * 8:(n + 1) * 8], stg[0:8, n * 96:(n + 1) * 96], idf[0:8, 0:8], S + [t("cst")], [self.pT(0)])
        self.vcopy(self.lp[0:96, l, :, :], pb[0:96, 0:64].rearrange("p (a b) -> p a b", b=8), [self.pT(0)], W)
        la = self.la
        self.act(la[0:96, l, :, 0], self.lp[0:96, l, :, 7], AF.Exp, W, W, scale=-1.0)
        self.act(la[0:96, l, :, 0], la[0:96, l, :, 0], AF.Ln, W, W, bias=1.0)
        self.ts(la[0:96, l, :, 1], la[0:96, l, :, 0], -8.0, None, ALU.mult, None, W, W)
        self.ts(la[0:96, l, :, 0], la[0:96, l, :, 0], -4.0, None, ALU.mult, None, W, W)
        self.ts(la[0:96, l, :, 2:4], self.lp[0:96, l, :, 5:7], 0.5, None, ALU.mult, None, W, W)
        P.dma("sp", stg[0:4, 0:1280], self.ssd_conv_w[l], writes=S)
        P.dma("sp", stg[4:5, 0:1280], self.ssd_conv_b[l:l + 1, :], writes=S)
        pb = self.bank(1)
        for c in range(10):
            self.tr(pb[:, c * 5:(c + 1) * 5], stg[0:5, c * 128:(c + 1) * 128], idf[0:5, 0:5], S + [t("cst")], [self.pT(1)])
        self.ts(self.spp[:, l, :, :], pb[:, 0:50].rearrange("p (a b) -> p a b", b=5), 0.5, None, ALU.mult, None, [self.pT(1)], W)
        P.dma("sp", stg[0:1, 0:768], self.ssd_norm_g[l:l + 1, :], writes=S)
        P.dma("sp", stg[1:2, 0:1024], self.ln_g[l:l + 1, :], writes=S)
        P.dma("sp", stg[2:3, 0:1024], self.ln_b[l:l + 1, :], writes=S)
        pb = self.bank(2)
        for c in range(6):
            self.tr(pb[:, c:c + 1], stg[0:1, c * 128:(c + 1) * 128], idf[0:1, 0:1], S + [t("cst")], [self.pT(2)])
        self.ts(self.gS[:, l, :], pb[:, 0:6], float(math.sqrt(768.0)), None, ALU.mult, None, [self.pT(2)], W)
        P.dma("sp", stg[0:1, 0:1024], self.ln_g[l:l + 1, :], writes=S)
        P.dma("sp", stg[1:2, 0:1024], self.ln_b[l:l + 1, :], writes=S)
        pb = self.bank(3)
        for c in range(8):
            self.tr(pb[:, 2 * c:2 * c + 2], stg[0:2, c * 128:(c + 1) * 128], idf[0:2, 0:2], S + [t("cst")], [self.pT(3)])
        self.vcopy(self.lnp[:, l, :, :], pb[:, 0:16].rearrange("p (a b) -> p a b", b=2), [self.pT(3)], W)
        P.dma("sp", self.sinkb[:, l, :], self.att_sinks[l:l + 1, :].partition_broadcast(128), writes=W)
        self.ts(self.nsinkb[:, l, :], self.sinkb[:, l, :], -1.0, None, ALU.mult, None, W, W)
        P.dma("sp", self.Ab[:, l, :], self.ssd_a_log[l:l + 1, :].partition_broadcast(128), writes=W)
        self.act(self.Ab[:, l, :], self.Ab[:, l, :], AF.Exp, W, W)
        self.ts(self.Ab[:, l, :], self.Ab[:, l, :], -1.0, None, ALU.mult, None, W, W)
        P.dma("sp", self.Db[:, l, :], self.ssd_d[l:l + 1, :].partition_broadcast(128), writes=W)
        P.dma("sp", self.dtb[0:12, l:l + 1], self.ssd_dt_bias[l].rearrange("(a b) -> a b", b=1), writes=W)
        P.dma("sp", self.adc[0:12, l, 0:1], self.ssd_a_log[l].rearrange("(a b) -> a b", b=1), writes=W)
        self.act(self.adc[0:12, l, 0:1], self.adc[0:12, l, 0:1], AF.Exp, W, W)
        self.ts(self.adc[0:12, l, 0:1], self.adc[0:12, l, 0:1], -1.0, None, ALU.mult, None, W, W)
        P.dma("sp", self.adc[0:12, l, 1:2], self.ssd_d[l].rearrange("(a b) -> a b", b=1), writes=W)
        P.dma("sp", self.sinkc[0:8, l, 0:1], self.att_sinks[l].rearrange("(a b) -> a b", b=1), writes=W)
        self.ts(self.sinkc[0:8, l, 1:2], self.sinkc[0:8, l, 0:1], -1.0, None, ALU.mult, None, W, W)

    def job(self, l, p):
        ph = getattr(self, "phases", "xjqalso")
        self.cast_layer = (l + 1) if (p == 0 and l + 1 < DEPTH) else None
        if "x" in ph: self.load_x(l, p)
        if "j" in ph: self.job_prologue(l, p)
        if "q" in ph: self.phase_qkv(l, p)
        if "a" in ph: self.phase_att(l, p)
        if "s" in ph: self.phase_ssd(l, p)
        if "l" in ph: self.phase_lru(l, p)
        if "s" in ph: self.phase_ssd_tiles(l, p)
        if l == 0 and p == 0:
            self.tap("mixT", self.mixT[:], [self.P.t("mixT", ec, tl) for ec in range(18) for tl in range(NT)])
        if "o" in ph: self.phase_out(l, p)
        if self.cast_layer is not None:
            self.flush_casts(0)
            self.flush_casts(self.cast_layer)
        self.cast_layer = None

    def load_x(self, l, p):
        P, t = self.P, self.P.t
        if l != 0:
            return
        idf = self.cst[:, K_ID:K_ID + 128]
        for tl in range(NT):
            gt = p * NT + tl
            P.dma("sp", self.iost[:], self.x_prompt[gt * 128:(gt + 1) * 128, :], writes=[t("iost")])
            for hb in range(2):
                pb = self.bank(hb)
                for j in range(4):
                    dc = hb * 4 + j
                    self.tr(pb[:, j * 128:(j + 1) * 128], self.iost[:, dc * 128:(dc + 1) * 128], idf,
                            [t("iost"), t("cst")], [self.pT(hb)])
                dst = self.xT32[:, hb * 4:hb * 4 + 4, tl * 128:(tl + 1) * 128]
                self.vcopy(dst, pb.rearrange("p (a b) -> p a b", b=128), [self.pT(hb)],
                           [t("xT32", dc) for dc in range(hb * 4, hb * 4 + 4)])
        for dc in range(8):
            self.acopy(self.xTb[:, dc, :], self.xT32[:, dc, :], [t("xT32", dc)], [t("xTb", dc)])
        if p == 0:
            self.tap("xT_in", self.xT32[:], [t("xT32", dc) for dc in range(8)])

    def job_prologue(self, l, p):
        P, t = self.P, self.P.t
        self.vcopy(self.kT[:, 0:128], self.ck[:, l, :], [t("ck", l)], [t("kT", 0)], eng="pool")
        self.vcopy(self.vtm[:, 0, :], self.cv[:, l, :], [t("cv", l)], [t("v", 0)], eng="pool")
        self.vcopy(self.hTb[:], self.hT32[:, l, :], [t("hT32", l, 0), t("hT32", l, 1)], [t("hTb", 0), t("hTb", 1)], eng="pool")
        P.dma("pool", self.wab[0:96, :, :], self.lru_wa[l].rearrange("n c d -> c n d"), writes=[t("wab")])
        P.dma("pool", self.wxb[0:96, :, :], self.lru_wx[l].rearrange("n c d -> c n d"), writes=[t("wxb")])
        src = self.w_in_bf[l].rearrange("(kc p) c -> p kc c", p=128)
        P.dma("sp", self.wqkv[:], src[:, :, 0:768], reads=self.win_T(l), writes=[t("wqkv", j) for j in range(6)])

    def phase_qkv(self, l, p):
        P, t = self.P, self.P.t
        xT = [t("xTb", kc) for kc in range(8)]
        last = (p == NPASS - 1)
        for tl in range(NT):
            gt = p * NT + tl
            pa, pb_, pc = (0, 1, 4) if tl % 2 == 0 else (2, 3, 5)
            A, B = self.bank(pa), self.bank(pb_)
            for kc in range(8):
                lhs = self.xTb[:, kc, tl * 128:(tl + 1) * 128]
                WQ = [t("wqkv", j) for j in range(6)]
                self.mm(A, lhs, self.wqkv[:, kc, 0:512], kc == 0, kc == 7, xT + WQ, [self.pT(pa)])
                self.mm(B[:, 0:256], lhs, self.wqkv[:, kc, 512:768], kc == 0, kc == 7, xT + WQ, [self.pT(pb_)])
            q32 = self.qkv32[tl % 2]
            Q = [t("qkv32", tl % 2)]
            self.acopy(q32[:, 0:512], A, [self.pT(pa)], Q)
            self.acopy(q32[:, 512:768], B[:, 0:256], [self.pT(pb_)], Q)
            hv = q32[:, 0:640].rearrange("p (h d) -> p h d", d=64)
            x1, x2 = hv[:, :, 0:8], hv[:, :, 8:16]
            cs = bc(self.cosT[:, gt, :], [128, 10, 8], 1)
            sn = bc(self.sinT[:, gt, :], [128, 10, 8], 1)
            ns = bc(self.nsinT[:, gt, :], [128, 10, 8], 1)
            R = [t("ropet")]
            ra, rb = self.ropet[:, 0, :, :], self.ropet[:, 1, :, :]
            self.tt(rb[:, :, 0:8], x2, ns, ALU.mult, Q + [t("rope")], R)
            self.tt(rb[:, :, 8:16], x1, sn, ALU.mult, Q + [t("rope")], R)
            self.tt(ra[:, :, 0:8], x1, cs, ALU.mult, Q + [t("rope")], R)
            self.tt(ra[:, :, 8:16], x2, cs, ALU.mult, Q + [t("rope")], R)
            self.tt(hv[:, :, 0:16], ra, rb, ALU.add, R, Q)
            if l == 0 and p == 0 and tl == 1:
                self.tap("qkv_t1", q32[:], Q)
            if last and tl == NT - 1:
                P.dma("pool", self.p_swa_k[l], q32[:, 512:640], reads=Q)
                P.dma("pool", self.p_swa_v[l], q32[:, 640:768], reads=Q)
            self.acopy(self.qkbf[:, 0:512].rearrange("p (c w d) -> p c w d", c=4, w=2),
                       q32[:, 0:512].rearrange("p (w c d) -> p c w d", w=2, c=4), Q, [t("qkbf")])
            self.acopy(self.qkbf[:, 512:640], q32[:, 512:640], Q, [t("qkbf")])
            self.vcopy(self.vtm[:, 1 + tl, :], q32[:, 640:768], Q, [t("v", 1 + tl)], eng="pool")
            C = self.bankbf(pc)
            for j in range(5):
                self.tr(C[:, j * 128:(j + 1) * 128], self.qkbf[:, j * 128:(j + 1) * 128], self.ident_bf[:],
                        [t("qkbf"), t("ident_bf")], [self.pT(pc)])
            self.vcopy(self.qT[:, :, tl * 128:(tl + 1) * 128], C[:, 0:512].rearrange("p (c q) -> p c q", q=128),
                       [self.pT(pc)], [t("qT", tl)])
            self.vcopy(self.kT[:, (1 + tl) * 128:(2 + tl) * 128], C[:, 512:640], [self.pT(pc)], [t("kT", 1 + tl)])
        self.vcopy(self.ck[:, l, :], self.kT[:, 512:640], [t("kT", 4)], [t("ck", l)], eng="pool")
        self.vcopy(self.cv[:, l, :], self.vtm[:, 4, :], [t("v", 4)], [t("cv", l)], eng="pool")

    def phase_att(self, l, p):
        P, t = self.P, self.P.t
        cnt = 0
        for c in range(4):
            k = self.load_win(l, [(C_GA + c * 64, 64, 0), (C_GA + (c + 4) * 64, 64, 64)], pool="att")
            gb = c % 2
            pg = self.bank(3)
            self.inproj(l, k, 128, pg, self.pT(3))
            G = self.G[gb]
            self.act(G[:], pg, AF.Tanh, [self.pT(3)], [t("G", gb)], scale=0.5)
            self.stt(G[:], G[:], 1.0, pg, ALU.add, ALU.mult, [self.pT(3), t("G", gb)], [t("G", gb)])
            for tl in range(NT):
                gt = p * NT + tl
                i = cnt % 2
                cnt += 1
                bS, bP, bO = 1, 2, 3
                S = self.bank(bS)
                mk = self.maskbf[:, 1 if gt == 0 else 0, :]
                for h in range(2):
                    rows = slice(h * 64, (h + 1) * 64)
                    self.mm(S[:, h * 256:(h + 1) * 256], self.qT[rows, c, tl * 128:(tl + 1) * 128],
                            self.kT[rows, tl * 128:tl * 128 + 256], True, False,
                            [t("qT", tl), t("kT", tl), t("kT", tl + 1)], [self.pT(bS)])
                    self.mm(S[:, h * 256:(h + 1) * 256], self.ident_bf[:], mk, False, True,
                            [t("ident_bf"), t("maskbf")], [self.pT(bS)])
                st = self.ast[i]
                A = [t("ast", i)]
                mx, negm, ssum, es, den, rden = (st[:, j, :] for j in range(6))
                P.op("dve", lambda e, mx=mx, S=S: e.reduce_max(out=mx, in_=S.rearrange("p (h k) -> p h k", k=256), axis=AX.X),
                     [self.pT(bS)], A)
                hsel = self.nsinkb[:, l, c:c + 5:4]
                self.stt(negm, mx, -SCALE, hsel, ALU.mult, ALU.min, A + [t("par", l)], A)
                Pm = self.Pm[i]
                for h in range(2):
                    self.act(Pm[:, h * 256:(h + 1) * 256], S[:, h * 256:(h + 1) * 256], AF.Exp,
                             [self.pT(bS)] + A, [t("Pm", i)] + A, scale=SCALE, bias=negm[:, h:h + 1], accum_out=ssum[:, h:h + 1])
                self.tt(es, negm, self.sinkb[:, l, c:c + 5:4], ALU.add, A + [t("par", l)], A)
                self.act(es, es, AF.Exp, A, A)
                self.tt(den, ssum, es, ALU.add, A, A)
                P.op("dve", lambda e, rden=rden, den=den: e.reciprocal(out=rden, in_=den), A, A)
                pv = Pm[:].rearrange("p (h k) -> p h k", k=256)
                self.tt(pv, pv, bc(rden, [128, 2, 256], 2), ALU.mult, [t("Pm", i)] + A, [t("Pm", i)])
                PTp = self.bankbf(bP)
                for h in range(2):
                    for kb in range(2):
                        j = h * 2 + kb
                        self.tr(PTp[:, j * 128:(j + 1) * 128], Pm[:, h * 256 + kb * 128:h * 256 + (kb + 1) * 128],
                                self.ident_bf[:], [t("Pm", i), t("ident_bf")], [self.pT(bP)])
                PT = self.PT[i]
                self.acopy(PT[:], PTp[:, 0:512], [self.pT(bP)], [t("PT", i)])
                O = self.bank(bO)
                for h in range(2):
                    for kb in range(2):
                        j = h * 2 + kb
                        self.mm(O[h * 64:(h + 1) * 64, 0:128], self.vtm[:, tl + kb, h * 64:(h + 1) * 64],
                                PT[:, j * 128:(j + 1) * 128], kb == 0, kb == 1,
                                [t("v", tl + kb), t("PT", i)], [self.pT(bO)])
                self.stt(self.mixT[:, c, tl * 128:(tl + 1) * 128], O[:, 0:128], 0.5, G[:, tl * 128:(tl + 1) * 128], ALU.mult, ALU.mult,
                         [self.pT(bO), t("G", gb)], [t("mixT", c, tl)])

    def conv4(self, out, xpad, par, n, M, rd, wr):
        self.act(out[0:M, :], xpad[0:M, 0:H], AF.Identity, rd, wr, scale=par[0:M, 0:1], bias=par[0:M, 4:5])
        for k in range(1, 4):
            self.stt(out[0:M, :], xpad[0:M, k:k + H], par[0:M, k:k + 1], out[0:M, :], ALU.mult, ALU.add, rd + wr, wr)

    def phase_lru(self, l, p):
        P, t = self.P, self.P.t
        last = (p == NPASS - 1)
        for n in range(8):
            kx = self.load_win(l, [(C_XL + n * 96, 96, 0)], pool="lru")
            kg = self.load_win(l, [(C_GL + n * 96, 96, 0)], pool="lru")
            px, pg, pr, pi = self.bank(5), self.bank(6), self.bank(5), self.bank(7)
            pxT, pgT, prT, piT = self.pT(5), self.pT(6), self.pT(5), self.pT(7)
            self.inproj(l, kx, 96, px, pxT)
            self.inproj(l, kg, 96, pg, pgT)
            par = self.lp[:, l, n, :]
            PR = [t("par", l)]
            xp, xl, xlb, rr, ig, EE = self.xpad, self.xl, self.xlb, self.rr, self.ig, self.EE
            self.vcopy(xp[0:96, 0:3], self.hist_l[0:96, l, n, :], [t("hist_l", l)], [t("xpad")])
            self.acopy(xp[0:96, 3:3 + H], px[0:96, :], [pxT], [t("xpad")])
            self.vcopy(self.hist_l[0:96, l, n, :], xp[0:96, H:H + 3], [t("xpad")], [t("hist_l", l)])
            self.conv4(xl, xp, par, n, 96, [t("xpad")] + PR, [t("xl")])
            self.acopy(xlb[0:96, :], xl[0:96, :], [t("xl")], [t("xlb")])
            self.mm(pr[0:96, :], self.wab[0:96, n, :], xlb[0:96, :], True, True, [t("wab"), t("xlb")], [prT])
            self.mm(pi[0:96, :], self.wxb[0:96, n, :], xlb[0:96, :], True, True, [t("wxb"), t("xlb")], [piT])
            lac = self.la[0:96, l, n, :]
            self.act(rr[0:96, :], pr[0:96, :], AF.Tanh, [prT] + PR, [t("rr")], scale=0.5, bias=lac[:, 2:3])
            self.act(ig[0:96, :], pi[0:96, :], AF.Tanh, [piT] + PR, [t("ig")], scale=0.5, bias=lac[:, 3:4])
            self.act(EE[0:96, :], rr[0:96, :], AF.Exp, [t("rr")] + PR, [t("EE")], scale=lac[:, 1:2], bias=lac[:, 1:2])
            self.act(rr[0:96, :], rr[0:96, :], AF.Exp, [t("rr")] + PR, [t("rr")], scale=lac[:, 0:1], bias=lac[:, 0:1])
            self.act(EE[0:96, :], EE[0:96, :], AF.Sqrt, [t("EE")], [t("EE")], scale=-0.25, bias=0.25)
            self.stt(ig[0:96, :], ig[0:96, :], 1.0, xl[0:96, :], ALU.add, ALU.mult, [t("ig"), t("xl")], [t("ig")])
            self.tt(ig[0:96, :], ig[0:96, :], EE[0:96, :], ALU.mult, [t("ig"), t("EE")], [t("ig")])
            P.op("dve", lambda e, n=n: e.tensor_tensor_scan(out=EE[0:96, :], data0=rr[0:96, :], data1=ig[0:96, :],
                                                            initial=self.hcar[0:96, l, n:n + 1], op0=ALU.mult, op1=ALU.add),
                 [t("rr"), t("ig"), t("hcar", l)], [t("EE")])
            self.vcopy(self.hcar[0:96, l, n:n + 1], EE[0:96, H - 1:H], [t("EE")], [t("hcar", l)])
            self.act(xl[0:96, :], pg[0:96, :], AF.Tanh, [pgT], [t("xl")], scale=0.5)
            self.stt(xl[0:96, :], xl[0:96, :], 1.0, pg[0:96, :], ALU.add, ALU.mult, [pgT, t("xl")], [t("xl")])
            self.stt(self.mixT[0:96, 4 + n, :], EE[0:96, :], 0.5, xl[0:96, :], ALU.mult, ALU.mult, [t("EE"), t("xl")],
                    [t("mixT", 4 + n, tl) for tl in range(NT)])
        if last:
            for k3 in range(3):
                P.dma("pool", self.p_lru_conv[l, k3].rearrange("(n p) -> p n", p=96), self.hist_l[0:96, l, :, k3],
                      reads=[t("hist_l", l)], allow_slow_non_contiguous=True)
            P.dma("pool", self.p_lru_h[l].rearrange("(n p) -> p n", p=96), self.hcar[0:96, l, :],
                  reads=[t("hcar", l)], allow_slow_non_contiguous=True)

    def phase_ssd(self, l, p):
        P, t = self.P, self.P.t
        last = (p == NPASS - 1)
        cst = self.cst
        idf = cst[:, K_ID:K_ID + 128]
        U = cst[:, K_U:K_U + 128]
        ones = cst[:, K_ONE:K_ONE + 128]
        PR = [t("par", l)]
        nb = 0
        for c in range(6):
            k = self.load_win(l, [(C_Z + c * 128, 128, 0)], pool="ssd")
            b = (0, 4)[c % 2]
            pb = self.bank(b); pbT = self.pT(b)
            self.inproj(l, k, 128, pb, pbT)
            self.act(self.zs[:, c, :], pb, AF.Tanh, [pbT], [t("zs", c)], scale=0.5)
            self.stt(self.zs[:, c, :], self.zs[:, c, :], 1.0, pb, ALU.add, ALU.mult, [pbT, t("zs", c)], [t("zs", c)])
        xp, acc, tnh = self.xpad2, self.lnq[0], self.lnq[1]
        for c in range(10):
            k = self.load_win(l, [(C_XBC + c * 128, 128, 0)], pool="ssd")
            b = (0, 4)[c % 2]
            pb = self.bank(b); pbT = self.pT(b)
            self.inproj(l, k, 128, pb, pbT)
            self.vcopy(xp[:, 0:3], self.hist_s[:, l, c, :], [t("hist_s", l)], [t("xpad2")])
            self.acopy(xp[:, 3:3 + H], pb, [pbT], [t("xpad2")])
            self.vcopy(self.hist_s[:, l, c, :], xp[:, H:H + 3], [t("xpad2")], [t("hist_s", l)])
            self.conv4(acc, xp, self.spp[:, l, c, :], c, 128, [t("xpad2")] + PR, [t("lnq", 0)])
            if c < 6:
                dst, dT = self.xsT[:, c, :], t("xsT", c)
            elif c < 8:
                dst, dT = self.BT[:, c - 6, :], t("BT", c - 6)
            else:
                dst, dT = self.CT[:, c - 8, :], t("CT", c - 8)
            self.act(tnh[:], acc[:], AF.Tanh, [t("lnq", 0)], [t("lnq", 1)])
            self.stt(dst, tnh[:], 1.0, acc[:], ALU.add, ALU.mult, [t("lnq", 1), t("lnq", 0)], [dT])
        k = self.load_win(l, [(C_DT, 12, 0)], pool="ssd")
        b = 4
        pb = self.bank(b); pbT = self.pT(b)
        self.inproj(l, k, 12, pb, pbT)
        u, v = self.dtT[0:12, :], self.dtt[0:12, :]
        self.act(u, pb[0:12, :], AF.Identity, [pbT] + PR, [t("dtT")], bias=self.dtb[0:12, l:l + 1])
        self.act(v, u, AF.Abs, [t("dtT")], [t("dtt")])
        self.act(v, v, AF.Exp, [t("dtt")], [t("dtt")], scale=-1.0)
        self.act(v, v, AF.Ln, [t("dtt")], [t("dtt")], bias=1.0)
        self.stt(u, u, 0.0, v, ALU.max, ALU.add, [t("dtT"), t("dtt")], [t("dtT")])
        if last:
            for k3 in range(3):
                P.dma("pool", self.p_ssd_conv[l, k3].rearrange("(c p) -> p c", p=128), self.hist_s[:, l, :, k3],
                      reads=[t("hist_s", l)], allow_slow_non_contiguous=True)

    def phase_ssd_tiles(self, l, p):
        P, t = self.P, self.P.t
        last = (p == NPASS - 1)
        cst = self.cst
        idf = cst[:, K_ID:K_ID + 128]
        U = cst[:, K_U:K_U + 128]
        ones = cst[:, K_ONE:K_ONE + 128]
        PR = [t("par", l)]
        tsm = self.tsm
        dt_tm, a_tm, acs_tm, cd, tmp, dec, dtdec = (tsm[:, j, :] for j in range(7))
        TS = [t("tsm")]
        b1 = self.bank(1)
        for tl in range(NT):
            sl = slice(tl * 128, (tl + 1) * 128)
            self.tr(b1[:, 0:12], self.dtT[0:12, sl], idf[0:12, 0:12], [t("dtT"), t("cst")], [self.pT(1)])
            self.vcopy(dt_tm, b1[:, 0:12], [self.pT(1)], TS)
            self.tt(a_tm, dt_tm, self.Ab[:, l, :], ALU.mult, TS + PR, TS)
            self.mm(b1[:, 16:28], U, a_tm, True, True, [t("cst")] + TS, [self.pT(1)])
            self.vcopy(acs_tm, b1[:, 16:28], [self.pT(1)], TS)
            for g in range(2):
                hs = slice(6 * g, 6 * g + 6)
                cs_ = slice(384 * g, 384 * (g + 1))
                TSg = [t("tsm", g)]
                p0 = self.bankbf(0)
                for j in range(3):
                    c = 3 * g + j
                    self.tr(p0[:, j * 128:(j + 1) * 128], self.xsT[:, c, sl], self.ident_bf[:], [t("xsT", c), t("ident_bf")], [self.pT(0)])
                self.tr(p0[:, 384:512], self.BT[:, g, sl], self.ident_bf[:], [t("BT", g), t("ident_bf")], [self.pT(0)])
                self.acopy(self.xstm[:, cs_], p0[:, 0:384], [self.pT(0)], [t("xstm", g)])
                self.acopy(self.Btm[:, g * 128:(g + 1) * 128], p0[:, 384:512], [self.pT(0)], [t("Btm", g)])
                cb = b1[:, 128 * (g + 1):128 * (g + 2)]
                self.mm(cb, self.BT[:, g, sl], self.CT[:, g, sl], True, True, [t("BT", g), t("CT", g)], [self.pT(1)])
                self.tt(self.CBm[:, g, :], cb, U, ALU.mult, [self.pT(1), t("cst")], [t("CBm", g)])
                aUg = self.aU[:, hs, :]
                self.tt(aUg, bc(U, [128, 6, 128], 1), bc(a_tm[:, hs], [128, 6, 128], 2), ALU.mult, TS + [t("cst")], [t("aU", g)])
                A0 = (2, 5)[g]
                YB = (4, 7)[g]
                pacs = self.ps[:, A0 * 512:A0 * 512 + 768]
                pacsT = [self.pT(A0), self.pT(A0 + 1)]
                aUf = aUg.rearrange("p e q -> p (e q)")
                self.mm(pacs[:, 0:512], ones, aUf[:, 0:512], True, True, [t("cst"), t("aU", g)], [self.pT(A0)])
                self.mm(pacs[:, 512:768], ones, aUf[:, 512:768], True, True, [t("cst"), t("aU", g)], [self.pT(A0 + 1)])
                pav = pacs.rearrange("p (e q) -> p e q", q=128)
                Dmg = self.Dm[:, hs, :]
                for j in range(6):
                    e = 6 * g + j
                    self.act(Dmg[:, j, :], pav[:, j, :], AF.Relu, [self.pT(A0 + j // 4)] + TS, [t("Dm", g)], scale=-1.0, bias=acs_tm[:, e:e + 1])
                self.act(Dmg, Dmg, AF.Exp, [t("Dm", g)], [t("Dm", g)], scale=-1.0)
                self.tt(self.GT[:, hs, :], Dmg, bc(self.CBm[:, g, :], [128, 6, 128], 1), ALU.mult, [t("Dm", g), t("CBm", g)], [t("GT", g)])
                self.act(self.Ebc[:, hs, :], pav, AF.Exp, pacsT, [t("Ebc", g)])
                cdg, tmpg, decg, dtdecg = cd[:, hs], tmp[:, hs], dec[:, hs], dtdec[:, hs]
                self.act(cdg, pav[:, :, 127], AF.Exp, pacsT, TSg)
                self.tt(tmpg, pav[:, :, 127], acs_tm[:, hs], ALU.subtract, pacsT + TS, TSg)
                self.act(decg, tmpg, AF.Exp, TSg, TSg)
                self.tt(dtdecg, dt_tm[:, hs], decg, ALU.mult, TS + TSg, TSg)
                self.tt(self.CE[:, hs, :], self.Ebc[:, hs, :], bc(self.CT[:, g, sl], [128, 6, 128], 1), ALU.mult,
                        [t("Ebc", g), t("CT", g)], [t("CE", g)])
                xs3 = self.xstm[:, cs_].rearrange("p (e d) -> p e d", d=64)
                g3 = lambda x: x[:, cs_].rearrange("p (e d) -> p e d", d=64)
                self.tt(g3(self.xdt), xs3, bc(dt_tm[:, hs], [128, 6, 64], 2), ALU.mult, [t("xstm", g)] + TS, [t("xdt", g)])
                self.tt(g3(self.xdd), xs3, bc(dtdecg, [128, 6, 64], 2), ALU.mult, [t("xstm", g)] + TSg, [t("xdd", g)])
                self.tt(g3(self.xsD), xs3, bc(self.Db[:, l, hs], [128, 6, 64], 2), ALU.mult, [t("xstm", g)] + PR, [t("xsD", g)])
                py = self.bank(YB)
                for j in range(6):
                    e = 6 * g + j
                    o = py[(e % 2) * 64:(e % 2) * 64 + 64, (j // 2) * 128:(j // 2) * 128 + 128]
                    es_ = slice(e * 64, (e + 1) * 64)
                    self.mm(o, self.xdt[:, es_], self.GT[:, e, :], True, False, [t("xdt", g), t("GT", g)], [self.pT(YB)])
                    self.mm(o, self.hTb[:, es_], self.CE[:, e, :], False, False, [t("hTb", g), t("CE", g)], [self.pT(YB)])
                    self.mm(o, self.xsD[:, es_], self.ident_bf[:], False, True, [t("xsD", g), t("ident_bf")], [self.pT(YB)])
                pst = self.bank(A0)[:, 0:384]
                self.mm(pst, self.Btm[:, g * 128:(g + 1) * 128], self.xdd[:, cs_], True, True, [t("Btm", g), t("xdd", g)], [self.pT(A0)])
                HT = [t("hT32", l, g)]
                h3 = self.hT32[:, l, cs_].rearrange("p (e d) -> p e d", d=64)
                self.tt(h3, h3, bc(cdg, [128, 6, 64], 2), ALU.mult, HT + TSg, HT)
                self.tt(self.hT32[:, l, cs_], self.hT32[:, l, cs_], pst, ALU.add, HT + [self.pT(A0)], HT)
                self.acopy(self.hTb[:, cs_], self.hT32[:, l, cs_], HT, [t("hTb", g)])
                self.stt(self.yg[:, 3 * g:3 * g + 3, :], py[:, 0:384].rearrange("p (c q) -> p c q", q=128), 0.5,
                         self.zs[:, 3 * g:3 * g + 3, sl], ALU.mult, ALU.mult,
                         [self.pT(YB)] + [t("zs", c) for c in range(3 * g, 3 * g + 3)], [t("yg", g)])
                self.act(self.sq[:, 3 * g:3 * g + 3, :], self.yg[:, 3 * g:3 * g + 3, :], AF.Square, [t("yg", g)], [t("sq", g)])
            ssb = b1[:, 384:512]
            for c in range(6):
                self.mm(ssb, ones, self.sq[:, c, :], c == 0, c == 5, [t("cst"), t("sq", c // 3)], [self.pT(1)])
            self.act(self.rstd[:], ssb, AF.Ln, [self.pT(1)], [t("rstd")], bias=768.0 * EPS)
            self.act(self.rstd[:], self.rstd[:], AF.Exp, [t("rstd")], [t("rstd")], scale=-0.5)
            for c in range(6):
                self.stt(self.mixT[:, 12 + c, sl], self.yg[:, c, :], self.gS[:, l, c:c + 1], self.rstd[:], ALU.mult, ALU.mult,
                         [t("yg", c // 3), t("rstd")] + PR, [t("mixT", 12 + c, tl)])
        if last:
            for hb in range(2):
                pb = self.bank(hb)
                for j in range(3):
                    c = hb * 3 + j
                    self.tr(pb[:, j * 128:(j + 1) * 128], self.hT32[:, l, c * 128:(c + 1) * 128], idf, [t("hT32", l, 0), t("hT32", l, 1), t("cst")], [self.pT(hb)])
                self.vcopy(self.iost[:, hb * 384:(hb + 1) * 384], pb[:, 0:384], [self.pT(hb)], [t("iost")])
            P.dma("pool", self.p_ssd_h[l].rearrange("(c p) n -> p c n", p=128), self.iost[:, 0:768].rearrange("p (c n) -> p c n", n=128),
                  reads=[t("iost")])

    def phase_out(self, l, p, nco=H, x32=None, xb=None, tag=""):
        P, t = self.P, self.P.t
        cst = self.cst
        idf = cst[:, K_ID:K_ID + 128]
        ones = cst[:, K_ONE:K_ONE + 128]
        PR = [t("par", l)]
        src = self.w_out_bf[l]
        x32 = self.xT32 if x32 is None else x32
        xb = self.xTb if xb is None else xb
        smp = (nco != H)
        XT = (lambda dc: t("SxT32", dc)) if smp else (lambda dc: t("xT32", dc))
        XB = (lambda dc: t("SxTb", dc)) if smp else (lambda dc: t("xTb", dc))
        MT = (lambda ec: [t("SmixT", ec)]) if smp else (lambda ec: [t("mixT", ec, tl) for tl in range(NT)])
        bk = lambda i: self.bank(i)[:, 0:nco]
        wqf = self.wqkv[:].rearrange("p a b -> p (a b)")
        for ec in range(18):
            if nco == H and ec % 7 != 6:
                j = ec % 7
                wv = wqf[:, j * 1024:(j + 1) * 1024]
                WT = t("wqkv", j)
            else:
                k = self.next_wbuf("all" if nco != H else "out")
                wv = self.wbuf[k][:].rearrange("p a b -> p (a b)")
                WT = t("wbuf", k)
            if ec < 4:
                R = 128
                pieces = [(ec * 64, 64, 0), ((ec + 4) * 64, 64, 64)]
            elif ec < 12:
                R = 96
                pieces = [(512 + (ec - 4) * 96, 96, 0)]
            else:
                R = 128
                pieces = [(1280 + (ec - 12) * 128, 128, 0)]
            for (r0, n, d0) in pieces:
                th = self.thr_tile()
                P.dma("sp", wv[d0:d0 + n, :], src[r0:r0 + n, :], reads=self.wout_T(l), writes=[WT, th])
                self.pace_cast(th)
            for dc in range(8):
                self.mm(bk(dc), wv[0:R, dc * 128:(dc + 1) * 128], self.mixT[0:R, ec, 0:nco], ec == 0, ec == 17,
                        [WT] + MT(ec), [self.pT(dc)])
        for dc in range(8):
            X = [XT(dc)]
            self.stt(x32[:, dc, :], x32[:, dc, :], ALPHA, bk(dc), ALU.mult, ALU.add, X + [self.pT(dc)], X)
        for dc in range(8):
            X = [XT(dc)]
            q = self.lnq[dc % 2]
            self.act(q[:, 0:nco], x32[:, dc, :], AF.Square, X, [t("lnq", dc % 2)])
            self.mm(bk(0), ones, x32[:, dc, :], dc == 0, dc == 7, [t("cst")] + X, [self.pT(0)])
            self.mm(bk(1), ones, q[:, 0:nco], dc == 0, dc == 7, [t("cst"), t("lnq", dc % 2)], [self.pT(1)])
        M, Rs = [t("mean")], [t("lrs")]
        mean, lrs = self.mean[:, 0:nco], self.lrs[:, 0:nco]
        self.ts(mean, bk(0), 1.0 / D, None, ALU.mult, None, [self.pT(0)], M)
        self.tt(lrs, mean, mean, ALU.mult, M, Rs)
        self.stt(lrs, bk(1), 1.0 / D, lrs, ALU.mult, ALU.subtract, [self.pT(1)] + Rs, Rs)
        self.act(lrs, lrs, AF.Sqrt, Rs, Rs, bias=EPS)
        P.op("dve", lambda e: e.reciprocal(out=lrs, in_=lrs), Rs, Rs)
        for dc in range(8):
            X = [XT(dc)]
            xv = x32[:, dc, :]
            self.tt(xv, xv, mean, ALU.subtract, X + M, X)
            self.tt(xv, xv, lrs, ALU.mult, X + Rs, X)
            self.act(xv, xv, AF.Identity, X + PR, X, scale=self.lnp[:, l, dc, 0:1], bias=self.lnp[:, l, dc, 1:2])
            if l < DEPTH - 1:
                self.acopy(xb[:, dc, 0:nco], xv, X, [XB(dc)])
        if l == 0 and p == 0:
            self.tap("x_l0p0", self.xT32[:], [t("xT32", dc) for dc in range(8)])
        if smp:
            if l == DEPTH - 1:
                pb = self.bank(2, 2)
                for dc in range(8):
                    self.tr(pb[0:NS, dc * 128:(dc + 1) * 128], x32[:, dc, :], idf, [XT(dc), t("cst")], [self.pT(2 + dc // 4)])
                self.vcopy(self.iost[0:NS, :], pb[0:NS, :], [self.pT(2), self.pT(3)], [t("iost")])
                P.dma("pool", self.y_sample, self.iost[0:NS, :], reads=[t("iost")])
        elif l == DEPTH - 1:
            for tl in range(NT):
                gt = p * NT + tl
                for hb in range(2):
                    pb = self.bank(2 + hb)
                    for j in range(4):
                        dc = hb * 4 + j
                        self.tr(pb[:, j * 128:(j + 1) * 128], self.xT32[:, dc, tl * 128:(tl + 1) * 128], idf,
                                [t("xT32", dc), t("cst")], [self.pT(2 + hb)])
                    self.vcopy(self.iost[:, hb * 512:(hb + 1) * 512], pb, [self.pT(2 + hb)], [t("iost")])
                P.dma("pool", self.y_prompt[gt * 128:(gt + 1) * 128, :], self.iost[:], reads=[t("iost")])

    def sample_job(self, l):
        P, t = self.P, self.P.t
        cst = self.cst
        idf = cst[:, K_ID:K_ID + 128]
        ones = cst[:, K_ONE:K_ONE + 128]
        i16 = idf[0:NS, 0:NS]
        PR = [t("par", l)]
        x32 = self.rstd[:].rearrange("p (a b) -> p a b", b=NS)
        xb = self.xTb
        XB = [t("SxTb", kc) for kc in range(8)]
        XT = [t("SxT32", dc) for dc in range(8)]
        Bt = lambda n: [t("B", n)]
        f2 = lambda ap: ap.rearrange("p a b -> p (a b)")

        def fm(k, M, out, bT):
            for kc in range(8):
                self.mm(out, self.wbuf[k][:, kc, 0:M], xb[:, kc, 0:NS], kc == 0, kc == 7, [t("wbuf", k)] + XB, [bT])

        def tm(k, M, out, bT):
            for kc in range(8):
                self.mm(out, xb[:, kc, 0:NS], self.wbuf[k][:, kc, 0:M], kc == 0, kc == 7, [t("wbuf", k)] + XB, [bT])

        if l == 0:
            P.dma("sp", self.iost[0:NS, :], self.x_sample, writes=[t("iost")])
            pb = self.bank(0)
            for dc in range(8):
                self.tr(pb[:, dc * NS:(dc + 1) * NS], self.iost[0:NS, dc * 128:(dc + 1) * 128], i16, [t("iost"), t("cst")], [self.pT(0)])
            self.vcopy(x32, pb[:, 0:128].rearrange("p (a b) -> p a b", b=NS), [self.pT(0)], XT)
            for dc in range(8):
                self.acopy(xb[:, dc, 0:NS], x32[:, dc, :], [XT[dc]], [XB[dc]])
        P.dma("pool", self.wab[0:96, :, :], self.lru_wa[l].rearrange("n c d -> c n d"), writes=[t("wab")])
        P.dma("pool", self.wxb[0:96, :, :], self.lru_wx[l].rearrange("n c d -> c n d"), writes=[t("wxb")])

        b0, b1 = self.bank(0), self.bank(1)
        for j in range(6):
            k = self.load_win(l, [(j * 128, 128, 0)])
            if j < 4:
                tm(k, 128, b0[0:NS, j * 128:(j + 1) * 128], self.pT(0))
            else:
                tm(k, 128, b1[0:NS, (j - 4) * 128:(j - 3) * 128], self.pT(1))
        q32 = self.qkv32[0]
        Q = Bt("q0")
        self.acopy(q32[0:NS, 0:512], b0[0:NS, :], [self.pT(0)], Q)
        self.acopy(q32[0:NS, 512:768], b1[0:NS, 0:256], [self.pT(1)], Q)
        hv = q32[0:NS, 0:640].rearrange("p (h d) -> p h d", d=64)
        x1, x2 = hv[:, :, 0:8], hv[:, :, 8:16]
        cs = bc(self.ropeS[0:NS, 0, :], [NS, 10, 8], 1)
        sn = bc(self.ropeS[0:NS, 1, :], [NS, 10, 8], 1)
        ns = bc(self.ropeS[0:NS, 2, :], [NS, 10, 8], 1)
        R = Bt("ropet")
        ra, rb = self.ropet[0:NS, 0, :, :], self.ropet[0:NS, 1, :, :]
        self.tt(rb[:, :, 0:8], x2, ns, ALU.mult, Q + [t("rope")], R)
        self.tt(rb[:, :, 8:16], x1, sn, ALU.mult, Q + [t("rope")], R)
        self.tt(ra[:, :, 0:8], x1, cs, ALU.mult, Q + [t("rope")], R)
        self.tt(ra[:, :, 8:16], x2, cs, ALU.mult, Q + [t("rope")], R)
        self.tt(hv[:, :, 0:16], ra, rb, ALU.add, R, Q)
        P.dma("pool", self.s_swa_k[l][:, 0:127, :], self.cache_k[l][:, 1:128, :])
        P.dma("pool", self.s_swa_v[l][:, 0:127, :], self.cache_v[l][:, 1:128, :])
        P.dma("pool", self.s_swa_k[l][:, 127, :], q32[0:NS, 512:640], reads=Q)
        P.dma("pool", self.s_swa_v[l][:, 127, :], q32[0:NS, 640:768], reads=Q)
        b2 = self.bank(2)
        for c in range(4):
            k = self.load_win(l, [(C_GA + c * 64, 64, 0), (C_GA + (c + 4) * 64, 64, 64)])
            fm(k, 128, b2[:, c * NS:(c + 1) * NS], self.pT(2))
        GsT = self.G[0][:, 0:64].rearrange("p (c b) -> p c b", b=NS)
        self.act(f2(GsT), b2[:, 0:64], AF.Silu, [self.pT(2)], Bt("G0"))
        kbf = self.qkbf[0:NS, 512:640]
        vnb = self.qkbf[0:NS, 0:128]
        self.acopy(kbf, q32[0:NS, 512:640], Q, Bt("qkbf"))
        self.acopy(vnb, q32[0:NS, 640:768], Q, Bt("qkbf"))
        qz = f2(self.CE[:])[0:NS, 0:1024].rearrange("p (h f) -> p h f", f=128)
        P.op("dve", lambda e: e.memset(f2(self.CE[:])[0:NS, 0:1024], 0.0), [], Bt("CE"))
        qh = q32[0:NS, 0:512].rearrange("p (h d) -> p h d", d=64)
        self.vcopy(qz[:, 0:4, 0:64], qh[:, 0:4, :], Q + Bt("CE"), Bt("CE"))
        self.vcopy(qz[:, 4:8, 64:128], qh[:, 4:8, :], Q + Bt("CE"), Bt("CE"))
        p3 = self.bankbf(3)
        ib16 = self.ident_bf[0:NS, 0:NS]
        for h in range(8):
            self.tr(p3[:, h * NS:(h + 1) * NS], qz[:, h, :], ib16, Bt("CE") + [t("ident_bf")], [self.pT(3)])
        self.tr(p3[:, 128:128 + NS], kbf, ib16, Bt("qkbf") + [t("ident_bf")], [self.pT(3)])
        qblk = self.PT[0][:, 0:128].rearrange("p (b h) -> p b h", h=8)
        knT = self.PT[0][:, 128:128 + NS]
        self.vcopy(qblk.rearrange("p b h -> p h b"), p3[:, 0:128].rearrange("p (h b) -> p h b", b=NS), [self.pT(3)], Bt("PT0"))
        self.vcopy(knT, p3[:, 128:128 + NS], [self.pT(3)], Bt("PT0"))
        st8 = self.wqkv[:].rearrange("p a b -> p (a b)").bitcast(F32)[:, 0:2048].rearrange("p (b f) -> p b f", f=128)
        P.dma("sp", st8, self.cache_k[l].rearrange("b k f -> k b f"), writes=Bt("st8"))
        for b in range(NS):
            self.tr(self.bank(4 + b // 4)[:, (b % 4) * 128:(b % 4 + 1) * 128], st8[:, b, :], idf, Bt("st8") + [t("cst")], [self.pT(4 + b // 4)])
        KcT = f2(self.xsT[:])[:, 0:2048].rearrange("p (b k) -> p b k", k=128)
        for i in range(4):
            self.acopy(f2(KcT[:, 4 * i:4 * i + 4, :]), self.bank(4 + i), [self.pT(4 + i)], Bt("xsT"))
        for b in range(NS):
            self.mm(self.bank(b // 4)[0:8, (b % 4) * 128:(b % 4 + 1) * 128], qblk[:, b, :], KcT[:, b, :], True, True,
                    Bt("PT0") + Bt("xsT"), [self.pT(b // 4)])
        b4 = self.bank(4)
        for b in range(NS):
            self.mm(b4[0:8, b:b + 1], qblk[:, b, :], knT[:, b:b + 1], True, True, Bt("PT0"), [self.pT(4)])
        stt_ = f2(self.tsm[:])[0:8, 0:96].rearrange("p (j b) -> p j b", b=NS)
        mx, negm, ssum, pnew, es, rden = (stt_[:, j, :] for j in range(6))
        A = Bt("tsm")
        S4 = self.ps[0:8, 0:2048].rearrange("p (b k) -> p b k", k=128)
        ST = [self.pT(i) for i in range(4)]
        P.op("dve", lambda e: e.reduce_max(out=mx, in_=S4, axis=AX.X), ST, A)
        self.tt(mx, mx, b4[0:8, 0:NS], ALU.max, A + [self.pT(4)], A)
        self.stt(negm, mx, -SCALE, self.sinkc[0:8, l, 1:2].to_broadcast([8, NS]), ALU.mult, ALU.min, A + PR, A)
        Pms = f2(self.Dm[:]).bitcast(BF16)[0:8, 0:2048]
        for b in range(NS):
            self.act(Pms[:, b * 128:(b + 1) * 128], S4[:, b, :], AF.Exp, [self.pT(b // 4)] + A, Bt("Dm") + A,
                     scale=SCALE, bias=negm[:, b:b + 1], accum_out=ssum[:, b:b + 1])
        self.stt(pnew, b4[0:8, 0:NS], SCALE, negm, ALU.mult, ALU.add, A + [self.pT(4)], A)
        self.act(pnew, pnew, AF.Exp, A, A)
        self.act(es, negm, AF.Exp, A + PR, A, bias=self.sinkc[0:8, l, 0:1])
        self.tt(ssum, ssum, pnew, ALU.add, A, A)
        self.tt(ssum, ssum, es, ALU.add, A, A)
        P.op("dve", lambda e: e.reciprocal(out=rden, in_=ssum), A, A)
        pv = Pms.rearrange("p (b k) -> p b k", k=128)
        self.tt(pv, pv, bc(rden, [8, NS, 128], 2), ALU.mult, Bt("Dm") + A, Bt("Dm"))
        self.tt(pnew, pnew, rden, ALU.mult, A, A)
        p5 = self.bankbf(5)
        for b in range(NS):
            self.tr(p5[:, b * 8:(b + 1) * 8], Pms[:, b * 128:(b + 1) * 128], self.ident_bf[0:8, 0:8], Bt("Dm") + [t("ident_bf")], [self.pT(5)])
        PTs = self.Pm[0][:, 0:128].rearrange("p (b h) -> p b h", h=8)
        self.acopy(f2(PTs), p5[:, 0:128], [self.pT(5)], Bt("Pm0"))
        b6 = self.bank(6)
        self.tr(b6[0:NS, 0:8], pnew, idf[0:8, 0:8], A + [t("cst")], [self.pT(6)])
        pnt = f2(self.ast[1][:])[0:NS, 0:8]
        self.vcopy(pnt, b6[0:NS, 0:8], [self.pT(6)], Bt("ast1"))
        psel = self.PT[1][0:NS, 0:128].rearrange("p (b h) -> p b h", h=8)
        self.tt(psel, bc(pnt, [NS, NS, 8], 1), bc(i16, [NS, NS, 8], 2), ALU.mult, Bt("ast1") + [t("cst")], Bt("PT1"))
        P.dma("sp", st8, self.cache_v[l].rearrange("b k f -> k b f"), writes=Bt("st8"))
        Vc = f2(self.qT[:]).rearrange("p (b f) -> p b f", f=128)
        self.vcopy(f2(Vc), f2(st8), Bt("st8"), Bt("qT"))
        b7 = self.bank(7)
        oT = b7[:, 0:128].rearrange("p (b h) -> p b h", h=8)
        for b in range(NS):
            self.mm(b7[:, b * 8:(b + 1) * 8], Vc[:, b, :], PTs[:, b, :], True, False, Bt("qT") + Bt("Pm0"), [self.pT(7)])
            self.mm(b7[:, b * 8:(b + 1) * 8], vnb, psel[:, b, :], False, True, Bt("qkbf") + Bt("PT1"), [self.pT(7)])
        for c in range(4):
            for w in range(2):
                rows = slice(w * 64, (w + 1) * 64)
                self.tt(self.mixT[rows, c, 0:NS], oT[rows, :, c + 4 * w], GsT[rows, c, :], ALU.mult,
                        [self.pT(7)] + Bt("G0"), [t("SmixT", c)])

        zf = f2(self.zs[:])
        stc = zf[0:NS, 0:2304]
        sth = zf[0:NS, 2304:3072]
        P.dma("sp", stc.rearrange("p (k f) -> p k f", f=768), self.st_lru_conv[l], writes=Bt("zs"))
        P.dma("sp", sth, self.st_lru_h[l], writes=Bt("zs"))
        P.dma("pool", self.s_lru_conv[l][:, 0:2, :], self.st_lru_conv[l][:, 1:3, :])
        b0, b1 = self.bank(0), self.bank(1)
        for n in range(8):
            for k3 in range(3):
                j = n * 3 + k3
                self.tr(b0[0:96, j * NS:(j + 1) * NS], stc[:, k3 * 768 + n * 96:k3 * 768 + (n + 1) * 96], i16, Bt("zs") + [t("cst")], [self.pT(0)])
            self.tr(b1[0:96, n * NS:(n + 1) * NS], sth[:, n * 96:(n + 1) * 96], i16, Bt("zs") + [t("cst")], [self.pT(1)])
        aUf = f2(self.aU[:])
        hs = aUf[0:96, 0:384].rearrange("p (n k b) -> p n k b", k=3, b=NS)
        h0T = aUf[0:96, 384:512].rearrange("p (n b) -> p n b", b=NS)
        self.vcopy(aUf[0:96, 0:384], b0[0:96, 0:384], [self.pT(0)], Bt("aU"))
        self.vcopy(aUf[0:96, 384:512], b1[0:96, 0:128], [self.pT(1)], Bt("aU"))
        b2, b5 = self.bank(2), self.bank(5)
        for n in range(8):
            kx = self.load_win(l, [(C_XL + n * 96, 96, 0)])
            fm(kx, 96, b2[0:96, n * NS:(n + 1) * NS], self.pT(2))
            tm(kx, 96, self.bank(3 + n // 4)[0:NS, (n % 4) * 128:(n % 4) * 128 + 96], self.pT(3 + n // 4))
            kg = self.load_win(l, [(C_GL + n * 96, 96, 0)])
            fm(kg, 96, b5[0:96, n * NS:(n + 1) * NS], self.pT(5))
        xltm = self.qkv32[1][0:NS, 0:768]
        self.vcopy(xltm.rearrange("p (n c) -> p n c", c=96),
                   self.ps[0:NS, 3 * 512:5 * 512].rearrange("p (n c) -> p n c", c=128)[:, :, 0:96], [self.pT(3), self.pT(4)], Bt("q1"))
        P.dma("pool", self.s_lru_conv[l][:, 2, :], xltm, reads=Bt("q1"))
        v3 = lambda ap: ap.rearrange("p (n b) -> p n b", b=NS)
        lpb = lambda j: bc(self.lp[0:96, l, :, j], [96, 8, NS], 2)
        acc = v3(self.xl[0:96, 0:128]); tmp = v3(self.xlb[0:96, 0:256].bitcast(F32))
        Xl, Tm = Bt("xl"), Bt("xlb")
        self.tt(acc, v3(b2[0:96, 0:128]), lpb(3), ALU.mult, [self.pT(2)] + PR, Xl)
        self.tt(acc, acc, lpb(4), ALU.add, Xl + PR, Xl)
        for k3 in range(3):
            self.tt(tmp, hs[:, :, k3, :], lpb(k3), ALU.mult, Bt("aU") + PR, Tm)
            self.tt(acc, acc, tmp, ALU.add, Xl + Tm, Xl)
        xlbf = self.dtt[0:96, 0:64].bitcast(BF16)
        self.acopy(xlbf, self.xl[0:96, 0:128], Xl, Bt("dtt"))
        b6, b7 = self.bank(6), self.bank(7)
        for n in range(8):
            self.mm(b6[0:96, n * NS:(n + 1) * NS], self.wab[0:96, n, :], xlbf[:, n * NS:(n + 1) * NS], True, True, [t("wab")] + Bt("dtt"), [self.pT(6)])
            self.mm(b7[0:96, n * NS:(n + 1) * NS], self.wxb[0:96, n, :], xlbf[:, n * NS:(n + 1) * NS], True, True, [t("wxb")] + Bt("dtt"), [self.pT(7)])
        rr, ig, EE = v3(self.rr[0:96, 0:128]), v3(self.ig[0:96, 0:128]), v3(self.EE[0:96, 0:128])
        Rr, Ig, Ee = Bt("rr"), Bt("ig"), Bt("EE")
        self.tt(rr, v3(b6[0:96, 0:128]), lpb(5), ALU.add, [self.pT(6)] + PR, Rr)
        self.act(rr, rr, AF.Sigmoid, Rr, Rr)
        self.tt(ig, v3(b7[0:96, 0:128]), lpb(6), ALU.add, [self.pT(7)] + PR, Ig)
        self.act(ig, ig, AF.Sigmoid, Ig, Ig)
        self.tt(EE, rr, bc(self.la[0:96, l, :, 1], [96, 8, NS], 2), ALU.mult, Rr + PR, Ee)
        self.act(EE, EE, AF.Exp, Ee, Ee, scale=2.0)
        self.act(EE, EE, AF.Sqrt, Ee, Ee, scale=-1.0, bias=1.0)
        self.tt(rr, rr, bc(self.la[0:96, l, :, 1], [96, 8, NS], 2), ALU.mult, Rr + PR, Rr)
        self.act(rr, rr, AF.Exp, Rr, Rr)
        self.tt(ig, ig, acc, ALU.mult, Ig + Xl, Ig)
        self.tt(ig, ig, EE, ALU.mult, Ig + Ee, Ig)
        self.tt(EE, rr, h0T, ALU.mult, Rr + Bt("aU"), Ee)
        self.tt(EE, EE, ig, ALU.add, Ee + Ig, Ee)
        for n in range(8):
            self.tr(self.bank(3 + n // 4)[0:NS, (n % 4) * 128:(n % 4) * 128 + 96], self.EE[0:96, n * NS:(n + 1) * NS], idf[0:96, 0:96],
                    Ee + [t("cst")], [self.pT(3 + n // 4)])
        h1tm = f2(self.yg[:])[0:NS, 0:768]
        self.vcopy(h1tm.rearrange("p (n c) -> p n c", c=96),
                   self.ps[0:NS, 3 * 512:5 * 512].rearrange("p (n c) -> p n c", c=128)[:, :, 0:96], [self.pT(3), self.pT(4)], Bt("yg"))
        P.dma("pool", self.s_lru_h[l], h1tm, reads=Bt("yg"))
        sg = v3(self.dtT[0:96, 0:128])
        self.act(sg, v3(b5[0:96, 0:128]), AF.Silu, [self.pT(5)], Bt("dtT"))
        self.tt(self.mixT[0:96, 4:12, 0:NS], EE, sg, ALU.mult, Ee + Bt("dtT"), [t("SmixT", 4 + n) for n in range(8)])

        Dmf = f2(self.Dm[:])
        srcs = (zf[0:NS, 0:1280], zf[0:NS, 1280:2560], Dmf[0:NS, 0:1280])
        srcT = (Bt("zs"), Bt("zs"), Bt("Dm"))
        for k3 in range(3):
            P.dma("sp", srcs[k3], self.st_ssd_conv[l][:, k3, :], writes=srcT[k3])
        P.dma("pool", self.s_ssd_conv[l][:, 0:2, :], self.st_ssd_conv[l][:, 1:3, :])
        b0 = self.bank(0)
        for c in range(10):
            for k3 in range(3):
                j = c * 3 + k3
                self.tr(b0[:, j * NS:(j + 1) * NS], srcs[k3][:, c * 128:(c + 1) * 128], i16, srcT[k3] + [t("cst")], [self.pT(0)])
        sqf = f2(self.sq[:])
        hss = sqf[:, 0:480].rearrange("p (c k b) -> p c k b", k=3, b=NS)
        self.vcopy(sqf[:, 0:480], b0[:, 0:480], [self.pT(0)], Bt("sq"))
        b1 = self.bank(1)
        for c in range(10):
            k = self.load_win(l, [(C_XBC + c * 128, 128, 0)])
            fm(k, 128, b1[:, c * NS:(c + 1) * NS], self.pT(1))
            tm(k, 128, self.bank(2 + c // 4)[0:NS, (c % 4) * 128:(c % 4 + 1) * 128], self.pT(2 + c // 4))
        xbtm = aUf[0:NS, 0:1280]
        self.vcopy(xbtm, self.ps[0:NS, 2 * 512:2 * 512 + 1280], [self.pT(2), self.pT(3), self.pT(4)], Bt("aU"))
        P.dma("pool", self.s_ssd_conv[l][:, 2, :], xbtm, reads=Bt("aU"))
        b5, b6 = self.bank(5), self.bank(6)
        for c in range(6):
            k = self.load_win(l, [(C_Z + c * 128, 128, 0)])
            fm(k, 128, b5[:, c * NS:(c + 1) * NS], self.pT(5))
        k = self.load_win(l, [(C_DT, 12, 0)])
        fm(k, 12, b6[0:12, 0:NS], self.pT(6))
        spb = lambda j: bc(self.spp[:, l, :, j], [128, 10, NS], 2)
        acc = v3(self.xl[:, 0:160]); tmp = v3(self.xlb[:, 0:320].bitcast(F32))
        self.tt(acc, v3(b1[:, 0:160]), spb(3), ALU.mult, [self.pT(1)] + PR, Xl)
        self.tt(acc, acc, spb(4), ALU.add, Xl + PR, Xl)
        for k3 in range(3):
            self.tt(tmp, hss[:, :, k3, :], spb(k3), ALU.mult, Bt("sq") + PR, Tm)
            self.tt(acc, acc, tmp, ALU.add, Xl + Tm, Xl)
        xc = v3(self.rr[:, 0:160])
        self.act(xc, acc, AF.Silu, Xl, Rr, scale=2.0)
        xsTs, BsT, CsT = xc[:, 0:6, :], xc[:, 6:8, :], xc[:, 8:10, :]
        zsT = v3(self.ig[:, 0:96])
        self.act(zsT, v3(b5[:, 0:96]), AF.Silu, [self.pT(5)], Ig)
        u, v = self.dtT[0:12, 0:NS], self.dtt[0:12, 0:NS]
        self.act(u, b6[0:12, 0:NS], AF.Identity, [self.pT(6)] + PR, Bt("dtT"), bias=self.dtb[0:12, l:l + 1])
        self.act(v, u, AF.Abs, Bt("dtT"), Bt("dtt"))
        self.act(v, v, AF.Exp, Bt("dtt"), Bt("dtt"), scale=-1.0)
        self.act(v, v, AF.Ln, Bt("dtt"), Bt("dtt"), bias=1.0)
        self.stt(u, u, 0.0, v, ALU.max, ALU.add, Bt("dtT") + Bt("dtt"), Bt("dtT"))
        Ex = cst[0:12, K_EX:K_EX + 768]
        b7 = self.bank(7)
        for c in range(6):
            Exc = Ex[:, c * 128:(c + 1) * 128]
            self.mm(b7[:, c * NS:(c + 1) * NS], Exc, u, True, True, [t("cst")] + Bt("dtT"), [self.pT(7)])
            self.mm(b7[:, 96 + c:97 + c], Exc, self.adc[0:12, l, 0:1], True, True, [t("cst")] + PR, [self.pT(7)])
            self.mm(b7[:, 104 + c:105 + c], Exc, self.adc[0:12, l, 1:2], True, True, [t("cst")] + PR, [self.pT(7)])
        ex = self.EE[:, 0:112]
        self.vcopy(ex, b7[:, 0:112], [self.pT(7)], Ee)
        dtx = v3(ex[:, 0:96]); Ax = ex[:, 96:102]; Dx = ex[:, 104:110]
        decT = v3(self.mean[:, 0:96]); x0T = v3(self.lrs[:, 0:96])
        self.tt(decT, dtx, bc(Ax, [128, 6, NS], 2), ALU.mult, Ee, [t("mean")])
        self.act(decT, decT, AF.Exp, [t("mean")], [t("mean")])
        self.tt(x0T, xsTs, dtx, ALU.mult, Rr + Ee, [t("lrs")])
        yT = v3(self.lnq[0][:, 0:96])
        hbuf = [f2(self.xT32[:])[:, i * 2048:(i + 1) * 2048].rearrange("p (b n) -> p b n", n=128) for i in range(2)]
        st8v = st8
        Bbc = self.ps[:, 0:2048].rearrange("p (b n) -> p b n", n=128)
        Cbc = self.ps[:, 2048:4096].rearrange("p (b n) -> p b n", n=128)
        idb = bc(idf, [128, NS, 128], 1)
        for g in range(2):
            self.tt(st8v, bc(BsT[:, g, :], [128, NS, 128], 2), idb, ALU.mult, Rr + [t("cst")], Bt("st8"))
            for j in range(4):
                self.mm(self.bank(j), ones, f2(st8v)[:, j * 512:(j + 1) * 512], True, True, [t("cst")] + Bt("st8"), [self.pT(j)])
            self.tt(st8v, bc(CsT[:, g, :], [128, NS, 128], 2), idb, ALU.mult, Rr + [t("cst")], Bt("st8"))
            for j in range(4):
                self.mm(self.bank(4 + j), ones, f2(st8v)[:, j * 512:(j + 1) * 512], True, True, [t("cst")] + Bt("st8"), [self.pT(4 + j)])
            for c in range(3 * g, 3 * g + 3):
                hb = hbuf[c % 2]
                Hb = [t("B", "hb", c % 2)]
                P.dma("sp", hb, self.st_ssd_h[l][:, 2 * c:2 * c + 2].rearrange("b e p n -> (e p) b n"), writes=Hb)
                self.tt(hb, hb, bc(decT[:, c, :], [128, NS, 128], 2), ALU.mult, Hb + [t("mean")], Hb)
                self.tt(st8v, Bbc, bc(x0T[:, c, :], [128, NS, 128], 2), ALU.mult, [self.pT(j) for j in range(4)] + [t("lrs")], Bt("st8"))
                self.tt(hb, hb, st8v, ALU.add, Hb + Bt("st8"), Hb)
                P.dma("pool", self.s_ssd_h[l][:, c * 128:(c + 1) * 128, :].rearrange("b f n -> f b n"), hb, reads=Hb)
                self.tt(st8v, hb, Cbc, ALU.mult, Hb + [self.pT(4 + j) for j in range(4)], Bt("st8"))
                P.op("dve", lambda e, c=c: e.reduce_sum(out=yT[:, c, :], in_=st8v, axis=AX.X), Bt("st8"), [t("lnq", 0)])
        self.tt(tmp[:, 0:6, :], xsTs, bc(Dx, [128, 6, NS], 2), ALU.mult, Rr + Ee, Tm)
        self.tt(yT, yT, tmp[:, 0:6, :], ALU.add, [t("lnq", 0)] + Tm, [t("lnq", 0)])
        self.tt(yT, yT, zsT, ALU.mult, [t("lnq", 0)] + Ig, [t("lnq", 0)])
        sqs = v3(self.lnq[1][:, 0:96])
        self.act(sqs, yT, AF.Square, [t("lnq", 0)], [t("lnq", 1)])
        b0 = self.bank(0)
        for c in range(6):
            self.mm(b0[:, 0:NS], ones, sqs[:, c, :], c == 0, c == 5, [t("cst"), t("lnq", 1)], [self.pT(0)])
        rs = self.dtt[:, 64:64 + NS]
        self.act(rs, b0[:, 0:NS], AF.Sqrt, [self.pT(0)], Bt("dtt"), bias=768.0 * EPS)
        P.op("dve", lambda e: e.reciprocal(out=rs, in_=rs), Bt("dtt"), Bt("dtt"))
        for c in range(6):
            self.stt(self.mixT[:, 12 + c, 0:NS], yT[:, c, :], self.gS[:, l, c:c + 1], rs, ALU.mult, ALU.mult,
                     [t("lnq", 0)] + Bt("dtt") + PR, [t("SmixT", 12 + c)])

        self.phase_out(l, None, nco=NS, x32=x32, xb=self.xTb)


    def build(self, sample=True, n_pass=NPASS, n_layers=DEPTH):
        self.declare_io()
        self.alloc()
        self.prologue()
        for p in range(n_pass):
            for l in range(n_layers):
                self.job(l, p)
        for l in range(DEPTH):
            self.flush_casts(l)
        if sample:
            self.P.fence()
            for l in range(n_layers):
                self.sample_job(l)
        self.P.emit()
        self.st.close()
        return self.nc


def make_consts():
    c = np.zeros((128, K_END), np.float32)
    c[:, K_ID:K_ID + 128] = np.eye(128, dtype=np.float32)
    s = np.arange(128)[:, None]
    q = np.arange(128)[None, :]
    c[:, K_U:K_U + 128] = (s <= q).astype(np.float32)
    c[:, K_ONE:K_ONE + 128] = 1.0
    i = np.arange(128)[:, None]
    j = np.arange(256)[None, :]
    band = (j >= i) & (j <= i + 128)
    c[:, K_MG:K_MG + 256] = np.where(band, 0.0, NEG)
    c[:, K_MF:K_MF + 256] = np.where(band & (j >= 128), 0.0, NEG)
    c[:, K_POS:K_POS + 16] = np.arange(128)[:, None] + 128.0 * np.arange(16)[None, :]
    c[:, K_INV:K_INV + 8] = (500000.0 ** (-np.arange(8, dtype=np.float32) / 8.0)).astype(np.float32)[None, :]
    for e in range(12):
        c[e, K_EX + e * 64:K_EX + (e + 1) * 64] = 1.0
    c[:, K_HM] = (np.arange(128) < 4)
    c[:, K_HM + 1] = (np.arange(128) >= 4)
    return c


WNAMES = ["w_in", "w_out", "att_sinks", "lru_conv_w", "lru_conv_b", "lru_wa", "lru_ba", "lru_wx", "lru_bx",
          "lru_lambda", "ssd_conv_w", "ssd_conv_b", "ssd_dt_bias", "ssd_a_log", "ssd_d", "ssd_norm_g", "ln_g", "ln_b"]


def make_in_maps(inputs, n=8):
    f = lambda a: np.ascontiguousarray(np.asarray(a, dtype=np.float32))
    shared = {k: f(inputs[k]) for k in WNAMES}
    shared["consts"] = make_consts()
    maps = []
    for i in range(n):
        s = slice(i * NS, (i + 1) * NS)
        m = dict(shared)
        m["x_prompt"] = f(inputs["x_prompt"][i])
        m["x_sample"] = f(np.asarray(inputs["x_sample"])[s, 0, :])
        m["cache_swa_k"] = f(np.asarray(inputs["cache_swa_k"])[:, s].reshape(DEPTH, NS, 128, 128))
        m["cache_swa_v"] = f(np.asarray(inputs["cache_swa_v"])[:, s].reshape(DEPTH, NS, 128, 128))
        m["state_lru_conv"] = f(np.asarray(inputs["state_lru_conv"])[:, s])
        m["state_lru_h"] = f(np.asarray(inputs["state_lru_h"])[:, s])
        m["state_ssd_conv"] = f(np.asarray(inputs["state_ssd_conv"])[:, s])
        m["state_ssd_h"] = f(np.asarray(inputs["state_ssd_h"])[:, s])
        maps.append(m)
    return maps


def kernel(**inputs):
    nc = Builder().build()
    res = run_bass_kernel_spmd(nc, make_in_maps(inputs), core_ids=list(range(8)))
    R = res.results
    cat = lambda k, ax: np.concatenate([np.asarray(r[k]) for r in R], axis=ax)
    stk = lambda k: np.stack([np.asarray(r[k]) for r in R], axis=1)
    y_prompt = np.stack([np.asarray(r["y_prompt"]) for r in R], 0)
    y_sample = cat("y_sample", 0).reshape(128, 1, D)
    p_swa_k = stk("p_swa_k").reshape(DEPTH, 8, 128, 2, 64)
    p_swa_v = stk("p_swa_v").reshape(DEPTH, 8, 128, 2, 64)
    p_lru_conv = stk("p_lru_conv")
    p_lru_h = stk("p_lru_h")
    p_ssd_conv = stk("p_ssd_conv")
    p_ssd_h = stk("p_ssd_h").reshape(DEPTH, 8, 12, 64, 128)
    s_swa_k = cat("s_swa_k", 1).reshape(DEPTH, 128, 128, 2, 64)
    s_swa_v = cat("s_swa_v", 1).reshape(DEPTH, 128, 128, 2, 64)
    s_lru_conv = cat("s_lru_conv", 1)
    s_lru_h = cat("s_lru_h", 1)
    s_ssd_conv = cat("s_ssd_conv", 1)
    s_ssd_h = cat("s_ssd_h", 1).reshape(DEPTH, 128, 12, 64, 128)
    outs = (y_prompt, y_sample, p_swa_k, p_swa_v, p_lru_conv, p_lru_h, p_ssd_conv, p_ssd_h,
            s_swa_k, s_swa_v, s_lru_conv, s_lru_h, s_ssd_conv, s_ssd_h)
    return tuple(np.ascontiguousarray(o, dtype=np.float32) for o in outs)
```
